# Optimizing a Trainium2 kernel written in Bass

```python
import jax, jax.numpy as jnp
from jax import lax
import numpy as np

D_MODEL = 1024
BATCH = 32
SEQ = 2048
DEPTH = 1

NORM_EPS = 1e-6
HG_HEADS = 4
HG_DK = 128
HG_DV = 128
HG_F = HG_HEADS * HG_DK
HG_WIDTH = HG_HEADS * HG_DV
HG_CHUNK = 64
MLA_HEADS = 4
MLA_NOPE = 128
MLA_ROPE = 64
MLA_V = 128
MLA_QK = MLA_NOPE + MLA_ROPE
Q_LORA = 384
KV_LORA = 256
MLA_WIDTH = MLA_HEADS * MLA_V
ROPE_THETA = 10000.0
Q_BLOCK = 128
D_MIX = HG_WIDTH + MLA_WIDTH
D_FF = 4 * D_MODEL
IN_SIZES = (HG_F, HG_F, HG_F, HG_WIDTH, HG_WIDTH, Q_LORA, KV_LORA, MLA_ROPE)
D_IN = HG_F * 3 + HG_WIDTH * 2 + Q_LORA + KV_LORA + MLA_ROPE

kernel_name = 'hymba_hgrn2_mla_sqrelu_encoder'


def rms_norm(t, gain):
    tf = t.astype(jnp.float32)
    y = tf * lax.rsqrt(jnp.mean(tf * tf, axis=-1, keepdims=True) + NORM_EPS)
    return (y * gain.astype(jnp.float32)).astype(t.dtype)


def apply_rope(t, cos, sin):
    half = t.shape[-1] // 2
    t1, t2 = t[..., :half], t[..., half:]
    cos = cos.astype(t.dtype)
    sin = sin.astype(t.dtype)
    return jnp.concatenate([t1 * cos - t2 * sin, t2 * cos + t1 * sin], axis=-1)


def rope_tables(positions):
    inv_freq = ROPE_THETA ** (-jnp.arange(0, MLA_ROPE, 2, dtype=jnp.float32) / MLA_ROPE)
    ang = positions.astype(jnp.float32)[..., None] * inv_freq
    return jnp.cos(ang), jnp.sin(ang)


def hgrn2_direction(q, k, v, log_f):
    b_, s_, h_, dk = q.shape
    dv = v.shape[-1]
    nc = s_ // HG_CHUNK

    def to_chunks(t):
        return t.reshape(b_, nc, HG_CHUNK, h_, t.shape[-1]).transpose(1, 0, 3, 2, 4)

    order_mask = jnp.tril(jnp.ones((HG_CHUNK, HG_CHUNK), dtype=bool))

    def step(state, inp):
        q_c, k_c, v_c, g_c = inp
        b = jnp.cumsum(g_c, axis=2)
        o_inter = jnp.einsum('bhtd,bhde->bhte', q_c * jnp.exp(b), state)
        diff = b[:, :, :, None, :] - b[:, :, None, :, :]
        decay = jnp.exp(jnp.where(order_mask[:, :, None], diff, -jnp.inf))
        scores = jnp.einsum('bhtsd,bhsd->bhts', q_c[:, :, :, None, :] * decay, k_c)
        o_intra = jnp.einsum('bhts,bhse->bhte', scores, v_c)
        b_last = b[:, :, -1:, :]
        new_state = (jnp.exp(b_last[:, :, 0, :, None]) * state
                     + jnp.einsum('bhsd,bhse->bhde', k_c * jnp.exp(b_last - b), v_c))
        return new_state, o_inter + o_intra

    state0 = jnp.zeros((b_, h_, dk, dv), jnp.float32)
    _, o = lax.scan(step, state0, (to_chunks(q), to_chunks(k), to_chunks(v), to_chunks(log_f)))
    return o.transpose(1, 0, 3, 2, 4).reshape(b_, s_, h_, dv)


def hgrn2_group(q_lin, ff_lin, fb_lin, i_lin, g_lin, lb, out_gain):
    b_, s_, _ = q_lin.shape
    dtype = q_lin.dtype
    heads_k = lambda t: t.astype(jnp.float32).reshape(b_, s_, HG_HEADS, HG_DK)
    q = heads_k(q_lin)
    v = i_lin.astype(jnp.float32).reshape(b_, s_, HG_HEADS, HG_DV)

    def gate(logit, lb_dir):
        f = lb_dir + (1.0 - lb_dir) * jax.nn.sigmoid(logit.astype(jnp.float32))
        return heads_k(jnp.log(f)), heads_k(1.0 - f)

    logf_f, k_f = gate(ff_lin, lb[0])
    logf_b, k_b = gate(fb_lin, lb[1])
    flip = lambda t: jnp.flip(t, axis=1)
    o_fwd = hgrn2_direction(q, k_f, v, logf_f)
    o_bwd = flip(hgrn2_direction(flip(q), flip(k_b), flip(v), flip(logf_b)))
    o = rms_norm(o_fwd + o_bwd, out_gain)
    o = o.reshape(b_, s_, HG_WIDTH) * jax.nn.silu(g_lin.astype(jnp.float32))
    return o.astype(dtype)


def blocked_attention(q, k, v):
    b_, s_, h_, d_ = q.shape
    nb = s_ // Q_BLOCK
    qb = (q * (MLA_QK ** -0.5)).reshape(b_, nb, Q_BLOCK, h_, d_).transpose(1, 0, 2, 3, 4)

    def one_block(qi):
        s = jnp.einsum('bqhd,bkhd->bhqk', qi, k).astype(jnp.float32)
        p = jax.nn.softmax(s, axis=-1).astype(v.dtype)
        return jnp.einsum('bhqk,bkhe->bqhe', p, v)

    o = lax.map(one_block, qb)
    return o.transpose(1, 0, 2, 3, 4).reshape(b_, s_, h_, v.shape[-1])


def mla_group(c_q, c_kv, k_rope, cos, sin, g_cq, w_q_up, g_ckv, w_kv_up, g_q_norm, g_k_norm, g_out):
    b_, s_, _ = c_q.shape
    q = (rms_norm(c_q, g_cq) @ w_q_up).reshape(b_, s_, MLA_HEADS, MLA_QK)
    kv = (rms_norm(c_kv, g_ckv) @ w_kv_up).reshape(b_, s_, MLA_HEADS, MLA_NOPE + MLA_V)
    k_nope, v = kv[..., :MLA_NOPE], kv[..., MLA_NOPE:]
    k_pe = jnp.broadcast_to(k_rope[:, :, None, :], (b_, s_, MLA_HEADS, MLA_ROPE))
    k = jnp.concatenate([k_nope, k_pe], axis=-1)
    q = rms_norm(q, g_q_norm)
    k = rms_norm(k, g_k_norm)
    cos_h, sin_h = cos[:, :, None, :], sin[:, :, None, :]
    q = jnp.concatenate([q[..., :MLA_NOPE], apply_rope(q[..., MLA_NOPE:], cos_h, sin_h)], axis=-1)
    k = jnp.concatenate([k[..., :MLA_NOPE], apply_rope(k[..., MLA_NOPE:], cos_h, sin_h)], axis=-1)
    o = blocked_attention(q, k, v).reshape(b_, s_, MLA_WIDTH)
    return rms_norm(o, g_out)


def setup_inputs(seed: int = 0) -> dict:
    key = jax.random.key(seed)
    ks = jax.random.split(key, 20)
    f32 = jnp.float32
    nrm = lambda k, shape, fan_in: jax.random.normal(k, shape, f32) * (fan_in ** -0.5)
    gain = lambda k, shape: 1.0 + 0.1 * jax.random.normal(k, shape, f32)
    x = jax.random.normal(ks[0], (BATCH, SEQ, D_MODEL), f32)
    offsets = jax.random.randint(ks[1], (BATCH, 1), 0, 512, dtype=jnp.int32)
    positions = jnp.arange(SEQ, dtype=jnp.int32)[None, :] + offsets
    return {
        'x': x,
        'positions': positions,
        'g_mix_norm': gain(ks[2], (DEPTH, D_MODEL)),
        'w_in': nrm(ks[3], (DEPTH, D_MODEL, D_IN), D_MODEL),
        'lb_param': jax.random.normal(ks[4], (2, DEPTH + 1, HG_F), f32),
        'g_hgrn_out': gain(ks[5], (DEPTH, HG_HEADS, HG_DV)),
        'g_cq': gain(ks[6], (DEPTH, Q_LORA)),
        'w_q_up': nrm(ks[7], (DEPTH, Q_LORA, MLA_HEADS * MLA_QK), Q_LORA),
        'g_ckv': gain(ks[8], (DEPTH, KV_LORA)),
        'w_kv_up': nrm(ks[9], (DEPTH, KV_LORA, MLA_HEADS * (MLA_NOPE + MLA_V)), KV_LORA),
        'g_q_norm': gain(ks[10], (DEPTH, MLA_QK)),
        'g_k_norm': gain(ks[11], (DEPTH, MLA_QK)),
        'g_mla_out': gain(ks[12], (DEPTH, MLA_WIDTH)),
        'w_out': nrm(ks[13], (DEPTH, D_MIX, D_MODEL), D_MIX),
        'g_ffn_norm': gain(ks[14], (DEPTH, D_MODEL)),
        'w_up': nrm(ks[15], (DEPTH, D_MODEL, D_FF), D_MODEL),
        'w_down': nrm(ks[16], (DEPTH, D_FF, D_MODEL), D_FF),
    }


def reference(x, positions, g_mix_norm, w_in, lb_param, g_hgrn_out, g_cq, w_q_up, g_ckv,
              w_kv_up, g_q_norm, g_k_norm, g_mla_out, w_out, g_ffn_norm, w_up, w_down):
    cos, sin = rope_tables(positions)
    lb_table = jnp.cumsum(jax.nn.softmax(lb_param.astype(jnp.float32), axis=1), axis=1)
    for layer in range(DEPTH):
        h = rms_norm(x, g_mix_norm[layer])
        proj = h @ w_in[layer]
        parts = []
        start = 0
        for size in IN_SIZES:
            parts.append(proj[..., start:start + size])
            start += size
        q_hg, ff_hg, fb_hg, i_hg, g_hg, c_q, c_kv, k_rope = parts
        o_a = hgrn2_group(q_hg, ff_hg, fb_hg, i_hg, g_hg, lb_table[:, layer], g_hgrn_out[layer])
        o_b = mla_group(c_q, c_kv, k_rope, cos, sin, g_cq[layer], w_q_up[layer], g_ckv[layer],
                        w_kv_up[layer], g_q_norm[layer], g_k_norm[layer], g_mla_out[layer])
        x = x + jnp.concatenate([o_a, o_b], axis=-1) @ w_out[layer]
        hf = rms_norm(x, g_ffn_norm[layer]) @ w_up[layer]
        x = x + jnp.square(jax.nn.relu(hf)) @ w_down[layer]
    return x
```

```python
import numpy as np
from contextlib import ExitStack
import concourse.bass as bass
import concourse.mybir as mybir
from concourse.bass_utils import run_bass_kernel_spmd

F32 = mybir.dt.float32
BF16 = mybir.dt.bfloat16
I32 = mybir.dt.int32
AF = mybir.ActivationFunctionType
ALU = mybir.AluOpType
AX = mybir.AxisListType

NCORES = 8
SEQ = 2048
NT = SEQ // 128
D = 1024
DIN = 3264
DFF = 4096
EPS = 1e-6
PI = float(np.pi)
ARENA_BYTES = 212480

ENGS = ("pe", "act", "dve", "pool", "sp")


def _cody_waite():
    two_pi = 2.0 * np.pi
    c1 = 6.28125
    r1 = two_pi - c1
    m, e = np.frexp(r1)
    c2 = float(np.ldexp(np.round(m * 2 ** 11) / 2 ** 11, e))
    c3 = float(np.float32(two_pi - c1 - c2))
    return c1, c2, c3


CW1, CW2, CW3 = _cody_waite()


class _Rec:
    def __getattr__(self, name):
        def f(*a, **k):
            self.call = (name, a, k)
            return self
        return f


class Sched:
    def __init__(self):
        self.q = {e: [] for e in ENGS}
        self.cnt = {e: 0 for e in ENGS}
        self.seen = {e: {} for e in ENGS}
        self.bufs = {}
        self.dma_tot = {}
        self.sem_names = set(ENGS)

    def _st(self, k):
        st = self.bufs.get(k)
        if st is None:
            st = self.bufs[k] = {"w": None, "r": {}}
        return st

    def _deps(self, eng, reads, writes):
        toks = []
        for k in reads:
            st = self._st(k)
            if st["w"] is not None:
                toks.append(st["w"])
        for k in writes:
            st = self._st(k)
            if st["w"] is not None:
                toks.append(st["w"])
            toks.extend(st["r"].items())
        waits = {}
        for (s, v) in toks:
            if s == "pe" and eng == "pe":
                continue
            if self.seen[eng].get(s, 0) >= v:
                continue
            if waits.get(s, 0) < v:
                waits[s] = v
        for s, v in waits.items():
            self.seen[eng][s] = v
        return list(waits.items())

    def _commit(self, tok, reads, writes):
        for k in reads:
            r = self._st(k)["r"]
            if r.get(tok[0], 0) < tok[1]:
                r[tok[0]] = tok[1]
        for k in writes:
            st = self._st(k)
            st["w"] = tok
            st["r"] = {}

    def op(self, eng, fn, reads=(), writes=()):
        rec = _Rec()
        fn(rec)
        waits = self._deps(eng, reads, writes)
        self.cnt[eng] += 1
        tok = (eng, self.cnt[eng])
        self.q[eng].append((rec.call, waits, (eng, 1)))
        self._commit(tok, reads, writes)

    def dma(self, eng, fn, sem, reads=(), writes=()):
        rec = _Rec()
        fn(rec)
        self.sem_names.add(sem)
        waits = self._deps(eng, reads, writes)
        self.dma_tot[sem] = self.dma_tot.get(sem, 0) + 16
        tok = (sem, self.dma_tot[sem])
        self.q[eng].append((rec.call, waits, (sem, 16)))
        self._commit(tok, reads, writes)

    def barrier(self):
        for e in ENGS:
            waits = []
            for e2 in ENGS:
                if self.cnt[e2] > self.seen[e].get(e2, 0):
                    waits.append((e2, self.cnt[e2]))
                    self.seen[e][e2] = self.cnt[e2]
            for s, tot in self.dma_tot.items():
                if tot > self.seen[e].get(s, 0):
                    waits.append((s, tot))
                    self.seen[e][s] = tot
            self.q[e].append((None, waits, None))
        self.bufs = {}

    def emit(self, block, sems):
        handles = {"pe": block.tensor, "act": block.scalar, "dve": block.vector,
                   "pool": block.gpsimd, "sp": block.sync}

        def mk(e):
            ops = self.q[e]

            def body(engine):
                for fn, waits, inc in ops:
                    for s, v in waits:
                        engine.wait_ge(sems[s], v)
                    if fn is not None:
                        name, a, k = fn
                        getattr(engine, name)(*a, **k).then_inc(sems[inc[0]], inc[1])
            return body

        for e in ENGS:
            handles[e](mk(e))


def _dsize(dt):
    return 2 if dt == BF16 else 4


class Bump:
    def __init__(self, arena, start, end):
        self.arena, self.cur, self.end = arena, start, end

    def take(self, shape, dt):
        n = int(np.prod(shape)) * _dsize(dt)
        off = (self.cur + 63) // 64 * 64
        n4 = (n + 3) // 4 * 4
        self.cur = off + n4
        assert self.cur <= self.end, (self.cur, self.end)
        ap = self.arena[:, off // 4:(off + n4) // 4]
        if dt != F32:
            ap = ap.bitcast(dt)
        if n4 != n:
            ap = ap[:, 0:int(np.prod(shape))]
        if len(shape) == 2:
            ap = ap.rearrange("p (a b) -> p a b", b=shape[1])
        elif len(shape) == 3:
            ap = ap.rearrange("p (a b c) -> p a b c", b=shape[1], c=shape[2])
        return ap


def build(nseq=4, dbg=None):
    nc = bass.Bass("TRN2", target_bir_lowering=False)
    S = Sched()
    dbg_out = {}

    def din(name, shape, dt=F32):
        return nc.dram_tensor(name, list(shape), dt, kind="ExternalInput").ap()

    x = din("x", [nseq, SEQ, D])
    pos = din("pos", [nseq, SEQ], I32)
    g_mix = din("g_mix", [1, D])
    w_in = din("w_in", [D, DIN])
    lb_param = din("lb_param", [2, 2, 512])
    g_hg = din("g_hg", [4, 128])
    g_cq = din("g_cq", [1, 384])
    wq = din("wq", [384, 768])
    g_ckv = din("g_ckv", [1, 256])
    wkv = din("wkv", [256, 1024])
    g_q = din("g_q", [1, 192])
    g_k = din("g_k", [1, 192])
    g_mo = din("g_mo", [1, 512])
    w_out = din("w_out", [D, D])
    g_ffn = din("g_ffn", [1, D])
    w_up = din("w_up", [D, DFF])
    w_down = din("w_down", [DFF, D])
    invf = din("invf", [1, 32])
    y = nc.dram_tensor("y", [nseq, SEQ, D], F32, kind="ExternalOutput").ap()
    x1s = nc.dram_tensor("x1s", [nseq, SEQ, D], F32).ap()

    es = ExitStack()
    arena = es.enter_context(nc.sbuf_tensor("arena", [128, ARENA_BYTES // 4], F32))[:]
    pp = [es.enter_context(nc.psum_tensor(f"pp{i}", [128, 1024], F32)) for i in range(4)]

    def bank(k):
        return pp[k // 2][:, (k % 2) * 512:(k % 2) * 512 + 512]

    def bankbf(k):
        return bank(k).bitcast(BF16)

    def bk(k):
        return ("bank", k)

    def PE(fn, r, w):
        S.op("pe", fn, r, w)

    def ACT(fn, r, w):
        S.op("act", fn, r, w)

    def DVE(fn, r, w):
        S.op("dve", fn, r, w)

    def POOL(fn, r, w):
        S.op("pool", fn, r, w)

    def mm(out, lhsT, rhs, start, stop, r, w, skip=False):
        if skip:
            PE(lambda e: e.matmul(out, lhsT=lhsT, rhs=rhs, start=start, stop=stop, skip_group_check=True), r, w)
        else:
            PE(lambda e: e.matmul(out, lhsT=lhsT, rhs=rhs, start=start, stop=stop), r, w)

    def tp(out, in_, r, w):
        PE(lambda e: e.transpose(out=out, in_=in_, identity=ident), list(r) + ["ident"], w)

    def tap(name, ap, key):
        if dbg is None or name not in dbg:
            return
        shp = list(ap.shape)
        t = nc.dram_tensor("dbg_" + name, shp, ap.dtype, kind="ExternalOutput").ap()
        dbg_out[name] = t
        S.dma("sp", lambda e: e.dma_start(out=t, in_=ap), "dbg", reads=key)

    P = Bump(arena, 0, ARENA_BYTES)
    ident = P.take([128], BF16)
    maskfb = P.take([256], I32)
    cols = P.take([64], F32)
    ones_b = P.take([2], BF16)
    nhalf = P.take([1], F32)
    ones_f = P.take([512], F32)
    MARK0 = P.cur
    C_GCQ, C_GCKV, C_GHG, C_LB, C_LNOML, C_LBP = 0, 3, 5, 9, 17, 25

    A = Bump(arena, MARK0, ARENA_BYTES)
    W_in = A.take([8, DIN], BF16)
    gmix_b = A.take([D], F32)
    gq_b = A.take([768], F32)
    gk_b = A.take([768], F32)
    gmo_b = A.take([512], F32)
    invf_b = A.take([32], F32)
    hT = A.take([8, SEQ], BF16)
    oT = A.take([8, SEQ], BF16)
    RMARK = A.cur

    R0 = Bump(arena, RMARK, ARENA_BYTES)
    identf = R0.take([128], F32)
    lbt = R0.take([8], F32)

    def cdma(out, in_, slow=False):
        if slow:
            S.dma("sp", lambda e: e.dma_start(out=out, in_=in_, allow_slow_non_contiguous=True), "const", writes=["const"])
        else:
            S.dma("sp", lambda e: e.dma_start(out=out, in_=in_), "const", writes=["const"])

    cdma(gmix_b, g_mix.partition_broadcast(128))
    for h in range(4):
        cdma(gq_b[:, h * 128:(h + 1) * 128], g_q[:, 0:128].partition_broadcast(128))
        cdma(gq_b[:, 512 + h * 64:512 + (h + 1) * 64], g_q[:, 128:192].partition_broadcast(128))
        cdma(gk_b[:, h * 128:(h + 1) * 128], g_k[:, 0:128].partition_broadcast(128))
        cdma(gk_b[:, 512 + h * 64:512 + (h + 1) * 64], g_k[:, 128:192].partition_broadcast(128))
    cdma(gmo_b, g_mo.partition_broadcast(128))
    cdma(invf_b, invf.partition_broadcast(128))
    cdma(cols[:, C_GCQ:C_GCQ + 3], g_cq[0].rearrange("(m p) -> p m", p=128), slow=True)
    cdma(cols[:, C_GCKV:C_GCKV + 2], g_ckv[0].rearrange("(m p) -> p m", p=128), slow=True)
    cdma(cols[:, C_GHG:C_GHG + 4], g_hg.rearrange("h e -> e h"), slow=True)
    for d_ in range(2):
        for s_ in range(2):
            o = C_LBP + (d_ * 2 + s_) * 4
            cdma(cols[:, o:o + 4], lb_param[d_, s_].rearrange("(h p) -> p h", p=128), slow=True)
    for kc in range(8):
        S.dma("pool", lambda e, kc=kc: e.dma_start(out=W_in[:, kc, :], in_=w_in[kc * 128:(kc + 1) * 128, :], max_dma_last_dim=4096),
              "w_in", writes=["W_in"])

    POOL(lambda e: e.memset(identf, 0.0), [], ["identf"])
    POOL(lambda e: e.affine_select(out=identf, in_=identf, pattern=[[-1, 128]], compare_op=ALU.not_equal, fill=1.0,
                                   base=0, channel_multiplier=1), ["identf"], ["identf"])
    DVE(lambda e: e.tensor_copy(out=ident, in_=identf), ["identf"], ["ident"])
    POOL(lambda e: e.iota(maskfb[:, 0:128], pattern=[[1, 128]], base=0, channel_multiplier=-1), [], ["maskfb"])
    POOL(lambda e: e.iota(maskfb[:, 128:256], pattern=[[-1, 128]], base=0, channel_multiplier=1), ["maskfb"], ["maskfb"])
    DVE(lambda e: e.tensor_single_scalar(out=maskfb, in_=maskfb, scalar=0, op=ALU.is_ge), ["maskfb"], ["maskfb"])
    POOL(lambda e: e.memset(ones_b, 1.0), [], ["ones_b"])
    POOL(lambda e: e.memset(nhalf, -0.5), [], ["nhalf"])
    POOL(lambda e: e.memset(ones_f, 1.0), [], ["ones_f"])
    DVE(lambda e: e.tensor_scalar(out=gq_b, in0=gq_b, scalar1=float(192 ** -0.5), scalar2=None, op0=ALU.mult), ["const"], ["const"])
    lbp = cols[:, C_LBP:C_LBP + 16].rearrange("p (d s h) -> p d s h", s=2, h=4)
    lbv = cols[:, C_LB:C_LB + 8]
    DVE(lambda e: e.tensor_tensor(out=lbt.rearrange("p (d h) -> p d h", h=4), in0=lbp[:, :, 1, :], in1=lbp[:, :, 0, :], op=ALU.subtract),
        ["const"], ["lbt"])
    ACT(lambda e: e.activation(out=lbt, in_=lbt, func=AF.Exp), ["lbt"], ["lbt"])
    DVE(lambda e: e.tensor_scalar(out=lbt, in0=lbt, scalar1=1.0, scalar2=None, op0=ALU.add), ["lbt"], ["lbt"])
    DVE(lambda e: e.reciprocal(out=lbv, in_=lbt), ["lbt", "const"], ["const"])
    ACT(lambda e: e.activation(out=cols[:, C_LNOML:C_LNOML + 8], in_=lbv, func=AF.Ln, scale=-1.0, bias=1.0), ["const"], ["const"])
    S.barrier()

    for b in range(nseq):
        R = Bump(arena, RMARK, ARENA_BYTES)
        V_all = R.take([NT, 512], BF16)
        XMARK = R.cur
        xin = [R.take([D], F32) for _ in range(2)]
        hbf = [R.take([D], BF16) for _ in range(2)]
        junk = R.take([D], BF16)
        st = R.take([64], F32)
        for i in range(NT):
            sl = i % 2
            S.dma("sp", lambda e, sl=sl, i=i: e.dma_start(out=xin[sl], in_=x[b, i * 128:(i + 1) * 128, :]), f"xin{sl}",
                  writes=[("xin", sl)])
            c0 = (i % 8) * 4
            ACT(lambda e, sl=sl, c0=c0: e.activation(out=junk, in_=xin[sl], func=AF.Square, accum_out=st[:, c0:c0 + 1]),
                [("xin", sl)], ["junk", ("st", c0)])
            DVE(lambda e, c0=c0: e.tensor_scalar(out=st[:, c0 + 1:c0 + 2], in0=st[:, c0:c0 + 1], scalar1=1.0 / D, scalar2=EPS,
                                                 op0=ALU.mult, op1=ALU.add), [("st", c0)], [("st", c0 + 1)])
            POOL(lambda e, c0=c0: e.tensor_tensor(out=st[:, c0 + 2:c0 + 3], in0=st[:, c0 + 1:c0 + 2], in1=nhalf, op=ALU.pow),
                 [("st", c0 + 1), "nhalf"], [("st", c0 + 2)])
            DVE(lambda e, sl=sl, c0=c0: e.scalar_tensor_tensor(out=hbf[sl], in0=xin[sl], scalar=st[:, c0 + 2:c0 + 3], in1=gmix_b,
                                                               op0=ALU.mult, op1=ALU.mult),
                [("xin", sl), ("st", c0 + 2), "const"], [("hbf", sl)])
            k = i % 2
            for kc in range(8):
                tp(bankbf(k)[:, kc * 128:(kc + 1) * 128], hbf[sl][:, kc * 128:(kc + 1) * 128], [("hbf", sl)], [bk(k)])
            ACT(lambda e, k=k, i=i: e.activation(out=hT[:, :, i * 128:(i + 1) * 128],
                                                 in_=bankbf(k).rearrange("p (k t) -> p k t", t=128), func=AF.Copy),
                [bk(k)], [("hT", i)])
        tap("hT", hT, [("hT", i) for i in range(NT)])
        for i in range(NT):
            k = 2 + i % 2
            for kc in range(8):
                mm(bank(k), hT[:, kc, i * 128:(i + 1) * 128], W_in[:, kc, 1536:2048], kc == 0, kc == 7,
                   [("hT", i), "W_in"], [bk(k)])
            DVE(lambda e, k=k, i=i: e.tensor_copy(out=V_all[:, i, :], in_=bank(k)), [bk(k)], [("V", i)])
        S.barrier()
        R = Bump(arena, XMARK, ARENA_BYTES)
        sgT = R.take([SEQ], BF16)
        TS = [[R.take([512], F32) for _ in range(6)] for _ in range(2)]
        QK = {(d_, w_): R.take([SEQ], BF16) for d_ in range(2) for w_ in "qk"}
        Zbf = [R.take([NT, 128], BF16) for _ in range(2)]
        Y = [[R.take([128], F32) for _ in range(2)] for _ in range(2)]
        Xs = [[R.take([128], F32) for _ in range(2)] for _ in range(2)]
        sqo = [TS[s_][0] for s_ in range(2)]
        onb = [TS[s_][1].bitcast(BF16)[:, 0:512] for s_ in range(2)]
        atm = [TS[s_][2].bitcast(BF16).rearrange("p (d n) -> p d n", d=2) for s_ in range(2)]
        ktok = [TS[s_][4].bitcast(BF16)[:, 0:512] for s_ in range(2)]
        Rall = [R.take([NT], F32) for _ in range(2)]
        Eall = [R.take([NT], F32) for _ in range(2)]
        dR = [R.take([3, NT], F32) for _ in range(2)]
        efac = [R.take([3, NT], F32) for _ in range(2)]
        carry = R.take([2], F32)
        st2 = R.take([64], F32)
        for h in range(4):
            for blk in range(4):
                tl = slice(blk * 512, (blk + 1) * 512)
                hk = [("hT", 4 * blk + j) for j in range(4)] + ["W_in"]
                kg = 6 + blk % 2
                for kc in range(8):
                    mm(bank(kg), W_in[:, kc, 2048 + h * 128:2048 + (h + 1) * 128], hT[:, kc, tl], kc == 0, kc == 7, hk, [bk(kg)])
                ACT(lambda e: e.activation(out=sgT[:, tl], in_=bank(kg), func=AF.Silu), [bk(kg)], [("sgT", blk)])

            def prep_mm(blk):
                tl = slice(blk * 512, (blk + 1) * 512)
                hk = [("hT", 4 * blk + j) for j in range(4)] + ["W_in"]
                for j, c0 in enumerate((0, 512, 1024)):
                    kk = (3 * blk + j) % 6
                    for kc in range(8):
                        mm(bank(kk), W_in[:, kc, c0 + h * 128:c0 + (h + 1) * 128], hT[:, kc, tl], kc == 0, kc == 7, hk, [bk(kk)])

            def front(it):
                blk, d_ = it // 2, it % 2
                T = TS[it % 2]
                tk = [("T", it % 2, j) for j in range(6)]
                kz = (3 * blk + 1 + d_) % 6
                lbc = cols[:, C_LB + d_ * 4 + h:C_LB + d_ * 4 + h + 1]
                ACT(lambda e: e.activation(out=T[0], in_=bank(kz), func=AF.Exp, scale=-1.0), [bk(kz)], [tk[0]])
                ACT(lambda e: e.activation(out=T[1], in_=T[0], func=AF.Ln, scale=lbc, bias=1.0), [tk[0], "const"], [tk[1]])
                ACT(lambda e: e.activation(out=T[2], in_=T[0], func=AF.Ln, scale=1.0, bias=1.0), [tk[0]], [tk[2]])
                POOL(lambda e: e.tensor_tensor(out=T[3], in0=T[1], in1=T[2], op=ALU.subtract), [tk[1], tk[2]], [tk[3]])
                DVE(lambda e: e.tensor_tensor(out=T[4], in0=bank(kz), in1=T[2], op=ALU.add), [bk(kz), tk[2]], [tk[4]])
                if blk == 0:
                    DVE(lambda e: e.tensor_tensor_scan(out=T[5], data0=ones_f, data1=T[3], initial=0.0, op0=ALU.mult, op1=ALU.add),
                        [tk[3], "ones_f"], [tk[5]])
                else:
                    DVE(lambda e: e.tensor_tensor_scan(out=T[5], data0=ones_f, data1=T[3], initial=carry[:, d_:d_ + 1],
                                                       op0=ALU.mult, op1=ALU.add), [tk[3], "ones_f", ("carry", d_)], [tk[5]])
                DVE(lambda e: e.tensor_copy(out=carry[:, d_:d_ + 1], in_=T[5][:, 511:512]), [tk[5]], [("carry", d_)])
                if d_ == 0:
                    Bx, kB = T[5], tk[5]
                else:
                    DVE(lambda e: e.tensor_tensor(out=T[1], in0=T[3], in1=T[5], op=ALU.subtract), [tk[3], tk[5]], [tk[1]])
                    Bx, kB = T[1], tk[1]
                DVE(lambda e: e.tensor_copy(out=Rall[d_][:, blk * 4:(blk + 1) * 4], in_=Bx[:, 63:512:128]), [kB], [("Rall", d_)])
                eo_ = 127 if d_ == 0 else 0
                DVE(lambda e: e.tensor_copy(out=Eall[d_][:, blk * 4:(blk + 1) * 4], in_=Bx[:, eo_:512:128]), [kB], [("Eall", d_)])
                DVE(lambda e: e.tensor_tensor(out=T[0].rearrange("p (c j) -> p c j", j=128), in0=Bx.rearrange("p (c j) -> p c j", j=128),
                                              in1=Rall[d_][:, blk * 4:(blk + 1) * 4].unsqueeze(2).to_broadcast([128, 4, 128]),
                                              op=ALU.subtract), [kB, ("Rall", d_)], [tk[0]])

            def back(it):
                blk, d_ = it // 2, it % 2
                tl = slice(blk * 512, (blk + 1) * 512)
                T = TS[it % 2]
                tk = [("T", it % 2, j) for j in range(6)]
                kq = (3 * blk) % 6
                lno = cols[:, C_LNOML + d_ * 4 + h:C_LNOML + d_ * 4 + h + 1]
                ACT(lambda e: e.activation(out=T[2], in_=T[0], func=AF.Exp), [tk[0]], [tk[2]])
                POOL(lambda e: e.tensor_tensor(out=T[3], in0=T[4], in1=T[0], op=ALU.add), [tk[4], tk[0]], [tk[3]])
                ACT(lambda e: e.activation(out=QK[(d_, "k")][:, tl], in_=T[3], func=AF.Exp, scale=-1.0, bias=lno),
                    [tk[3], "const"], [("QK", d_, "k", blk)])
                DVE(lambda e: e.tensor_tensor(out=QK[(d_, "q")][:, tl], in0=bank(kq), in1=T[2], op=ALU.mult),
                    [bk(kq), tk[2]], [("QK", d_, "q", blk)])

            prep_mm(0)
            front(0)
            for it in range(8):
                if it % 2 == 0 and it // 2 + 1 < 4:
                    prep_mm(it // 2 + 1)
                if it + 1 < 8:
                    front(it + 1)
                back(it)
            for d_ in range(2):
                if d_ == 0:
                    DVE(lambda e: e.tensor_tensor(out=dR[0][:, 0, 1:16], in0=Eall[0][:, 1:16], in1=Eall[0][:, 0:15], op=ALU.subtract),
                        [("Eall", 0)], [("dR", 0)])
                    DVE(lambda e: e.tensor_tensor(out=dR[0][:, 2, 1:16], in0=Rall[0][:, 1:16], in1=Eall[0][:, 0:15], op=ALU.subtract),
                        [("Eall", 0), ("Rall", 0), ("dR", 0)], [("dR", 0)])
                    DVE(lambda e: e.memset(dR[0][:, :, 0:1], 0.0), [("dR", 0)], [("dR", 0)])
                else:
                    DVE(lambda e: e.tensor_tensor(out=dR[1][:, 0, 0:15], in0=Eall[1][:, 0:15], in1=Eall[1][:, 1:16], op=ALU.subtract),
                        [("Eall", 1)], [("dR", 1)])
                    DVE(lambda e: e.tensor_tensor(out=dR[1][:, 2, 0:15], in0=Rall[1][:, 0:15], in1=Eall[1][:, 1:16], op=ALU.subtract),
                        [("Eall", 1), ("Rall", 1), ("dR", 1)], [("dR", 1)])
                    DVE(lambda e: e.memset(dR[1][:, :, 15:16], 0.0), [("dR", 1)], [("dR", 1)])
                DVE(lambda e: e.tensor_tensor(out=dR[d_][:, 1, :], in0=Eall[d_], in1=Rall[d_], op=ALU.subtract),
                    [("Eall", d_), ("Rall", d_), ("dR", d_)], [("dR", d_)])
                ACT(lambda e: e.activation(out=efac[d_], in_=dR[d_], func=AF.Exp), [("dR", d_)], [("efac", d_)])
            if h == 0 and b == 0:
                tap("Qf", QK[(0, "q")], [("QK", 0, "q", j) for j in range(4)])
                tap("Kf", QK[(0, "k")], [("QK", 0, "k", j) for j in range(4)])
            order = [list(range(15)), list(range(15, 0, -1))]
            groups = []
            for gi in range(4):
                for d_ in range(2):
                    cs_ = order[d_][gi * 4:gi * 4 + 4]
                    n_ = len(groups)
                    groups.append((d_, cs_, n_ % 2, 2 + n_ % 4, n_ % 2))
            pslot = {}
            for (d_, cs_, kT, kP, ks) in groups:
                for j, c in enumerate(cs_):
                    tp(bankbf(kT)[:, j * 128:(j + 1) * 128], QK[(d_, "k")][:, c * 128:(c + 1) * 128], [("QK", d_, "k", c // 4)], [bk(kT)])
                nn = len(cs_) * 128
                ACT(lambda e: e.activation(out=ktok[ks][:, 0:nn], in_=bankbf(kT)[:, 0:nn], func=AF.Copy), [bk(kT)], [("T", ks, 4)])
                for j, c in enumerate(cs_):
                    mm(bank(kP)[:, j * 128:(j + 1) * 128], ktok[ks][:, j * 128:(j + 1) * 128], V_all[:, c, h * 128:(h + 1) * 128], True, True,
                       [("T", ks, 4), ("V", c)], [bk(kP)])
                    pslot[(d_, c)] = (kP, j)
                for j, c in enumerate(cs_):
                    idx = order[d_].index(c)
                    kP_, j_ = pslot[(d_, c)]
                    pv = bank(kP_)[:, j_ * 128:(j_ + 1) * 128]
                    yc, yp = Y[d_][idx % 2], Y[d_][(idx + 1) % 2]
                    wcol = efac[d_][:, 1, c:c + 1]
                    if idx == 0:
                        DVE(lambda e: e.tensor_scalar(out=yc, in0=pv, scalar1=wcol, scalar2=None, op0=ALU.mult),
                            [bk(kP_), ("efac", d_)], [("Y", d_, idx % 2)])
                    else:
                        xs_ = Xs[d_][idx % 2]
                        ACT(lambda e: e.activation(out=xs_, in_=yp, func=AF.Copy, scale=efac[d_][:, 0, c:c + 1]),
                            [("Y", d_, (idx + 1) % 2), ("efac", d_)], [("Xs", d_, idx % 2)])
                        DVE(lambda e: e.scalar_tensor_tensor(out=yc, in0=pv, scalar=wcol, in1=xs_, op0=ALU.mult, op1=ALU.add),
                            [("Xs", d_, idx % 2), ("efac", d_), bk(kP_)], [("Y", d_, idx % 2)])
                    nxt = c + 1 if d_ == 0 else c - 1
                    ACT(lambda e: e.activation(out=Zbf[d_][:, nxt, :], in_=yc, func=AF.Copy, scale=efac[d_][:, 2, nxt:nxt + 1]),
                        [("Y", d_, idx % 2), ("efac", d_)], [("Zbf", d_, nxt)])
            for kk in range(4):
                DVE(lambda e: e.memset(bank(kk), 0.0), [], [bk(kk)])
            for s_ in range(2):
                POOL(lambda e: e.memset(atm[s_], 0.0), [], [("T", s_, 2)])

            def at_mm(g):
                ka = (g % 2) * 2
                for d_ in range(2):
                    for j in range(4):
                        c = g * 4 + j
                        q_, k_ = QK[(d_, "q")], QK[(d_, "k")]
                        o_ = j * 128
                        rk = [("QK", d_, "k", g), ("QK", d_, "q", g)]
                        if d_ == 0:
                            mm(bank(ka)[0:64, o_:o_ + 128], k_[:, c * 128:c * 128 + 64], q_[:, c * 128:(c + 1) * 128], True, True, rk, [bk(ka)])
                            mm(bank(ka)[64:128, o_ + 64:o_ + 128], k_[:, c * 128 + 64:(c + 1) * 128], q_[:, c * 128 + 64:(c + 1) * 128],
                               True, True, rk, [bk(ka)])
                        else:
                            mm(bank(ka + 1)[0:64, o_:o_ + 64], k_[:, c * 128:c * 128 + 64], q_[:, c * 128:c * 128 + 64], True, True, rk, [bk(ka + 1)])
                            mm(bank(ka + 1)[64:128, o_:o_ + 128], k_[:, c * 128 + 64:(c + 1) * 128], q_[:, c * 128:(c + 1) * 128],
                               True, True, rk, [bk(ka + 1)])

            def mask_copy(g):
                ka = (g % 2) * 2
                sa = g % 2
                for d_ in range(2):
                    mk = maskfb[:, d_ * 128:(d_ + 1) * 128].unsqueeze(1).to_broadcast([128, 4, 128])
                    DVE(lambda e: e.copy_predicated(out=atm[sa][:, d_, :].rearrange("p (c j) -> p c j", j=128), mask=mk,
                                                    data=bank(ka + d_).rearrange("p (c j) -> p c j", j=128)),
                        [bk(ka + d_), "maskfb", ("T", sa, 2)], [("T", sa, 2)])

            def o_mm(g):
                ko = 4 + g % 2
                sa = g % 2
                for j in range(4):
                    c = g * 4 + j
                    cs = slice(c * 128, (c + 1) * 128)
                    grp = []
                    if c > 0:
                        grp.append((QK[(0, "q")][:, cs], Zbf[0][:, c, :], [("QK", 0, "q", g), ("Zbf", 0, c)]))
                    if c < NT - 1:
                        grp.append((QK[(1, "q")][:, cs], Zbf[1][:, c, :], [("QK", 1, "q", g), ("Zbf", 1, c)]))
                    vv = V_all[:, c, h * 128:(h + 1) * 128]
                    grp.append((atm[sa][:, 0, j * 128:(j + 1) * 128], vv, [("T", sa, 2), ("V", c)]))
                    grp.append((atm[sa][:, 1, j * 128:(j + 1) * 128], vv, [("T", sa, 2), ("V", c)]))
                    for gi_, (l_, r_, rk) in enumerate(grp):
                        mm(bank(ko)[:, j * 128:(j + 1) * 128], l_, r_, gi_ == 0, gi_ == len(grp) - 1, rk, [bk(ko)])

            def epi_a(g):
                ko = 4 + g % 2
                sa = g % 2
                c0 = (g % 4) * 12
                ACT(lambda e: e.activation(out=sqo[sa], in_=bank(ko), func=AF.Square), [bk(ko)], [("T", sa, 0)])
                DVE(lambda e: e.tensor_reduce(out=st2[:, c0:c0 + 4], in_=sqo[sa].rearrange("p (c j) -> p c j", j=128), axis=AX.X, op=ALU.add),
                    [("T", sa, 0)], [("st2", c0)])
                DVE(lambda e: e.tensor_scalar(out=st2[:, c0 + 4:c0 + 8], in0=st2[:, c0:c0 + 4], scalar1=1.0 / 128, scalar2=EPS,
                                              op0=ALU.mult, op1=ALU.add), [("st2", c0)], [("st2", c0 + 4)])
                POOL(lambda e: e.tensor_tensor(out=st2[:, c0 + 8:c0 + 12], in0=st2[:, c0 + 4:c0 + 8], in1=nhalf.to_broadcast([128, 4]),
                                               op=ALU.pow), [("st2", c0 + 4), "nhalf"], [("st2", c0 + 8)])
                DVE(lambda e: e.tensor_tensor(out=onb[sa].rearrange("p (c j) -> p c j", j=128), in0=bank(ko).rearrange("p (c j) -> p c j", j=128),
                                              in1=st2[:, c0 + 8:c0 + 12].unsqueeze(2).to_broadcast([128, 4, 128]), op=ALU.mult),
                    [bk(ko), ("st2", c0 + 8)], [("T", sa, 1)])

            def epi_b(g):
                kt = 6 + g % 2
                sa = g % 2
                for j in range(4):
                    tp(bankbf(kt)[:, j * 128:(j + 1) * 128], onb[sa][:, j * 128:(j + 1) * 128], [("T", sa, 1)], [bk(kt)])
                gs = slice(g * 512, (g + 1) * 512)
                DVE(lambda e: e.scalar_tensor_tensor(out=oT[:, h, gs], in0=bankbf(kt)[:, 0:512], scalar=cols[:, C_GHG + h:C_GHG + h + 1],
                                                     in1=sgT[:, gs], op0=ALU.mult, op1=ALU.mult),
                    [bk(kt), "const", ("sgT", g)], [("oT", h, g)])

            at_mm(0); mask_copy(0); at_mm(1); o_mm(0); mask_copy(1); at_mm(2); epi_a(0); o_mm(1); mask_copy(2); at_mm(3)
            epi_b(0); epi_a(1); o_mm(2); mask_copy(3); epi_b(1); epi_a(2); o_mm(3); epi_b(2); epi_a(3); epi_b(3)
        tap("oTa", oT[:, 0:4, :], [("oT", h, g) for h in range(4) for g in range(4)])
        S.barrier()
        R = Bump(arena, RMARK, ARENA_BYTES)
        KT = R.take([6, SEQ], BF16)
        Vaug = R.take([NT, 4, 130], BF16)
        SMARK = R.cur
        Wq = R.take([3, 768], BF16)
        Wkv = R.take([2, 1024], BF16)
        posi = R.take([NT], I32)
        posf = R.take([NT], F32)
        cosT = R.take([NT, 32], F32)
        sinT = R.take([NT, 32], F32)
        TMARK = R.cur
        ang = R.take([NT, 32], F32)
        kqi = R.take([NT, 32], I32)
        t_a = R.take([NT, 32], F32)
        t_b = R.take([NT, 32], F32)
        for m in range(3):
            S.dma("pool", lambda e, m=m: e.dma_start(out=Wq[:, m, :], in_=wq[m * 128:(m + 1) * 128, :]), "wq", writes=["Wq"])
        for m in range(2):
            S.dma("pool", lambda e, m=m: e.dma_start(out=Wkv[:, m, :], in_=wkv[m * 128:(m + 1) * 128, :]), "wkv", writes=["Wkv"])
        POOL(lambda e: e.memset(Vaug[:, :, :, 128:130], 1.0), [], ["Vones"])
        S.dma("sp", lambda e: e.dma_start(out=posi, in_=pos[b].rearrange("(n p) -> p n", p=128), allow_slow_non_contiguous=True),
              "posi", writes=["posi"])
        DVE(lambda e: e.tensor_copy(out=posf, in_=posi), ["posi"], ["posf"])
        DVE(lambda e: e.tensor_tensor(out=ang, in0=posf.unsqueeze(2).to_broadcast([128, NT, 32]),
                                      in1=invf_b.unsqueeze(1).to_broadcast([128, NT, 32]), op=ALU.mult), ["posf", "const"], ["ang"])
        DVE(lambda e: e.tensor_scalar(out=kqi, in0=ang, scalar1=float(1.0 / (2 * PI)), scalar2=None, op0=ALU.mult), ["ang"], ["kqi"])
        DVE(lambda e: e.tensor_copy(out=t_a, in_=kqi), ["kqi"], ["t_a"])
        C1, C2, C3 = CW1, CW2, CW3
        DVE(lambda e: e.scalar_tensor_tensor(out=t_b, in0=t_a, scalar=-C1, in1=ang, op0=ALU.mult, op1=ALU.add), ["t_a", "ang"], ["t_b"])
        DVE(lambda e: e.scalar_tensor_tensor(out=ang, in0=t_a, scalar=-C2, in1=t_b, op0=ALU.mult, op1=ALU.add), ["t_a", "t_b", "ang"], ["ang"])
        DVE(lambda e: e.scalar_tensor_tensor(out=t_b, in0=t_a, scalar=-C3, in1=ang, op0=ALU.mult, op1=ALU.add), ["t_a", "ang", "t_b"], ["t_b"])
        DVE(lambda e: e.tensor_scalar(out=ang, in0=t_b, scalar1=PI, scalar2=-PI, op0=ALU.min, op1=ALU.max), ["t_b", "ang"], ["ang"])
        ACT(lambda e: e.activation(out=sinT, in_=ang, func=AF.Sin), ["ang"], ["sinT"])
        DVE(lambda e: e.tensor_scalar(out=t_a, in0=t_b, scalar1=PI / 2, scalar2=None, op0=ALU.add), ["t_b", "t_a"], ["t_a"])
        DVE(lambda e: e.tensor_scalar(out=t_b, in0=t_a, scalar1=PI, scalar2=2 * PI, op0=ALU.is_gt, op1=ALU.mult), ["t_a", "t_b"], ["t_b"])
        DVE(lambda e: e.tensor_tensor(out=t_a, in0=t_a, in1=t_b, op=ALU.subtract), ["t_a", "t_b"], ["t_a"])
        DVE(lambda e: e.tensor_scalar(out=t_a, in0=t_a, scalar1=PI, scalar2=-PI, op0=ALU.min, op1=ALU.max), ["t_a"], ["t_a"])
        ACT(lambda e: e.activation(out=cosT, in_=t_a, func=AF.Sin), ["t_a"], ["cosT"])
        if b == 0:
            tap("cosT", cosT, ["cosT"])
            tap("sinT", sinT, ["sinT"])
        S.barrier()
        R = Bump(arena, TMARK, ARENA_BYTES)
        cT = R.take([5, 512], BF16)
        sq = R.take([5, 512], BF16)
        sqq = R.take([768], F32)
        t1 = R.take([768], F32)
        qf = R.take([1024], BF16)
        kf = R.take([768], BF16)
        krs = R.take([64], F32)
        krr = R.take([64], F32)
        ta = R.take([256], F32)
        tb = R.take([256], F32)
        st3 = R.take([64], F32)
        junk = R.take([64], BF16)
        POOL(lambda e: e.memset(qf[:, 512:1024], 0.0), [], ["qf"])
        for blk in range(4):
            tl = slice(blk * 512, (blk + 1) * 512)
            hk = [("hT", 4 * blk + j) for j in range(4)] + ["W_in"]
            for m in range(5):
                c0 = 2560 + m * 128
                kc_ = (m % 2) * 2
                for kc in range(8):
                    mm(bank(kc_), W_in[:, kc, c0:c0 + 128], hT[:, kc, tl], kc == 0, kc == 7, hk, [bk(kc_)])
                gcol = cols[:, C_GCQ + m:C_GCQ + m + 1]
                ACT(lambda e, m=m, gcol=gcol: e.activation(out=cT[:, m, :], in_=bank(kc_), func=AF.Copy, scale=gcol),
                    [bk(kc_), "const"], [("cT", m)])
                ACT(lambda e, m=m: e.activation(out=sq[:, m, :], in_=bank(kc_), func=AF.Square), [bk(kc_)], [("sq", m)])
            for j in range(4):
                i = blk * 4 + j
                js = slice(j * 128, (j + 1) * 128)
                its = slice(i * 128, (i + 1) * 128)
                c0 = (i % 2) * 32
                for m in range(3):
                    mm(bank(1)[:, 0:1], sq[:, m, js], ones_b[:, 0:1], m == 0, m == 2, [("sq", m), "ones_b"], [bk(1)])
                for m in range(3, 5):
                    mm(bank(1)[:, 2:3], sq[:, m, js], ones_b[:, 0:1], m == 3, m == 4, [("sq", m), "ones_b"], [bk(1)])
                for kc in range(8):
                    mm(bank(1)[:, 64:128], hT[:, kc, its], W_in[:, kc, 3200:3264], kc == 0, kc == 7, [("hT", i), "W_in"], [bk(1)])
                sc = lambda o: st3[:, c0 + o:c0 + o + 1]
                skey = lambda o: ("st3", c0 + o)
                DVE(lambda e, sc=sc: e.tensor_scalar(out=sc(0), in0=bank(1)[:, 0:1], scalar1=1.0 / 384, scalar2=EPS, op0=ALU.mult, op1=ALU.add),
                    [bk(1)], [skey(0)])
                DVE(lambda e, sc=sc: e.tensor_scalar(out=sc(1), in0=bank(1)[:, 2:3], scalar1=1.0 / 256, scalar2=EPS, op0=ALU.mult, op1=ALU.add),
                    [bk(1)], [skey(1)])
                POOL(lambda e, sc=sc: e.tensor_tensor(out=sc(2), in0=sc(0), in1=nhalf, op=ALU.pow), [skey(0), "nhalf"], [skey(2)])
                POOL(lambda e, sc=sc: e.tensor_tensor(out=sc(3), in0=sc(1), in1=nhalf, op=ALU.pow), [skey(1), "nhalf"], [skey(3)])
                for m in range(3):
                    mm(bank(3), cT[:, m, js], Wq[:, m, 0:512], m == 0, m == 2, [("cT", m), "Wq"], [bk(3)])
                for m in range(3):
                    mm(bank(4)[:, 0:256], cT[:, m, js], Wq[:, m, 512:768], m == 0, m == 2, [("cT", m), "Wq"], [bk(4)])
                for m in range(2):
                    mm(bank(5), cT[:, 3 + m, js], Wkv[:, m, 0:512], m == 0, m == 1, [("cT", 3 + m), "Wkv"], [bk(5)])
                for m in range(2):
                    mm(bank(6), cT[:, 3 + m, js], Wkv[:, m, 512:1024], m == 0, m == 1, [("cT", 3 + m), "Wkv"], [bk(6)])
                ACT(lambda e: e.activation(out=sqq[:, 0:512], in_=bank(3), func=AF.Square), [bk(3)], ["sqq"])
                ACT(lambda e: e.activation(out=sqq[:, 512:768], in_=bank(4)[:, 0:256], func=AF.Square), [bk(4), "sqq"], ["sqq"])
                DVE(lambda e, sc=sc: e.tensor_reduce(out=st3[:, c0 + 4:c0 + 8], in_=sqq[:, 0:512].rearrange("p (h j) -> p h j", j=128),
                                                     axis=AX.X, op=ALU.add), ["sqq"], [skey(4)])
                DVE(lambda e, sc=sc: e.tensor_reduce(out=st3[:, c0 + 8:c0 + 12], in_=sqq[:, 512:768].rearrange("p (h j) -> p h j", j=64),
                                                     axis=AX.X, op=ALU.add), ["sqq"], [skey(8)])
                DVE(lambda e: e.tensor_tensor(out=st3[:, c0 + 4:c0 + 8], in0=st3[:, c0 + 4:c0 + 8], in1=st3[:, c0 + 8:c0 + 12], op=ALU.add),
                    [skey(4), skey(8)], [skey(4)])
                DVE(lambda e, sc=sc: e.tensor_tensor(out=sc(12), in0=sc(2), in1=sc(2), op=ALU.mult), [skey(2)], [skey(12)])
                DVE(lambda e, sc=sc: e.tensor_scalar(out=st3[:, c0 + 4:c0 + 8], in0=st3[:, c0 + 4:c0 + 8], scalar1=sc(12), scalar2=1.0 / 192,
                                                     op0=ALU.mult, op1=ALU.mult), [skey(4), skey(12)], [skey(4)])
                DVE(lambda e: e.tensor_scalar(out=st3[:, c0 + 4:c0 + 8], in0=st3[:, c0 + 4:c0 + 8], scalar1=EPS, scalar2=None, op0=ALU.add),
                    [skey(4)], [skey(4)])
                POOL(lambda e: e.tensor_tensor(out=st3[:, c0 + 8:c0 + 12], in0=st3[:, c0 + 4:c0 + 8],
                                               in1=nhalf.to_broadcast([128, 4]), op=ALU.pow), [skey(4), "nhalf", skey(8)], [skey(8)])
                DVE(lambda e, sc=sc: e.tensor_scalar(out=st3[:, c0 + 8:c0 + 12], in0=st3[:, c0 + 8:c0 + 12], scalar1=sc(2), scalar2=None,
                                                     op0=ALU.mult), [skey(8), skey(2)], [skey(8)])
                fq = st3[:, c0 + 8:c0 + 12]
                DVE(lambda e, fq=fq: e.tensor_tensor(out=t1[:, 0:512].rearrange("p (h j) -> p h j", j=128),
                                                     in0=bank(3).rearrange("p (h j) -> p h j", j=128),
                                                     in1=fq.unsqueeze(2).to_broadcast([128, 4, 128]), op=ALU.mult), [bk(3), skey(8)], ["t1"])
                DVE(lambda e: e.tensor_tensor(out=qf[:, 0:512], in0=t1[:, 0:512], in1=gq_b[:, 0:512], op=ALU.mult), ["t1", "const"], ["qf"])
                DVE(lambda e, fq=fq: e.tensor_tensor(out=t1[:, 512:768].rearrange("p (h j) -> p h j", j=64),
                                                     in0=bank(4)[:, 0:256].rearrange("p (h j) -> p h j", j=64),
                                                     in1=fq.unsqueeze(2).to_broadcast([128, 4, 64]), op=ALU.mult), [bk(4), skey(8), "t1"], ["t1"])
                DVE(lambda e: e.tensor_tensor(out=t1[:, 512:768], in0=t1[:, 512:768], in1=gq_b[:, 512:768], op=ALU.mult), ["t1", "const"], ["t1"])

                def rope(src, nh_, dst, rk, wk):
                    s4 = src.rearrange("p (h a r) -> p h a r", a=2, r=32)
                    a4 = ta[:, 0:nh_ * 64].rearrange("p (h a r) -> p h a r", a=2, r=32)
                    b4 = tb[:, 0:nh_ * 64].rearrange("p (h a r) -> p h a r", a=2, r=32)
                    cb = cosT[:, i, :].unsqueeze(1).unsqueeze(1).to_broadcast([128, nh_, 2, 32])
                    sb_ = sinT[:, i, :].unsqueeze(1).to_broadcast([128, nh_, 32])
                    DVE(lambda e: e.tensor_tensor(out=a4, in0=s4, in1=cb, op=ALU.mult), rk + ["cosT"], ["ta"])
                    DVE(lambda e: e.scalar_tensor_tensor(out=b4[:, :, 0, :], in0=s4[:, :, 1, :], scalar=-1.0, in1=sb_, op0=ALU.mult, op1=ALU.mult),
                        rk + ["sinT"], ["tb"])
                    DVE(lambda e: e.tensor_tensor(out=b4[:, :, 1, :], in0=s4[:, :, 0, :], in1=sb_, op=ALU.mult), rk + ["sinT", "tb"], ["tb"])
                    if isinstance(dst, list):
                        a3 = ta[:, 0:nh_ * 64].rearrange("p (h j) -> p h j", j=64)
                        b3 = tb[:, 0:nh_ * 64].rearrange("p (h j) -> p h j", j=64)
                        for par, dv in enumerate(dst):
                            DVE(lambda e, par=par, dv=dv: e.tensor_tensor(out=dv, in0=a3[:, par::2, :], in1=b3[:, par::2, :], op=ALU.add),
                                ["ta", "tb"] + wk, wk)
                    else:
                        DVE(lambda e: e.tensor_tensor(out=dst, in0=ta[:, 0:nh_ * 64], in1=tb[:, 0:nh_ * 64], op=ALU.add), ["ta", "tb"], wk)

                qz = qf[:, 512:1024].rearrange("p (i r) -> p i r", r=256)
                rope(t1[:, 512:768], 4, [qz[:, :, 0:64], qz[:, :, 192:256]], ["t1"], ["qf"])
                ACT(lambda e: e.activation(out=sqq[:, 0:512], in_=bank(5), func=AF.Square), [bk(5), "sqq"], ["sqq"])
                DVE(lambda e: e.tensor_reduce(out=st3[:, c0 + 16:c0 + 20], in_=sqq[:, 0:512].rearrange("p (h j) -> p h j", j=128),
                                              axis=AX.X, op=ALU.add), ["sqq"], [skey(16)])
                DVE(lambda e: e.tensor_copy(out=krs, in_=bank(1)[:, 64:128]), [bk(1)], ["krs"])
                ACT(lambda e, sc=sc: e.activation(out=junk[:, 0:64], in_=krs, func=AF.Square, accum_out=sc(13)), ["krs"], ["junk", skey(13)])
                DVE(lambda e, sc=sc: e.tensor_tensor(out=sc(14), in0=sc(3), in1=sc(3), op=ALU.mult), [skey(3)], [skey(14)])
                DVE(lambda e, sc=sc: e.tensor_scalar(out=st3[:, c0 + 16:c0 + 20], in0=st3[:, c0 + 16:c0 + 20], scalar1=sc(14), scalar2=sc(13),
                                                     op0=ALU.mult, op1=ALU.add), [skey(16), skey(14), skey(13)], [skey(16)])
                DVE(lambda e: e.tensor_scalar(out=st3[:, c0 + 16:c0 + 20], in0=st3[:, c0 + 16:c0 + 20], scalar1=1.0 / 192, scalar2=EPS,
                                              op0=ALU.mult, op1=ALU.add), [skey(16)], [skey(16)])
                POOL(lambda e: e.tensor_tensor(out=st3[:, c0 + 20:c0 + 24], in0=st3[:, c0 + 16:c0 + 20],
                                               in1=nhalf.to_broadcast([128, 4]), op=ALU.pow), [skey(16), "nhalf"], [skey(20)])
                DVE(lambda e, sc=sc: e.tensor_scalar(out=st3[:, c0 + 24:c0 + 28], in0=st3[:, c0 + 20:c0 + 24], scalar1=sc(3), scalar2=None,
                                                     op0=ALU.mult), [skey(20), skey(3)], [skey(24)])
                rk_ = st3[:, c0 + 20:c0 + 24]
                fkn = st3[:, c0 + 24:c0 + 28]
                DVE(lambda e, fkn=fkn: e.tensor_tensor(out=t1[:, 0:512].rearrange("p (h j) -> p h j", j=128),
                                                       in0=bank(5).rearrange("p (h j) -> p h j", j=128),
                                                       in1=fkn.unsqueeze(2).to_broadcast([128, 4, 128]), op=ALU.mult),
                    [bk(5), skey(24), "t1"], ["t1"])
                DVE(lambda e: e.tensor_tensor(out=kf[:, 0:512], in0=t1[:, 0:512], in1=gk_b[:, 0:512], op=ALU.mult), ["t1", "const"], ["kf"])
                DVE(lambda e: e.tensor_tensor(out=krs, in0=krs, in1=gk_b[:, 512:576], op=ALU.mult), ["krs", "const"], ["krs"])
                rope(krs, 1, krr, ["krs"], ["krr"])
                DVE(lambda e, rk_=rk_: e.tensor_tensor(out=kf[:, 512:768].rearrange("p (h j) -> p h j", j=64),
                                                       in0=krr.unsqueeze(1).to_broadcast([128, 4, 64]),
                                                       in1=rk_.unsqueeze(2).to_broadcast([128, 4, 64]), op=ALU.mult),
                    ["krr", skey(20), "kf"], ["kf"])
                DVE(lambda e, sc=sc, i=i: e.tensor_scalar(out=Vaug[:, i, :, 0:128], in0=bank(6).rearrange("p (h j) -> p h j", j=128),
                                                          scalar1=sc(3), scalar2=None, op0=ALU.mult), [bk(6), skey(3)], [("Vaug", i)])
                for m in range(8):
                    tp(bankbf(7)[:, m * 128:(m + 1) * 128], qf[:, m * 128:(m + 1) * 128], ["qf"], [bk(7)])
                ACT(lambda e, its=its: e.activation(out=hT[:, :, its], in_=bankbf(7).rearrange("p (m t) -> p m t", t=128),
                                                    func=AF.Copy), [bk(7)], [("hT", i)])
                for m in range(6):
                    tp(bankbf(7)[:, m * 128:(m + 1) * 128], kf[:, m * 128:(m + 1) * 128], ["kf"], [bk(7)])
                ACT(lambda e, its=its: e.activation(out=KT[:, :, its], in_=bankbf(7)[:, 0:768].rearrange("p (m t) -> p m t", t=128),
                                                    func=AF.Copy), [bk(7)], [("KT", i)])
        if b == 0:
            tap("QT", hT[:, 0:6, :], [("hT", i) for i in range(NT)])
            tap("KT", KT, [("KT", i) for i in range(NT)])
            tap("Vaug", Vaug, [("Vaug", i) for i in range(NT)] + ["Vones"])
        S.barrier()
        R = Bump(arena, SMARK, ARENA_BYTES)
        PT = [R.take([512], BF16) for _ in range(3)]
        obuf = R.take([4, 512], F32)
        obn = [R.take([512], BF16) for _ in range(2)]
        st4 = R.take([64], F32)
        junk = R.take([512], BF16)
        QT = hT
        it = 0
        npt = 0
        for qb in range(4):
            qs = slice(qb * 512, (qb + 1) * 512)
            qkeys = [("hT", 4 * qb + j) for j in range(4)]
            for h in range(4):
                ko = 2 + (it % 2) * 2
                it += 1
                DVE(lambda e, ko=ko: e.memset(pp[ko // 2][:, :], 0.0), [], [bk(ko), bk(ko + 1)])
                rp = slice((h % 2) * 64, (h % 2) * 64 + 64)
                rc = 4 + h // 2
                def qk(kc):
                    ksl = slice(kc * 128, (kc + 1) * 128)
                    ks_ = kc % 2
                    mm(bank(ks_), KT[:, h, ksl], QT[:, h, qs], True, False, [("KT", kc)] + qkeys, [bk(ks_)])
                    mm(bank(ks_), KT[:, rc, ksl], QT[:, 4 + h, qs], False, True, [("KT", kc)] + qkeys, [bk(ks_)])

                qk(0)
                for kc in range(NT):
                    ks_ = kc % 2
                    ps_ = npt % 3
                    npt += 1
                    ACT(lambda e, ks_=ks_, ps_=ps_: e.activation(out=PT[ps_], in_=bank(ks_), func=AF.Exp), [bk(ks_)], [("PT", ps_)])
                    if kc + 1 < NT:
                        qk(kc + 1)
                    for j in range(4):
                        ob = ko + j // 2
                        mm(bank(ob)[:, (j % 2) * 256:(j % 2) * 256 + 129], PT[ps_][:, j * 128:(j + 1) * 128], Vaug[:, kc, h, 0:129],
                           False, False, [("PT", ps_), ("Vaug", kc), "Vones"], [bk(ob)], skip=True)
                for j in range(4):
                    ob = ko + j // 2
                    o0 = (j % 2) * 256
                    c0 = ((it * 4 + j) % 16) * 2
                    DVE(lambda e, ob=ob, o0=o0, c0=c0: e.reciprocal(out=st4[:, c0:c0 + 1], in_=bank(ob)[:, o0 + 128:o0 + 129]),
                        [bk(ob)], [("st4", c0)])
                    DVE(lambda e, ob=ob, o0=o0, c0=c0, j=j, h=h: e.tensor_scalar(out=obuf[:, j, h * 128:(h + 1) * 128],
                                                                                 in0=bank(ob)[:, o0:o0 + 128], scalar1=st4[:, c0:c0 + 1],
                                                                                 scalar2=None, op0=ALU.mult),
                        [bk(ob), ("st4", c0)], [("obuf", j)])
            for j in range(4):
                i = qb * 4 + j
                c0 = 32 + (i % 8) * 4
                sj = i % 2
                ACT(lambda e, j=j, c0=c0: e.activation(out=junk[:, 0:512], in_=obuf[:, j, :], func=AF.Square, accum_out=st4[:, c0:c0 + 1]),
                    [("obuf", j)], ["junk", ("st4", c0)])
                DVE(lambda e, c0=c0: e.tensor_scalar(out=st4[:, c0 + 1:c0 + 2], in0=st4[:, c0:c0 + 1], scalar1=1.0 / 512, scalar2=EPS,
                                                     op0=ALU.mult, op1=ALU.add), [("st4", c0)], [("st4", c0 + 1)])
                POOL(lambda e, c0=c0: e.tensor_tensor(out=st4[:, c0 + 2:c0 + 3], in0=st4[:, c0 + 1:c0 + 2], in1=nhalf, op=ALU.pow),
                     [("st4", c0 + 1), "nhalf"], [("st4", c0 + 2)])
                DVE(lambda e, j=j, c0=c0, sj=sj: e.scalar_tensor_tensor(out=obn[sj], in0=obuf[:, j, :], scalar=st4[:, c0 + 2:c0 + 3],
                                                                        in1=gmo_b, op0=ALU.mult, op1=ALU.mult),
                    [("obuf", j), ("st4", c0 + 2), "const"], [("obn", sj)])
                kt = 6 + i % 2
                for m in range(4):
                    tp(bankbf(kt)[:, m * 128:(m + 1) * 128], obn[sj][:, m * 128:(m + 1) * 128], [("obn", sj)], [bk(kt)])
                ACT(lambda e, kt=kt, i=i: e.activation(out=oT[:, 4:8, i * 128:(i + 1) * 128],
                                                       in_=bankbf(kt)[:, 0:512].rearrange("p (m t) -> p m t", t=128), func=AF.Copy),
                    [bk(kt)], [("oT", 4, i)])
        tap("oT", oT, [("oT", 4, i) for i in range(NT)])
        S.barrier()
        R = Bump(arena, RMARK, ARENA_BYTES)
        W_o = R.take([8, D], BF16)
        xin = [R.take([D], F32) for _ in range(2)]
        x1o = [R.take([D], F32) for _ in range(2)]
        for kc in range(8):
            S.dma("pool", lambda e, kc=kc: e.dma_start(out=W_o[:, kc, :], in_=w_out[kc * 128:(kc + 1) * 128, :]), "w_o", writes=["W_o"])
        for i in range(NT):
            sl = i % 2
            its = slice(i * 128, (i + 1) * 128)
            S.dma("sp", lambda e, sl=sl, i=i: e.dma_start(out=xin[sl], in_=x[b, i * 128:(i + 1) * 128, :]), f"xin{sl}",
                  writes=[("xin", sl)])
            kp = (i % 2) * 2
            for half in range(2):
                for m in range(8):
                    mm(bank(kp + half), oT[:, m, its], W_o[:, m, half * 512:(half + 1) * 512], m == 0, m == 7, ["W_o"], [bk(kp + half)])
            DVE(lambda e, sl=sl, kp=kp: e.tensor_tensor(out=x1o[sl], in0=pp[kp // 2][:, :], in1=xin[sl], op=ALU.add),
                [bk(kp), bk(kp + 1), ("xin", sl), ("x1o", sl)], [("x1o", sl)])
            S.dma("sp", lambda e, sl=sl, i=i: e.dma_start(out=x1s[b, i * 128:(i + 1) * 128, :], in_=x1o[sl]), f"x1o{sl}",
                  reads=[("x1o", sl)])
        S.barrier()

    Bm = Bump(arena, MARK0, ARENA_BYTES)
    W_up = Bm.take([8, DFF], BF16)
    W_dn = Bm.take([32, D], BF16)
    gffn_b = Bm.take([D], F32)
    TB = 256
    NJ = TB // 128
    x1t = [[Bm.take([D], F32) for _ in range(NJ)] for _ in range(2)]
    hbf = [Bm.take([D], BF16) for _ in range(2)]
    h2T = Bm.take([8, TB], BF16)
    aT = Bm.take([32, TB], BF16)
    rl = [Bm.take([512], F32) for _ in range(2)]
    yo = [Bm.take([D], F32) for _ in range(2)]
    junk = Bm.take([D], BF16)
    st5 = Bm.take([64], F32)
    S.dma("sp", lambda e: e.dma_start(out=gffn_b, in_=g_ffn.partition_broadcast(128)), "const2", writes=["gffn"])
    for kc in range(8):
        for q4 in range(4):
            S.dma("pool", lambda e, kc=kc, q4=q4: e.dma_start(out=W_up[:, kc, q4 * 1024:(q4 + 1) * 1024],
                                                             in_=w_up[kc * 128:(kc + 1) * 128, q4 * 1024:(q4 + 1) * 1024]),
                  "w_up", writes=["W_up"])
    for c in range(32):
        S.dma("pool", lambda e, c=c: e.dma_start(out=W_dn[:, c, :], in_=w_down[c * 128:(c + 1) * 128, :]), "w_dn", writes=["W_dn"])
    x1f = x1s.rearrange("b s d -> (b s) d")
    yf = y.rearrange("b s d -> (b s) d")
    nblk = nseq * SEQ // TB
    nyo = 0
    for blk in range(nblk):
        xs = blk % 2
        for j in range(NJ):
            t = blk * NJ + j
            S.dma("sp", lambda e, xs=xs, j=j, t=t: e.dma_start(out=x1t[xs][j], in_=x1f[t * 128:(t + 1) * 128, :]), f"x1t{xs}{j}",
                  writes=[("x1t", xs, j)])
            c0 = (t % 8) * 4
            sl = t % 2
            ACT(lambda e, xs=xs, j=j, c0=c0: e.activation(out=junk, in_=x1t[xs][j], func=AF.Square, accum_out=st5[:, c0:c0 + 1]),
                [("x1t", xs, j)], ["junk", ("st5", c0)])
            DVE(lambda e, c0=c0: e.tensor_scalar(out=st5[:, c0 + 1:c0 + 2], in0=st5[:, c0:c0 + 1], scalar1=1.0 / D, scalar2=EPS,
                                                 op0=ALU.mult, op1=ALU.add), [("st5", c0)], [("st5", c0 + 1)])
            POOL(lambda e, c0=c0: e.tensor_tensor(out=st5[:, c0 + 2:c0 + 3], in0=st5[:, c0 + 1:c0 + 2], in1=nhalf, op=ALU.pow),
                 [("st5", c0 + 1), "nhalf"], [("st5", c0 + 2)])
            DVE(lambda e, xs=xs, j=j, c0=c0, sl=sl: e.scalar_tensor_tensor(out=hbf[sl], in0=x1t[xs][j], scalar=st5[:, c0 + 2:c0 + 3],
                                                                           in1=gffn_b, op0=ALU.mult, op1=ALU.mult),
                [("x1t", xs, j), ("st5", c0 + 2), "gffn"], [("hbf", sl)])
            for kc in range(8):
                tp(bankbf(0)[:, kc * 128:(kc + 1) * 128], hbf[sl][:, kc * 128:(kc + 1) * 128], [("hbf", sl)], [bk(0)])
            ACT(lambda e, j=j: e.activation(out=h2T[:, :, j * 128:(j + 1) * 128], in_=bankbf(0).rearrange("p (k t) -> p k t", t=128),
                                            func=AF.Copy), [bk(0)], [("h2T", j)])
        hk2 = [("h2T", j) for j in range(NJ)] + ["W_up"]
        for cp in range(16):
            ku = 1 + cp % 2
            for c in range(2):
                cc = cp * 2 + c
                for kc in range(8):
                    mm(bank(ku)[:, c * TB:(c + 1) * TB], W_up[:, kc, cc * 128:(cc + 1) * 128], h2T[:, kc, :], kc == 0, kc == 7, hk2, [bk(ku)])
            rs = cp % 2
            ACT(lambda e, ku=ku, rs=rs: e.activation(out=rl[rs], in_=bank(ku), func=AF.Relu), [bk(ku)], [("rl", rs)])
            POOL(lambda e, rs=rs, cp=cp: e.tensor_tensor(out=aT[:, 2 * cp:2 * cp + 2, :], in0=rl[rs].rearrange("p (c t) -> p c t", t=TB),
                                                         in1=rl[rs].rearrange("p (c t) -> p c t", t=TB), op=ALU.mult),
                 [("rl", rs)], [("aT", cp)])
        for j in range(NJ):
            t = blk * NJ + j
            kd = 4 + (t % 2) * 2
            for half in range(2):
                for c in range(32):
                    mm(bank(kd + half), aT[:, c, j * 128:(j + 1) * 128], W_dn[:, c, half * 512:(half + 1) * 512], c == 0, c == 31,
                       [("aT", c // 2), "W_dn"], [bk(kd + half)])
            ys = nyo % 2
            nyo += 1
            DVE(lambda e, ys=ys, kd=kd, xs=xs, j=j: e.tensor_tensor(out=yo[ys], in0=pp[kd // 2][:, :], in1=x1t[xs][j], op=ALU.add),
                [bk(kd), bk(kd + 1), ("x1t", xs, j), ("yo", ys)], [("yo", ys)])
            S.dma("sp", lambda e, ys=ys, t=t: e.dma_start(out=yf[t * 128:(t + 1) * 128, :], in_=yo[ys]), f"yo{ys}", reads=[("yo", ys)])
    S.barrier()

    sems = {n: es.enter_context(nc.semaphore(n)) for n in sorted(S.sem_names)}
    with nc.Block() as block:
        S.emit(block, sems)
    es.close()
    return nc, dbg_out


def _prep_inputs(inputs, nseq, ncores):
    f32 = np.float32
    g = lambda k: np.ascontiguousarray(np.asarray(inputs[k]))
    wq_ = g("w_q_up")[0].reshape(384, 4, 192)
    wq_p = np.concatenate([wq_[:, :, :128].reshape(384, 512), wq_[:, :, 128:].reshape(384, 256)], axis=1)
    wkv_ = g("w_kv_up")[0].reshape(256, 4, 256)
    wkv_p = np.concatenate([wkv_[:, :, :128].reshape(256, 512), wkv_[:, :, 128:].reshape(256, 512)], axis=1)
    invf = (10000.0 ** (-(np.arange(0, 64, 2, dtype=f32)) / f32(64))).astype(f32).reshape(1, 32)
    shared = {
        "g_mix": g("g_mix_norm").reshape(1, D), "w_in": g("w_in")[0], "lb_param": g("lb_param")[:, 0:2, :],
        "g_hg": g("g_hgrn_out")[0], "g_cq": g("g_cq").reshape(1, 384), "wq": np.ascontiguousarray(wq_p),
        "g_ckv": g("g_ckv").reshape(1, 256), "wkv": np.ascontiguousarray(wkv_p), "g_q": g("g_q_norm").reshape(1, 192),
        "g_k": g("g_k_norm").reshape(1, 192), "g_mo": g("g_mla_out").reshape(1, 512), "w_out": g("w_out")[0],
        "g_ffn": g("g_ffn_norm").reshape(1, D), "w_up": g("w_up")[0], "w_down": g("w_down")[0], "invf": invf,
    }
    x = g("x")
    pos = g("positions").astype(np.int32)
    maps = []
    for c in range(ncores):
        m = dict(shared)
        m["x"] = np.ascontiguousarray(x[c * nseq:(c + 1) * nseq])
        m["pos"] = np.ascontiguousarray(pos[c * nseq:(c + 1) * nseq])
        maps.append(m)
    return maps


def kernel(**inputs):
    nseq = 4
    nc, _ = build(nseq)
    maps = _prep_inputs(inputs, nseq, NCORES)
    res = run_bass_kernel_spmd(nc, maps, core_ids=list(range(NCORES)))
    out = np.concatenate([np.asarray(r["y"]) for r in res.results], axis=0)
    return out.astype(np.float32, copy=False)
```

```python
import numpy as np
from contextlib import ExitStack
import concourse.bass as bass
import concourse.mybir as mybir
from concourse.bass_utils import run_bass_kernel_spmd

F32 = mybir.dt.float32
BF16 = mybir.dt.bfloat16
I32 = mybir.dt.int32
AF = mybir.ActivationFunctionType
ALU = mybir.AluOpType
AX = mybir.AxisListType

NCORES = 8
SEQ = 2048
NT = SEQ // 128
D = 1024
DIN = 3264
DFF = 4096
EPS = 1e-6
PI = float(np.pi)
ARENA_BYTES = 212480

ENGS = ("pe", "act", "dve", "pool", "sp")


def _cody_waite():
    two_pi = 2.0 * np.pi
    c1 = 6.28125
    r1 = two_pi - c1
    m, e = np.frexp(r1)
    c2 = float(np.ldexp(np.round(m * 2 ** 11) / 2 ** 11, e))
    c3 = float(np.float32(two_pi - c1 - c2))
    return c1, c2, c3


CW1, CW2, CW3 = _cody_waite()


class _Rec:
    def __getattr__(self, name):
        def f(*a, **k):
            self.call = (name, a, k)
            return self
        return f


class Sched:
    def __init__(self):
        self.q = {e: [] for e in ENGS}
        self.cnt = {e: 0 for e in ENGS}
        self.seen = {e: {} for e in ENGS}
        self.bufs = {}
        self.dma_tot = {}
        self.sem_names = set(ENGS)

    def _st(self, k):
        st = self.bufs.get(k)
        if st is None:
            st = self.bufs[k] = {"w": None, "r": {}}
        return st

    def _deps(self, eng, reads, writes):
        toks = []
        for k in reads:
            st = self._st(k)
            if st["w"] is not None:
                toks.append(st["w"])
        for k in writes:
            st = self._st(k)
            if st["w"] is not None:
                toks.append(st["w"])
            toks.extend(st["r"].items())
        waits = {}
        for (s, v) in toks:
            if s == "pe" and eng == "pe":
                continue
            if self.seen[eng].get(s, 0) >= v:
                continue
            if waits.get(s, 0) < v:
                waits[s] = v
        for s, v in waits.items():
            self.seen[eng][s] = v
        return list(waits.items())

    def _commit(self, tok, reads, writes):
        for k in reads:
            r = self._st(k)["r"]
            if r.get(tok[0], 0) < tok[1]:
                r[tok[0]] = tok[1]
        for k in writes:
            st = self._st(k)
            st["w"] = tok
            st["r"] = {}

    def op(self, eng, fn, reads=(), writes=()):
        rec = _Rec()
        fn(rec)
        waits = self._deps(eng, reads, writes)
        self.cnt[eng] += 1
        tok = (eng, self.cnt[eng])
        self.q[eng].append((rec.call, waits, (eng, 1)))
        self._commit(tok, reads, writes)

    def dma(self, eng, fn, sem, reads=(), writes=()):
        rec = _Rec()
        fn(rec)
        self.sem_names.add(sem)
        waits = self._deps(eng, reads, writes)
        self.dma_tot[sem] = self.dma_tot.get(sem, 0) + 16
        tok = (sem, self.dma_tot[sem])
        self.q[eng].append((rec.call, waits, (sem, 16)))
        self._commit(tok, reads, writes)

    def barrier(self):
        for e in ENGS:
            waits = []
            for e2 in ENGS:
                if self.cnt[e2] > self.seen[e].get(e2, 0):
                    waits.append((e2, self.cnt[e2]))
                    self.seen[e][e2] = self.cnt[e2]
            for s, tot in self.dma_tot.items():
                if tot > self.seen[e].get(s, 0):
                    waits.append((s, tot))
                    self.seen[e][s] = tot
            self.q[e].append((None, waits, None))
        self.bufs = {}

    def emit(self, block, sems):
        handles = {"pe": block.tensor, "act": block.scalar, "dve": block.vector,
                   "pool": block.gpsimd, "sp": block.sync}

        def mk(e):
            ops = self.q[e]

            def body(engine):
                for fn, waits, inc in ops:
                    for s, v in waits:
                        engine.wait_ge(sems[s], v)
                    if fn is not None:
                        name, a, k = fn
                        getattr(engine, name)(*a, **k).then_inc(sems[inc[0]], inc[1])
            return body

        for e in ENGS:
            handles[e](mk(e))


def _dsize(dt):
    return 2 if dt == BF16 else 4


class Bump:
    def __init__(self, arena, start, end):
        self.arena, self.cur, self.end = arena, start, end

    def take(self, shape, dt):
        n = int(np.prod(shape)) * _dsize(dt)
        off = (self.cur + 63) // 64 * 64
        n4 = (n + 3) // 4 * 4
        self.cur = off + n4
        assert self.cur <= self.end, (self.cur, self.end)
        ap = self.arena[:, off // 4:(off + n4) // 4]
        if dt != F32:
            ap = ap.bitcast(dt)
        if n4 != n:
            ap = ap[:, 0:int(np.prod(shape))]
        if len(shape) == 2:
            ap = ap.rearrange("p (a b) -> p a b", b=shape[1])
        elif len(shape) == 3:
            ap = ap.rearrange("p (a b c) -> p a b c", b=shape[1], c=shape[2])
        return ap


def build(nseq=4, dbg=None):
    nc = bass.Bass("TRN2", target_bir_lowering=False)
    S = Sched()
    dbg_out = {}

    def din(name, shape, dt=F32):
        return nc.dram_tensor(name, list(shape), dt, kind="ExternalInput").ap()

    x = din("x", [nseq, SEQ, D])
    pos = din("pos", [nseq, SEQ], I32)
    g_mix = din("g_mix", [1, D])
    w_in = din("w_in", [D, DIN])
    lb_param = din("lb_param", [2, 2, 512])
    g_hg = din("g_hg", [4, 128])
    g_cq = din("g_cq", [1, 384])
    wq = din("wq", [384, 768])
    g_ckv = din("g_ckv", [1, 256])
    wkv = din("wkv", [256, 1024])
    g_q = din("g_q", [1, 192])
    g_k = din("g_k", [1, 192])
    g_mo = din("g_mo", [1, 512])
    w_out = din("w_out", [D, D])
    g_ffn = din("g_ffn", [1, D])
    w_up = din("w_up", [D, DFF])
    w_down = din("w_down", [DFF, D])
    invf = din("invf", [1, 32])
    y = nc.dram_tensor("y", [nseq, SEQ, D], F32, kind="ExternalOutput").ap()
    x1s = nc.dram_tensor("x1s", [nseq, SEQ, D], F32).ap()

    es = ExitStack()
    arena = es.enter_context(nc.sbuf_tensor("arena", [128, ARENA_BYTES // 4], F32))[:]
    pp = [es.enter_context(nc.psum_tensor(f"pp{i}", [128, 1024], F32)) for i in range(4)]

    def bank(k):
        return pp[k // 2][:, (k % 2) * 512:(k % 2) * 512 + 512]

    def bankbf(k):
        return bank(k).bitcast(BF16)

    def bk(k):
        return ("bank", k)

    def PE(fn, r, w):
        S.op("pe", fn, r, w)

    def ACT(fn, r, w):
        S.op("act", fn, r, w)

    def DVE(fn, r, w):
        S.op("dve", fn, r, w)

    def POOL(fn, r, w):
        S.op("pool", fn, r, w)

    def mm(out, lhsT, rhs, start, stop, r, w, skip=False):
        if skip:
            PE(lambda e: e.matmul(out, lhsT=lhsT, rhs=rhs, start=start, stop=stop, skip_group_check=True), r, w)
        else:
            PE(lambda e: e.matmul(out, lhsT=lhsT, rhs=rhs, start=start, stop=stop), r, w)

    def tp(out, in_, r, w):
        PE(lambda e: e.transpose(out=out, in_=in_, identity=ident), list(r) + ["ident"], w)

    def tap(name, ap, key):
        if dbg is None or name not in dbg:
            return
        shp = list(ap.shape)
        t = nc.dram_tensor("dbg_" + name, shp, ap.dtype, kind="ExternalOutput").ap()
        dbg_out[name] = t
        S.dma("sp", lambda e: e.dma_start(out=t, in_=ap), "dbg", reads=key)

    P = Bump(arena, 0, ARENA_BYTES)
    ident = P.take([128], BF16)
    maskfb = P.take([256], I32)
    cols = P.take([64], F32)
    ones_b = P.take([2], BF16)
    nhalf = P.take([1], F32)
    ones_f = P.take([512], F32)
    MARK0 = P.cur
    C_GCQ, C_GCKV, C_GHG, C_LB, C_LNOML, C_LBP = 0, 3, 5, 9, 17, 25

    A = Bump(arena, MARK0, ARENA_BYTES)
    W_in = A.take([8, DIN], BF16)
    gmix_b = A.take([D], F32)
    gq_b = A.take([768], F32)
    gk_b = A.take([768], F32)
    gmo_b = A.take([512], F32)
    invf_b = A.take([32], F32)
    hT = A.take([8, SEQ], BF16)
    oT = A.take([8, SEQ], BF16)
    RMARK = A.cur

    R0 = Bump(arena, RMARK, ARENA_BYTES)
    identf = R0.take([128], F32)
    lbt = R0.take([8], F32)

    def cdma(out, in_, slow=False):
        if slow:
            S.dma("sp", lambda e: e.dma_start(out=out, in_=in_, allow_slow_non_contiguous=True), "const", writes=["const"])
        else:
            S.dma("sp", lambda e: e.dma_start(out=out, in_=in_), "const", writes=["const"])

    cdma(gmix_b, g_mix.partition_broadcast(128))
    for h in range(4):
        cdma(gq_b[:, h * 128:(h + 1) * 128], g_q[:, 0:128].partition_broadcast(128))
        cdma(gq_b[:, 512 + h * 64:512 + (h + 1) * 64], g_q[:, 128:192].partition_broadcast(128))
        cdma(gk_b[:, h * 128:(h + 1) * 128], g_k[:, 0:128].partition_broadcast(128))
        cdma(gk_b[:, 512 + h * 64:512 + (h + 1) * 64], g_k[:, 128:192].partition_broadcast(128))
    cdma(gmo_b, g_mo.partition_broadcast(128))
    cdma(invf_b, invf.partition_broadcast(128))
    cdma(cols[:, C_GCQ:C_GCQ + 3], g_cq[0].rearrange("(m p) -> p m", p=128), slow=True)
    cdma(cols[:, C_GCKV:C_GCKV + 2], g_ckv[0].rearrange("(m p) -> p m", p=128), slow=True)
    cdma(cols[:, C_GHG:C_GHG + 4], g_hg.rearrange("h e -> e h"), slow=True)
    for d_ in range(2):
        for s_ in range(2):
            o = C_LBP + (d_ * 2 + s_) * 4
            cdma(cols[:, o:o + 4], lb_param[d_, s_].rearrange("(h p) -> p h", p=128), slow=True)
    for kc in range(8):
        S.dma("pool", lambda e, kc=kc: e.dma_start(out=W_in[:, kc, :], in_=w_in[kc * 128:(kc + 1) * 128, :], max_dma_last_dim=4096),
              "w_in", writes=["W_in"])

    POOL(lambda e: e.memset(identf, 0.0), [], ["identf"])
    POOL(lambda e: e.affine_select(out=identf, in_=identf, pattern=[[-1, 128]], compare_op=ALU.not_equal, fill=1.0,
                                   base=0, channel_multiplier=1), ["identf"], ["identf"])
    DVE(lambda e: e.tensor_copy(out=ident, in_=identf), ["identf"], ["ident"])
    POOL(lambda e: e.iota(maskfb[:, 0:128], pattern=[[1, 128]], base=0, channel_multiplier=-1), [], ["maskfb"])
    POOL(lambda e: e.iota(maskfb[:, 128:256], pattern=[[-1, 128]], base=0, channel_multiplier=1), ["maskfb"], ["maskfb"])
    DVE(lambda e: e.tensor_single_scalar(out=maskfb, in_=maskfb, scalar=0, op=ALU.is_ge), ["maskfb"], ["maskfb"])
    POOL(lambda e: e.memset(ones_b, 1.0), [], ["ones_b"])
    POOL(lambda e: e.memset(nhalf, -0.5), [], ["nhalf"])
    POOL(lambda e: e.memset(ones_f, 1.0), [], ["ones_f"])
    DVE(lambda e: e.tensor_scalar(out=gq_b, in0=gq_b, scalar1=float(192 ** -0.5), scalar2=None, op0=ALU.mult), ["const"], ["const"])
    lbp = cols[:, C_LBP:C_LBP + 16].rearrange("p (d s h) -> p d s h", s=2, h=4)
    lbv = cols[:, C_LB:C_LB + 8]
    DVE(lambda e: e.tensor_tensor(out=lbt.rearrange("p (d h) -> p d h", h=4), in0=lbp[:, :, 1, :], in1=lbp[:, :, 0, :], op=ALU.subtract),
        ["const"], ["lbt"])
    ACT(lambda e: e.activation(out=lbt, in_=lbt, func=AF.Exp), ["lbt"], ["lbt"])
    DVE(lambda e: e.tensor_scalar(out=lbt, in0=lbt, scalar1=1.0, scalar2=None, op0=ALU.add), ["lbt"], ["lbt"])
    DVE(lambda e: e.reciprocal(out=lbv, in_=lbt), ["lbt", "const"], ["const"])
    ACT(lambda e: e.activation(out=cols[:, C_LNOML:C_LNOML + 8], in_=lbv, func=AF.Ln, scale=-1.0, bias=1.0), ["const"], ["const"])
    S.barrier()

    for b in range(nseq):
        R = Bump(arena, RMARK, ARENA_BYTES)
        V_all = R.take([NT, 512], BF16)
        XMARK = R.cur
        xin = [R.take([D], F32) for _ in range(2)]
        hbf = [R.take([D], BF16) for _ in range(2)]
        junk = R.take([D], BF16)
        st = R.take([64], F32)
        for i in range(NT):
            sl = i % 2
            S.dma("sp", lambda e, sl=sl, i=i: e.dma_start(out=xin[sl], in_=x[b, i * 128:(i + 1) * 128, :]), f"xin{sl}",
                  writes=[("xin", sl)])
            c0 = (i % 8) * 4
            ACT(lambda e, sl=sl, c0=c0: e.activation(out=junk, in_=xin[sl], func=AF.Square, accum_out=st[:, c0:c0 + 1]),
                [("xin", sl)], ["junk", ("st", c0)])
            DVE(lambda e, c0=c0: e.tensor_scalar(out=st[:, c0 + 1:c0 + 2], in0=st[:, c0:c0 + 1], scalar1=1.0 / D, scalar2=EPS,
                                                 op0=ALU.mult, op1=ALU.add), [("st", c0)], [("st", c0 + 1)])
            POOL(lambda e, c0=c0: e.tensor_tensor(out=st[:, c0 + 2:c0 + 3], in0=st[:, c0 + 1:c0 + 2], in1=nhalf, op=ALU.pow),
                 [("st", c0 + 1), "nhalf"], [("st", c0 + 2)])
            DVE(lambda e, sl=sl, c0=c0: e.scalar_tensor_tensor(out=hbf[sl], in0=xin[sl], scalar=st[:, c0 + 2:c0 + 3], in1=gmix_b,
                                                               op0=ALU.mult, op1=ALU.mult),
                [("xin", sl), ("st", c0 + 2), "const"], [("hbf", sl)])
            k = i % 2
            for kc in range(8):
                tp(bankbf(k)[:, kc * 128:(kc + 1) * 128], hbf[sl][:, kc * 128:(kc + 1) * 128], [("hbf", sl)], [bk(k)])
            ACT(lambda e, k=k, i=i: e.activation(out=hT[:, :, i * 128:(i + 1) * 128],
                                                 in_=bankbf(k).rearrange("p (k t) -> p k t", t=128), func=AF.Copy),
                [bk(k)], [("hT", i)])
        tap("hT", hT, [("hT", i) for i in range(NT)])
        for i in range(NT):
            k = 2 + i % 2
            for kc in range(8):
                mm(bank(k), hT[:, kc, i * 128:(i + 1) * 128], W_in[:, kc, 1536:2048], kc == 0, kc == 7,
                   [("hT", i), "W_in"], [bk(k)])
            DVE(lambda e, k=k, i=i: e.tensor_copy(out=V_all[:, i, :], in_=bank(k)), [bk(k)], [("V", i)])
        S.barrier()
        R = Bump(arena, XMARK, ARENA_BYTES)
        sgT = R.take([SEQ], BF16)
        TS = [[R.take([512], F32) for _ in range(6)] for _ in range(2)]
        QK = {(d_, w_): R.take([SEQ], BF16) for d_ in range(2) for w_ in "qk"}
        Zbf = [R.take([NT, 128], BF16) for _ in range(2)]
        Y = [[R.take([128], F32) for _ in range(2)] for _ in range(2)]
        Xs = [[R.take([128], F32) for _ in range(2)] for _ in range(2)]
        sqo = [TS[s_][0] for s_ in range(2)]
        onb = [TS[s_][1].bitcast(BF16)[:, 0:512] for s_ in range(2)]
        atm = [TS[s_][2].bitcast(BF16).rearrange("p (d n) -> p d n", d=2) for s_ in range(2)]
        ktok = [TS[s_][4].bitcast(BF16)[:, 0:512] for s_ in range(2)]
        Rall = [R.take([NT], F32) for _ in range(2)]
        Eall = [R.take([NT], F32) for _ in range(2)]
        dR = [R.take([3, NT], F32) for _ in range(2)]
        efac = [R.take([3, NT], F32) for _ in range(2)]
        carry = R.take([2], F32)
        st2 = R.take([64], F32)
        for h in range(4):
            for blk in range(4):
                tl = slice(blk * 512, (blk + 1) * 512)
                hk = [("hT", 4 * blk + j) for j in range(4)] + ["W_in"]
                kg = 6 + blk % 2
                for kc in range(8):
                    mm(bank(kg), W_in[:, kc, 2048 + h * 128:2048 + (h + 1) * 128], hT[:, kc, tl], kc == 0, kc == 7, hk, [bk(kg)])
                ACT(lambda e: e.activation(out=sgT[:, tl], in_=bank(kg), func=AF.Silu), [bk(kg)], [("sgT", blk)])

            def prep_mm(blk):
                tl = slice(blk * 512, (blk + 1) * 512)
                hk = [("hT", 4 * blk + j) for j in range(4)] + ["W_in"]
                for j, c0 in enumerate((0, 512, 1024)):
                    kk = (3 * blk + j) % 6
                    for kc in range(8):
                        mm(bank(kk), W_in[:, kc, c0 + h * 128:c0 + (h + 1) * 128], hT[:, kc, tl], kc == 0, kc == 7, hk, [bk(kk)])

            def front(it):
                blk, d_ = it // 2, it % 2
                T = TS[it % 2]
                tk = [("T", it % 2, j) for j in range(6)]
                kz = (3 * blk + 1 + d_) % 6
                lbc = cols[:, C_LB + d_ * 4 + h:C_LB + d_ * 4 + h + 1]
                ACT(lambda e: e.activation(out=T[0], in_=bank(kz), func=AF.Exp, scale=-1.0), [bk(kz)], [tk[0]])
                ACT(lambda e: e.activation(out=T[1], in_=T[0], func=AF.Ln, scale=lbc, bias=1.0), [tk[0], "const"], [tk[1]])
                ACT(lambda e: e.activation(out=T[2], in_=T[0], func=AF.Ln, scale=1.0, bias=1.0), [tk[0]], [tk[2]])
                POOL(lambda e: e.tensor_tensor(out=T[3], in0=T[1], in1=T[2], op=ALU.subtract), [tk[1], tk[2]], [tk[3]])
                DVE(lambda e: e.tensor_tensor(out=T[4], in0=bank(kz), in1=T[2], op=ALU.add), [bk(kz), tk[2]], [tk[4]])
                if blk == 0:
                    DVE(lambda e: e.tensor_tensor_scan(out=T[5], data0=ones_f, data1=T[3], initial=0.0, op0=ALU.mult, op1=ALU.add),
                        [tk[3], "ones_f"], [tk[5]])
                else:
                    DVE(lambda e: e.tensor_tensor_scan(out=T[5], data0=ones_f, data1=T[3], initial=carry[:, d_:d_ + 1],
                                                       op0=ALU.mult, op1=ALU.add), [tk[3], "ones_f", ("carry", d_)], [tk[5]])
                DVE(lambda e: e.tensor_copy(out=carry[:, d_:d_ + 1], in_=T[5][:, 511:512]), [tk[5]], [("carry", d_)])
                if d_ == 0:
                    Bx, kB = T[5], tk[5]
                else:
                    DVE(lambda e: e.tensor_tensor(out=T[1], in0=T[3], in1=T[5], op=ALU.subtract), [tk[3], tk[5]], [tk[1]])
                    Bx, kB = T[1], tk[1]
                DVE(lambda e: e.tensor_copy(out=Rall[d_][:, blk * 4:(blk + 1) * 4], in_=Bx[:, 63:512:128]), [kB], [("Rall", d_)])
                eo_ = 127 if d_ == 0 else 0
                DVE(lambda e: e.tensor_copy(out=Eall[d_][:, blk * 4:(blk + 1) * 4], in_=Bx[:, eo_:512:128]), [kB], [("Eall", d_)])
                DVE(lambda e: e.tensor_tensor(out=T[0].rearrange("p (c j) -> p c j", j=128), in0=Bx.rearrange("p (c j) -> p c j", j=128),
                                              in1=Rall[d_][:, blk * 4:(blk + 1) * 4].unsqueeze(2).to_broadcast([128, 4, 128]),
                                              op=ALU.subtract), [kB, ("Rall", d_)], [tk[0]])

            def back(it):
                blk, d_ = it // 2, it % 2
                tl = slice(blk * 512, (blk + 1) * 512)
                T = TS[it % 2]
                tk = [("T", it % 2, j) for j in range(6)]
                kq = (3 * blk) % 6
                lno = cols[:, C_LNOML + d_ * 4 + h:C_LNOML + d_ * 4 + h + 1]
                ACT(lambda e: e.activation(out=T[2], in_=T[0], func=AF.Exp), [tk[0]], [tk[2]])
                POOL(lambda e: e.tensor_tensor(out=T[3], in0=T[4], in1=T[0], op=ALU.add), [tk[4], tk[0]], [tk[3]])
                ACT(lambda e: e.activation(out=QK[(d_, "k")][:, tl], in_=T[3], func=AF.Exp, scale=-1.0, bias=lno),
                    [tk[3], "const"], [("QK", d_, "k", blk)])
                DVE(lambda e: e.tensor_tensor(out=QK[(d_, "q")][:, tl], in0=bank(kq), in1=T[2], op=ALU.mult),
                    [bk(kq), tk[2]], [("QK", d_, "q", blk)])

            prep_mm(0)
            front(0)
            for it in range(8):
                if it % 2 == 0 and it // 2 + 1 < 4:
                    prep_mm(it // 2 + 1)
                if it + 1 < 8:
                    front(it + 1)
                back(it)
            for d_ in range(2):
                if d_ == 0:
                    DVE(lambda e: e.tensor_tensor(out=dR[0][:, 0, 1:16], in0=Eall[0][:, 1:16], in1=Eall[0][:, 0:15], op=ALU.subtract),
                        [("Eall", 0)], [("dR", 0)])
                    DVE(lambda e: e.tensor_tensor(out=dR[0][:, 2, 1:16], in0=Rall[0][:, 1:16], in1=Eall[0][:, 0:15], op=ALU.subtract),
                        [("Eall", 0), ("Rall", 0), ("dR", 0)], [("dR", 0)])
                    DVE(lambda e: e.memset(dR[0][:, :, 0:1], 0.0), [("dR", 0)], [("dR", 0)])
                else:
                    DVE(lambda e: e.tensor_tensor(out=dR[1][:, 0, 0:15], in0=Eall[1][:, 0:15], in1=Eall[1][:, 1:16], op=ALU.subtract),
                        [("Eall", 1)], [("dR", 1)])
                    DVE(lambda e: e.tensor_tensor(out=dR[1][:, 2, 0:15], in0=Rall[1][:, 0:15], in1=Eall[1][:, 1:16], op=ALU.subtract),
                        [("Eall", 1), ("Rall", 1), ("dR", 1)], [("dR", 1)])
                    DVE(lambda e: e.memset(dR[1][:, :, 15:16], 0.0), [("dR", 1)], [("dR", 1)])
                DVE(lambda e: e.tensor_tensor(out=dR[d_][:, 1, :], in0=Eall[d_], in1=Rall[d_], op=ALU.subtract),
                    [("Eall", d_), ("Rall", d_), ("dR", d_)], [("dR", d_)])
                ACT(lambda e: e.activation(out=efac[d_], in_=dR[d_], func=AF.Exp), [("dR", d_)], [("efac", d_)])
            if h == 0 and b == 0:
                tap("Qf", QK[(0, "q")], [("QK", 0, "q", j) for j in range(4)])
                tap("Kf", QK[(0, "k")], [("QK", 0, "k", j) for j in range(4)])
            order = [list(range(15)), list(range(15, 0, -1))]
            pslot = {}
            ngrp = 0
            for gi in range(4):
                for d_ in range(2):
                    cs_ = order[d_][gi * 4:gi * 4 + 4]
                    kT, kP, ks = ngrp % 2, 2 + ngrp % 4, ngrp % 2
                    ngrp += 1
                    for j, c in enumerate(cs_):
                        tp(bankbf(kT)[:, j * 128:(j + 1) * 128], QK[(d_, "k")][:, c * 128:(c + 1) * 128], [("QK", d_, "k", c // 4)], [bk(kT)])
                    nn = len(cs_) * 128
                    ACT(lambda e: e.activation(out=ktok[ks][:, 0:nn], in_=bankbf(kT)[:, 0:nn], func=AF.Copy), [bk(kT)], [("T", ks, 4)])
                    for j, c in enumerate(cs_):
                        mm(bank(kP)[:, j * 128:(j + 1) * 128], ktok[ks][:, j * 128:(j + 1) * 128], V_all[:, c, h * 128:(h + 1) * 128], True, True,
                           [("T", ks, 4), ("V", c)], [bk(kP)])
                        pslot[(d_, c)] = (kP, j)
                for idx in range(gi * 4, min(gi * 4 + 4, 15)):
                    for d_ in range(2):
                        c = order[d_][idx]
                        kP_, j_ = pslot[(d_, c)]
                        pw = Xs[d_][idx % 2]
                        ACT(lambda e: e.activation(out=pw, in_=bank(kP_)[:, j_ * 128:(j_ + 1) * 128], func=AF.Copy, scale=efac[d_][:, 1, c:c + 1]),
                            [bk(kP_), ("efac", d_)], [("Xs", d_, idx % 2)])
                    for d_ in range(2):
                        c = order[d_][idx]
                        pw = Xs[d_][idx % 2]
                        yc, yp = Y[d_][idx % 2], Y[d_][(idx + 1) % 2]
                        if idx == 0:
                            DVE(lambda e: e.tensor_copy(out=yc, in_=pw), [("Xs", d_, idx % 2)], [("Y", d_, idx % 2)])
                        else:
                            DVE(lambda e: e.scalar_tensor_tensor(out=yc, in0=yp, scalar=efac[d_][:, 0, c:c + 1], in1=pw, op0=ALU.mult, op1=ALU.add),
                                [("Y", d_, (idx + 1) % 2), ("Xs", d_, idx % 2), ("efac", d_)], [("Y", d_, idx % 2)])
                    for d_ in range(2):
                        c = order[d_][idx]
                        yc = Y[d_][idx % 2]
                        nxt = c + 1 if d_ == 0 else c - 1
                        ACT(lambda e: e.activation(out=Zbf[d_][:, nxt, :], in_=yc, func=AF.Copy, scale=efac[d_][:, 2, nxt:nxt + 1]),
                            [("Y", d_, idx % 2), ("efac", d_)], [("Zbf", d_, nxt)])
            for kk in range(4):
                DVE(lambda e: e.memset(bank(kk), 0.0), [], [bk(kk)])
            for s_ in range(2):
                POOL(lambda e: e.memset(atm[s_], 0.0), [], [("T", s_, 2)])

            def at_mm(g):
                ka = (g % 2) * 2
                for d_ in range(2):
                    for j in range(4):
                        c = g * 4 + j
                        q_, k_ = QK[(d_, "q")], QK[(d_, "k")]
                        o_ = j * 128
                        rk = [("QK", d_, "k", g), ("QK", d_, "q", g)]
                        if d_ == 0:
                            mm(bank(ka)[0:64, o_:o_ + 128], k_[:, c * 128:c * 128 + 64], q_[:, c * 128:(c + 1) * 128], True, True, rk, [bk(ka)])
                            mm(bank(ka)[64:128, o_ + 64:o_ + 128], k_[:, c * 128 + 64:(c + 1) * 128], q_[:, c * 128 + 64:(c + 1) * 128],
                               True, True, rk, [bk(ka)])
                        else:
                            mm(bank(ka + 1)[0:64, o_:o_ + 64], k_[:, c * 128:c * 128 + 64], q_[:, c * 128:c * 128 + 64], True, True, rk, [bk(ka + 1)])
                            mm(bank(ka + 1)[64:128, o_:o_ + 128], k_[:, c * 128 + 64:(c + 1) * 128], q_[:, c * 128:(c + 1) * 128],
                               True, True, rk, [bk(ka + 1)])

            def mask_copy(g):
                ka = (g % 2) * 2
                sa = g % 2
                for d_ in range(2):
                    mk = maskfb[:, d_ * 128:(d_ + 1) * 128].unsqueeze(1).to_broadcast([128, 4, 128])
                    DVE(lambda e: e.copy_predicated(out=atm[sa][:, d_, :].rearrange("p (c j) -> p c j", j=128), mask=mk,
                                                    data=bank(ka + d_).rearrange("p (c j) -> p c j", j=128)),
                        [bk(ka + d_), "maskfb", ("T", sa, 2)], [("T", sa, 2)])

            def o_mm(g):
                ko = 4 + g % 2
                sa = g % 2
                for j in range(4):
                    c = g * 4 + j
                    cs = slice(c * 128, (c + 1) * 128)
                    grp = []
                    if c > 0:
                        grp.append((QK[(0, "q")][:, cs], Zbf[0][:, c, :], [("QK", 0, "q", g), ("Zbf", 0, c)]))
                    if c < NT - 1:
                        grp.append((QK[(1, "q")][:, cs], Zbf[1][:, c, :], [("QK", 1, "q", g), ("Zbf", 1, c)]))
                    vv = V_all[:, c, h * 128:(h + 1) * 128]
                    grp.append((atm[sa][:, 0, j * 128:(j + 1) * 128], vv, [("T", sa, 2), ("V", c)]))
                    grp.append((atm[sa][:, 1, j * 128:(j + 1) * 128], vv, [("T", sa, 2), ("V", c)]))
                    for gi_, (l_, r_, rk) in enumerate(grp):
                        mm(bank(ko)[:, j * 128:(j + 1) * 128], l_, r_, gi_ == 0, gi_ == len(grp) - 1, rk, [bk(ko)])

            def epi_a(g):
                ko = 4 + g % 2
                sa = g % 2
                c0 = (g % 4) * 12
                ACT(lambda e: e.activation(out=sqo[sa], in_=bank(ko), func=AF.Square), [bk(ko)], [("T", sa, 0)])
                DVE(lambda e: e.tensor_reduce(out=st2[:, c0:c0 + 4], in_=sqo[sa].rearrange("p (c j) -> p c j", j=128), axis=AX.X, op=ALU.add),
                    [("T", sa, 0)], [("st2", c0)])
                DVE(lambda e: e.tensor_scalar(out=st2[:, c0 + 4:c0 + 8], in0=st2[:, c0:c0 + 4], scalar1=1.0 / 128, scalar2=EPS,
                                              op0=ALU.mult, op1=ALU.add), [("st2", c0)], [("st2", c0 + 4)])
                POOL(lambda e: e.tensor_tensor(out=st2[:, c0 + 8:c0 + 12], in0=st2[:, c0 + 4:c0 + 8], in1=nhalf.to_broadcast([128, 4]),
                                               op=ALU.pow), [("st2", c0 + 4), "nhalf"], [("st2", c0 + 8)])
                DVE(lambda e: e.tensor_tensor(out=onb[sa].rearrange("p (c j) -> p c j", j=128), in0=bank(ko).rearrange("p (c j) -> p c j", j=128),
                                              in1=st2[:, c0 + 8:c0 + 12].unsqueeze(2).to_broadcast([128, 4, 128]), op=ALU.mult),
                    [bk(ko), ("st2", c0 + 8)], [("T", sa, 1)])

            def epi_b(g):
                kt = 6 + g % 2
                sa = g % 2
                for j in range(4):
                    tp(bankbf(kt)[:, j * 128:(j + 1) * 128], onb[sa][:, j * 128:(j + 1) * 128], [("T", sa, 1)], [bk(kt)])
                gs = slice(g * 512, (g + 1) * 512)
                DVE(lambda e: e.scalar_tensor_tensor(out=oT[:, h, gs], in0=bankbf(kt)[:, 0:512], scalar=cols[:, C_GHG + h:C_GHG + h + 1],
                                                     in1=sgT[:, gs], op0=ALU.mult, op1=ALU.mult),
                    [bk(kt), "const", ("sgT", g)], [("oT", h, g)])

            at_mm(0); mask_copy(0); at_mm(1); o_mm(0); mask_copy(1); at_mm(2); epi_a(0); o_mm(1); mask_copy(2); at_mm(3)
            epi_b(0); epi_a(1); o_mm(2); mask_copy(3); epi_b(1); epi_a(2); o_mm(3); epi_b(2); epi_a(3); epi_b(3)
        tap("oTa", oT[:, 0:4, :], [("oT", h, g) for h in range(4) for g in range(4)])
        S.barrier()
        R = Bump(arena, RMARK, ARENA_BYTES)
        KT = R.take([6, SEQ], BF16)
        Vaug = R.take([NT, 4, 130], BF16)
        SMARK = R.cur
        Wq = R.take([3, 768], BF16)
        Wkv = R.take([2, 1024], BF16)
        posi = R.take([NT], I32)
        posf = R.take([NT], F32)
        cosT = R.take([NT, 32], F32)
        sinT = R.take([NT, 32], F32)
        TMARK = R.cur
        ang = R.take([NT, 32], F32)
        kqi = R.take([NT, 32], I32)
        t_a = R.take([NT, 32], F32)
        t_b = R.take([NT, 32], F32)
        for m in range(3):
            S.dma("pool", lambda e, m=m: e.dma_start(out=Wq[:, m, :], in_=wq[m * 128:(m + 1) * 128, :]), "wq", writes=["Wq"])
        for m in range(2):
            S.dma("pool", lambda e, m=m: e.dma_start(out=Wkv[:, m, :], in_=wkv[m * 128:(m + 1) * 128, :]), "wkv", writes=["Wkv"])
        POOL(lambda e: e.memset(Vaug[:, :, :, 128:130], 1.0), [], ["Vones"])
        S.dma("sp", lambda e: e.dma_start(out=posi, in_=pos[b].rearrange("(n p) -> p n", p=128), allow_slow_non_contiguous=True),
              "posi", writes=["posi"])
        DVE(lambda e: e.tensor_copy(out=posf, in_=posi), ["posi"], ["posf"])
        DVE(lambda e: e.tensor_tensor(out=ang, in0=posf.unsqueeze(2).to_broadcast([128, NT, 32]),
                                      in1=invf_b.unsqueeze(1).to_broadcast([128, NT, 32]), op=ALU.mult), ["posf", "const"], ["ang"])
        DVE(lambda e: e.tensor_scalar(out=kqi, in0=ang, scalar1=float(1.0 / (2 * PI)), scalar2=None, op0=ALU.mult), ["ang"], ["kqi"])
        DVE(lambda e: e.tensor_copy(out=t_a, in_=kqi), ["kqi"], ["t_a"])
        C1, C2, C3 = CW1, CW2, CW3
        DVE(lambda e: e.scalar_tensor_tensor(out=t_b, in0=t_a, scalar=-C1, in1=ang, op0=ALU.mult, op1=ALU.add), ["t_a", "ang"], ["t_b"])
        DVE(lambda e: e.scalar_tensor_tensor(out=ang, in0=t_a, scalar=-C2, in1=t_b, op0=ALU.mult, op1=ALU.add), ["t_a", "t_b", "ang"], ["ang"])
        DVE(lambda e: e.scalar_tensor_tensor(out=t_b, in0=t_a, scalar=-C3, in1=ang, op0=ALU.mult, op1=ALU.add), ["t_a", "ang", "t_b"], ["t_b"])
        DVE(lambda e: e.tensor_scalar(out=ang, in0=t_b, scalar1=PI, scalar2=-PI, op0=ALU.min, op1=ALU.max), ["t_b", "ang"], ["ang"])
        ACT(lambda e: e.activation(out=sinT, in_=ang, func=AF.Sin), ["ang"], ["sinT"])
        DVE(lambda e: e.tensor_scalar(out=t_a, in0=t_b, scalar1=PI / 2, scalar2=None, op0=ALU.add), ["t_b", "t_a"], ["t_a"])
        DVE(lambda e: e.tensor_scalar(out=t_b, in0=t_a, scalar1=PI, scalar2=2 * PI, op0=ALU.is_gt, op1=ALU.mult), ["t_a", "t_b"], ["t_b"])
        DVE(lambda e: e.tensor_tensor(out=t_a, in0=t_a, in1=t_b, op=ALU.subtract), ["t_a", "t_b"], ["t_a"])
        DVE(lambda e: e.tensor_scalar(out=t_a, in0=t_a, scalar1=PI, scalar2=-PI, op0=ALU.min, op1=ALU.max), ["t_a"], ["t_a"])
        ACT(lambda e: e.activation(out=cosT, in_=t_a, func=AF.Sin), ["t_a"], ["cosT"])
        if b == 0:
            tap("cosT", cosT, ["cosT"])
            tap("sinT", sinT, ["sinT"])
        S.barrier()
        R = Bump(arena, TMARK, ARENA_BYTES)
        cT = R.take([5, 512], BF16)
        sq = R.take([5, 512], BF16)
        sqq = R.take([768], F32)
        t1 = R.take([768], F32)
        qf = R.take([1024], BF16)
        kf = R.take([768], BF16)
        krs = R.take([64], F32)
        krr = R.take([64], F32)
        ta = R.take([256], F32)
        tb = R.take([256], F32)
        st3 = R.take([64], F32)
        junk = R.take([64], BF16)
        POOL(lambda e: e.memset(qf[:, 512:1024], 0.0), [], ["qf"])
        for blk in range(4):
            tl = slice(blk * 512, (blk + 1) * 512)
            hk = [("hT", 4 * blk + j) for j in range(4)] + ["W_in"]
            for m in range(5):
                c0 = 2560 + m * 128
                kc_ = (m % 2) * 2
                for kc in range(8):
                    mm(bank(kc_), W_in[:, kc, c0:c0 + 128], hT[:, kc, tl], kc == 0, kc == 7, hk, [bk(kc_)])
                gcol = cols[:, C_GCQ + m:C_GCQ + m + 1]
                ACT(lambda e, m=m, gcol=gcol: e.activation(out=cT[:, m, :], in_=bank(kc_), func=AF.Copy, scale=gcol),
                    [bk(kc_), "const"], [("cT", m)])
                ACT(lambda e, m=m: e.activation(out=sq[:, m, :], in_=bank(kc_), func=AF.Square), [bk(kc_)], [("sq", m)])
            for j in range(4):
                i = blk * 4 + j
                js = slice(j * 128, (j + 1) * 128)
                its = slice(i * 128, (i + 1) * 128)
                c0 = (i % 2) * 32
                for m in range(3):
                    mm(bank(1)[:, 0:1], sq[:, m, js], ones_b[:, 0:1], m == 0, m == 2, [("sq", m), "ones_b"], [bk(1)])
                for m in range(3, 5):
                    mm(bank(1)[:, 2:3], sq[:, m, js], ones_b[:, 0:1], m == 3, m == 4, [("sq", m), "ones_b"], [bk(1)])
                for kc in range(8):
                    mm(bank(1)[:, 64:128], hT[:, kc, its], W_in[:, kc, 3200:3264], kc == 0, kc == 7, [("hT", i), "W_in"], [bk(1)])
                sc = lambda o: st3[:, c0 + o:c0 + o + 1]
                skey = lambda o: ("st3", c0 + o)
                DVE(lambda e, sc=sc: e.tensor_scalar(out=sc(0), in0=bank(1)[:, 0:1], scalar1=1.0 / 384, scalar2=EPS, op0=ALU.mult, op1=ALU.add),
                    [bk(1)], [skey(0)])
                DVE(lambda e, sc=sc: e.tensor_scalar(out=sc(1), in0=bank(1)[:, 2:3], scalar1=1.0 / 256, scalar2=EPS, op0=ALU.mult, op1=ALU.add),
                    [bk(1)], [skey(1)])
                POOL(lambda e, sc=sc: e.tensor_tensor(out=sc(2), in0=sc(0), in1=nhalf, op=ALU.pow), [skey(0), "nhalf"], [skey(2)])
                POOL(lambda e, sc=sc: e.tensor_tensor(out=sc(3), in0=sc(1), in1=nhalf, op=ALU.pow), [skey(1), "nhalf"], [skey(3)])
                for m in range(3):
                    mm(bank(3), cT[:, m, js], Wq[:, m, 0:512], m == 0, m == 2, [("cT", m), "Wq"], [bk(3)])
                for m in range(3):
                    mm(bank(4)[:, 0:256], cT[:, m, js], Wq[:, m, 512:768], m == 0, m == 2, [("cT", m), "Wq"], [bk(4)])
                for m in range(2):
                    mm(bank(5), cT[:, 3 + m, js], Wkv[:, m, 0:512], m == 0, m == 1, [("cT", 3 + m), "Wkv"], [bk(5)])
                for m in range(2):
                    mm(bank(6), cT[:, 3 + m, js], Wkv[:, m, 512:1024], m == 0, m == 1, [("cT", 3 + m), "Wkv"], [bk(6)])
                ACT(lambda e: e.activation(out=sqq[:, 0:512], in_=bank(3), func=AF.Square), [bk(3)], ["sqq"])
                ACT(lambda e: e.activation(out=sqq[:, 512:768], in_=bank(4)[:, 0:256], func=AF.Square), [bk(4), "sqq"], ["sqq"])
                DVE(lambda e, sc=sc: e.tensor_reduce(out=st3[:, c0 + 4:c0 + 8], in_=sqq[:, 0:512].rearrange("p (h j) -> p h j", j=128),
                                                     axis=AX.X, op=ALU.add), ["sqq"], [skey(4)])
                DVE(lambda e, sc=sc: e.tensor_reduce(out=st3[:, c0 + 8:c0 + 12], in_=sqq[:, 512:768].rearrange("p (h j) -> p h j", j=64),
                                                     axis=AX.X, op=ALU.add), ["sqq"], [skey(8)])
                DVE(lambda e: e.tensor_tensor(out=st3[:, c0 + 4:c0 + 8], in0=st3[:, c0 + 4:c0 + 8], in1=st3[:, c0 + 8:c0 + 12], op=ALU.add),
                    [skey(4), skey(8)], [skey(4)])
                DVE(lambda e, sc=sc: e.tensor_tensor(out=sc(12), in0=sc(2), in1=sc(2), op=ALU.mult), [skey(2)], [skey(12)])
                DVE(lambda e, sc=sc: e.tensor_scalar(out=st3[:, c0 + 4:c0 + 8], in0=st3[:, c0 + 4:c0 + 8], scalar1=sc(12), scalar2=1.0 / 192,
                                                     op0=ALU.mult, op1=ALU.mult), [skey(4), skey(12)], [skey(4)])
                DVE(lambda e: e.tensor_scalar(out=st3[:, c0 + 4:c0 + 8], in0=st3[:, c0 + 4:c0 + 8], scalar1=EPS, scalar2=None, op0=ALU.add),
                    [skey(4)], [skey(4)])
                POOL(lambda e: e.tensor_tensor(out=st3[:, c0 + 8:c0 + 12], in0=st3[:, c0 + 4:c0 + 8],
                                               in1=nhalf.to_broadcast([128, 4]), op=ALU.pow), [skey(4), "nhalf", skey(8)], [skey(8)])
                DVE(lambda e, sc=sc: e.tensor_scalar(out=st3[:, c0 + 8:c0 + 12], in0=st3[:, c0 + 8:c0 + 12], scalar1=sc(2), scalar2=None,
                                                     op0=ALU.mult), [skey(8), skey(2)], [skey(8)])
                fq = st3[:, c0 + 8:c0 + 12]
                DVE(lambda e, fq=fq: e.tensor_tensor(out=t1[:, 0:512].rearrange("p (h j) -> p h j", j=128),
                                                     in0=bank(3).rearrange("p (h j) -> p h j", j=128),
                                                     in1=fq.unsqueeze(2).to_broadcast([128, 4, 128]), op=ALU.mult), [bk(3), skey(8)], ["t1"])
                DVE(lambda e: e.tensor_tensor(out=qf[:, 0:512], in0=t1[:, 0:512], in1=gq_b[:, 0:512], op=ALU.mult), ["t1", "const"], ["qf"])
                DVE(lambda e, fq=fq: e.tensor_tensor(out=t1[:, 512:768].rearrange("p (h j) -> p h j", j=64),
                                                     in0=bank(4)[:, 0:256].rearrange("p (h j) -> p h j", j=64),
                                                     in1=fq.unsqueeze(2).to_broadcast([128, 4, 64]), op=ALU.mult), [bk(4), skey(8), "t1"], ["t1"])
                DVE(lambda e: e.tensor_tensor(out=t1[:, 512:768], in0=t1[:, 512:768], in1=gq_b[:, 512:768], op=ALU.mult), ["t1", "const"], ["t1"])

                def rope(src, nh_, dst, rk, wk):
                    s4 = src.rearrange("p (h a r) -> p h a r", a=2, r=32)
                    a4 = ta[:, 0:nh_ * 64].rearrange("p (h a r) -> p h a r", a=2, r=32)
                    b4 = tb[:, 0:nh_ * 64].rearrange("p (h a r) -> p h a r", a=2, r=32)
                    cb = cosT[:, i, :].unsqueeze(1).unsqueeze(1).to_broadcast([128, nh_, 2, 32])
                    sb_ = sinT[:, i, :].unsqueeze(1).to_broadcast([128, nh_, 32])
                    DVE(lambda e: e.tensor_tensor(out=a4, in0=s4, in1=cb, op=ALU.mult), rk + ["cosT"], ["ta"])
                    DVE(lambda e: e.scalar_tensor_tensor(out=b4[:, :, 0, :], in0=s4[:, :, 1, :], scalar=-1.0, in1=sb_, op0=ALU.mult, op1=ALU.mult),
                        rk + ["sinT"], ["tb"])
                    DVE(lambda e: e.tensor_tensor(out=b4[:, :, 1, :], in0=s4[:, :, 0, :], in1=sb_, op=ALU.mult), rk + ["sinT", "tb"], ["tb"])
                    if isinstance(dst, list):
                        a3 = ta[:, 0:nh_ * 64].rearrange("p (h j) -> p h j", j=64)
                        b3 = tb[:, 0:nh_ * 64].rearrange("p (h j) -> p h j", j=64)
                        for par, dv in enumerate(dst):
                            DVE(lambda e, par=par, dv=dv: e.tensor_tensor(out=dv, in0=a3[:, par::2, :], in1=b3[:, par::2, :], op=ALU.add),
                                ["ta", "tb"] + wk, wk)
                    else:
                        DVE(lambda e: e.tensor_tensor(out=dst, in0=ta[:, 0:nh_ * 64], in1=tb[:, 0:nh_ * 64], op=ALU.add), ["ta", "tb"], wk)

                qz = qf[:, 512:1024].rearrange("p (i r) -> p i r", r=256)
                rope(t1[:, 512:768], 4, [qz[:, :, 0:64], qz[:, :, 192:256]], ["t1"], ["qf"])
                ACT(lambda e: e.activation(out=sqq[:, 0:512], in_=bank(5), func=AF.Square), [bk(5), "sqq"], ["sqq"])
                DVE(lambda e: e.tensor_reduce(out=st3[:, c0 + 16:c0 + 20], in_=sqq[:, 0:512].rearrange("p (h j) -> p h j", j=128),
                                              axis=AX.X, op=ALU.add), ["sqq"], [skey(16)])
                DVE(lambda e: e.tensor_copy(out=krs, in_=bank(1)[:, 64:128]), [bk(1)], ["krs"])
                ACT(lambda e, sc=sc: e.activation(out=junk[:, 0:64], in_=krs, func=AF.Square, accum_out=sc(13)), ["krs"], ["junk", skey(13)])
                DVE(lambda e, sc=sc: e.tensor_tensor(out=sc(14), in0=sc(3), in1=sc(3), op=ALU.mult), [skey(3)], [skey(14)])
                DVE(lambda e, sc=sc: e.tensor_scalar(out=st3[:, c0 + 16:c0 + 20], in0=st3[:, c0 + 16:c0 + 20], scalar1=sc(14), scalar2=sc(13),
                                                     op0=ALU.mult, op1=ALU.add), [skey(16), skey(14), skey(13)], [skey(16)])
                DVE(lambda e: e.tensor_scalar(out=st3[:, c0 + 16:c0 + 20], in0=st3[:, c0 + 16:c0 + 20], scalar1=1.0 / 192, scalar2=EPS,
                                              op0=ALU.mult, op1=ALU.add), [skey(16)], [skey(16)])
                POOL(lambda e: e.tensor_tensor(out=st3[:, c0 + 20:c0 + 24], in0=st3[:, c0 + 16:c0 + 20],
                                               in1=nhalf.to_broadcast([128, 4]), op=ALU.pow), [skey(16), "nhalf"], [skey(20)])
                DVE(lambda e, sc=sc: e.tensor_scalar(out=st3[:, c0 + 24:c0 + 28], in0=st3[:, c0 + 20:c0 + 24], scalar1=sc(3), scalar2=None,
                                                     op0=ALU.mult), [skey(20), skey(3)], [skey(24)])
                rk_ = st3[:, c0 + 20:c0 + 24]
                fkn = st3[:, c0 + 24:c0 + 28]
                DVE(lambda e, fkn=fkn: e.tensor_tensor(out=t1[:, 0:512].rearrange("p (h j) -> p h j", j=128),
                                                       in0=bank(5).rearrange("p (h j) -> p h j", j=128),
                                                       in1=fkn.unsqueeze(2).to_broadcast([128, 4, 128]), op=ALU.mult),
                    [bk(5), skey(24), "t1"], ["t1"])
                DVE(lambda e: e.tensor_tensor(out=kf[:, 0:512], in0=t1[:, 0:512], in1=gk_b[:, 0:512], op=ALU.mult), ["t1", "const"], ["kf"])
                DVE(lambda e: e.tensor_tensor(out=krs, in0=krs, in1=gk_b[:, 512:576], op=ALU.mult), ["krs", "const"], ["krs"])
                rope(krs, 1, krr, ["krs"], ["krr"])
                DVE(lambda e, rk_=rk_: e.tensor_tensor(out=kf[:, 512:768].rearrange("p (h j) -> p h j", j=64),
                                                       in0=krr.unsqueeze(1).to_broadcast([128, 4, 64]),
                                                       in1=rk_.unsqueeze(2).to_broadcast([128, 4, 64]), op=ALU.mult),
                    ["krr", skey(20), "kf"], ["kf"])
                DVE(lambda e, sc=sc, i=i: e.tensor_scalar(out=Vaug[:, i, :, 0:128], in0=bank(6).rearrange("p (h j) -> p h j", j=128),
                                                          scalar1=sc(3), scalar2=None, op0=ALU.mult), [bk(6), skey(3)], [("Vaug", i)])
                for m in range(8):
                    tp(bankbf(7)[:, m * 128:(m + 1) * 128], qf[:, m * 128:(m + 1) * 128], ["qf"], [bk(7)])
                ACT(lambda e, its=its: e.activation(out=hT[:, :, its], in_=bankbf(7).rearrange("p (m t) -> p m t", t=128),
                                                    func=AF.Copy), [bk(7)], [("hT", i)])
                for m in range(6):
                    tp(bankbf(7)[:, m * 128:(m + 1) * 128], kf[:, m * 128:(m + 1) * 128], ["kf"], [bk(7)])
                ACT(lambda e, its=its: e.activation(out=KT[:, :, its], in_=bankbf(7)[:, 0:768].rearrange("p (m t) -> p m t", t=128),
                                                    func=AF.Copy), [bk(7)], [("KT", i)])
        if b == 0:
            tap("QT", hT[:, 0:6, :], [("hT", i) for i in range(NT)])
            tap("KT", KT, [("KT", i) for i in range(NT)])
            tap("Vaug", Vaug, [("Vaug", i) for i in range(NT)] + ["Vones"])
        S.barrier()
        R = Bump(arena, SMARK, ARENA_BYTES)
        PT = [R.take([512], BF16) for _ in range(3)]
        obuf = R.take([4, 512], F32)
        obn = [R.take([512], BF16) for _ in range(2)]
        st4 = R.take([64], F32)
        junk = R.take([512], BF16)
        QT = hT
        it = 0
        npt = 0
        for qb in range(4):
            qs = slice(qb * 512, (qb + 1) * 512)
            qkeys = [("hT", 4 * qb + j) for j in range(4)]
            for h in range(4):
                ko = 2 + (it % 2) * 2
                it += 1
                DVE(lambda e, ko=ko: e.memset(pp[ko // 2][:, :], 0.0), [], [bk(ko), bk(ko + 1)])
                rp = slice((h % 2) * 64, (h % 2) * 64 + 64)
                rc = 4 + h // 2
                def qk(kc):
                    ksl = slice(kc * 128, (kc + 1) * 128)
                    ks_ = kc % 2
                    mm(bank(ks_), KT[:, h, ksl], QT[:, h, qs], True, False, [("KT", kc)] + qkeys, [bk(ks_)])
                    mm(bank(ks_), KT[:, rc, ksl], QT[:, 4 + h, qs], False, True, [("KT", kc)] + qkeys, [bk(ks_)])

                qk(0)
                for kc in range(NT):
                    ks_ = kc % 2
                    ps_ = npt % 3
                    npt += 1
                    ACT(lambda e, ks_=ks_, ps_=ps_: e.activation(out=PT[ps_], in_=bank(ks_), func=AF.Exp), [bk(ks_)], [("PT", ps_)])
                    if kc + 1 < NT:
                        qk(kc + 1)
                    for j in range(4):
                        ob = ko + j // 2
                        mm(bank(ob)[:, (j % 2) * 256:(j % 2) * 256 + 129], PT[ps_][:, j * 128:(j + 1) * 128], Vaug[:, kc, h, 0:129],
                           False, False, [("PT", ps_), ("Vaug", kc), "Vones"], [bk(ob)], skip=True)
                for j in range(4):
                    ob = ko + j // 2
                    o0 = (j % 2) * 256
                    c0 = ((it * 4 + j) % 16) * 2
                    DVE(lambda e, ob=ob, o0=o0, c0=c0: e.reciprocal(out=st4[:, c0:c0 + 1], in_=bank(ob)[:, o0 + 128:o0 + 129]),
                        [bk(ob)], [("st4", c0)])
                    DVE(lambda e, ob=ob, o0=o0, c0=c0, j=j, h=h: e.tensor_scalar(out=obuf[:, j, h * 128:(h + 1) * 128],
                                                                                 in0=bank(ob)[:, o0:o0 + 128], scalar1=st4[:, c0:c0 + 1],
                                                                                 scalar2=None, op0=ALU.mult),
                        [bk(ob), ("st4", c0)], [("obuf", j)])
            for j in range(4):
                i = qb * 4 + j
                c0 = 32 + (i % 8) * 4
                sj = i % 2
                ACT(lambda e, j=j, c0=c0: e.activation(out=junk[:, 0:512], in_=obuf[:, j, :], func=AF.Square, accum_out=st4[:, c0:c0 + 1]),
                    [("obuf", j)], ["junk", ("st4", c0)])
                DVE(lambda e, c0=c0: e.tensor_scalar(out=st4[:, c0 + 1:c0 + 2], in0=st4[:, c0:c0 + 1], scalar1=1.0 / 512, scalar2=EPS,
                                                     op0=ALU.mult, op1=ALU.add), [("st4", c0)], [("st4", c0 + 1)])
                POOL(lambda e, c0=c0: e.tensor_tensor(out=st4[:, c0 + 2:c0 + 3], in0=st4[:, c0 + 1:c0 + 2], in1=nhalf, op=ALU.pow),
                     [("st4", c0 + 1), "nhalf"], [("st4", c0 + 2)])
                DVE(lambda e, j=j, c0=c0, sj=sj: e.scalar_tensor_tensor(out=obn[sj], in0=obuf[:, j, :], scalar=st4[:, c0 + 2:c0 + 3],
                                                                        in1=gmo_b, op0=ALU.mult, op1=ALU.mult),
                    [("obuf", j), ("st4", c0 + 2), "const"], [("obn", sj)])
                kt = 6 + i % 2
                for m in range(4):
                    tp(bankbf(kt)[:, m * 128:(m + 1) * 128], obn[sj][:, m * 128:(m + 1) * 128], [("obn", sj)], [bk(kt)])
                ACT(lambda e, kt=kt, i=i: e.activation(out=oT[:, 4:8, i * 128:(i + 1) * 128],
                                                       in_=bankbf(kt)[:, 0:512].rearrange("p (m t) -> p m t", t=128), func=AF.Copy),
                    [bk(kt)], [("oT", 4, i)])
        tap("oT", oT, [("oT", 4, i) for i in range(NT)])
        S.barrier()
        R = Bump(arena, RMARK, ARENA_BYTES)
        W_o = R.take([8, D], BF16)
        xin = [R.take([D], F32) for _ in range(2)]
        x1o = [R.take([D], F32) for _ in range(2)]
        for kc in range(8):
            S.dma("pool", lambda e, kc=kc: e.dma_start(out=W_o[:, kc, :], in_=w_out[kc * 128:(kc + 1) * 128, :]), "w_o", writes=["W_o"])
        for i in range(NT):
            sl = i % 2
            its = slice(i * 128, (i + 1) * 128)
            S.dma("sp", lambda e, sl=sl, i=i: e.dma_start(out=xin[sl], in_=x[b, i * 128:(i + 1) * 128, :]), f"xin{sl}",
                  writes=[("xin", sl)])
            kp = (i % 2) * 2
            for half in range(2):
                for m in range(8):
                    mm(bank(kp + half), oT[:, m, its], W_o[:, m, half * 512:(half + 1) * 512], m == 0, m == 7, ["W_o"], [bk(kp + half)])
            DVE(lambda e, sl=sl, kp=kp: e.tensor_tensor(out=x1o[sl], in0=pp[kp // 2][:, :], in1=xin[sl], op=ALU.add),
                [bk(kp), bk(kp + 1), ("xin", sl), ("x1o", sl)], [("x1o", sl)])
            S.dma("sp", lambda e, sl=sl, i=i: e.dma_start(out=x1s[b, i * 128:(i + 1) * 128, :], in_=x1o[sl]), f"x1o{sl}",
                  reads=[("x1o", sl)])
        S.barrier()

    Bm = Bump(arena, MARK0, ARENA_BYTES)
    W_up = Bm.take([8, DFF], BF16)
    W_dn = Bm.take([32, D], BF16)
    gffn_b = Bm.take([D], F32)
    TB = 256
    NJ = TB // 128
    x1t = [[Bm.take([D], F32) for _ in range(NJ)] for _ in range(2)]
    hbf = [Bm.take([D], BF16) for _ in range(2)]
    h2T = Bm.take([8, TB], BF16)
    aT = Bm.take([32, TB], BF16)
    rl = [Bm.take([512], F32) for _ in range(2)]
    yo = [Bm.take([D], F32) for _ in range(2)]
    junk = Bm.take([D], BF16)
    st5 = Bm.take([64], F32)
    S.dma("sp", lambda e: e.dma_start(out=gffn_b, in_=g_ffn.partition_broadcast(128)), "const2", writes=["gffn"])
    for kc in range(8):
        for q4 in range(4):
            S.dma("pool", lambda e, kc=kc, q4=q4: e.dma_start(out=W_up[:, kc, q4 * 1024:(q4 + 1) * 1024],
                                                             in_=w_up[kc * 128:(kc + 1) * 128, q4 * 1024:(q4 + 1) * 1024]),
                  "w_up", writes=["W_up"])
    for c in range(32):
        S.dma("pool", lambda e, c=c: e.dma_start(out=W_dn[:, c, :], in_=w_down[c * 128:(c + 1) * 128, :]), "w_dn", writes=["W_dn"])
    x1f = x1s.rearrange("b s d -> (b s) d")
    yf = y.rearrange("b s d -> (b s) d")
    nblk = nseq * SEQ // TB
    nyo = 0
    for blk in range(nblk):
        xs = blk % 2
        for j in range(NJ):
            t = blk * NJ + j
            S.dma("sp", lambda e, xs=xs, j=j, t=t: e.dma_start(out=x1t[xs][j], in_=x1f[t * 128:(t + 1) * 128, :]), f"x1t{xs}{j}",
                  writes=[("x1t", xs, j)])
            c0 = (t % 8) * 4
            sl = t % 2
            ACT(lambda e, xs=xs, j=j, c0=c0: e.activation(out=junk, in_=x1t[xs][j], func=AF.Square, accum_out=st5[:, c0:c0 + 1]),
                [("x1t", xs, j)], ["junk", ("st5", c0)])
            DVE(lambda e, c0=c0: e.tensor_scalar(out=st5[:, c0 + 1:c0 + 2], in0=st5[:, c0:c0 + 1], scalar1=1.0 / D, scalar2=EPS,
                                                 op0=ALU.mult, op1=ALU.add), [("st5", c0)], [("st5", c0 + 1)])
            POOL(lambda e, c0=c0: e.tensor_tensor(out=st5[:, c0 + 2:c0 + 3], in0=st5[:, c0 + 1:c0 + 2], in1=nhalf, op=ALU.pow),
                 [("st5", c0 + 1), "nhalf"], [("st5", c0 + 2)])
            DVE(lambda e, xs=xs, j=j, c0=c0, sl=sl: e.scalar_tensor_tensor(out=hbf[sl], in0=x1t[xs][j], scalar=st5[:, c0 + 2:c0 + 3],
                                                                           in1=gffn_b, op0=ALU.mult, op1=ALU.mult),
                [("x1t", xs, j), ("st5", c0 + 2), "gffn"], [("hbf", sl)])
            for kc in range(8):
                tp(bankbf(0)[:, kc * 128:(kc + 1) * 128], hbf[sl][:, kc * 128:(kc + 1) * 128], [("hbf", sl)], [bk(0)])
            ACT(lambda e, j=j: e.activation(out=h2T[:, :, j * 128:(j + 1) * 128], in_=bankbf(0).rearrange("p (k t) -> p k t", t=128),
                                            func=AF.Copy), [bk(0)], [("h2T", j)])
        hk2 = [("h2T", j) for j in range(NJ)] + ["W_up"]
        for cp in range(16):
            ku = 1 + cp % 2
            for c in range(2):
                cc = cp * 2 + c
                for kc in range(8):
                    mm(bank(ku)[:, c * TB:(c + 1) * TB], W_up[:, kc, cc * 128:(cc + 1) * 128], h2T[:, kc, :], kc == 0, kc == 7, hk2, [bk(ku)])
            rs = cp % 2
            ACT(lambda e, ku=ku, rs=rs: e.activation(out=rl[rs], in_=bank(ku), func=AF.Relu), [bk(ku)], [("rl", rs)])
            POOL(lambda e, rs=rs, cp=cp: e.tensor_tensor(out=aT[:, 2 * cp:2 * cp + 2, :], in0=rl[rs].rearrange("p (c t) -> p c t", t=TB),
                                                         in1=rl[rs].rearrange("p (c t) -> p c t", t=TB), op=ALU.mult),
                 [("rl", rs)], [("aT", cp)])
        for j in range(NJ):
            t = blk * NJ + j
            kd = 4 + (t % 2) * 2
            for half in range(2):
                for c in range(32):
                    mm(bank(kd + half), aT[:, c, j * 128:(j + 1) * 128], W_dn[:, c, half * 512:(half + 1) * 512], c == 0, c == 31,
                       [("aT", c // 2), "W_dn"], [bk(kd + half)])
            ys = nyo % 2
            nyo += 1
            DVE(lambda e, ys=ys, kd=kd, xs=xs, j=j: e.tensor_tensor(out=yo[ys], in0=pp[kd // 2][:, :], in1=x1t[xs][j], op=ALU.add),
                [bk(kd), bk(kd + 1), ("x1t", xs, j), ("yo", ys)], [("yo", ys)])
            S.dma("sp", lambda e, ys=ys, t=t: e.dma_start(out=yf[t * 128:(t + 1) * 128, :], in_=yo[ys]), f"yo{ys}", reads=[("yo", ys)])
    S.barrier()

    sems = {n: es.enter_context(nc.semaphore(n)) for n in sorted(S.sem_names)}
    with nc.Block() as block:
        S.emit(block, sems)
    es.close()
    return nc, dbg_out


def _prep_inputs(inputs, nseq, ncores):
    f32 = np.float32
    g = lambda k: np.ascontiguousarray(np.asarray(inputs[k]))
    wq_ = g("w_q_up")[0].reshape(384, 4, 192)
    wq_p = np.concatenate([wq_[:, :, :128].reshape(384, 512), wq_[:, :, 128:].reshape(384, 256)], axis=1)
    wkv_ = g("w_kv_up")[0].reshape(256, 4, 256)
    wkv_p = np.concatenate([wkv_[:, :, :128].reshape(256, 512), wkv_[:, :, 128:].reshape(256, 512)], axis=1)
    invf = (10000.0 ** (-(np.arange(0, 64, 2, dtype=f32)) / f32(64))).astype(f32).reshape(1, 32)
    shared = {
        "g_mix": g("g_mix_norm").reshape(1, D), "w_in": g("w_in")[0], "lb_param": g("lb_param")[:, 0:2, :],
        "g_hg": g("g_hgrn_out")[0], "g_cq": g("g_cq").reshape(1, 384), "wq": np.ascontiguousarray(wq_p),
        "g_ckv": g("g_ckv").reshape(1, 256), "wkv": np.ascontiguousarray(wkv_p), "g_q": g("g_q_norm").reshape(1, 192),
        "g_k": g("g_k_norm").reshape(1, 192), "g_mo": g("g_mla_out").reshape(1, 512), "w_out": g("w_out")[0],
        "g_ffn": g("g_ffn_norm").reshape(1, D), "w_up": g("w_up")[0], "w_down": g("w_down")[0], "invf": invf,
    }
    x = g("x")
    pos = g("positions").astype(np.int32)
    maps = []
    for c in range(ncores):
        m = dict(shared)
        m["x"] = np.ascontiguousarray(x[c * nseq:(c + 1) * nseq])
        m["pos"] = np.ascontiguousarray(pos[c * nseq:(c + 1) * nseq])
        maps.append(m)
    return maps


def kernel(**inputs):
    nseq = 4
    nc, _ = build(nseq)
    maps = _prep_inputs(inputs, nseq, NCORES)
    res = run_bass_kernel_spmd(nc, maps, core_ids=list(range(NCORES)))
    out = np.concatenate([np.asarray(r["y"]) for r in res.results], axis=0)
    return out.astype(np.float32, copy=False)
```

```python
import numpy as np
from contextlib import ExitStack
import concourse.bass as bass
import concourse.mybir as mybir
from concourse.bass_utils import run_bass_kernel_spmd

F32 = mybir.dt.float32
BF16 = mybir.dt.bfloat16
I32 = mybir.dt.int32
AF = mybir.ActivationFunctionType
ALU = mybir.AluOpType
AX = mybir.AxisListType

NCORES = 8
SEQ = 2048
NT = SEQ // 128
D = 1024
DIN = 3264
DFF = 4096
EPS = 1e-6
PI = float(np.pi)
ARENA_BYTES = 212480

ENGS = ("pe", "act", "dve", "pool", "sp")


def _cody_waite():
    two_pi = 2.0 * np.pi
    c1 = 6.28125
    r1 = two_pi - c1
    m, e = np.frexp(r1)
    c2 = float(np.ldexp(np.round(m * 2 ** 11) / 2 ** 11, e))
    c3 = float(np.float32(two_pi - c1 - c2))
    return c1, c2, c3


CW1, CW2, CW3 = _cody_waite()


class _Rec:
    def __getattr__(self, name):
        def f(*a, **k):
            self.call = (name, a, k)
            return self
        return f


class Sched:
    def __init__(self):
        self.q = {e: [] for e in ENGS}
        self.cnt = {e: 0 for e in ENGS}
        self.seen = {e: {} for e in ENGS}
        self.bufs = {}
        self.dma_tot = {}
        self.sem_names = set(ENGS)

    def _st(self, k):
        st = self.bufs.get(k)
        if st is None:
            st = self.bufs[k] = {"w": None, "r": {}}
        return st

    def _deps(self, eng, reads, writes):
        toks = []
        for k in reads:
            st = self._st(k)
            if st["w"] is not None:
                toks.append(st["w"])
        for k in writes:
            st = self._st(k)
            if st["w"] is not None:
                toks.append(st["w"])
            toks.extend(st["r"].items())
        waits = {}
        for (s, v) in toks:
            if s == "pe" and eng == "pe":
                continue
            if self.seen[eng].get(s, 0) >= v:
                continue
            if waits.get(s, 0) < v:
                waits[s] = v
        for s, v in waits.items():
            self.seen[eng][s] = v
        return list(waits.items())

    def _commit(self, tok, reads, writes):
        for k in reads:
            r = self._st(k)["r"]
            if r.get(tok[0], 0) < tok[1]:
                r[tok[0]] = tok[1]
        for k in writes:
            st = self._st(k)
            st["w"] = tok
            st["r"] = {}

    def op(self, eng, fn, reads=(), writes=()):
        rec = _Rec()
        fn(rec)
        waits = self._deps(eng, reads, writes)
        self.cnt[eng] += 1
        tok = (eng, self.cnt[eng])
        self.q[eng].append((rec.call, waits, (eng, 1)))
        self._commit(tok, reads, writes)

    def dma(self, eng, fn, sem, reads=(), writes=()):
        rec = _Rec()
        fn(rec)
        self.sem_names.add(sem)
        waits = self._deps(eng, reads, writes)
        self.dma_tot[sem] = self.dma_tot.get(sem, 0) + 16
        tok = (sem, self.dma_tot[sem])
        self.q[eng].append((rec.call, waits, (sem, 16)))
        self._commit(tok, reads, writes)

    def barrier(self):
        for e in ENGS:
            waits = []
            for e2 in ENGS:
                if self.cnt[e2] > self.seen[e].get(e2, 0):
                    waits.append((e2, self.cnt[e2]))
                    self.seen[e][e2] = self.cnt[e2]
            for s, tot in self.dma_tot.items():
                if tot > self.seen[e].get(s, 0):
                    waits.append((s, tot))
                    self.seen[e][s] = tot
            self.q[e].append((None, waits, None))
        self.bufs = {}

    def emit(self, block, sems):
        handles = {"pe": block.tensor, "act": block.scalar, "dve": block.vector,
                   "pool": block.gpsimd, "sp": block.sync}

        def mk(e):
            ops = self.q[e]

            def body(engine):
                for fn, waits, inc in ops:
                    for s, v in waits:
                        engine.wait_ge(sems[s], v)
                    if fn is not None:
                        name, a, k = fn
                        getattr(engine, name)(*a, **k).then_inc(sems[inc[0]], inc[1])
            return body

        for e in ENGS:
            handles[e](mk(e))


def _dsize(dt):
    return 2 if dt == BF16 else 4


class Bump:
    def __init__(self, arena, start, end):
        self.arena, self.cur, self.end = arena, start, end

    def take(self, shape, dt):
        n = int(np.prod(shape)) * _dsize(dt)
        off = (self.cur + 63) // 64 * 64
        n4 = (n + 3) // 4 * 4
        self.cur = off + n4
        assert self.cur <= self.end, (self.cur, self.end)
        ap = self.arena[:, off // 4:(off + n4) // 4]
        if dt != F32:
            ap = ap.bitcast(dt)
        if n4 != n:
            ap = ap[:, 0:int(np.prod(shape))]
        if len(shape) == 2:
            ap = ap.rearrange("p (a b) -> p a b", b=shape[1])
        elif len(shape) == 3:
            ap = ap.rearrange("p (a b c) -> p a b c", b=shape[1], c=shape[2])
        return ap


def build(nseq=4, dbg=None):
    nc = bass.Bass("TRN2", target_bir_lowering=False)
    S = Sched()
    dbg_out = {}

    def din(name, shape, dt=F32):
        return nc.dram_tensor(name, list(shape), dt, kind="ExternalInput").ap()

    x = din("x", [nseq, SEQ, D])
    pos = din("pos", [nseq, SEQ], I32)
    g_mix = din("g_mix", [1, D])
    w_in = din("w_in", [D, DIN])
    lb_param = din("lb_param", [2, 2, 512])
    g_hg = din("g_hg", [4, 128])
    g_cq = din("g_cq", [1, 384])
    wq = din("wq", [384, 768])
    g_ckv = din("g_ckv", [1, 256])
    wkv = din("wkv", [256, 1024])
    g_q = din("g_q", [1, 192])
    g_k = din("g_k", [1, 192])
    g_mo = din("g_mo", [1, 512])
    w_out = din("w_out", [D, D])
    g_ffn = din("g_ffn", [1, D])
    w_up = din("w_up", [D, DFF])
    w_down = din("w_down", [DFF, D])
    invf = din("invf", [1, 32])
    y = nc.dram_tensor("y", [nseq, SEQ, D], F32, kind="ExternalOutput").ap()
    x1s = nc.dram_tensor("x1s", [nseq, SEQ, D], F32).ap()

    es = ExitStack()
    arena = es.enter_context(nc.sbuf_tensor("arena", [128, ARENA_BYTES // 4], F32))[:]
    pp = [es.enter_context(nc.psum_tensor(f"pp{i}", [128, 1024], F32)) for i in range(4)]

    def bank(k):
        return pp[k // 2][:, (k % 2) * 512:(k % 2) * 512 + 512]

    def bankbf(k):
        return bank(k).bitcast(BF16)

    def bk(k):
        return ("bank", k)

    def PE(fn, r, w):
        S.op("pe", fn, r, w)

    def ACT(fn, r, w):
        S.op("act", fn, r, w)

    def DVE(fn, r, w):
        S.op("dve", fn, r, w)

    def POOL(fn, r, w):
        S.op("pool", fn, r, w)

    def mm(out, lhsT, rhs, start, stop, r, w, skip=False):
        if skip:
            PE(lambda e: e.matmul(out, lhsT=lhsT, rhs=rhs, start=start, stop=stop, skip_group_check=True), r, w)
        else:
            PE(lambda e: e.matmul(out, lhsT=lhsT, rhs=rhs, start=start, stop=stop), r, w)

    def tp(out, in_, r, w):
        PE(lambda e: e.transpose(out=out, in_=in_, identity=ident), list(r) + ["ident"], w)

    def tap(name, ap, key):
        if dbg is None or name not in dbg:
            return
        shp = list(ap.shape)
        t = nc.dram_tensor("dbg_" + name, shp, ap.dtype, kind="ExternalOutput").ap()
        dbg_out[name] = t
        S.dma("sp", lambda e: e.dma_start(out=t, in_=ap), "dbg", reads=key)

    P = Bump(arena, 0, ARENA_BYTES)
    ident = P.take([128], BF16)
    maskfb = P.take([256], I32)
    cols = P.take([64], F32)
    ones_b = P.take([2], BF16)
    nhalf = P.take([1], F32)
    ones_f = P.take([512], F32)
    MARK0 = P.cur
    C_GCQ, C_GCKV, C_GHG, C_LB, C_LNOML, C_LBP = 0, 3, 5, 9, 17, 25

    A = Bump(arena, MARK0, ARENA_BYTES)
    W_in = A.take([8, DIN], BF16)
    gmix_b = A.take([D], F32)
    gq_b = A.take([768], F32)
    gk_b = A.take([768], F32)
    gmo_b = A.take([512], F32)
    invf_b = A.take([32], F32)
    hT = A.take([8, SEQ], BF16)
    oT = A.take([8, SEQ], BF16)
    RMARK = A.cur

    R0 = Bump(arena, RMARK, ARENA_BYTES)
    identf = R0.take([128], F32)
    lbt = R0.take([8], F32)

    def cdma(out, in_, slow=False):
        if slow:
            S.dma("sp", lambda e: e.dma_start(out=out, in_=in_, allow_slow_non_contiguous=True), "const", writes=["const"])
        else:
            S.dma("sp", lambda e: e.dma_start(out=out, in_=in_), "const", writes=["const"])

    cdma(gmix_b, g_mix.partition_broadcast(128))
    for h in range(4):
        cdma(gq_b[:, h * 128:(h + 1) * 128], g_q[:, 0:128].partition_broadcast(128))
        cdma(gq_b[:, 512 + h * 64:512 + (h + 1) * 64], g_q[:, 128:192].partition_broadcast(128))
        cdma(gk_b[:, h * 128:(h + 1) * 128], g_k[:, 0:128].partition_broadcast(128))
        cdma(gk_b[:, 512 + h * 64:512 + (h + 1) * 64], g_k[:, 128:192].partition_broadcast(128))
    cdma(gmo_b, g_mo.partition_broadcast(128))
    cdma(invf_b, invf.partition_broadcast(128))
    cdma(cols[:, C_GCQ:C_GCQ + 3], g_cq[0].rearrange("(m p) -> p m", p=128), slow=True)
    cdma(cols[:, C_GCKV:C_GCKV + 2], g_ckv[0].rearrange("(m p) -> p m", p=128), slow=True)
    cdma(cols[:, C_GHG:C_GHG + 4], g_hg.rearrange("h e -> e h"), slow=True)
    for d_ in range(2):
        for s_ in range(2):
            o = C_LBP + (d_ * 2 + s_) * 4
            cdma(cols[:, o:o + 4], lb_param[d_, s_].rearrange("(h p) -> p h", p=128), slow=True)
    for kc in range(8):
        S.dma("pool", lambda e, kc=kc: e.dma_start(out=W_in[:, kc, :], in_=w_in[kc * 128:(kc + 1) * 128, :], max_dma_last_dim=4096),
              "w_in", writes=["W_in"])

    POOL(lambda e: e.memset(identf, 0.0), [], ["identf"])
    POOL(lambda e: e.affine_select(out=identf, in_=identf, pattern=[[-1, 128]], compare_op=ALU.not_equal, fill=1.0,
                                   base=0, channel_multiplier=1), ["identf"], ["identf"])
    DVE(lambda e: e.tensor_copy(out=ident, in_=identf), ["identf"], ["ident"])
    POOL(lambda e: e.iota(maskfb[:, 0:128], pattern=[[1, 128]], base=0, channel_multiplier=-1), [], ["maskfb"])
    POOL(lambda e: e.iota(maskfb[:, 128:256], pattern=[[-1, 128]], base=0, channel_multiplier=1), ["maskfb"], ["maskfb"])
    DVE(lambda e: e.tensor_single_scalar(out=maskfb, in_=maskfb, scalar=0, op=ALU.is_ge), ["maskfb"], ["maskfb"])
    POOL(lambda e: e.memset(ones_b, 1.0), [], ["ones_b"])
    POOL(lambda e: e.memset(nhalf, -0.5), [], ["nhalf"])
    POOL(lambda e: e.memset(ones_f, 1.0), [], ["ones_f"])
    DVE(lambda e: e.tensor_scalar(out=gq_b, in0=gq_b, scalar1=float(192 ** -0.5), scalar2=None, op0=ALU.mult), ["const"], ["const"])
    lbp = cols[:, C_LBP:C_LBP + 16].rearrange("p (d s h) -> p d s h", s=2, h=4)
    lbv = cols[:, C_LB:C_LB + 8]
    DVE(lambda e: e.tensor_tensor(out=lbt.rearrange("p (d h) -> p d h", h=4), in0=lbp[:, :, 1, :], in1=lbp[:, :, 0, :], op=ALU.subtract),
        ["const"], ["lbt"])
    ACT(lambda e: e.activation(out=lbt, in_=lbt, func=AF.Exp), ["lbt"], ["lbt"])
    DVE(lambda e: e.tensor_scalar(out=lbt, in0=lbt, scalar1=1.0, scalar2=None, op0=ALU.add), ["lbt"], ["lbt"])
    DVE(lambda e: e.reciprocal(out=lbv, in_=lbt), ["lbt", "const"], ["const"])
    ACT(lambda e: e.activation(out=cols[:, C_LNOML:C_LNOML + 8], in_=lbv, func=AF.Ln, scale=-1.0, bias=1.0), ["const"], ["const"])
    S.barrier()

    for b in range(nseq):
        R = Bump(arena, RMARK, ARENA_BYTES)
        V_all = R.take([NT, 512], BF16)
        XMARK = R.cur
        xin = [R.take([D], F32) for _ in range(2)]
        hbf = [R.take([D], BF16) for _ in range(2)]
        junk = R.take([D], BF16)
        st = R.take([64], F32)
        for i in range(NT):
            sl = i % 2
            S.dma("sp", lambda e, sl=sl, i=i: e.dma_start(out=xin[sl], in_=x[b, i * 128:(i + 1) * 128, :]), f"xin{sl}",
                  writes=[("xin", sl)])
            c0 = (i % 8) * 4
            ACT(lambda e, sl=sl, c0=c0: e.activation(out=junk, in_=xin[sl], func=AF.Square, accum_out=st[:, c0:c0 + 1]),
                [("xin", sl)], ["junk", ("st", c0)])
            DVE(lambda e, c0=c0: e.tensor_scalar(out=st[:, c0 + 1:c0 + 2], in0=st[:, c0:c0 + 1], scalar1=1.0 / D, scalar2=EPS,
                                                 op0=ALU.mult, op1=ALU.add), [("st", c0)], [("st", c0 + 1)])
            POOL(lambda e, c0=c0: e.tensor_tensor(out=st[:, c0 + 2:c0 + 3], in0=st[:, c0 + 1:c0 + 2], in1=nhalf, op=ALU.pow),
                 [("st", c0 + 1), "nhalf"], [("st", c0 + 2)])
            DVE(lambda e, sl=sl, c0=c0: e.scalar_tensor_tensor(out=hbf[sl], in0=xin[sl], scalar=st[:, c0 + 2:c0 + 3], in1=gmix_b,
                                                               op0=ALU.mult, op1=ALU.mult),
                [("xin", sl), ("st", c0 + 2), "const"], [("hbf", sl)])
            k = i % 2
            for kc in range(8):
                tp(bankbf(k)[:, kc * 128:(kc + 1) * 128], hbf[sl][:, kc * 128:(kc + 1) * 128], [("hbf", sl)], [bk(k)])
            ACT(lambda e, k=k, i=i: e.activation(out=hT[:, :, i * 128:(i + 1) * 128],
                                                 in_=bankbf(k).rearrange("p (k t) -> p k t", t=128), func=AF.Copy),
                [bk(k)], [("hT", i)])
        tap("hT", hT, [("hT", i) for i in range(NT)])
        for i in range(NT):
            k = 2 + i % 2
            for kc in range(8):
                mm(bank(k), hT[:, kc, i * 128:(i + 1) * 128], W_in[:, kc, 1536:2048], kc == 0, kc == 7,
                   [("hT", i), "W_in"], [bk(k)])
            DVE(lambda e, k=k, i=i: e.tensor_copy(out=V_all[:, i, :], in_=bank(k)), [bk(k)], [("V", i)])
        S.barrier()
        R = Bump(arena, XMARK, ARENA_BYTES)
        sgT = R.take([SEQ], BF16)
        TS = [[R.take([512], F32) for _ in range(6)] for _ in range(2)]
        QK = {(d_, w_): R.take([SEQ], BF16) for d_ in range(2) for w_ in "qk"}
        Zbf = [R.take([NT, 128], BF16) for _ in range(2)]
        Y = [[R.take([128], F32) for _ in range(2)] for _ in range(2)]
        Xs = [[R.take([128], F32) for _ in range(2)] for _ in range(2)]
        sqo = [TS[s_][0] for s_ in range(2)]
        onb = [TS[s_][1].bitcast(BF16)[:, 0:512] for s_ in range(2)]
        atm = [TS[s_][2].bitcast(BF16).rearrange("p (d n) -> p d n", d=2) for s_ in range(2)]
        ktok = [TS[s_][4].bitcast(BF16)[:, 0:512] for s_ in range(2)]
        Rall = [R.take([NT], F32) for _ in range(2)]
        Eall = [R.take([NT], F32) for _ in range(2)]
        dR = [R.take([3, NT], F32) for _ in range(2)]
        efac = [R.take([3, NT], F32) for _ in range(2)]
        carry = R.take([2], F32)
        st2 = R.take([64], F32)
        for h in range(4):
            for blk in range(4):
                tl = slice(blk * 512, (blk + 1) * 512)
                hk = [("hT", 4 * blk + j) for j in range(4)] + ["W_in"]
                kg = 6 + blk % 2
                for kc in range(8):
                    mm(bank(kg), W_in[:, kc, 2048 + h * 128:2048 + (h + 1) * 128], hT[:, kc, tl], kc == 0, kc == 7, hk, [bk(kg)])
                ACT(lambda e: e.activation(out=sgT[:, tl], in_=bank(kg), func=AF.Silu), [bk(kg)], [("sgT", blk)])

            def prep_mm(blk):
                tl = slice(blk * 512, (blk + 1) * 512)
                hk = [("hT", 4 * blk + j) for j in range(4)] + ["W_in"]
                for j, c0 in enumerate((0, 512, 1024)):
                    kk = (3 * blk + j) % 6
                    for kc in range(8):
                        mm(bank(kk), W_in[:, kc, c0 + h * 128:c0 + (h + 1) * 128], hT[:, kc, tl], kc == 0, kc == 7, hk, [bk(kk)])

            def front(it):
                blk, d_ = it // 2, it % 2
                T = TS[it % 2]
                tk = [("T", it % 2, j) for j in range(6)]
                kz = (3 * blk + 1 + d_) % 6
                lbc = cols[:, C_LB + d_ * 4 + h:C_LB + d_ * 4 + h + 1]
                ACT(lambda e: e.activation(out=T[0], in_=bank(kz), func=AF.Exp, scale=-1.0), [bk(kz)], [tk[0]])
                ACT(lambda e: e.activation(out=T[1], in_=T[0], func=AF.Ln, scale=lbc, bias=1.0), [tk[0], "const"], [tk[1]])
                ACT(lambda e: e.activation(out=T[2], in_=T[0], func=AF.Ln, scale=1.0, bias=1.0), [tk[0]], [tk[2]])
                POOL(lambda e: e.tensor_tensor(out=T[3], in0=T[1], in1=T[2], op=ALU.subtract), [tk[1], tk[2]], [tk[3]])
                DVE(lambda e: e.tensor_tensor(out=T[4], in0=bank(kz), in1=T[2], op=ALU.add), [bk(kz), tk[2]], [tk[4]])
                if blk == 0:
                    DVE(lambda e: e.tensor_tensor_scan(out=T[5], data0=ones_f, data1=T[3], initial=0.0, op0=ALU.mult, op1=ALU.add),
                        [tk[3], "ones_f"], [tk[5]])
                else:
                    DVE(lambda e: e.tensor_tensor_scan(out=T[5], data0=ones_f, data1=T[3], initial=carry[:, d_:d_ + 1],
                                                       op0=ALU.mult, op1=ALU.add), [tk[3], "ones_f", ("carry", d_)], [tk[5]])
                DVE(lambda e: e.tensor_copy(out=carry[:, d_:d_ + 1], in_=T[5][:, 511:512]), [tk[5]], [("carry", d_)])
                if d_ == 0:
                    Bx, kB = T[5], tk[5]
                else:
                    DVE(lambda e: e.tensor_tensor(out=T[1], in0=T[3], in1=T[5], op=ALU.subtract), [tk[3], tk[5]], [tk[1]])
                    Bx, kB = T[1], tk[1]
                DVE(lambda e: e.tensor_copy(out=Rall[d_][:, blk * 4:(blk + 1) * 4], in_=Bx[:, 63:512:128]), [kB], [("Rall", d_)])
                eo_ = 127 if d_ == 0 else 0
                DVE(lambda e: e.tensor_copy(out=Eall[d_][:, blk * 4:(blk + 1) * 4], in_=Bx[:, eo_:512:128]), [kB], [("Eall", d_)])
                DVE(lambda e: e.tensor_tensor(out=T[0].rearrange("p (c j) -> p c j", j=128), in0=Bx.rearrange("p (c j) -> p c j", j=128),
                                              in1=Rall[d_][:, blk * 4:(blk + 1) * 4].unsqueeze(2).to_broadcast([128, 4, 128]),
                                              op=ALU.subtract), [kB, ("Rall", d_)], [tk[0]])

            def back(it):
                blk, d_ = it // 2, it % 2
                tl = slice(blk * 512, (blk + 1) * 512)
                T = TS[it % 2]
                tk = [("T", it % 2, j) for j in range(6)]
                kq = (3 * blk) % 6
                lno = cols[:, C_LNOML + d_ * 4 + h:C_LNOML + d_ * 4 + h + 1]
                ACT(lambda e: e.activation(out=T[2], in_=T[0], func=AF.Exp), [tk[0]], [tk[2]])
                POOL(lambda e: e.tensor_tensor(out=T[3], in0=T[4], in1=T[0], op=ALU.add), [tk[4], tk[0]], [tk[3]])
                ACT(lambda e: e.activation(out=QK[(d_, "k")][:, tl], in_=T[3], func=AF.Exp, scale=-1.0, bias=lno),
                    [tk[3], "const"], [("QK", d_, "k", blk)])
                DVE(lambda e: e.tensor_tensor(out=QK[(d_, "q")][:, tl], in0=bank(kq), in1=T[2], op=ALU.mult),
                    [bk(kq), tk[2]], [("QK", d_, "q", blk)])

            prep_mm(0)
            front(0)
            for it in range(8):
                if it % 2 == 0 and it // 2 + 1 < 4:
                    prep_mm(it // 2 + 1)
                if it + 1 < 8:
                    front(it + 1)
                back(it)
            for d_ in range(2):
                if d_ == 0:
                    DVE(lambda e: e.tensor_tensor(out=dR[0][:, 0, 1:16], in0=Eall[0][:, 1:16], in1=Eall[0][:, 0:15], op=ALU.subtract),
                        [("Eall", 0)], [("dR", 0)])
                    DVE(lambda e: e.tensor_tensor(out=dR[0][:, 2, 1:16], in0=Rall[0][:, 1:16], in1=Eall[0][:, 0:15], op=ALU.subtract),
                        [("Eall", 0), ("Rall", 0), ("dR", 0)], [("dR", 0)])
                    DVE(lambda e: e.memset(dR[0][:, :, 0:1], 0.0), [("dR", 0)], [("dR", 0)])
                else:
                    DVE(lambda e: e.tensor_tensor(out=dR[1][:, 0, 0:15], in0=Eall[1][:, 0:15], in1=Eall[1][:, 1:16], op=ALU.subtract),
                        [("Eall", 1)], [("dR", 1)])
                    DVE(lambda e: e.tensor_tensor(out=dR[1][:, 2, 0:15], in0=Rall[1][:, 0:15], in1=Eall[1][:, 1:16], op=ALU.subtract),
                        [("Eall", 1), ("Rall", 1), ("dR", 1)], [("dR", 1)])
                    DVE(lambda e: e.memset(dR[1][:, :, 15:16], 0.0), [("dR", 1)], [("dR", 1)])
                DVE(lambda e: e.tensor_tensor(out=dR[d_][:, 1, :], in0=Eall[d_], in1=Rall[d_], op=ALU.subtract),
                    [("Eall", d_), ("Rall", d_), ("dR", d_)], [("dR", d_)])
                ACT(lambda e: e.activation(out=efac[d_], in_=dR[d_], func=AF.Exp), [("dR", d_)], [("efac", d_)])
            if h == 0 and b == 0:
                tap("Qf", QK[(0, "q")], [("QK", 0, "q", j) for j in range(4)])
                tap("Kf", QK[(0, "k")], [("QK", 0, "k", j) for j in range(4)])
            order = [list(range(15)), list(range(15, 0, -1))]
            pslot = {}
            ngrp = 0
            for gi in range(4):
                for d_ in range(2):
                    cs_ = order[d_][gi * 4:gi * 4 + 4]
                    kT, kP, ks = ngrp % 2, 2 + ngrp % 4, ngrp % 2
                    ngrp += 1
                    for j, c in enumerate(cs_):
                        tp(bankbf(kT)[:, j * 128:(j + 1) * 128], QK[(d_, "k")][:, c * 128:(c + 1) * 128], [("QK", d_, "k", c // 4)], [bk(kT)])
                    nn = len(cs_) * 128
                    ACT(lambda e: e.activation(out=ktok[ks][:, 0:nn], in_=bankbf(kT)[:, 0:nn], func=AF.Copy), [bk(kT)], [("T", ks, 4)])
                    for j, c in enumerate(cs_):
                        mm(bank(kP)[:, j * 128:(j + 1) * 128], ktok[ks][:, j * 128:(j + 1) * 128], V_all[:, c, h * 128:(h + 1) * 128], True, True,
                           [("T", ks, 4), ("V", c)], [bk(kP)])
                        pslot[(d_, c)] = (kP, j)
                for idx in range(gi * 4, min(gi * 4 + 4, 15)):
                    for d_ in range(2):
                        c = order[d_][idx]
                        kP_, j_ = pslot[(d_, c)]
                        pw = Xs[d_][idx % 2]
                        ACT(lambda e: e.activation(out=pw, in_=bank(kP_)[:, j_ * 128:(j_ + 1) * 128], func=AF.Copy, scale=efac[d_][:, 1, c:c + 1]),
                            [bk(kP_), ("efac", d_)], [("Xs", d_, idx % 2)])
                    for d_ in range(2):
                        c = order[d_][idx]
                        pw = Xs[d_][idx % 2]
                        yc, yp = Y[d_][idx % 2], Y[d_][(idx + 1) % 2]
                        if idx == 0:
                            DVE(lambda e: e.tensor_copy(out=yc, in_=pw), [("Xs", d_, idx % 2)], [("Y", d_, idx % 2)])
                        else:
                            DVE(lambda e: e.scalar_tensor_tensor(out=yc, in0=yp, scalar=efac[d_][:, 0, c:c + 1], in1=pw, op0=ALU.mult, op1=ALU.add),
                                [("Y", d_, (idx + 1) % 2), ("Xs", d_, idx % 2), ("efac", d_)], [("Y", d_, idx % 2)])
                    for d_ in range(2):
                        c = order[d_][idx]
                        yc = Y[d_][idx % 2]
                        nxt = c + 1 if d_ == 0 else c - 1
                        ACT(lambda e: e.activation(out=Zbf[d_][:, nxt, :], in_=yc, func=AF.Copy, scale=efac[d_][:, 2, nxt:nxt + 1]),
                            [("Y", d_, idx % 2), ("efac", d_)], [("Zbf", d_, nxt)])
            for kk in range(4):
                DVE(lambda e: e.memset(bank(kk), 0.0), [], [bk(kk)])
            for s_ in range(2):
                POOL(lambda e: e.memset(atm[s_], 0.0), [], [("T", s_, 2)])

            def at_mm(g):
                ka = (g % 2) * 2
                for d_ in range(2):
                    for j in range(4):
                        c = g * 4 + j
                        q_, k_ = QK[(d_, "q")], QK[(d_, "k")]
                        o_ = j * 128
                        rk = [("QK", d_, "k", g), ("QK", d_, "q", g)]
                        if d_ == 0:
                            mm(bank(ka)[0:64, o_:o_ + 128], k_[:, c * 128:c * 128 + 64], q_[:, c * 128:(c + 1) * 128], True, True, rk, [bk(ka)])
                            mm(bank(ka)[64:128, o_ + 64:o_ + 128], k_[:, c * 128 + 64:(c + 1) * 128], q_[:, c * 128 + 64:(c + 1) * 128],
                               True, True, rk, [bk(ka)])
                        else:
                            mm(bank(ka + 1)[0:64, o_:o_ + 64], k_[:, c * 128:c * 128 + 64], q_[:, c * 128:c * 128 + 64], True, True, rk, [bk(ka + 1)])
                            mm(bank(ka + 1)[64:128, o_:o_ + 128], k_[:, c * 128 + 64:(c + 1) * 128], q_[:, c * 128:(c + 1) * 128],
                               True, True, rk, [bk(ka + 1)])

            def mask_copy(g):
                ka = (g % 2) * 2
                sa = g % 2
                for d_ in range(2):
                    mk = maskfb[:, d_ * 128:(d_ + 1) * 128].unsqueeze(1).to_broadcast([128, 4, 128])
                    DVE(lambda e: e.copy_predicated(out=atm[sa][:, d_, :].rearrange("p (c j) -> p c j", j=128), mask=mk,
                                                    data=bank(ka + d_).rearrange("p (c j) -> p c j", j=128)),
                        [bk(ka + d_), "maskfb", ("T", sa, 2)], [("T", sa, 2)])

            def o_mm(g):
                ko = 4 + g % 2
                sa = g % 2
                for j in range(4):
                    c = g * 4 + j
                    cs = slice(c * 128, (c + 1) * 128)
                    grp = []
                    if c > 0:
                        grp.append((QK[(0, "q")][:, cs], Zbf[0][:, c, :], [("QK", 0, "q", g), ("Zbf", 0, c)]))
                    if c < NT - 1:
                        grp.append((QK[(1, "q")][:, cs], Zbf[1][:, c, :], [("QK", 1, "q", g), ("Zbf", 1, c)]))
                    vv = V_all[:, c, h * 128:(h + 1) * 128]
                    grp.append((atm[sa][:, 0, j * 128:(j + 1) * 128], vv, [("T", sa, 2), ("V", c)]))
                    grp.append((atm[sa][:, 1, j * 128:(j + 1) * 128], vv, [("T", sa, 2), ("V", c)]))
                    for gi_, (l_, r_, rk) in enumerate(grp):
                        mm(bank(ko)[:, j * 128:(j + 1) * 128], l_, r_, gi_ == 0, gi_ == len(grp) - 1, rk, [bk(ko)])

            def epi_a(g):
                ko = 4 + g % 2
                sa = g % 2
                c0 = (g % 4) * 12
                ACT(lambda e: e.activation(out=sqo[sa], in_=bank(ko), func=AF.Square), [bk(ko)], [("T", sa, 0)])
                DVE(lambda e: e.tensor_reduce(out=st2[:, c0:c0 + 4], in_=sqo[sa].rearrange("p (c j) -> p c j", j=128), axis=AX.X, op=ALU.add),
                    [("T", sa, 0)], [("st2", c0)])
                DVE(lambda e: e.tensor_scalar(out=st2[:, c0 + 4:c0 + 8], in0=st2[:, c0:c0 + 4], scalar1=1.0 / 128, scalar2=EPS,
                                              op0=ALU.mult, op1=ALU.add), [("st2", c0)], [("st2", c0 + 4)])
                POOL(lambda e: e.tensor_tensor(out=st2[:, c0 + 8:c0 + 12], in0=st2[:, c0 + 4:c0 + 8], in1=nhalf.to_broadcast([128, 4]),
                                               op=ALU.pow), [("st2", c0 + 4), "nhalf"], [("st2", c0 + 8)])
                DVE(lambda e: e.tensor_tensor(out=onb[sa].rearrange("p (c j) -> p c j", j=128), in0=bank(ko).rearrange("p (c j) -> p c j", j=128),
                                              in1=st2[:, c0 + 8:c0 + 12].unsqueeze(2).to_broadcast([128, 4, 128]), op=ALU.mult),
                    [bk(ko), ("st2", c0 + 8)], [("T", sa, 1)])

            def epi_b(g):
                kt = 6 + g % 2
                sa = g % 2
                for j in range(4):
                    tp(bankbf(kt)[:, j * 128:(j + 1) * 128], onb[sa][:, j * 128:(j + 1) * 128], [("T", sa, 1)], [bk(kt)])
                gs = slice(g * 512, (g + 1) * 512)
                DVE(lambda e: e.scalar_tensor_tensor(out=oT[:, h, gs], in0=bankbf(kt)[:, 0:512], scalar=cols[:, C_GHG + h:C_GHG + h + 1],
                                                     in1=sgT[:, gs], op0=ALU.mult, op1=ALU.mult),
                    [bk(kt), "const", ("sgT", g)], [("oT", h, g)])

            at_mm(0); mask_copy(0); at_mm(1); o_mm(0); mask_copy(1); at_mm(2); epi_a(0); o_mm(1); mask_copy(2); at_mm(3)
            epi_b(0); epi_a(1); o_mm(2); mask_copy(3); epi_b(1); epi_a(2); o_mm(3); epi_b(2); epi_a(3); epi_b(3)
        tap("oTa", oT[:, 0:4, :], [("oT", h, g) for h in range(4) for g in range(4)])
        S.barrier()
        R = Bump(arena, RMARK, ARENA_BYTES)
        KT = R.take([6, SEQ], BF16)
        Vaug = R.take([NT, 4, 130], BF16)
        SMARK = R.cur
        Wq = R.take([3, 768], BF16)
        Wkv = R.take([2, 1024], BF16)
        posi = R.take([NT], I32)
        posf = R.take([NT], F32)
        cosT = R.take([NT, 32], F32)
        sinT = R.take([NT, 32], F32)
        TMARK = R.cur
        ang = R.take([NT, 32], F32)
        kqi = R.take([NT, 32], I32)
        t_a = R.take([NT, 32], F32)
        t_b = R.take([NT, 32], F32)
        for m in range(3):
            S.dma("pool", lambda e, m=m: e.dma_start(out=Wq[:, m, :], in_=wq[m * 128:(m + 1) * 128, :]), "wq", writes=["Wq"])
        for m in range(2):
            S.dma("pool", lambda e, m=m: e.dma_start(out=Wkv[:, m, :], in_=wkv[m * 128:(m + 1) * 128, :]), "wkv", writes=["Wkv"])
        POOL(lambda e: e.memset(Vaug[:, :, :, 128:130], 1.0), [], ["Vones"])
        S.dma("sp", lambda e: e.dma_start(out=posi, in_=pos[b].rearrange("(n p) -> p n", p=128), allow_slow_non_contiguous=True),
              "posi", writes=["posi"])
        DVE(lambda e: e.tensor_copy(out=posf, in_=posi), ["posi"], ["posf"])
        DVE(lambda e: e.tensor_tensor(out=ang, in0=posf.unsqueeze(2).to_broadcast([128, NT, 32]),
                                      in1=invf_b.unsqueeze(1).to_broadcast([128, NT, 32]), op=ALU.mult), ["posf", "const"], ["ang"])
        DVE(lambda e: e.tensor_scalar(out=kqi, in0=ang, scalar1=float(1.0 / (2 * PI)), scalar2=None, op0=ALU.mult), ["ang"], ["kqi"])
        DVE(lambda e: e.tensor_copy(out=t_a, in_=kqi), ["kqi"], ["t_a"])
        C1, C2, C3 = CW1, CW2, CW3
        DVE(lambda e: e.scalar_tensor_tensor(out=t_b, in0=t_a, scalar=-C1, in1=ang, op0=ALU.mult, op1=ALU.add), ["t_a", "ang"], ["t_b"])
        DVE(lambda e: e.scalar_tensor_tensor(out=ang, in0=t_a, scalar=-C2, in1=t_b, op0=ALU.mult, op1=ALU.add), ["t_a", "t_b", "ang"], ["ang"])
        DVE(lambda e: e.scalar_tensor_tensor(out=t_b, in0=t_a, scalar=-C3, in1=ang, op0=ALU.mult, op1=ALU.add), ["t_a", "ang", "t_b"], ["t_b"])
        DVE(lambda e: e.tensor_scalar(out=ang, in0=t_b, scalar1=PI, scalar2=-PI, op0=ALU.min, op1=ALU.max), ["t_b", "ang"], ["ang"])
        ACT(lambda e: e.activation(out=sinT, in_=ang, func=AF.Sin), ["ang"], ["sinT"])
        DVE(lambda e: e.tensor_scalar(out=t_a, in0=t_b, scalar1=PI / 2, scalar2=None, op0=ALU.add), ["t_b", "t_a"], ["t_a"])
        DVE(lambda e: e.tensor_scalar(out=t_b, in0=t_a, scalar1=PI, scalar2=2 * PI, op0=ALU.is_gt, op1=ALU.mult), ["t_a", "t_b"], ["t_b"])
        DVE(lambda e: e.tensor_tensor(out=t_a, in0=t_a, in1=t_b, op=ALU.subtract), ["t_a", "t_b"], ["t_a"])
        DVE(lambda e: e.tensor_scalar(out=t_a, in0=t_a, scalar1=PI, scalar2=-PI, op0=ALU.min, op1=ALU.max), ["t_a"], ["t_a"])
        ACT(lambda e: e.activation(out=cosT, in_=t_a, func=AF.Sin), ["t_a"], ["cosT"])
        if b == 0:
            tap("cosT", cosT, ["cosT"])
            tap("sinT", sinT, ["sinT"])
        S.barrier()
        R = Bump(arena, TMARK, ARENA_BYTES)
        cT = R.take([5, 512], BF16)
        sq = R.take([5, 512], BF16)
        sqq = R.take([768], BF16)
        t1 = R.take([768], F32)
        qf = R.take([1024], BF16)
        kf = R.take([768], BF16)
        krs = R.take([64], F32)
        krr = R.take([64], F32)
        ta = R.take([256], F32)
        tb = R.take([256], F32)
        tak_ = R.take([64], F32)
        tbk_ = R.take([64], F32)
        sqk = R.take([512], F32)
        st3 = R.take([64], F32)
        junk = R.take([64], BF16)
        POOL(lambda e: e.memset(qf[:, 512:1024], 0.0), [], ["qf"])
        for blk in range(4):
            tl = slice(blk * 512, (blk + 1) * 512)
            hk = [("hT", 4 * blk + j) for j in range(4)] + ["W_in"]
            for m in range(5):
                c0 = 2560 + m * 128
                kc_ = (m % 2) * 2
                for kc in range(8):
                    mm(bank(kc_), W_in[:, kc, c0:c0 + 128], hT[:, kc, tl], kc == 0, kc == 7, hk, [bk(kc_)])
                gcol = cols[:, C_GCQ + m:C_GCQ + m + 1]
                ACT(lambda e, m=m, gcol=gcol: e.activation(out=cT[:, m, :], in_=bank(kc_), func=AF.Copy, scale=gcol),
                    [bk(kc_), "const"], [("cT", m)])
                ACT(lambda e, m=m: e.activation(out=sq[:, m, :], in_=bank(kc_), func=AF.Square), [bk(kc_)], [("sq", m)])
            for j in range(4):
                i = blk * 4 + j
                js = slice(j * 128, (j + 1) * 128)
                its = slice(i * 128, (i + 1) * 128)
                c0 = (i % 2) * 32
                for m in range(3):
                    mm(bank(1)[:, 0:1], sq[:, m, js], ones_b[:, 0:1], m == 0, m == 2, [("sq", m), "ones_b"], [bk(1)])
                for m in range(3, 5):
                    mm(bank(1)[:, 2:3], sq[:, m, js], ones_b[:, 0:1], m == 3, m == 4, [("sq", m), "ones_b"], [bk(1)])
                for kc in range(8):
                    mm(bank(1)[:, 64:128], hT[:, kc, its], W_in[:, kc, 3200:3264], kc == 0, kc == 7, [("hT", i), "W_in"], [bk(1)])
                sc = lambda o: st3[:, c0 + o:c0 + o + 1]
                skey = lambda o: ("st3", c0 + o)
                DVE(lambda e, sc=sc: e.tensor_scalar(out=sc(0), in0=bank(1)[:, 0:1], scalar1=1.0 / 384, scalar2=EPS, op0=ALU.mult, op1=ALU.add),
                    [bk(1)], [skey(0)])
                DVE(lambda e, sc=sc: e.tensor_scalar(out=sc(1), in0=bank(1)[:, 2:3], scalar1=1.0 / 256, scalar2=EPS, op0=ALU.mult, op1=ALU.add),
                    [bk(1)], [skey(1)])
                POOL(lambda e, sc=sc: e.tensor_tensor(out=sc(2), in0=sc(0), in1=nhalf, op=ALU.pow), [skey(0), "nhalf"], [skey(2)])
                POOL(lambda e, sc=sc: e.tensor_tensor(out=sc(3), in0=sc(1), in1=nhalf, op=ALU.pow), [skey(1), "nhalf"], [skey(3)])
                for m in range(3):
                    mm(bank(3), cT[:, m, js], Wq[:, m, 0:512], m == 0, m == 2, [("cT", m), "Wq"], [bk(3)])
                for m in range(3):
                    mm(bank(4)[:, 0:256], cT[:, m, js], Wq[:, m, 512:768], m == 0, m == 2, [("cT", m), "Wq"], [bk(4)])
                for m in range(2):
                    mm(bank(5), cT[:, 3 + m, js], Wkv[:, m, 0:512], m == 0, m == 1, [("cT", 3 + m), "Wkv"], [bk(5)])
                for m in range(2):
                    mm(bank(6), cT[:, 3 + m, js], Wkv[:, m, 512:1024], m == 0, m == 1, [("cT", 3 + m), "Wkv"], [bk(6)])
                DVE(lambda e: e.tensor_copy(out=krs, in_=bank(1)[:, 64:128]), [bk(1)], ["krs"])
                ACT(lambda e: e.activation(out=sqq[:, 0:512], in_=bank(3), func=AF.Square), [bk(3)], ["sqq"])
                ACT(lambda e: e.activation(out=sqq[:, 512:768], in_=bank(4)[:, 0:256], func=AF.Square), [bk(4), "sqq"], ["sqq"])
                ACT(lambda e: e.activation(out=sqk, in_=bank(5), func=AF.Square), [bk(5)], ["sqk"])
                ACT(lambda e, sc=sc: e.activation(out=junk[:, 0:64], in_=krs, func=AF.Square, accum_out=sc(13)), ["krs"], ["junk", skey(13)])
                DVE(lambda e, sc=sc: e.tensor_reduce(out=st3[:, c0 + 4:c0 + 8], in_=sqq[:, 0:512].rearrange("p (h j) -> p h j", j=128),
                                                     axis=AX.X, op=ALU.add), ["sqq"], [skey(4)])
                DVE(lambda e, sc=sc: e.tensor_reduce(out=st3[:, c0 + 8:c0 + 12], in_=sqq[:, 512:768].rearrange("p (h j) -> p h j", j=64),
                                                     axis=AX.X, op=ALU.add), ["sqq"], [skey(8)])
                DVE(lambda e: e.tensor_reduce(out=st3[:, c0 + 16:c0 + 20], in_=sqk.rearrange("p (h j) -> p h j", j=128),
                                              axis=AX.X, op=ALU.add), ["sqk"], [skey(16)])
                DVE(lambda e: e.tensor_tensor(out=st3[:, c0 + 4:c0 + 8], in0=st3[:, c0 + 4:c0 + 8], in1=st3[:, c0 + 8:c0 + 12], op=ALU.add),
                    [skey(4), skey(8)], [skey(4)])
                DVE(lambda e, sc=sc: e.tensor_tensor(out=sc(12), in0=sc(2), in1=sc(2), op=ALU.mult), [skey(2)], [skey(12)])
                DVE(lambda e, sc=sc: e.tensor_scalar(out=st3[:, c0 + 4:c0 + 8], in0=st3[:, c0 + 4:c0 + 8], scalar1=sc(12), scalar2=1.0 / 192,
                                                     op0=ALU.mult, op1=ALU.mult), [skey(4), skey(12)], [skey(4)])
                DVE(lambda e: e.tensor_scalar(out=st3[:, c0 + 4:c0 + 8], in0=st3[:, c0 + 4:c0 + 8], scalar1=EPS, scalar2=None, op0=ALU.add),
                    [skey(4)], [skey(4)])
                DVE(lambda e, sc=sc: e.tensor_tensor(out=sc(14), in0=sc(3), in1=sc(3), op=ALU.mult), [skey(3)], [skey(14)])
                DVE(lambda e, sc=sc: e.tensor_scalar(out=st3[:, c0 + 16:c0 + 20], in0=st3[:, c0 + 16:c0 + 20], scalar1=sc(14), scalar2=sc(13),
                                                     op0=ALU.mult, op1=ALU.add), [skey(16), skey(14), skey(13)], [skey(16)])
                DVE(lambda e: e.tensor_scalar(out=st3[:, c0 + 16:c0 + 20], in0=st3[:, c0 + 16:c0 + 20], scalar1=1.0 / 192, scalar2=EPS,
                                              op0=ALU.mult, op1=ALU.add), [skey(16)], [skey(16)])
                POOL(lambda e: e.tensor_tensor(out=st3[:, c0 + 8:c0 + 12], in0=st3[:, c0 + 4:c0 + 8],
                                               in1=nhalf.to_broadcast([128, 4]), op=ALU.pow), [skey(4), "nhalf", skey(8)], [skey(8)])
                POOL(lambda e: e.tensor_tensor(out=st3[:, c0 + 20:c0 + 24], in0=st3[:, c0 + 16:c0 + 20],
                                               in1=nhalf.to_broadcast([128, 4]), op=ALU.pow), [skey(16), "nhalf"], [skey(20)])
                DVE(lambda e, sc=sc: e.tensor_scalar(out=st3[:, c0 + 8:c0 + 12], in0=st3[:, c0 + 8:c0 + 12], scalar1=sc(2), scalar2=None,
                                                     op0=ALU.mult), [skey(8), skey(2)], [skey(8)])
                DVE(lambda e, sc=sc: e.tensor_scalar(out=st3[:, c0 + 24:c0 + 28], in0=st3[:, c0 + 20:c0 + 24], scalar1=sc(3), scalar2=None,
                                                     op0=ALU.mult), [skey(20), skey(3)], [skey(24)])
                fq = st3[:, c0 + 8:c0 + 12]
                rk_ = st3[:, c0 + 20:c0 + 24]
                fkn = st3[:, c0 + 24:c0 + 28]
                DVE(lambda e, fq=fq: e.tensor_tensor(out=t1[:, 0:512].rearrange("p (h j) -> p h j", j=128),
                                                     in0=bank(3).rearrange("p (h j) -> p h j", j=128),
                                                     in1=fq.unsqueeze(2).to_broadcast([128, 4, 128]), op=ALU.mult), [bk(3), skey(8)], ["t1"])
                DVE(lambda e, fq=fq: e.tensor_tensor(out=t1[:, 512:768].rearrange("p (h j) -> p h j", j=64),
                                                     in0=bank(4)[:, 0:256].rearrange("p (h j) -> p h j", j=64),
                                                     in1=fq.unsqueeze(2).to_broadcast([128, 4, 64]), op=ALU.mult), [bk(4), skey(8), "t1"], ["t1"])
                DVE(lambda e, fkn=fkn: e.tensor_tensor(out=sqk.rearrange("p (h j) -> p h j", j=128),
                                                       in0=bank(5).rearrange("p (h j) -> p h j", j=128),
                                                       in1=fkn.unsqueeze(2).to_broadcast([128, 4, 128]), op=ALU.mult),
                    [bk(5), skey(24), "sqk"], ["sqk"])
                ACT(lambda e, sc=sc, i=i: e.activation(out=Vaug[:, i, :, 0:128], in_=bank(6).rearrange("p (h j) -> p h j", j=128),
                                                       func=AF.Copy, scale=sc(3)), [bk(6), skey(3)], [("Vaug", i)])
                DVE(lambda e: e.tensor_tensor(out=qf[:, 0:512], in0=t1[:, 0:512], in1=gq_b[:, 0:512], op=ALU.mult), ["t1", "const"], ["qf"])
                DVE(lambda e: e.tensor_tensor(out=t1[:, 512:768], in0=t1[:, 512:768], in1=gq_b[:, 512:768], op=ALU.mult), ["t1", "const"], ["t1"])

                def rope(E, src, nh_, dst, rk, wk, ta, tb, tak, tbk):
                    s4 = src.rearrange("p (h a r) -> p h a r", a=2, r=32)
                    a4 = ta[:, 0:nh_ * 64].rearrange("p (h a r) -> p h a r", a=2, r=32)
                    b4 = tb[:, 0:nh_ * 64].rearrange("p (h a r) -> p h a r", a=2, r=32)
                    cb = cosT[:, i, :].unsqueeze(1).unsqueeze(1).to_broadcast([128, nh_, 2, 32])
                    sb_ = sinT[:, i, :].unsqueeze(1).to_broadcast([128, nh_, 32])
                    E(lambda e: e.tensor_tensor(out=a4, in0=s4, in1=cb, op=ALU.mult), rk + ["cosT"], [tak])
                    if isinstance(dst, list):
                        E(lambda e: e.scalar_tensor_tensor(out=b4[:, :, 0, :], in0=s4[:, :, 1, :], scalar=-1.0, in1=sb_, op0=ALU.mult, op1=ALU.mult),
                          rk + ["sinT"], [tbk])
                        E(lambda e: e.tensor_tensor(out=b4[:, :, 1, :], in0=s4[:, :, 0, :], in1=sb_, op=ALU.mult), rk + ["sinT", tbk], [tbk])
                        a3 = ta[:, 0:nh_ * 64].rearrange("p (h j) -> p h j", j=64)
                        b3 = tb[:, 0:nh_ * 64].rearrange("p (h j) -> p h j", j=64)
                        for par, dv in enumerate(dst):
                            E(lambda e, par=par, dv=dv: e.tensor_tensor(out=dv, in0=a3[:, par::2, :], in1=b3[:, par::2, :], op=ALU.add),
                              [tak, tbk] + wk, wk)
                    else:
                        E(lambda e: e.tensor_tensor(out=b4[:, :, 0, :], in0=s4[:, :, 1, :], in1=sb_, op=ALU.mult), rk + ["sinT"], [tbk])
                        E(lambda e: e.tensor_tensor(out=b4[:, :, 1, :], in0=s4[:, :, 0, :], in1=sb_, op=ALU.mult), rk + ["sinT", tbk], [tbk])
                        E(lambda e: e.tensor_tensor(out=dst[:, 0:32], in0=ta[:, 0:32], in1=tb[:, 0:32], op=ALU.subtract), [tak, tbk] + wk, wk)
                        E(lambda e: e.tensor_tensor(out=dst[:, 32:64], in0=ta[:, 32:64], in1=tb[:, 32:64], op=ALU.add), [tak, tbk] + wk, wk)

                qz = qf[:, 512:1024].rearrange("p (i r) -> p i r", r=256)
                rope(DVE, t1[:, 512:768], 4, [qz[:, :, 0:64], qz[:, :, 192:256]], ["t1"], ["qf"], ta, tb, "ta", "tb")
                POOL(lambda e: e.tensor_tensor(out=kf[:, 0:512], in0=sqk, in1=gk_b[:, 0:512], op=ALU.mult), ["sqk", "const"], ["kf"])
                POOL(lambda e: e.tensor_tensor(out=krs, in0=krs, in1=gk_b[:, 512:576], op=ALU.mult), ["krs", "const"], ["krs"])
                rope(POOL, krs, 1, krr, ["krs"], ["krr"], tak_, tbk_, "tak", "tbk")
                POOL(lambda e, rk_=rk_: e.tensor_tensor(out=kf[:, 512:768].rearrange("p (h j) -> p h j", j=64),
                                                        in0=krr.unsqueeze(1).to_broadcast([128, 4, 64]),
                                                        in1=rk_.unsqueeze(2).to_broadcast([128, 4, 64]), op=ALU.mult),
                     ["krr", skey(20), "kf"], ["kf"])
                for m in range(8):
                    tp(bankbf(7)[:, m * 128:(m + 1) * 128], qf[:, m * 128:(m + 1) * 128], ["qf"], [bk(7)])
                ACT(lambda e, its=its: e.activation(out=hT[:, :, its], in_=bankbf(7).rearrange("p (m t) -> p m t", t=128),
                                                    func=AF.Copy), [bk(7)], [("hT", i)])
                for m in range(6):
                    tp(bankbf(7)[:, m * 128:(m + 1) * 128], kf[:, m * 128:(m + 1) * 128], ["kf"], [bk(7)])
                ACT(lambda e, its=its: e.activation(out=KT[:, :, its], in_=bankbf(7)[:, 0:768].rearrange("p (m t) -> p m t", t=128),
                                                    func=AF.Copy), [bk(7)], [("KT", i)])
        if b == 0:
            tap("QT", hT[:, 0:6, :], [("hT", i) for i in range(NT)])
            tap("KT", KT, [("KT", i) for i in range(NT)])
            tap("Vaug", Vaug, [("Vaug", i) for i in range(NT)] + ["Vones"])
        S.barrier()
        R = Bump(arena, SMARK, ARENA_BYTES)
        PT = [R.take([512], BF16) for _ in range(3)]
        obuf = R.take([4, 512], F32)
        obn = [R.take([512], BF16) for _ in range(2)]
        st4 = R.take([64], F32)
        junk = R.take([512], BF16)
        QT = hT
        it = 0
        npt = 0
        for qb in range(4):
            qs = slice(qb * 512, (qb + 1) * 512)
            qkeys = [("hT", 4 * qb + j) for j in range(4)]
            for h in range(4):
                ko = 2 + (it % 2) * 2
                it += 1
                DVE(lambda e, ko=ko: e.memset(pp[ko // 2][:, :], 0.0), [], [bk(ko), bk(ko + 1)])
                rp = slice((h % 2) * 64, (h % 2) * 64 + 64)
                rc = 4 + h // 2
                def qk(kc):
                    ksl = slice(kc * 128, (kc + 1) * 128)
                    ks_ = kc % 2
                    mm(bank(ks_), KT[:, h, ksl], QT[:, h, qs], True, False, [("KT", kc)] + qkeys, [bk(ks_)])
                    mm(bank(ks_), KT[:, rc, ksl], QT[:, 4 + h, qs], False, True, [("KT", kc)] + qkeys, [bk(ks_)])

                qk(0)
                for kc in range(NT):
                    ks_ = kc % 2
                    ps_ = npt % 3
                    npt += 1
                    ACT(lambda e, ks_=ks_, ps_=ps_: e.activation(out=PT[ps_], in_=bank(ks_), func=AF.Exp), [bk(ks_)], [("PT", ps_)])
                    if kc + 1 < NT:
                        qk(kc + 1)
                    for j in range(4):
                        ob = ko + j // 2
                        mm(bank(ob)[:, (j % 2) * 256:(j % 2) * 256 + 129], PT[ps_][:, j * 128:(j + 1) * 128], Vaug[:, kc, h, 0:129],
                           False, False, [("PT", ps_), ("Vaug", kc), "Vones"], [bk(ob)], skip=True)
                for j in range(4):
                    ob = ko + j // 2
                    o0 = (j % 2) * 256
                    c0 = ((it * 4 + j) % 16) * 2
                    DVE(lambda e, ob=ob, o0=o0, c0=c0: e.reciprocal(out=st4[:, c0:c0 + 1], in_=bank(ob)[:, o0 + 128:o0 + 129]),
                        [bk(ob)], [("st4", c0)])
                    DVE(lambda e, ob=ob, o0=o0, c0=c0, j=j, h=h: e.tensor_scalar(out=obuf[:, j, h * 128:(h + 1) * 128],
                                                                                 in0=bank(ob)[:, o0:o0 + 128], scalar1=st4[:, c0:c0 + 1],
                                                                                 scalar2=None, op0=ALU.mult),
                        [bk(ob), ("st4", c0)], [("obuf", j)])
            for j in range(4):
                i = qb * 4 + j
                c0 = 32 + (i % 8) * 4
                sj = i % 2
                ACT(lambda e, j=j, c0=c0: e.activation(out=junk[:, 0:512], in_=obuf[:, j, :], func=AF.Square, accum_out=st4[:, c0:c0 + 1]),
                    [("obuf", j)], ["junk", ("st4", c0)])
                DVE(lambda e, c0=c0: e.tensor_scalar(out=st4[:, c0 + 1:c0 + 2], in0=st4[:, c0:c0 + 1], scalar1=1.0 / 512, scalar2=EPS,
                                                     op0=ALU.mult, op1=ALU.add), [("st4", c0)], [("st4", c0 + 1)])
                POOL(lambda e, c0=c0: e.tensor_tensor(out=st4[:, c0 + 2:c0 + 3], in0=st4[:, c0 + 1:c0 + 2], in1=nhalf, op=ALU.pow),
                     [("st4", c0 + 1), "nhalf"], [("st4", c0 + 2)])
                DVE(lambda e, j=j, c0=c0, sj=sj: e.scalar_tensor_tensor(out=obn[sj], in0=obuf[:, j, :], scalar=st4[:, c0 + 2:c0 + 3],
                                                                        in1=gmo_b, op0=ALU.mult, op1=ALU.mult),
                    [("obuf", j), ("st4", c0 + 2), "const"], [("obn", sj)])
                kt = 6 + i % 2
                for m in range(4):
                    tp(bankbf(kt)[:, m * 128:(m + 1) * 128], obn[sj][:, m * 128:(m + 1) * 128], [("obn", sj)], [bk(kt)])
                ACT(lambda e, kt=kt, i=i: e.activation(out=oT[:, 4:8, i * 128:(i + 1) * 128],
                                                       in_=bankbf(kt)[:, 0:512].rearrange("p (m t) -> p m t", t=128), func=AF.Copy),
                    [bk(kt)], [("oT", 4, i)])
        tap("oT", oT, [("oT", 4, i) for i in range(NT)])
        S.barrier()
        R = Bump(arena, RMARK, ARENA_BYTES)
        W_o = R.take([8, D], BF16)
        xin = [R.take([D], F32) for _ in range(2)]
        x1o = [R.take([D], F32) for _ in range(2)]
        for kc in range(8):
            S.dma("pool", lambda e, kc=kc: e.dma_start(out=W_o[:, kc, :], in_=w_out[kc * 128:(kc + 1) * 128, :]), "w_o", writes=["W_o"])
        for i in range(NT):
            sl = i % 2
            its = slice(i * 128, (i + 1) * 128)
            S.dma("sp", lambda e, sl=sl, i=i: e.dma_start(out=xin[sl], in_=x[b, i * 128:(i + 1) * 128, :]), f"xin{sl}",
                  writes=[("xin", sl)])
            kp = (i % 2) * 2
            for half in range(2):
                for m in range(8):
                    mm(bank(kp + half), oT[:, m, its], W_o[:, m, half * 512:(half + 1) * 512], m == 0, m == 7, ["W_o"], [bk(kp + half)])
            DVE(lambda e, sl=sl, kp=kp: e.tensor_tensor(out=x1o[sl], in0=pp[kp // 2][:, :], in1=xin[sl], op=ALU.add),
                [bk(kp), bk(kp + 1), ("xin", sl), ("x1o", sl)], [("x1o", sl)])
            S.dma("sp", lambda e, sl=sl, i=i: e.dma_start(out=x1s[b, i * 128:(i + 1) * 128, :], in_=x1o[sl]), f"x1o{sl}",
                  reads=[("x1o", sl)])
        S.barrier()

    Bm = Bump(arena, MARK0, ARENA_BYTES)
    W_up = Bm.take([8, DFF], BF16)
    W_dn = Bm.take([32, D], BF16)
    gffn_b = Bm.take([D], F32)
    TB = 256
    NJ = TB // 128
    x1t = [[Bm.take([D], F32) for _ in range(NJ)] for _ in range(2)]
    hbf = [Bm.take([D], BF16) for _ in range(2)]
    h2T = Bm.take([8, TB], BF16)
    aT = Bm.take([32, TB], BF16)
    rl = [Bm.take([512], F32) for _ in range(2)]
    yo = [Bm.take([D], F32) for _ in range(2)]
    junk = Bm.take([D], BF16)
    st5 = Bm.take([64], F32)
    S.dma("sp", lambda e: e.dma_start(out=gffn_b, in_=g_ffn.partition_broadcast(128)), "const2", writes=["gffn"])
    for kc in range(8):
        for q4 in range(4):
            S.dma("pool", lambda e, kc=kc, q4=q4: e.dma_start(out=W_up[:, kc, q4 * 1024:(q4 + 1) * 1024],
                                                             in_=w_up[kc * 128:(kc + 1) * 128, q4 * 1024:(q4 + 1) * 1024]),
                  "w_up", writes=["W_up"])
    for c in range(32):
        S.dma("pool", lambda e, c=c: e.dma_start(out=W_dn[:, c, :], in_=w_down[c * 128:(c + 1) * 128, :]), "w_dn", writes=["W_dn"])
    x1f = x1s.rearrange("b s d -> (b s) d")
    yf = y.rearrange("b s d -> (b s) d")
    nblk = nseq * SEQ // TB
    nyo = 0
    for blk in range(nblk):
        xs = blk % 2
        for j in range(NJ):
            t = blk * NJ + j
            S.dma("sp", lambda e, xs=xs, j=j, t=t: e.dma_start(out=x1t[xs][j], in_=x1f[t * 128:(t + 1) * 128, :]), f"x1t{xs}{j}",
                  writes=[("x1t", xs, j)])
            c0 = (t % 8) * 4
            sl = t % 2
            ACT(lambda e, xs=xs, j=j, c0=c0: e.activation(out=junk, in_=x1t[xs][j], func=AF.Square, accum_out=st5[:, c0:c0 + 1]),
                [("x1t", xs, j)], ["junk", ("st5", c0)])
            DVE(lambda e, c0=c0: e.tensor_scalar(out=st5[:, c0 + 1:c0 + 2], in0=st5[:, c0:c0 + 1], scalar1=1.0 / D, scalar2=EPS,
                                                 op0=ALU.mult, op1=ALU.add), [("st5", c0)], [("st5", c0 + 1)])
            POOL(lambda e, c0=c0: e.tensor_tensor(out=st5[:, c0 + 2:c0 + 3], in0=st5[:, c0 + 1:c0 + 2], in1=nhalf, op=ALU.pow),
                 [("st5", c0 + 1), "nhalf"], [("st5", c0 + 2)])
            DVE(lambda e, xs=xs, j=j, c0=c0, sl=sl: e.scalar_tensor_tensor(out=hbf[sl], in0=x1t[xs][j], scalar=st5[:, c0 + 2:c0 + 3],
                                                                           in1=gffn_b, op0=ALU.mult, op1=ALU.mult),
                [("x1t", xs, j), ("st5", c0 + 2), "gffn"], [("hbf", sl)])
            for kc in range(8):
                tp(bankbf(0)[:, kc * 128:(kc + 1) * 128], hbf[sl][:, kc * 128:(kc + 1) * 128], [("hbf", sl)], [bk(0)])
            ACT(lambda e, j=j: e.activation(out=h2T[:, :, j * 128:(j + 1) * 128], in_=bankbf(0).rearrange("p (k t) -> p k t", t=128),
                                            func=AF.Copy), [bk(0)], [("h2T", j)])
        hk2 = [("h2T", j) for j in range(NJ)] + ["W_up"]
        for cp in range(16):
            ku = 1 + cp % 2
            for c in range(2):
                cc = cp * 2 + c
                for kc in range(8):
                    mm(bank(ku)[:, c * TB:(c + 1) * TB], W_up[:, kc, cc * 128:(cc + 1) * 128], h2T[:, kc, :], kc == 0, kc == 7, hk2, [bk(ku)])
            rs = cp % 2
            ACT(lambda e, ku=ku, rs=rs: e.activation(out=rl[rs], in_=bank(ku), func=AF.Relu), [bk(ku)], [("rl", rs)])
            POOL(lambda e, rs=rs, cp=cp: e.tensor_tensor(out=aT[:, 2 * cp:2 * cp + 2, :], in0=rl[rs].rearrange("p (c t) -> p c t", t=TB),
                                                         in1=rl[rs].rearrange("p (c t) -> p c t", t=TB), op=ALU.mult),
                 [("rl", rs)], [("aT", cp)])
        for j in range(NJ):
            t = blk * NJ + j
            kd = 4 + (t % 2) * 2
            for half in range(2):
                for c in range(32):
                    mm(bank(kd + half), aT[:, c, j * 128:(j + 1) * 128], W_dn[:, c, half * 512:(half + 1) * 512], c == 0, c == 31,
                       [("aT", c // 2), "W_dn"], [bk(kd + half)])
            ys = nyo % 2
            nyo += 1
            DVE(lambda e, ys=ys, kd=kd, xs=xs, j=j: e.tensor_tensor(out=yo[ys], in0=pp[kd // 2][:, :], in1=x1t[xs][j], op=ALU.add),
                [bk(kd), bk(kd + 1), ("x1t", xs, j), ("yo", ys)], [("yo", ys)])
            S.dma("sp", lambda e, ys=ys, t=t: e.dma_start(out=yf[t * 128:(t + 1) * 128, :], in_=yo[ys]), f"yo{ys}", reads=[("yo", ys)])
    S.barrier()

    sems = {n: es.enter_context(nc.semaphore(n)) for n in sorted(S.sem_names)}
    with nc.Block() as block:
        S.emit(block, sems)
    es.close()
    return nc, dbg_out


def _prep_inputs(inputs, nseq, ncores):
    f32 = np.float32
    g = lambda k: np.ascontiguousarray(np.asarray(inputs[k]))
    wq_ = g("w_q_up")[0].reshape(384, 4, 192)
    wq_p = np.concatenate([wq_[:, :, :128].reshape(384, 512), wq_[:, :, 128:].reshape(384, 256)], axis=1)
    wkv_ = g("w_kv_up")[0].reshape(256, 4, 256)
    wkv_p = np.concatenate([wkv_[:, :, :128].reshape(256, 512), wkv_[:, :, 128:].reshape(256, 512)], axis=1)
    invf = (10000.0 ** (-(np.arange(0, 64, 2, dtype=f32)) / f32(64))).astype(f32).reshape(1, 32)
    shared = {
        "g_mix": g("g_mix_norm").reshape(1, D), "w_in": g("w_in")[0], "lb_param": g("lb_param")[:, 0:2, :],
        "g_hg": g("g_hgrn_out")[0], "g_cq": g("g_cq").reshape(1, 384), "wq": np.ascontiguousarray(wq_p),
        "g_ckv": g("g_ckv").reshape(1, 256), "wkv": np.ascontiguousarray(wkv_p), "g_q": g("g_q_norm").reshape(1, 192),
        "g_k": g("g_k_norm").reshape(1, 192), "g_mo": g("g_mla_out").reshape(1, 512), "w_out": g("w_out")[0],
        "g_ffn": g("g_ffn_norm").reshape(1, D), "w_up": g("w_up")[0], "w_down": g("w_down")[0], "invf": invf,
    }
    x = g("x")
    pos = g("positions").astype(np.int32)
    maps = []
    for c in range(ncores):
        m = dict(shared)
        m["x"] = np.ascontiguousarray(x[c * nseq:(c + 1) * nseq])
        m["pos"] = np.ascontiguousarray(pos[c * nseq:(c + 1) * nseq])
        maps.append(m)
    return maps


def kernel(**inputs):
    nseq = 4
    nc, _ = build(nseq)
    maps = _prep_inputs(inputs, nseq, NCORES)
    res = run_bass_kernel_spmd(nc, maps, core_ids=list(range(NCORES)))
    out = np.concatenate([np.asarray(r["y"]) for r in res.results], axis=0)
    return out.astype(np.float32, copy=False)
```

```python
import numpy as np
from contextlib import ExitStack
import concourse.bass as bass
import concourse.mybir as mybir
from concourse.bass_utils import run_bass_kernel_spmd

F32 = mybir.dt.float32
BF16 = mybir.dt.bfloat16
I32 = mybir.dt.int32
AF = mybir.ActivationFunctionType
ALU = mybir.AluOpType
AX = mybir.AxisListType

NCORES = 8
SEQ = 2048
NT = SEQ // 128
D = 1024
DIN = 3264
DFF = 4096
EPS = 1e-6
PI = float(np.pi)
ARENA_BYTES = 212480

ENGS = ("pe", "act", "dve", "pool", "sp")


def _cody_waite():
    two_pi = 2.0 * np.pi
    c1 = 6.28125
    r1 = two_pi - c1
    m, e = np.frexp(r1)
    c2 = float(np.ldexp(np.round(m * 2 ** 11) / 2 ** 11, e))
    c3 = float(np.float32(two_pi - c1 - c2))
    return c1, c2, c3


CW1, CW2, CW3 = _cody_waite()


class _Rec:
    def __getattr__(self, name):
        def f(*a, **k):
            self.call = (name, a, k)
            return self
        return f


class Sched:
    def __init__(self):
        self.q = {e: [] for e in ENGS}
        self.cnt = {e: 0 for e in ENGS}
        self.seen = {e: {} for e in ENGS}
        self.bufs = {}
        self.dma_tot = {}
        self.sem_names = set(ENGS)

    def _st(self, k):
        st = self.bufs.get(k)
        if st is None:
            st = self.bufs[k] = {"w": None, "r": {}}
        return st

    def _deps(self, eng, reads, writes):
        toks = []
        for k in reads:
            st = self._st(k)
            if st["w"] is not None:
                toks.append(st["w"])
        for k in writes:
            st = self._st(k)
            if st["w"] is not None:
                toks.append(st["w"])
            toks.extend(st["r"].items())
        waits = {}
        for (s, v) in toks:
            if s == "pe" and eng == "pe":
                continue
            if self.seen[eng].get(s, 0) >= v:
                continue
            if waits.get(s, 0) < v:
                waits[s] = v
        for s, v in waits.items():
            self.seen[eng][s] = v
        return list(waits.items())

    def _commit(self, tok, reads, writes):
        for k in reads:
            r = self._st(k)["r"]
            if r.get(tok[0], 0) < tok[1]:
                r[tok[0]] = tok[1]
        for k in writes:
            st = self._st(k)
            st["w"] = tok
            st["r"] = {}

    def op(self, eng, fn, reads=(), writes=()):
        rec = _Rec()
        fn(rec)
        waits = self._deps(eng, reads, writes)
        self.cnt[eng] += 1
        tok = (eng, self.cnt[eng])
        self.q[eng].append((rec.call, waits, (eng, 1)))
        self._commit(tok, reads, writes)

    def dma(self, eng, fn, sem, reads=(), writes=()):
        rec = _Rec()
        fn(rec)
        self.sem_names.add(sem)
        waits = self._deps(eng, reads, writes)
        self.dma_tot[sem] = self.dma_tot.get(sem, 0) + 16
        tok = (sem, self.dma_tot[sem])
        self.q[eng].append((rec.call, waits, (sem, 16)))
        self._commit(tok, reads, writes)

    def barrier(self):
        for e in ENGS:
            waits = []
            for e2 in ENGS:
                if self.cnt[e2] > self.seen[e].get(e2, 0):
                    waits.append((e2, self.cnt[e2]))
                    self.seen[e][e2] = self.cnt[e2]
            for s, tot in self.dma_tot.items():
                if tot > self.seen[e].get(s, 0):
                    waits.append((s, tot))
                    self.seen[e][s] = tot
            self.q[e].append((None, waits, None))
        self.bufs = {}

    def emit(self, block, sems):
        handles = {"pe": block.tensor, "act": block.scalar, "dve": block.vector,
                   "pool": block.gpsimd, "sp": block.sync}

        def mk(e):
            ops = self.q[e]

            def body(engine):
                for fn, waits, inc in ops:
                    for s, v in waits:
                        engine.wait_ge(sems[s], v)
                    if fn is not None:
                        name, a, k = fn
                        getattr(engine, name)(*a, **k).then_inc(sems[inc[0]], inc[1])
            return body

        for e in ENGS:
            handles[e](mk(e))


def _dsize(dt):
    return 2 if dt == BF16 else 4


class Bump:
    def __init__(self, arena, start, end):
        self.arena, self.cur, self.end = arena, start, end

    def take(self, shape, dt):
        n = int(np.prod(shape)) * _dsize(dt)
        off = (self.cur + 63) // 64 * 64
        n4 = (n + 3) // 4 * 4
        self.cur = off + n4
        assert self.cur <= self.end, (self.cur, self.end)
        ap = self.arena[:, off // 4:(off + n4) // 4]
        if dt != F32:
            ap = ap.bitcast(dt)
        if n4 != n:
            ap = ap[:, 0:int(np.prod(shape))]
        if len(shape) == 2:
            ap = ap.rearrange("p (a b) -> p a b", b=shape[1])
        elif len(shape) == 3:
            ap = ap.rearrange("p (a b c) -> p a b c", b=shape[1], c=shape[2])
        return ap


def build(nseq=4, dbg=None):
    nc = bass.Bass("TRN2", target_bir_lowering=False)
    S = Sched()
    dbg_out = {}

    def din(name, shape, dt=F32):
        return nc.dram_tensor(name, list(shape), dt, kind="ExternalInput").ap()

    x = din("x", [nseq, SEQ, D])
    pos = din("pos", [nseq, SEQ], I32)
    g_mix = din("g_mix", [1, D])
    w_in = din("w_in", [D, DIN])
    lb_param = din("lb_param", [2, 2, 512])
    g_hg = din("g_hg", [4, 128])
    g_cq = din("g_cq", [1, 384])
    wq = din("wq", [384, 768])
    g_ckv = din("g_ckv", [1, 256])
    wkv = din("wkv", [256, 1024])
    g_q = din("g_q", [1, 192])
    g_k = din("g_k", [1, 192])
    g_mo = din("g_mo", [1, 512])
    w_out = din("w_out", [D, D])
    g_ffn = din("g_ffn", [1, D])
    w_up = din("w_up", [D, DFF])
    w_down = din("w_down", [DFF, D])
    invf = din("invf", [1, 32])
    y = nc.dram_tensor("y", [nseq, SEQ, D], F32, kind="ExternalOutput").ap()
    x1s = nc.dram_tensor("x1s", [nseq, SEQ, D], F32).ap()

    es = ExitStack()
    arena = es.enter_context(nc.sbuf_tensor("arena", [128, ARENA_BYTES // 4], F32))[:]
    pp = [es.enter_context(nc.psum_tensor(f"pp{i}", [128, 1024], F32)) for i in range(4)]

    def bank(k):
        return pp[k // 2][:, (k % 2) * 512:(k % 2) * 512 + 512]

    def bankbf(k):
        return bank(k).bitcast(BF16)

    def bk(k):
        return ("bank", k)

    def PE(fn, r, w):
        S.op("pe", fn, r, w)

    def ACT(fn, r, w):
        S.op("act", fn, r, w)

    def DVE(fn, r, w):
        S.op("dve", fn, r, w)

    def POOL(fn, r, w):
        S.op("pool", fn, r, w)

    def mm(out, lhsT, rhs, start, stop, r, w, skip=False):
        if skip:
            PE(lambda e: e.matmul(out, lhsT=lhsT, rhs=rhs, start=start, stop=stop, skip_group_check=True), r, w)
        else:
            PE(lambda e: e.matmul(out, lhsT=lhsT, rhs=rhs, start=start, stop=stop), r, w)

    def tp(out, in_, r, w):
        PE(lambda e: e.transpose(out=out, in_=in_, identity=ident), list(r) + ["ident"], w)

    def tap(name, ap, key):
        if dbg is None or name not in dbg:
            return
        shp = list(ap.shape)
        t = nc.dram_tensor("dbg_" + name, shp, ap.dtype, kind="ExternalOutput").ap()
        dbg_out[name] = t
        S.dma("sp", lambda e: e.dma_start(out=t, in_=ap), "dbg", reads=key)

    P = Bump(arena, 0, ARENA_BYTES)
    ident = P.take([128], BF16)
    maskfb = P.take([256], I32)
    cols = P.take([64], F32)
    ones_b = P.take([2], BF16)
    nhalf = P.take([1], F32)
    ones_f = P.take([512], F32)
    MARK0 = P.cur
    C_GCQ, C_GCKV, C_GHG, C_LB, C_LNOML, C_LBP = 0, 3, 5, 9, 17, 25

    A = Bump(arena, MARK0, ARENA_BYTES)
    W_in = A.take([8, DIN], BF16)
    gmix_b = A.take([D], F32)
    gq_b = A.take([768], F32)
    gk_b = A.take([768], F32)
    gmo_b = A.take([512], F32)
    invf_b = A.take([32], F32)
    hT = A.take([8, SEQ], BF16)
    oT = A.take([8, SEQ], BF16)
    RMARK = A.cur

    R0 = Bump(arena, RMARK, ARENA_BYTES)
    identf = R0.take([128], F32)
    lbt = R0.take([8], F32)

    def cdma(out, in_, slow=False):
        if slow:
            S.dma("sp", lambda e: e.dma_start(out=out, in_=in_, allow_slow_non_contiguous=True), "const", writes=["const"])
        else:
            S.dma("sp", lambda e: e.dma_start(out=out, in_=in_), "const", writes=["const"])

    cdma(gmix_b, g_mix.partition_broadcast(128))
    for h in range(4):
        cdma(gq_b[:, h * 128:(h + 1) * 128], g_q[:, 0:128].partition_broadcast(128))
        cdma(gq_b[:, 512 + h * 64:512 + (h + 1) * 64], g_q[:, 128:192].partition_broadcast(128))
        cdma(gk_b[:, h * 128:(h + 1) * 128], g_k[:, 0:128].partition_broadcast(128))
        cdma(gk_b[:, 512 + h * 64:512 + (h + 1) * 64], g_k[:, 128:192].partition_broadcast(128))
    cdma(gmo_b, g_mo.partition_broadcast(128))
    cdma(invf_b, invf.partition_broadcast(128))
    cdma(cols[:, C_GCQ:C_GCQ + 3], g_cq[0].rearrange("(m p) -> p m", p=128), slow=True)
    cdma(cols[:, C_GCKV:C_GCKV + 2], g_ckv[0].rearrange("(m p) -> p m", p=128), slow=True)
    cdma(cols[:, C_GHG:C_GHG + 4], g_hg.rearrange("h e -> e h"), slow=True)
    for d_ in range(2):
        for s_ in range(2):
            o = C_LBP + (d_ * 2 + s_) * 4
            cdma(cols[:, o:o + 4], lb_param[d_, s_].rearrange("(h p) -> p h", p=128), slow=True)
    for kc in range(8):
        S.dma("pool", lambda e, kc=kc: e.dma_start(out=W_in[:, kc, :], in_=w_in[kc * 128:(kc + 1) * 128, :], max_dma_last_dim=4096),
              "w_in", writes=["W_in"])

    POOL(lambda e: e.memset(identf, 0.0), [], ["identf"])
    POOL(lambda e: e.affine_select(out=identf, in_=identf, pattern=[[-1, 128]], compare_op=ALU.not_equal, fill=1.0,
                                   base=0, channel_multiplier=1), ["identf"], ["identf"])
    DVE(lambda e: e.tensor_copy(out=ident, in_=identf), ["identf"], ["ident"])
    POOL(lambda e: e.iota(maskfb[:, 0:128], pattern=[[1, 128]], base=0, channel_multiplier=-1), [], ["maskfb"])
    POOL(lambda e: e.iota(maskfb[:, 128:256], pattern=[[-1, 128]], base=0, channel_multiplier=1), ["maskfb"], ["maskfb"])
    DVE(lambda e: e.tensor_single_scalar(out=maskfb, in_=maskfb, scalar=0, op=ALU.is_ge), ["maskfb"], ["maskfb"])
    POOL(lambda e: e.memset(ones_b, 1.0), [], ["ones_b"])
    POOL(lambda e: e.memset(nhalf, -0.5), [], ["nhalf"])
    POOL(lambda e: e.memset(ones_f, 1.0), [], ["ones_f"])
    DVE(lambda e: e.tensor_scalar(out=gq_b, in0=gq_b, scalar1=float(192 ** -0.5), scalar2=None, op0=ALU.mult), ["const"], ["const"])
    lbp = cols[:, C_LBP:C_LBP + 16].rearrange("p (d s h) -> p d s h", s=2, h=4)
    lbv = cols[:, C_LB:C_LB + 8]
    DVE(lambda e: e.tensor_tensor(out=lbt.rearrange("p (d h) -> p d h", h=4), in0=lbp[:, :, 1, :], in1=lbp[:, :, 0, :], op=ALU.subtract),
        ["const"], ["lbt"])
    ACT(lambda e: e.activation(out=lbt, in_=lbt, func=AF.Exp), ["lbt"], ["lbt"])
    DVE(lambda e: e.tensor_scalar(out=lbt, in0=lbt, scalar1=1.0, scalar2=None, op0=ALU.add), ["lbt"], ["lbt"])
    DVE(lambda e: e.reciprocal(out=lbv, in_=lbt), ["lbt", "const"], ["const"])
    ACT(lambda e: e.activation(out=cols[:, C_LNOML:C_LNOML + 8], in_=lbv, func=AF.Ln, scale=-1.0, bias=1.0), ["const"], ["const"])
    S.barrier()

    for b in range(nseq):
        R = Bump(arena, RMARK, ARENA_BYTES)
        V_all = R.take([NT, 512], BF16)
        XMARK = R.cur
        xin = [R.take([D], F32) for _ in range(2)]
        hbf = [R.take([D], BF16) for _ in range(2)]
        junk = R.take([D], BF16)
        st = R.take([64], F32)
        for i in range(NT):
            sl = i % 2
            S.dma("sp", lambda e, sl=sl, i=i: e.dma_start(out=xin[sl], in_=x[b, i * 128:(i + 1) * 128, :]), f"xin{sl}",
                  writes=[("xin", sl)])
            c0 = (i % 8) * 4
            ACT(lambda e, sl=sl, c0=c0: e.activation(out=junk, in_=xin[sl], func=AF.Square, accum_out=st[:, c0:c0 + 1]),
                [("xin", sl)], ["junk", ("st", c0)])
            DVE(lambda e, c0=c0: e.tensor_scalar(out=st[:, c0 + 1:c0 + 2], in0=st[:, c0:c0 + 1], scalar1=1.0 / D, scalar2=EPS,
                                                 op0=ALU.mult, op1=ALU.add), [("st", c0)], [("st", c0 + 1)])
            POOL(lambda e, c0=c0: e.tensor_tensor(out=st[:, c0 + 2:c0 + 3], in0=st[:, c0 + 1:c0 + 2], in1=nhalf, op=ALU.pow),
                 [("st", c0 + 1), "nhalf"], [("st", c0 + 2)])
            DVE(lambda e, sl=sl, c0=c0: e.scalar_tensor_tensor(out=hbf[sl], in0=xin[sl], scalar=st[:, c0 + 2:c0 + 3], in1=gmix_b,
                                                               op0=ALU.mult, op1=ALU.mult),
                [("xin", sl), ("st", c0 + 2), "const"], [("hbf", sl)])
            k = i % 2
            for kc in range(8):
                tp(bankbf(k)[:, kc * 128:(kc + 1) * 128], hbf[sl][:, kc * 128:(kc + 1) * 128], [("hbf", sl)], [bk(k)])
            ACT(lambda e, k=k, i=i: e.activation(out=hT[:, :, i * 128:(i + 1) * 128],
                                                 in_=bankbf(k).rearrange("p (k t) -> p k t", t=128), func=AF.Copy),
                [bk(k)], [("hT", i)])
        tap("hT", hT, [("hT", i) for i in range(NT)])
        for i in range(NT):
            k = 2 + i % 2
            for kc in range(8):
                mm(bank(k), hT[:, kc, i * 128:(i + 1) * 128], W_in[:, kc, 1536:2048], kc == 0, kc == 7,
                   [("hT", i), "W_in"], [bk(k)])
            DVE(lambda e, k=k, i=i: e.tensor_copy(out=V_all[:, i, :], in_=bank(k)), [bk(k)], [("V", i)])
        S.barrier()
        R = Bump(arena, XMARK, ARENA_BYTES)
        sgT = R.take([SEQ], BF16)
        TS = [[R.take([512], F32) for _ in range(6)] for _ in range(2)]
        QK = {(d_, w_): R.take([SEQ], BF16) for d_ in range(2) for w_ in "qk"}
        Zbf = [R.take([NT, 128], BF16) for _ in range(2)]
        Y = [[R.take([128], F32) for _ in range(2)] for _ in range(2)]
        Xs = [[R.take([128], F32) for _ in range(2)] for _ in range(2)]
        sqo = [TS[s_][0] for s_ in range(2)]
        onb = [TS[s_][1].bitcast(BF16)[:, 0:512] for s_ in range(2)]
        atm = [TS[s_][2].bitcast(BF16).rearrange("p (d n) -> p d n", d=2) for s_ in range(2)]
        ktok = [TS[s_][4].bitcast(BF16)[:, 0:512] for s_ in range(2)]
        Rall = [R.take([NT], F32) for _ in range(2)]
        Eall = [R.take([NT], F32) for _ in range(2)]
        dR = [R.take([3, NT], F32) for _ in range(2)]
        efac = [R.take([3, NT], F32) for _ in range(2)]
        carry = R.take([2], F32)
        st2 = R.take([64], F32)
        for h in range(4):
            for blk in range(4):
                tl = slice(blk * 512, (blk + 1) * 512)
                hk = [("hT", 4 * blk + j) for j in range(4)] + ["W_in"]
                kg = 6 + blk % 2
                for kc in range(8):
                    mm(bank(kg), W_in[:, kc, 2048 + h * 128:2048 + (h + 1) * 128], hT[:, kc, tl], kc == 0, kc == 7, hk, [bk(kg)])
                ACT(lambda e: e.activation(out=sgT[:, tl], in_=bank(kg), func=AF.Silu), [bk(kg)], [("sgT", blk)])

            def prep_mm(blk):
                tl = slice(blk * 512, (blk + 1) * 512)
                hk = [("hT", 4 * blk + j) for j in range(4)] + ["W_in"]
                for j, c0 in enumerate((0, 512, 1024)):
                    kk = (3 * blk + j) % 6
                    for kc in range(8):
                        mm(bank(kk), W_in[:, kc, c0 + h * 128:c0 + (h + 1) * 128], hT[:, kc, tl], kc == 0, kc == 7, hk, [bk(kk)])

            def front(it):
                blk, d_ = it // 2, it % 2
                T = TS[it % 2]
                tk = [("T", it % 2, j) for j in range(6)]
                kz = (3 * blk + 1 + d_) % 6
                lbc = cols[:, C_LB + d_ * 4 + h:C_LB + d_ * 4 + h + 1]
                ACT(lambda e: e.activation(out=T[0], in_=bank(kz), func=AF.Exp, scale=-1.0), [bk(kz)], [tk[0]])
                ACT(lambda e: e.activation(out=T[1], in_=T[0], func=AF.Ln, scale=lbc, bias=1.0), [tk[0], "const"], [tk[1]])
                ACT(lambda e: e.activation(out=T[2], in_=T[0], func=AF.Ln, scale=1.0, bias=1.0), [tk[0]], [tk[2]])
                POOL(lambda e: e.tensor_tensor(out=T[3], in0=T[1], in1=T[2], op=ALU.subtract), [tk[1], tk[2]], [tk[3]])
                DVE(lambda e: e.tensor_tensor(out=T[4], in0=bank(kz), in1=T[2], op=ALU.add), [bk(kz), tk[2]], [tk[4]])
                if blk == 0:
                    DVE(lambda e: e.tensor_tensor_scan(out=T[5], data0=ones_f, data1=T[3], initial=0.0, op0=ALU.mult, op1=ALU.add),
                        [tk[3], "ones_f"], [tk[5]])
                else:
                    DVE(lambda e: e.tensor_tensor_scan(out=T[5], data0=ones_f, data1=T[3], initial=carry[:, d_:d_ + 1],
                                                       op0=ALU.mult, op1=ALU.add), [tk[3], "ones_f", ("carry", d_)], [tk[5]])
                DVE(lambda e: e.tensor_copy(out=carry[:, d_:d_ + 1], in_=T[5][:, 511:512]), [tk[5]], [("carry", d_)])
                if d_ == 0:
                    Bx, kB = T[5], tk[5]
                else:
                    DVE(lambda e: e.tensor_tensor(out=T[1], in0=T[3], in1=T[5], op=ALU.subtract), [tk[3], tk[5]], [tk[1]])
                    Bx, kB = T[1], tk[1]
                DVE(lambda e: e.tensor_copy(out=Rall[d_][:, blk * 4:(blk + 1) * 4], in_=Bx[:, 63:512:128]), [kB], [("Rall", d_)])
                eo_ = 127 if d_ == 0 else 0
                DVE(lambda e: e.tensor_copy(out=Eall[d_][:, blk * 4:(blk + 1) * 4], in_=Bx[:, eo_:512:128]), [kB], [("Eall", d_)])
                DVE(lambda e: e.tensor_tensor(out=T[0].rearrange("p (c j) -> p c j", j=128), in0=Bx.rearrange("p (c j) -> p c j", j=128),
                                              in1=Rall[d_][:, blk * 4:(blk + 1) * 4].unsqueeze(2).to_broadcast([128, 4, 128]),
                                              op=ALU.subtract), [kB, ("Rall", d_)], [tk[0]])

            def back(it):
                blk, d_ = it // 2, it % 2
                tl = slice(blk * 512, (blk + 1) * 512)
                T = TS[it % 2]
                tk = [("T", it % 2, j) for j in range(6)]
                kq = (3 * blk) % 6
                lno = cols[:, C_LNOML + d_ * 4 + h:C_LNOML + d_ * 4 + h + 1]
                ACT(lambda e: e.activation(out=T[2], in_=T[0], func=AF.Exp), [tk[0]], [tk[2]])
                POOL(lambda e: e.tensor_tensor(out=T[3], in0=T[4], in1=T[0], op=ALU.add), [tk[4], tk[0]], [tk[3]])
                ACT(lambda e: e.activation(out=QK[(d_, "k")][:, tl], in_=T[3], func=AF.Exp, scale=-1.0, bias=lno),
                    [tk[3], "const"], [("QK", d_, "k", blk)])
                DVE(lambda e: e.tensor_tensor(out=QK[(d_, "q")][:, tl], in0=bank(kq), in1=T[2], op=ALU.mult),
                    [bk(kq), tk[2]], [("QK", d_, "q", blk)])

            prep_mm(0)
            front(0)
            for it in range(8):
                if it % 2 == 0 and it // 2 + 1 < 4:
                    prep_mm(it // 2 + 1)
                if it + 1 < 8:
                    front(it + 1)
                back(it)
            for d_ in range(2):
                if d_ == 0:
                    DVE(lambda e: e.tensor_tensor(out=dR[0][:, 0, 1:16], in0=Eall[0][:, 1:16], in1=Eall[0][:, 0:15], op=ALU.subtract),
                        [("Eall", 0)], [("dR", 0)])
                    DVE(lambda e: e.tensor_tensor(out=dR[0][:, 2, 1:16], in0=Rall[0][:, 1:16], in1=Eall[0][:, 0:15], op=ALU.subtract),
                        [("Eall", 0), ("Rall", 0), ("dR", 0)], [("dR", 0)])
                    DVE(lambda e: e.memset(dR[0][:, :, 0:1], 0.0), [("dR", 0)], [("dR", 0)])
                else:
                    DVE(lambda e: e.tensor_tensor(out=dR[1][:, 0, 0:15], in0=Eall[1][:, 0:15], in1=Eall[1][:, 1:16], op=ALU.subtract),
                        [("Eall", 1)], [("dR", 1)])
                    DVE(lambda e: e.tensor_tensor(out=dR[1][:, 2, 0:15], in0=Rall[1][:, 0:15], in1=Eall[1][:, 1:16], op=ALU.subtract),
                        [("Eall", 1), ("Rall", 1), ("dR", 1)], [("dR", 1)])
                    DVE(lambda e: e.memset(dR[1][:, :, 15:16], 0.0), [("dR", 1)], [("dR", 1)])
                DVE(lambda e: e.tensor_tensor(out=dR[d_][:, 1, :], in0=Eall[d_], in1=Rall[d_], op=ALU.subtract),
                    [("Eall", d_), ("Rall", d_), ("dR", d_)], [("dR", d_)])
                ACT(lambda e: e.activation(out=efac[d_], in_=dR[d_], func=AF.Exp), [("dR", d_)], [("efac", d_)])
            if h == 0 and b == 0:
                tap("Qf", QK[(0, "q")], [("QK", 0, "q", j) for j in range(4)])
                tap("Kf", QK[(0, "k")], [("QK", 0, "k", j) for j in range(4)])
            order = [list(range(15)), list(range(15, 0, -1))]
            pslot = {}
            ngrp = 0
            for gi in range(4):
                for d_ in range(2):
                    cs_ = order[d_][gi * 4:gi * 4 + 4]
                    kT, kP, ks = ngrp % 2, 2 + ngrp % 4, ngrp % 2
                    ngrp += 1
                    for j, c in enumerate(cs_):
                        tp(bankbf(kT)[:, j * 128:(j + 1) * 128], QK[(d_, "k")][:, c * 128:(c + 1) * 128], [("QK", d_, "k", c // 4)], [bk(kT)])
                    nn = len(cs_) * 128
                    ACT(lambda e: e.activation(out=ktok[ks][:, 0:nn], in_=bankbf(kT)[:, 0:nn], func=AF.Copy), [bk(kT)], [("T", ks, 4)])
                    for j, c in enumerate(cs_):
                        mm(bank(kP)[:, j * 128:(j + 1) * 128], ktok[ks][:, j * 128:(j + 1) * 128], V_all[:, c, h * 128:(h + 1) * 128], True, True,
                           [("T", ks, 4), ("V", c)], [bk(kP)])
                        pslot[(d_, c)] = (kP, j)
                for idx in range(gi * 4, min(gi * 4 + 4, 15)):
                    for d_ in range(2):
                        c = order[d_][idx]
                        kP_, j_ = pslot[(d_, c)]
                        pw = Xs[d_][idx % 2]
                        ACT(lambda e: e.activation(out=pw, in_=bank(kP_)[:, j_ * 128:(j_ + 1) * 128], func=AF.Copy, scale=efac[d_][:, 1, c:c + 1]),
                            [bk(kP_), ("efac", d_)], [("Xs", d_, idx % 2)])
                    for d_ in range(2):
                        c = order[d_][idx]
                        pw = Xs[d_][idx % 2]
                        yc, yp = Y[d_][idx % 2], Y[d_][(idx + 1) % 2]
                        if idx == 0:
                            DVE(lambda e: e.tensor_copy(out=yc, in_=pw), [("Xs", d_, idx % 2)], [("Y", d_, idx % 2)])
                        else:
                            DVE(lambda e: e.scalar_tensor_tensor(out=yc, in0=yp, scalar=efac[d_][:, 0, c:c + 1], in1=pw, op0=ALU.mult, op1=ALU.add),
                                [("Y", d_, (idx + 1) % 2), ("Xs", d_, idx % 2), ("efac", d_)], [("Y", d_, idx % 2)])
                    for d_ in range(2):
                        c = order[d_][idx]
                        yc = Y[d_][idx % 2]
                        nxt = c + 1 if d_ == 0 else c - 1
                        ACT(lambda e: e.activation(out=Zbf[d_][:, nxt, :], in_=yc, func=AF.Copy, scale=efac[d_][:, 2, nxt:nxt + 1]),
                            [("Y", d_, idx % 2), ("efac", d_)], [("Zbf", d_, nxt)])
            for kk in range(4):
                DVE(lambda e: e.memset(bank(kk), 0.0), [], [bk(kk)])
            for s_ in range(2):
                POOL(lambda e: e.memset(atm[s_], 0.0), [], [("T", s_, 2)])

            def at_mm(g):
                ka = (g % 2) * 2
                for d_ in range(2):
                    for j in range(4):
                        c = g * 4 + j
                        q_, k_ = QK[(d_, "q")], QK[(d_, "k")]
                        o_ = j * 128
                        rk = [("QK", d_, "k", g), ("QK", d_, "q", g)]
                        if d_ == 0:
                            mm(bank(ka)[0:64, o_:o_ + 128], k_[:, c * 128:c * 128 + 64], q_[:, c * 128:(c + 1) * 128], True, True, rk, [bk(ka)])
                            mm(bank(ka)[64:128, o_ + 64:o_ + 128], k_[:, c * 128 + 64:(c + 1) * 128], q_[:, c * 128 + 64:(c + 1) * 128],
                               True, True, rk, [bk(ka)])
                        else:
                            mm(bank(ka + 1)[0:64, o_:o_ + 64], k_[:, c * 128:c * 128 + 64], q_[:, c * 128:c * 128 + 64], True, True, rk, [bk(ka + 1)])
                            mm(bank(ka + 1)[64:128, o_:o_ + 128], k_[:, c * 128 + 64:(c + 1) * 128], q_[:, c * 128:(c + 1) * 128],
                               True, True, rk, [bk(ka + 1)])

            def mask_copy(g):
                ka = (g % 2) * 2
                sa = g % 2
                for d_ in range(2):
                    mk = maskfb[:, d_ * 128:(d_ + 1) * 128].unsqueeze(1).to_broadcast([128, 4, 128])
                    DVE(lambda e: e.copy_predicated(out=atm[sa][:, d_, :].rearrange("p (c j) -> p c j", j=128), mask=mk,
                                                    data=bank(ka + d_).rearrange("p (c j) -> p c j", j=128)),
                        [bk(ka + d_), "maskfb", ("T", sa, 2)], [("T", sa, 2)])

            def o_mm(g):
                ko = 4 + g % 2
                sa = g % 2
                for j in range(4):
                    c = g * 4 + j
                    cs = slice(c * 128, (c + 1) * 128)
                    grp = []
                    if c > 0:
                        grp.append((QK[(0, "q")][:, cs], Zbf[0][:, c, :], [("QK", 0, "q", g), ("Zbf", 0, c)]))
                    if c < NT - 1:
                        grp.append((QK[(1, "q")][:, cs], Zbf[1][:, c, :], [("QK", 1, "q", g), ("Zbf", 1, c)]))
                    vv = V_all[:, c, h * 128:(h + 1) * 128]
                    grp.append((atm[sa][:, 0, j * 128:(j + 1) * 128], vv, [("T", sa, 2), ("V", c)]))
                    grp.append((atm[sa][:, 1, j * 128:(j + 1) * 128], vv, [("T", sa, 2), ("V", c)]))
                    for gi_, (l_, r_, rk) in enumerate(grp):
                        mm(bank(ko)[:, j * 128:(j + 1) * 128], l_, r_, gi_ == 0, gi_ == len(grp) - 1, rk, [bk(ko)])

            def epi_a(g):
                ko = 4 + g % 2
                sa = g % 2
                c0 = (g % 4) * 12
                ACT(lambda e: e.activation(out=sqo[sa], in_=bank(ko), func=AF.Square), [bk(ko)], [("T", sa, 0)])
                DVE(lambda e: e.tensor_reduce(out=st2[:, c0:c0 + 4], in_=sqo[sa].rearrange("p (c j) -> p c j", j=128), axis=AX.X, op=ALU.add),
                    [("T", sa, 0)], [("st2", c0)])
                DVE(lambda e: e.tensor_scalar(out=st2[:, c0 + 4:c0 + 8], in0=st2[:, c0:c0 + 4], scalar1=1.0 / 128, scalar2=EPS,
                                              op0=ALU.mult, op1=ALU.add), [("st2", c0)], [("st2", c0 + 4)])
                POOL(lambda e: e.tensor_tensor(out=st2[:, c0 + 8:c0 + 12], in0=st2[:, c0 + 4:c0 + 8], in1=nhalf.to_broadcast([128, 4]),
                                               op=ALU.pow), [("st2", c0 + 4), "nhalf"], [("st2", c0 + 8)])
                DVE(lambda e: e.tensor_tensor(out=onb[sa].rearrange("p (c j) -> p c j", j=128), in0=bank(ko).rearrange("p (c j) -> p c j", j=128),
                                              in1=st2[:, c0 + 8:c0 + 12].unsqueeze(2).to_broadcast([128, 4, 128]), op=ALU.mult),
                    [bk(ko), ("st2", c0 + 8)], [("T", sa, 1)])

            def epi_b(g):
                kt = 6 + g % 2
                sa = g % 2
                for j in range(4):
                    tp(bankbf(kt)[:, j * 128:(j + 1) * 128], onb[sa][:, j * 128:(j + 1) * 128], [("T", sa, 1)], [bk(kt)])
                gs = slice(g * 512, (g + 1) * 512)
                DVE(lambda e: e.scalar_tensor_tensor(out=oT[:, h, gs], in0=bankbf(kt)[:, 0:512], scalar=cols[:, C_GHG + h:C_GHG + h + 1],
                                                     in1=sgT[:, gs], op0=ALU.mult, op1=ALU.mult),
                    [bk(kt), "const", ("sgT", g)], [("oT", h, g)])

            at_mm(0); mask_copy(0); at_mm(1); o_mm(0); mask_copy(1); at_mm(2); epi_a(0); o_mm(1); mask_copy(2); at_mm(3)
            epi_b(0); epi_a(1); o_mm(2); mask_copy(3); epi_b(1); epi_a(2); o_mm(3); epi_b(2); epi_a(3); epi_b(3)
        tap("oTa", oT[:, 0:4, :], [("oT", h, g) for h in range(4) for g in range(4)])
        S.barrier()
        R = Bump(arena, RMARK, ARENA_BYTES)
        KT = R.take([6, SEQ], BF16)
        Vaug = R.take([NT, 4, 130], BF16)
        SMARK = R.cur
        Wq = R.take([3, 768], BF16)
        Wkv = R.take([2, 1024], BF16)
        posi = R.take([NT], I32)
        posf = R.take([NT], F32)
        cosT = R.take([NT, 32], F32)
        sinT = R.take([NT, 32], F32)
        TMARK = R.cur
        ang = R.take([NT, 32], F32)
        kqi = R.take([NT, 32], I32)
        t_a = R.take([NT, 32], F32)
        t_b = R.take([NT, 32], F32)
        for m in range(3):
            S.dma("pool", lambda e, m=m: e.dma_start(out=Wq[:, m, :], in_=wq[m * 128:(m + 1) * 128, :]), "wq", writes=["Wq"])
        for m in range(2):
            S.dma("pool", lambda e, m=m: e.dma_start(out=Wkv[:, m, :], in_=wkv[m * 128:(m + 1) * 128, :]), "wkv", writes=["Wkv"])
        POOL(lambda e: e.memset(Vaug[:, :, :, 128:130], 1.0), [], ["Vones"])
        S.dma("sp", lambda e: e.dma_start(out=posi, in_=pos[b].rearrange("(n p) -> p n", p=128), allow_slow_non_contiguous=True),
              "posi", writes=["posi"])
        DVE(lambda e: e.tensor_copy(out=posf, in_=posi), ["posi"], ["posf"])
        DVE(lambda e: e.tensor_tensor(out=ang, in0=posf.unsqueeze(2).to_broadcast([128, NT, 32]),
                                      in1=invf_b.unsqueeze(1).to_broadcast([128, NT, 32]), op=ALU.mult), ["posf", "const"], ["ang"])
        DVE(lambda e: e.tensor_scalar(out=kqi, in0=ang, scalar1=float(1.0 / (2 * PI)), scalar2=None, op0=ALU.mult), ["ang"], ["kqi"])
        DVE(lambda e: e.tensor_copy(out=t_a, in_=kqi), ["kqi"], ["t_a"])
        C1, C2, C3 = CW1, CW2, CW3
        DVE(lambda e: e.scalar_tensor_tensor(out=t_b, in0=t_a, scalar=-C1, in1=ang, op0=ALU.mult, op1=ALU.add), ["t_a", "ang"], ["t_b"])
        DVE(lambda e: e.scalar_tensor_tensor(out=ang, in0=t_a, scalar=-C2, in1=t_b, op0=ALU.mult, op1=ALU.add), ["t_a", "t_b", "ang"], ["ang"])
        DVE(lambda e: e.scalar_tensor_tensor(out=t_b, in0=t_a, scalar=-C3, in1=ang, op0=ALU.mult, op1=ALU.add), ["t_a", "ang", "t_b"], ["t_b"])
        DVE(lambda e: e.tensor_scalar(out=ang, in0=t_b, scalar1=PI, scalar2=-PI, op0=ALU.min, op1=ALU.max), ["t_b", "ang"], ["ang"])
        ACT(lambda e: e.activation(out=sinT, in_=ang, func=AF.Sin), ["ang"], ["sinT"])
        DVE(lambda e: e.tensor_scalar(out=t_a, in0=t_b, scalar1=PI / 2, scalar2=None, op0=ALU.add), ["t_b", "t_a"], ["t_a"])
        DVE(lambda e: e.tensor_scalar(out=t_b, in0=t_a, scalar1=PI, scalar2=2 * PI, op0=ALU.is_gt, op1=ALU.mult), ["t_a", "t_b"], ["t_b"])
        DVE(lambda e: e.tensor_tensor(out=t_a, in0=t_a, in1=t_b, op=ALU.subtract), ["t_a", "t_b"], ["t_a"])
        DVE(lambda e: e.tensor_scalar(out=t_a, in0=t_a, scalar1=PI, scalar2=-PI, op0=ALU.min, op1=ALU.max), ["t_a"], ["t_a"])
        ACT(lambda e: e.activation(out=cosT, in_=t_a, func=AF.Sin), ["t_a"], ["cosT"])
        if b == 0:
            tap("cosT", cosT, ["cosT"])
            tap("sinT", sinT, ["sinT"])
        S.barrier()
        R = Bump(arena, TMARK, ARENA_BYTES)
        cT = R.take([5, 512], BF16)
        sq = R.take([5, 512], BF16)
        sqq = R.take([768], BF16)
        t1 = R.take([768], F32)
        qf = R.take([1024], BF16)
        kf = R.take([768], BF16)
        krs = R.take([64], F32)
        krr = R.take([64], F32)
        ta = R.take([256], F32)
        tb = R.take([256], F32)
        tak_ = R.take([64], F32)
        tbk_ = R.take([64], F32)
        sqk = R.take([512], F32)
        st3 = R.take([64], F32)
        junk = R.take([64], BF16)
        POOL(lambda e: e.memset(qf[:, 512:1024], 0.0), [], ["qf"])
        pending_tr = []
        for blk in range(4):
            tl = slice(blk * 512, (blk + 1) * 512)
            hk = [("hT", 4 * blk + j) for j in range(4)] + ["W_in"]
            for m in range(5):
                c0 = 2560 + m * 128
                kc_ = (m % 2) * 2
                for kc in range(8):
                    mm(bank(kc_), W_in[:, kc, c0:c0 + 128], hT[:, kc, tl], kc == 0, kc == 7, hk, [bk(kc_)])
                gcol = cols[:, C_GCQ + m:C_GCQ + m + 1]
                ACT(lambda e, m=m, gcol=gcol: e.activation(out=cT[:, m, :], in_=bank(kc_), func=AF.Copy, scale=gcol),
                    [bk(kc_), "const"], [("cT", m)])
                ACT(lambda e, m=m: e.activation(out=sq[:, m, :], in_=bank(kc_), func=AF.Square), [bk(kc_)], [("sq", m)])
            for j in range(4):
                i = blk * 4 + j
                js = slice(j * 128, (j + 1) * 128)
                its = slice(i * 128, (i + 1) * 128)
                c0 = (i % 2) * 32
                for m in range(3):
                    mm(bank(1)[:, 0:1], sq[:, m, js], ones_b[:, 0:1], m == 0, m == 2, [("sq", m), "ones_b"], [bk(1)])
                for m in range(3, 5):
                    mm(bank(1)[:, 2:3], sq[:, m, js], ones_b[:, 0:1], m == 3, m == 4, [("sq", m), "ones_b"], [bk(1)])
                for kc in range(8):
                    mm(bank(1)[:, 64:128], hT[:, kc, its], W_in[:, kc, 3200:3264], kc == 0, kc == 7, [("hT", i), "W_in"], [bk(1)])
                sc = lambda o: st3[:, c0 + o:c0 + o + 1]
                skey = lambda o: ("st3", c0 + o)
                DVE(lambda e, sc=sc: e.tensor_scalar(out=sc(0), in0=bank(1)[:, 0:1], scalar1=1.0 / 384, scalar2=EPS, op0=ALU.mult, op1=ALU.add),
                    [bk(1)], [skey(0)])
                DVE(lambda e, sc=sc: e.tensor_scalar(out=sc(1), in0=bank(1)[:, 2:3], scalar1=1.0 / 256, scalar2=EPS, op0=ALU.mult, op1=ALU.add),
                    [bk(1)], [skey(1)])
                POOL(lambda e, sc=sc: e.tensor_tensor(out=sc(2), in0=sc(0), in1=nhalf, op=ALU.pow), [skey(0), "nhalf"], [skey(2)])
                POOL(lambda e, sc=sc: e.tensor_tensor(out=sc(3), in0=sc(1), in1=nhalf, op=ALU.pow), [skey(1), "nhalf"], [skey(3)])
                for m in range(3):
                    mm(bank(3), cT[:, m, js], Wq[:, m, 0:512], m == 0, m == 2, [("cT", m), "Wq"], [bk(3)])
                for m in range(3):
                    mm(bank(4)[:, 0:256], cT[:, m, js], Wq[:, m, 512:768], m == 0, m == 2, [("cT", m), "Wq"], [bk(4)])
                for m in range(2):
                    mm(bank(5), cT[:, 3 + m, js], Wkv[:, m, 0:512], m == 0, m == 1, [("cT", 3 + m), "Wkv"], [bk(5)])
                for m in range(2):
                    mm(bank(6), cT[:, 3 + m, js], Wkv[:, m, 512:1024], m == 0, m == 1, [("cT", 3 + m), "Wkv"], [bk(6)])
                prev_tr = pending_tr
                pending_tr = []
                for pe_part, _ in prev_tr:
                    pe_part()
                DVE(lambda e: e.tensor_copy(out=krs, in_=bank(1)[:, 64:128]), [bk(1)], ["krs"])
                ACT(lambda e: e.activation(out=sqq[:, 0:512], in_=bank(3), func=AF.Square), [bk(3)], ["sqq"])
                ACT(lambda e: e.activation(out=sqq[:, 512:768], in_=bank(4)[:, 0:256], func=AF.Square), [bk(4), "sqq"], ["sqq"])
                ACT(lambda e: e.activation(out=sqk, in_=bank(5), func=AF.Square), [bk(5)], ["sqk"])
                ACT(lambda e, sc=sc: e.activation(out=junk[:, 0:64], in_=krs, func=AF.Square, accum_out=sc(13)), ["krs"], ["junk", skey(13)])
                for _, act_part in prev_tr:
                    act_part()
                DVE(lambda e, sc=sc: e.tensor_reduce(out=st3[:, c0 + 4:c0 + 8], in_=sqq[:, 0:512].rearrange("p (h j) -> p h j", j=128),
                                                     axis=AX.X, op=ALU.add), ["sqq"], [skey(4)])
                DVE(lambda e, sc=sc: e.tensor_reduce(out=st3[:, c0 + 8:c0 + 12], in_=sqq[:, 512:768].rearrange("p (h j) -> p h j", j=64),
                                                     axis=AX.X, op=ALU.add), ["sqq"], [skey(8)])
                DVE(lambda e: e.tensor_reduce(out=st3[:, c0 + 16:c0 + 20], in_=sqk.rearrange("p (h j) -> p h j", j=128),
                                              axis=AX.X, op=ALU.add), ["sqk"], [skey(16)])
                DVE(lambda e: e.tensor_tensor(out=st3[:, c0 + 4:c0 + 8], in0=st3[:, c0 + 4:c0 + 8], in1=st3[:, c0 + 8:c0 + 12], op=ALU.add),
                    [skey(4), skey(8)], [skey(4)])
                DVE(lambda e, sc=sc: e.tensor_tensor(out=sc(12), in0=sc(2), in1=sc(2), op=ALU.mult), [skey(2)], [skey(12)])
                DVE(lambda e, sc=sc: e.tensor_scalar(out=st3[:, c0 + 4:c0 + 8], in0=st3[:, c0 + 4:c0 + 8], scalar1=sc(12), scalar2=1.0 / 192,
                                                     op0=ALU.mult, op1=ALU.mult), [skey(4), skey(12)], [skey(4)])
                DVE(lambda e: e.tensor_scalar(out=st3[:, c0 + 4:c0 + 8], in0=st3[:, c0 + 4:c0 + 8], scalar1=EPS, scalar2=None, op0=ALU.add),
                    [skey(4)], [skey(4)])
                DVE(lambda e, sc=sc: e.tensor_tensor(out=sc(14), in0=sc(3), in1=sc(3), op=ALU.mult), [skey(3)], [skey(14)])
                DVE(lambda e, sc=sc: e.tensor_scalar(out=st3[:, c0 + 16:c0 + 20], in0=st3[:, c0 + 16:c0 + 20], scalar1=sc(14), scalar2=sc(13),
                                                     op0=ALU.mult, op1=ALU.add), [skey(16), skey(14), skey(13)], [skey(16)])
                DVE(lambda e: e.tensor_scalar(out=st3[:, c0 + 16:c0 + 20], in0=st3[:, c0 + 16:c0 + 20], scalar1=1.0 / 192, scalar2=EPS,
                                              op0=ALU.mult, op1=ALU.add), [skey(16)], [skey(16)])
                POOL(lambda e: e.tensor_tensor(out=st3[:, c0 + 8:c0 + 12], in0=st3[:, c0 + 4:c0 + 8],
                                               in1=nhalf.to_broadcast([128, 4]), op=ALU.pow), [skey(4), "nhalf", skey(8)], [skey(8)])
                POOL(lambda e: e.tensor_tensor(out=st3[:, c0 + 20:c0 + 24], in0=st3[:, c0 + 16:c0 + 20],
                                               in1=nhalf.to_broadcast([128, 4]), op=ALU.pow), [skey(16), "nhalf"], [skey(20)])
                DVE(lambda e, sc=sc: e.tensor_scalar(out=st3[:, c0 + 8:c0 + 12], in0=st3[:, c0 + 8:c0 + 12], scalar1=sc(2), scalar2=None,
                                                     op0=ALU.mult), [skey(8), skey(2)], [skey(8)])
                DVE(lambda e, sc=sc: e.tensor_scalar(out=st3[:, c0 + 24:c0 + 28], in0=st3[:, c0 + 20:c0 + 24], scalar1=sc(3), scalar2=None,
                                                     op0=ALU.mult), [skey(20), skey(3)], [skey(24)])
                fq = st3[:, c0 + 8:c0 + 12]
                rk_ = st3[:, c0 + 20:c0 + 24]
                fkn = st3[:, c0 + 24:c0 + 28]
                DVE(lambda e, fq=fq: e.tensor_tensor(out=t1[:, 0:512].rearrange("p (h j) -> p h j", j=128),
                                                     in0=bank(3).rearrange("p (h j) -> p h j", j=128),
                                                     in1=fq.unsqueeze(2).to_broadcast([128, 4, 128]), op=ALU.mult), [bk(3), skey(8)], ["t1"])
                DVE(lambda e, fq=fq: e.tensor_tensor(out=t1[:, 512:768].rearrange("p (h j) -> p h j", j=64),
                                                     in0=bank(4)[:, 0:256].rearrange("p (h j) -> p h j", j=64),
                                                     in1=fq.unsqueeze(2).to_broadcast([128, 4, 64]), op=ALU.mult), [bk(4), skey(8), "t1"], ["t1"])
                DVE(lambda e, fkn=fkn: e.tensor_tensor(out=sqk.rearrange("p (h j) -> p h j", j=128),
                                                       in0=bank(5).rearrange("p (h j) -> p h j", j=128),
                                                       in1=fkn.unsqueeze(2).to_broadcast([128, 4, 128]), op=ALU.mult),
                    [bk(5), skey(24), "sqk"], ["sqk"])
                ACT(lambda e, sc=sc, i=i: e.activation(out=Vaug[:, i, :, 0:128], in_=bank(6).rearrange("p (h j) -> p h j", j=128),
                                                       func=AF.Copy, scale=sc(3)), [bk(6), skey(3)], [("Vaug", i)])
                DVE(lambda e: e.tensor_tensor(out=qf[:, 0:512], in0=t1[:, 0:512], in1=gq_b[:, 0:512], op=ALU.mult), ["t1", "const"], ["qf"])
                DVE(lambda e: e.tensor_tensor(out=t1[:, 512:768], in0=t1[:, 512:768], in1=gq_b[:, 512:768], op=ALU.mult), ["t1", "const"], ["t1"])

                def rope(E, src, nh_, dst, rk, wk, ta, tb, tak, tbk):
                    s4 = src.rearrange("p (h a r) -> p h a r", a=2, r=32)
                    a4 = ta[:, 0:nh_ * 64].rearrange("p (h a r) -> p h a r", a=2, r=32)
                    b4 = tb[:, 0:nh_ * 64].rearrange("p (h a r) -> p h a r", a=2, r=32)
                    cb = cosT[:, i, :].unsqueeze(1).unsqueeze(1).to_broadcast([128, nh_, 2, 32])
                    sb_ = sinT[:, i, :].unsqueeze(1).to_broadcast([128, nh_, 32])
                    E(lambda e: e.tensor_tensor(out=a4, in0=s4, in1=cb, op=ALU.mult), rk + ["cosT"], [tak])
                    if isinstance(dst, list):
                        E(lambda e: e.scalar_tensor_tensor(out=b4[:, :, 0, :], in0=s4[:, :, 1, :], scalar=-1.0, in1=sb_, op0=ALU.mult, op1=ALU.mult),
                          rk + ["sinT"], [tbk])
                        E(lambda e: e.tensor_tensor(out=b4[:, :, 1, :], in0=s4[:, :, 0, :], in1=sb_, op=ALU.mult), rk + ["sinT", tbk], [tbk])
                        a3 = ta[:, 0:nh_ * 64].rearrange("p (h j) -> p h j", j=64)
                        b3 = tb[:, 0:nh_ * 64].rearrange("p (h j) -> p h j", j=64)
                        for par, dv in enumerate(dst):
                            E(lambda e, par=par, dv=dv: e.tensor_tensor(out=dv, in0=a3[:, par::2, :], in1=b3[:, par::2, :], op=ALU.add),
                              [tak, tbk] + wk, wk)
                    else:
                        E(lambda e: e.tensor_tensor(out=b4[:, :, 0, :], in0=s4[:, :, 1, :], in1=sb_, op=ALU.mult), rk + ["sinT"], [tbk])
                        E(lambda e: e.tensor_tensor(out=b4[:, :, 1, :], in0=s4[:, :, 0, :], in1=sb_, op=ALU.mult), rk + ["sinT", tbk], [tbk])
                        E(lambda e: e.tensor_tensor(out=dst[:, 0:32], in0=ta[:, 0:32], in1=tb[:, 0:32], op=ALU.subtract), [tak, tbk] + wk, wk)
                        E(lambda e: e.tensor_tensor(out=dst[:, 32:64], in0=ta[:, 32:64], in1=tb[:, 32:64], op=ALU.add), [tak, tbk] + wk, wk)

                qz = qf[:, 512:1024].rearrange("p (i r) -> p i r", r=256)
                rope(DVE, t1[:, 512:768], 4, [qz[:, :, 0:64], qz[:, :, 192:256]], ["t1"], ["qf"], ta, tb, "ta", "tb")
                POOL(lambda e: e.tensor_tensor(out=kf[:, 0:512], in0=sqk, in1=gk_b[:, 0:512], op=ALU.mult), ["sqk", "const"], ["kf"])
                POOL(lambda e: e.tensor_tensor(out=krs, in0=krs, in1=gk_b[:, 512:576], op=ALU.mult), ["krs", "const"], ["krs"])
                rope(POOL, krs, 1, krr, ["krs"], ["krr"], tak_, tbk_, "tak", "tbk")
                POOL(lambda e, rk_=rk_: e.tensor_tensor(out=kf[:, 512:768].rearrange("p (h j) -> p h j", j=64),
                                                        in0=krr.unsqueeze(1).to_broadcast([128, 4, 64]),
                                                        in1=rk_.unsqueeze(2).to_broadcast([128, 4, 64]), op=ALU.mult),
                     ["krr", skey(20), "kf"], ["kf"])
                def mk_tr(i=i, its=its):
                    def pe_part():
                        for m in range(8):
                            tp(bankbf(7)[:, m * 128:(m + 1) * 128], qf[:, m * 128:(m + 1) * 128], ["qf"], [bk(7)])
                        for m in range(6):
                            tp(bankbf(0)[:, m * 128:(m + 1) * 128], kf[:, m * 128:(m + 1) * 128], ["kf"], [bk(0)])

                    def act_part():
                        ACT(lambda e: e.activation(out=hT[:, :, its], in_=bankbf(7).rearrange("p (m t) -> p m t", t=128), func=AF.Copy),
                            [bk(7)], [("hT", i)])
                        ACT(lambda e: e.activation(out=KT[:, :, its], in_=bankbf(0)[:, 0:768].rearrange("p (m t) -> p m t", t=128), func=AF.Copy),
                            [bk(0)], [("KT", i)])
                    return pe_part, act_part
                pending_tr.append(mk_tr())
        for pe_part, act_part in pending_tr:
            pe_part()
            act_part()
        pending_tr = []
        if b == 0:
            tap("QT", hT[:, 0:6, :], [("hT", i) for i in range(NT)])
            tap("KT", KT, [("KT", i) for i in range(NT)])
            tap("Vaug", Vaug, [("Vaug", i) for i in range(NT)] + ["Vones"])
        S.barrier()
        R = Bump(arena, SMARK, ARENA_BYTES)
        PT = [R.take([512], BF16) for _ in range(3)]
        obuf = R.take([4, 512], F32)
        obn = [R.take([512], BF16) for _ in range(2)]
        st4 = R.take([64], F32)
        junk = R.take([512], BF16)
        QT = hT
        it = 0
        npt = 0
        for qb in range(4):
            qs = slice(qb * 512, (qb + 1) * 512)
            qkeys = [("hT", 4 * qb + j) for j in range(4)]
            for h in range(4):
                ko = 2 + (it % 2) * 2
                it += 1
                DVE(lambda e, ko=ko: e.memset(pp[ko // 2][:, :], 0.0), [], [bk(ko), bk(ko + 1)])
                rp = slice((h % 2) * 64, (h % 2) * 64 + 64)
                rc = 4 + h // 2
                def qk(kc):
                    ksl = slice(kc * 128, (kc + 1) * 128)
                    ks_ = kc % 2
                    mm(bank(ks_), KT[:, h, ksl], QT[:, h, qs], True, False, [("KT", kc)] + qkeys, [bk(ks_)])
                    mm(bank(ks_), KT[:, rc, ksl], QT[:, 4 + h, qs], False, True, [("KT", kc)] + qkeys, [bk(ks_)])

                qk(0)
                for kc in range(NT):
                    ks_ = kc % 2
                    ps_ = npt % 3
                    npt += 1
                    ACT(lambda e, ks_=ks_, ps_=ps_: e.activation(out=PT[ps_], in_=bank(ks_), func=AF.Exp), [bk(ks_)], [("PT", ps_)])
                    if kc + 1 < NT:
                        qk(kc + 1)
                    for j in range(4):
                        ob = ko + j // 2
                        mm(bank(ob)[:, (j % 2) * 256:(j % 2) * 256 + 129], PT[ps_][:, j * 128:(j + 1) * 128], Vaug[:, kc, h, 0:129],
                           False, False, [("PT", ps_), ("Vaug", kc), "Vones"], [bk(ob)], skip=True)
                for j in range(4):
                    ob = ko + j // 2
                    o0 = (j % 2) * 256
                    c0 = ((it * 4 + j) % 16) * 2
                    DVE(lambda e, ob=ob, o0=o0, c0=c0: e.reciprocal(out=st4[:, c0:c0 + 1], in_=bank(ob)[:, o0 + 128:o0 + 129]),
                        [bk(ob)], [("st4", c0)])
                    DVE(lambda e, ob=ob, o0=o0, c0=c0, j=j, h=h: e.tensor_scalar(out=obuf[:, j, h * 128:(h + 1) * 128],
                                                                                 in0=bank(ob)[:, o0:o0 + 128], scalar1=st4[:, c0:c0 + 1],
                                                                                 scalar2=None, op0=ALU.mult),
                        [bk(ob), ("st4", c0)], [("obuf", j)])
            for j in range(4):
                i = qb * 4 + j
                c0 = 32 + (i % 8) * 4
                sj = i % 2
                ACT(lambda e, j=j, c0=c0: e.activation(out=junk[:, 0:512], in_=obuf[:, j, :], func=AF.Square, accum_out=st4[:, c0:c0 + 1]),
                    [("obuf", j)], ["junk", ("st4", c0)])
                DVE(lambda e, c0=c0: e.tensor_scalar(out=st4[:, c0 + 1:c0 + 2], in0=st4[:, c0:c0 + 1], scalar1=1.0 / 512, scalar2=EPS,
                                                     op0=ALU.mult, op1=ALU.add), [("st4", c0)], [("st4", c0 + 1)])
                POOL(lambda e, c0=c0: e.tensor_tensor(out=st4[:, c0 + 2:c0 + 3], in0=st4[:, c0 + 1:c0 + 2], in1=nhalf, op=ALU.pow),
                     [("st4", c0 + 1), "nhalf"], [("st4", c0 + 2)])
                DVE(lambda e, j=j, c0=c0, sj=sj: e.scalar_tensor_tensor(out=obn[sj], in0=obuf[:, j, :], scalar=st4[:, c0 + 2:c0 + 3],
                                                                        in1=gmo_b, op0=ALU.mult, op1=ALU.mult),
                    [("obuf", j), ("st4", c0 + 2), "const"], [("obn", sj)])
                kt = 6 + i % 2
                for m in range(4):
                    tp(bankbf(kt)[:, m * 128:(m + 1) * 128], obn[sj][:, m * 128:(m + 1) * 128], [("obn", sj)], [bk(kt)])
                ACT(lambda e, kt=kt, i=i: e.activation(out=oT[:, 4:8, i * 128:(i + 1) * 128],
                                                       in_=bankbf(kt)[:, 0:512].rearrange("p (m t) -> p m t", t=128), func=AF.Copy),
                    [bk(kt)], [("oT", 4, i)])
        tap("oT", oT, [("oT", 4, i) for i in range(NT)])
        S.barrier()
        R = Bump(arena, RMARK, ARENA_BYTES)
        W_o = R.take([8, D], BF16)
        xin = [R.take([D], F32) for _ in range(2)]
        x1o = [R.take([D], F32) for _ in range(2)]
        for kc in range(8):
            S.dma("pool", lambda e, kc=kc: e.dma_start(out=W_o[:, kc, :], in_=w_out[kc * 128:(kc + 1) * 128, :]), "w_o", writes=["W_o"])
        for i in range(NT):
            sl = i % 2
            its = slice(i * 128, (i + 1) * 128)
            S.dma("sp", lambda e, sl=sl, i=i: e.dma_start(out=xin[sl], in_=x[b, i * 128:(i + 1) * 128, :]), f"xin{sl}",
                  writes=[("xin", sl)])
            kp = (i % 2) * 2
            for half in range(2):
                for m in range(8):
                    mm(bank(kp + half), oT[:, m, its], W_o[:, m, half * 512:(half + 1) * 512], m == 0, m == 7, ["W_o"], [bk(kp + half)])
            DVE(lambda e, sl=sl, kp=kp: e.tensor_tensor(out=x1o[sl], in0=pp[kp // 2][:, :], in1=xin[sl], op=ALU.add),
                [bk(kp), bk(kp + 1), ("xin", sl), ("x1o", sl)], [("x1o", sl)])
            S.dma("sp", lambda e, sl=sl, i=i: e.dma_start(out=x1s[b, i * 128:(i + 1) * 128, :], in_=x1o[sl]), f"x1o{sl}",
                  reads=[("x1o", sl)])
        S.barrier()

    Bm = Bump(arena, MARK0, ARENA_BYTES)
    W_up = Bm.take([8, DFF], BF16)
    W_dn = Bm.take([32, D], BF16)
    gffn_b = Bm.take([D], F32)
    TB = 256
    NJ = TB // 128
    x1t = [[Bm.take([D], F32) for _ in range(NJ)] for _ in range(2)]
    hbf = [Bm.take([D], BF16) for _ in range(2)]
    h2T = Bm.take([8, TB], BF16)
    aT = Bm.take([32, TB], BF16)
    rl = [Bm.take([512], F32) for _ in range(2)]
    yo = [Bm.take([D], F32) for _ in range(2)]
    junk = Bm.take([D], BF16)
    st5 = Bm.take([64], F32)
    S.dma("sp", lambda e: e.dma_start(out=gffn_b, in_=g_ffn.partition_broadcast(128)), "const2", writes=["gffn"])
    for kc in range(8):
        for q4 in range(4):
            S.dma("pool", lambda e, kc=kc, q4=q4: e.dma_start(out=W_up[:, kc, q4 * 1024:(q4 + 1) * 1024],
                                                             in_=w_up[kc * 128:(kc + 1) * 128, q4 * 1024:(q4 + 1) * 1024]),
                  "w_up", writes=["W_up"])
    for c in range(32):
        S.dma("pool", lambda e, c=c: e.dma_start(out=W_dn[:, c, :], in_=w_down[c * 128:(c + 1) * 128, :]), "w_dn", writes=["W_dn"])
    x1f = x1s.rearrange("b s d -> (b s) d")
    yf = y.rearrange("b s d -> (b s) d")
    nblk = nseq * SEQ // TB
    nyo = 0
    for blk in range(nblk):
        xs = blk % 2
        for j in range(NJ):
            t = blk * NJ + j
            S.dma("sp", lambda e, xs=xs, j=j, t=t: e.dma_start(out=x1t[xs][j], in_=x1f[t * 128:(t + 1) * 128, :]), f"x1t{xs}{j}",
                  writes=[("x1t", xs, j)])
            c0 = (t % 8) * 4
            sl = t % 2
            ACT(lambda e, xs=xs, j=j, c0=c0: e.activation(out=junk, in_=x1t[xs][j], func=AF.Square, accum_out=st5[:, c0:c0 + 1]),
                [("x1t", xs, j)], ["junk", ("st5", c0)])
            DVE(lambda e, c0=c0: e.tensor_scalar(out=st5[:, c0 + 1:c0 + 2], in0=st5[:, c0:c0 + 1], scalar1=1.0 / D, scalar2=EPS,
                                                 op0=ALU.mult, op1=ALU.add), [("st5", c0)], [("st5", c0 + 1)])
            POOL(lambda e, c0=c0: e.tensor_tensor(out=st5[:, c0 + 2:c0 + 3], in0=st5[:, c0 + 1:c0 + 2], in1=nhalf, op=ALU.pow),
                 [("st5", c0 + 1), "nhalf"], [("st5", c0 + 2)])
            DVE(lambda e, xs=xs, j=j, c0=c0, sl=sl: e.scalar_tensor_tensor(out=hbf[sl], in0=x1t[xs][j], scalar=st5[:, c0 + 2:c0 + 3],
                                                                           in1=gffn_b, op0=ALU.mult, op1=ALU.mult),
                [("x1t", xs, j), ("st5", c0 + 2), "gffn"], [("hbf", sl)])
            for kc in range(8):
                tp(bankbf(0)[:, kc * 128:(kc + 1) * 128], hbf[sl][:, kc * 128:(kc + 1) * 128], [("hbf", sl)], [bk(0)])
            ACT(lambda e, j=j: e.activation(out=h2T[:, :, j * 128:(j + 1) * 128], in_=bankbf(0).rearrange("p (k t) -> p k t", t=128),
                                            func=AF.Copy), [bk(0)], [("h2T", j)])
        hk2 = [("h2T", j) for j in range(NJ)] + ["W_up"]
        for cp in range(16):
            ku = 1 + cp % 2
            for c in range(2):
                cc = cp * 2 + c
                for kc in range(8):
                    mm(bank(ku)[:, c * TB:(c + 1) * TB], W_up[:, kc, cc * 128:(cc + 1) * 128], h2T[:, kc, :], kc == 0, kc == 7, hk2, [bk(ku)])
            rs = cp % 2
            ACT(lambda e, ku=ku, rs=rs: e.activation(out=rl[rs], in_=bank(ku), func=AF.Relu), [bk(ku)], [("rl", rs)])
            POOL(lambda e, rs=rs, cp=cp: e.tensor_tensor(out=aT[:, 2 * cp:2 * cp + 2, :], in0=rl[rs].rearrange("p (c t) -> p c t", t=TB),
                                                         in1=rl[rs].rearrange("p (c t) -> p c t", t=TB), op=ALU.mult),
                 [("rl", rs)], [("aT", cp)])
        for j in range(NJ):
            t = blk * NJ + j
            kd = 4 + (t % 2) * 2
            for half in range(2):
                for c in range(32):
                    mm(bank(kd + half), aT[:, c, j * 128:(j + 1) * 128], W_dn[:, c, half * 512:(half + 1) * 512], c == 0, c == 31,
                       [("aT", c // 2), "W_dn"], [bk(kd + half)])
            ys = nyo % 2
            nyo += 1
            DVE(lambda e, ys=ys, kd=kd, xs=xs, j=j: e.tensor_tensor(out=yo[ys], in0=pp[kd // 2][:, :], in1=x1t[xs][j], op=ALU.add),
                [bk(kd), bk(kd + 1), ("x1t", xs, j), ("yo", ys)], [("yo", ys)])
            S.dma("sp", lambda e, ys=ys, t=t: e.dma_start(out=yf[t * 128:(t + 1) * 128, :], in_=yo[ys]), f"yo{ys}", reads=[("yo", ys)])
    S.barrier()

    sems = {n: es.enter_context(nc.semaphore(n)) for n in sorted(S.sem_names)}
    with nc.Block() as block:
        S.emit(block, sems)
    es.close()
    return nc, dbg_out


def _prep_inputs(inputs, nseq, ncores):
    f32 = np.float32
    g = lambda k: np.ascontiguousarray(np.asarray(inputs[k]))
    wq_ = g("w_q_up")[0].reshape(384, 4, 192)
    wq_p = np.concatenate([wq_[:, :, :128].reshape(384, 512), wq_[:, :, 128:].reshape(384, 256)], axis=1)
    wkv_ = g("w_kv_up")[0].reshape(256, 4, 256)
    wkv_p = np.concatenate([wkv_[:, :, :128].reshape(256, 512), wkv_[:, :, 128:].reshape(256, 512)], axis=1)
    invf = (10000.0 ** (-(np.arange(0, 64, 2, dtype=f32)) / f32(64))).astype(f32).reshape(1, 32)
    shared = {
        "g_mix": g("g_mix_norm").reshape(1, D), "w_in": g("w_in")[0], "lb_param": g("lb_param")[:, 0:2, :],
        "g_hg": g("g_hgrn_out")[0], "g_cq": g("g_cq").reshape(1, 384), "wq": np.ascontiguousarray(wq_p),
        "g_ckv": g("g_ckv").reshape(1, 256), "wkv": np.ascontiguousarray(wkv_p), "g_q": g("g_q_norm").reshape(1, 192),
        "g_k": g("g_k_norm").reshape(1, 192), "g_mo": g("g_mla_out").reshape(1, 512), "w_out": g("w_out")[0],
        "g_ffn": g("g_ffn_norm").reshape(1, D), "w_up": g("w_up")[0], "w_down": g("w_down")[0], "invf": invf,
    }
    x = g("x")
    pos = g("positions").astype(np.int32)
    maps = []
    for c in range(ncores):
        m = dict(shared)
        m["x"] = np.ascontiguousarray(x[c * nseq:(c + 1) * nseq])
        m["pos"] = np.ascontiguousarray(pos[c * nseq:(c + 1) * nseq])
        maps.append(m)
    return maps


def kernel(**inputs):
    nseq = 4
    nc, _ = build(nseq)
    maps = _prep_inputs(inputs, nseq, NCORES)
    res = run_bass_kernel_spmd(nc, maps, core_ids=list(range(NCORES)))
    out = np.concatenate([np.asarray(r["y"]) for r in res.results], axis=0)
    return out.astype(np.float32, copy=False)
```

```python
import numpy as np
from contextlib import ExitStack
import concourse.bass as bass
import concourse.mybir as mybir
from concourse.bass_utils import run_bass_kernel_spmd

F32 = mybir.dt.float32
BF16 = mybir.dt.bfloat16
I32 = mybir.dt.int32
AF = mybir.ActivationFunctionType
ALU = mybir.AluOpType
AX = mybir.AxisListType

NCORES = 8
SEQ = 2048
NT = SEQ // 128
D = 1024
DIN = 3264
DFF = 4096
EPS = 1e-6
PI = float(np.pi)
ARENA_BYTES = 212480

ENGS = ("pe", "act", "dve", "pool", "sp")


def _cody_waite():
    two_pi = 2.0 * np.pi
    c1 = 6.28125
    r1 = two_pi - c1
    m, e = np.frexp(r1)
    c2 = float(np.ldexp(np.round(m * 2 ** 11) / 2 ** 11, e))
    c3 = float(np.float32(two_pi - c1 - c2))
    return c1, c2, c3


CW1, CW2, CW3 = _cody_waite()


class _Rec:
    def __getattr__(self, name):
        def f(*a, **k):
            self.call = (name, a, k)
            return self
        return f


class Sched:
    def __init__(self):
        self.q = {e: [] for e in ENGS}
        self.cnt = {e: 0 for e in ENGS}
        self.seen = {e: {} for e in ENGS}
        self.bufs = {}
        self.dma_tot = {}
        self.sem_names = set(ENGS)

    def _st(self, k):
        st = self.bufs.get(k)
        if st is None:
            st = self.bufs[k] = {"w": None, "r": {}}
        return st

    def _deps(self, eng, reads, writes):
        toks = []
        for k in reads:
            st = self._st(k)
            if st["w"] is not None:
                toks.append(st["w"])
        for k in writes:
            st = self._st(k)
            if st["w"] is not None:
                toks.append(st["w"])
            toks.extend(st["r"].items())
        waits = {}
        for (s, v) in toks:
            if s == "pe" and eng == "pe":
                continue
            if self.seen[eng].get(s, 0) >= v:
                continue
            if waits.get(s, 0) < v:
                waits[s] = v
        for s, v in waits.items():
            self.seen[eng][s] = v
        return list(waits.items())

    def _commit(self, tok, reads, writes):
        for k in reads:
            r = self._st(k)["r"]
            if r.get(tok[0], 0) < tok[1]:
                r[tok[0]] = tok[1]
        for k in writes:
            st = self._st(k)
            st["w"] = tok
            st["r"] = {}

    def op(self, eng, fn, reads=(), writes=()):
        rec = _Rec()
        fn(rec)
        waits = self._deps(eng, reads, writes)
        self.cnt[eng] += 1
        tok = (eng, self.cnt[eng])
        self.q[eng].append((rec.call, waits, (eng, 1)))
        self._commit(tok, reads, writes)

    def dma(self, eng, fn, sem, reads=(), writes=()):
        rec = _Rec()
        fn(rec)
        self.sem_names.add(sem)
        waits = self._deps(eng, reads, writes)
        self.dma_tot[sem] = self.dma_tot.get(sem, 0) + 16
        tok = (sem, self.dma_tot[sem])
        self.q[eng].append((rec.call, waits, (sem, 16)))
        self._commit(tok, reads, writes)

    def barrier(self):
        for e in ENGS:
            waits = []
            for e2 in ENGS:
                if self.cnt[e2] > self.seen[e].get(e2, 0):
                    waits.append((e2, self.cnt[e2]))
                    self.seen[e][e2] = self.cnt[e2]
            for s, tot in self.dma_tot.items():
                if tot > self.seen[e].get(s, 0):
                    waits.append((s, tot))
                    self.seen[e][s] = tot
            self.q[e].append((None, waits, None))
        self.bufs = {}

    def emit(self, block, sems):
        handles = {"pe": block.tensor, "act": block.scalar, "dve": block.vector,
                   "pool": block.gpsimd, "sp": block.sync}

        def mk(e):
            ops = self.q[e]

            def body(engine):
                for fn, waits, inc in ops:
                    for s, v in waits:
                        engine.wait_ge(sems[s], v)
                    if fn is not None:
                        name, a, k = fn
                        getattr(engine, name)(*a, **k).then_inc(sems[inc[0]], inc[1])
            return body

        for e in ENGS:
            handles[e](mk(e))


def _dsize(dt):
    return 2 if dt == BF16 else 4


class Bump:
    def __init__(self, arena, start, end):
        self.arena, self.cur, self.end = arena, start, end

    def take(self, shape, dt):
        n = int(np.prod(shape)) * _dsize(dt)
        off = (self.cur + 63) // 64 * 64
        n4 = (n + 3) // 4 * 4
        self.cur = off + n4
        assert self.cur <= self.end, (self.cur, self.end)
        ap = self.arena[:, off // 4:(off + n4) // 4]
        if dt != F32:
            ap = ap.bitcast(dt)
        if n4 != n:
            ap = ap[:, 0:int(np.prod(shape))]
        if len(shape) == 2:
            ap = ap.rearrange("p (a b) -> p a b", b=shape[1])
        elif len(shape) == 3:
            ap = ap.rearrange("p (a b c) -> p a b c", b=shape[1], c=shape[2])
        return ap


def build(nseq=4, dbg=None):
    nc = bass.Bass("TRN2", target_bir_lowering=False)
    S = Sched()
    dbg_out = {}

    def din(name, shape, dt=F32):
        return nc.dram_tensor(name, list(shape), dt, kind="ExternalInput").ap()

    x = din("x", [nseq, SEQ, D])
    pos = din("pos", [nseq, SEQ], I32)
    g_mix = din("g_mix", [1, D])
    w_in = din("w_in", [D, DIN])
    lb_param = din("lb_param", [2, 2, 512])
    g_hg = din("g_hg", [4, 128])
    g_cq = din("g_cq", [1, 384])
    wq = din("wq", [384, 768])
    g_ckv = din("g_ckv", [1, 256])
    wkv = din("wkv", [256, 1024])
    g_q = din("g_q", [1, 192])
    g_k = din("g_k", [1, 192])
    g_mo = din("g_mo", [1, 512])
    w_out = din("w_out", [D, D])
    g_ffn = din("g_ffn", [1, D])
    w_up = din("w_up", [D, DFF])
    w_down = din("w_down", [DFF, D])
    invf = din("invf", [1, 32])
    y = nc.dram_tensor("y", [nseq, SEQ, D], F32, kind="ExternalOutput").ap()
    x1s = nc.dram_tensor("x1s", [nseq, SEQ, D], F32).ap()

    es = ExitStack()
    arena = es.enter_context(nc.sbuf_tensor("arena", [128, ARENA_BYTES // 4], F32))[:]
    pp = [es.enter_context(nc.psum_tensor(f"pp{i}", [128, 1024], F32)) for i in range(4)]

    def bank(k):
        return pp[k // 2][:, (k % 2) * 512:(k % 2) * 512 + 512]

    def bankbf(k):
        return bank(k).bitcast(BF16)

    def bk(k):
        return ("bank", k)

    def PE(fn, r, w):
        S.op("pe", fn, r, w)

    def ACT(fn, r, w):
        S.op("act", fn, r, w)

    def DVE(fn, r, w):
        S.op("dve", fn, r, w)

    def POOL(fn, r, w):
        S.op("pool", fn, r, w)

    def mm(out, lhsT, rhs, start, stop, r, w, skip=False):
        if skip:
            PE(lambda e: e.matmul(out, lhsT=lhsT, rhs=rhs, start=start, stop=stop, skip_group_check=True), r, w)
        else:
            PE(lambda e: e.matmul(out, lhsT=lhsT, rhs=rhs, start=start, stop=stop), r, w)

    def tp(out, in_, r, w):
        PE(lambda e: e.transpose(out=out, in_=in_, identity=ident), list(r) + ["ident"], w)

    def tap(name, ap, key):
        if dbg is None or name not in dbg:
            return
        shp = list(ap.shape)
        t = nc.dram_tensor("dbg_" + name, shp, ap.dtype, kind="ExternalOutput").ap()
        dbg_out[name] = t
        S.dma("sp", lambda e: e.dma_start(out=t, in_=ap), "dbg", reads=key)

    P = Bump(arena, 0, ARENA_BYTES)
    ident = P.take([128], BF16)
    maskfb = P.take([256], I32)
    cols = P.take([64], F32)
    ones_b = P.take([2], BF16)
    nhalf = P.take([1], F32)
    ones_f = P.take([512], F32)
    MARK0 = P.cur
    C_GCQ, C_GCKV, C_GHG, C_LB, C_LNOML, C_LBP = 0, 3, 5, 9, 17, 25

    A = Bump(arena, MARK0, ARENA_BYTES)
    W_in = A.take([8, DIN], BF16)
    gmix_b = A.take([D], F32)
    gq_b = A.take([768], F32)
    gk_b = A.take([768], F32)
    gmo_b = A.take([512], F32)
    invf_b = A.take([32], F32)
    hT = A.take([8, SEQ], BF16)
    oT = A.take([8, SEQ], BF16)
    RMARK = A.cur

    R0 = Bump(arena, RMARK, ARENA_BYTES)
    identf = R0.take([128], F32)
    lbt = R0.take([8], F32)

    def cdma(out, in_, slow=False):
        if slow:
            S.dma("sp", lambda e: e.dma_start(out=out, in_=in_, allow_slow_non_contiguous=True), "const", writes=["const"])
        else:
            S.dma("sp", lambda e: e.dma_start(out=out, in_=in_), "const", writes=["const"])

    cdma(gmix_b, g_mix.partition_broadcast(128))
    for h in range(4):
        cdma(gq_b[:, h * 128:(h + 1) * 128], g_q[:, 0:128].partition_broadcast(128))
        cdma(gq_b[:, 512 + h * 64:512 + (h + 1) * 64], g_q[:, 128:192].partition_broadcast(128))
        cdma(gk_b[:, h * 128:(h + 1) * 128], g_k[:, 0:128].partition_broadcast(128))
        cdma(gk_b[:, 512 + h * 64:512 + (h + 1) * 64], g_k[:, 128:192].partition_broadcast(128))
    cdma(gmo_b, g_mo.partition_broadcast(128))
    cdma(invf_b, invf.partition_broadcast(128))
    cdma(cols[:, C_GCQ:C_GCQ + 3], g_cq[0].rearrange("(m p) -> p m", p=128), slow=True)
    cdma(cols[:, C_GCKV:C_GCKV + 2], g_ckv[0].rearrange("(m p) -> p m", p=128), slow=True)
    cdma(cols[:, C_GHG:C_GHG + 4], g_hg.rearrange("h e -> e h"), slow=True)
    for d_ in range(2):
        for s_ in range(2):
            o = C_LBP + (d_ * 2 + s_) * 4
            cdma(cols[:, o:o + 4], lb_param[d_, s_].rearrange("(h p) -> p h", p=128), slow=True)
    for kc in range(8):
        S.dma("pool", lambda e, kc=kc: e.dma_start(out=W_in[:, kc, :], in_=w_in[kc * 128:(kc + 1) * 128, :], max_dma_last_dim=4096),
              "w_in", writes=["W_in"])

    POOL(lambda e: e.memset(identf, 0.0), [], ["identf"])
    POOL(lambda e: e.affine_select(out=identf, in_=identf, pattern=[[-1, 128]], compare_op=ALU.not_equal, fill=1.0,
                                   base=0, channel_multiplier=1), ["identf"], ["identf"])
    DVE(lambda e: e.tensor_copy(out=ident, in_=identf), ["identf"], ["ident"])
    POOL(lambda e: e.iota(maskfb[:, 0:128], pattern=[[1, 128]], base=0, channel_multiplier=-1), [], ["maskfb"])
    POOL(lambda e: e.iota(maskfb[:, 128:256], pattern=[[-1, 128]], base=0, channel_multiplier=1), ["maskfb"], ["maskfb"])
    DVE(lambda e: e.tensor_single_scalar(out=maskfb, in_=maskfb, scalar=0, op=ALU.is_ge), ["maskfb"], ["maskfb"])
    POOL(lambda e: e.memset(ones_b, 1.0), [], ["ones_b"])
    POOL(lambda e: e.memset(nhalf, -0.5), [], ["nhalf"])
    POOL(lambda e: e.memset(ones_f, 1.0), [], ["ones_f"])
    DVE(lambda e: e.tensor_scalar(out=gq_b, in0=gq_b, scalar1=float(192 ** -0.5), scalar2=None, op0=ALU.mult), ["const"], ["const"])
    lbp = cols[:, C_LBP:C_LBP + 16].rearrange("p (d s h) -> p d s h", s=2, h=4)
    lbv = cols[:, C_LB:C_LB + 8]
    DVE(lambda e: e.tensor_tensor(out=lbt.rearrange("p (d h) -> p d h", h=4), in0=lbp[:, :, 1, :], in1=lbp[:, :, 0, :], op=ALU.subtract),
        ["const"], ["lbt"])
    ACT(lambda e: e.activation(out=lbt, in_=lbt, func=AF.Exp), ["lbt"], ["lbt"])
    DVE(lambda e: e.tensor_scalar(out=lbt, in0=lbt, scalar1=1.0, scalar2=None, op0=ALU.add), ["lbt"], ["lbt"])
    DVE(lambda e: e.reciprocal(out=lbv, in_=lbt), ["lbt", "const"], ["const"])
    ACT(lambda e: e.activation(out=cols[:, C_LNOML:C_LNOML + 8], in_=lbv, func=AF.Ln, scale=-1.0, bias=1.0), ["const"], ["const"])
    S.barrier()

    for b in range(nseq):
        R = Bump(arena, RMARK, ARENA_BYTES)
        V_all = R.take([NT, 512], BF16)
        XMARK = R.cur
        xin = [R.take([D], F32) for _ in range(2)]
        hbf = [R.take([D], BF16) for _ in range(2)]
        junk = R.take([D], BF16)
        st = R.take([64], F32)
        for i in range(NT):
            sl = i % 2
            S.dma("sp", lambda e, sl=sl, i=i: e.dma_start(out=xin[sl], in_=x[b, i * 128:(i + 1) * 128, :]), f"xin{sl}",
                  writes=[("xin", sl)])
            c0 = (i % 8) * 4
            ACT(lambda e, sl=sl, c0=c0: e.activation(out=junk, in_=xin[sl], func=AF.Square, accum_out=st[:, c0:c0 + 1]),
                [("xin", sl)], ["junk", ("st", c0)])
            DVE(lambda e, c0=c0: e.tensor_scalar(out=st[:, c0 + 1:c0 + 2], in0=st[:, c0:c0 + 1], scalar1=1.0 / D, scalar2=EPS,
                                                 op0=ALU.mult, op1=ALU.add), [("st", c0)], [("st", c0 + 1)])
            POOL(lambda e, c0=c0: e.tensor_tensor(out=st[:, c0 + 2:c0 + 3], in0=st[:, c0 + 1:c0 + 2], in1=nhalf, op=ALU.pow),
                 [("st", c0 + 1), "nhalf"], [("st", c0 + 2)])
            DVE(lambda e, sl=sl, c0=c0: e.scalar_tensor_tensor(out=hbf[sl], in0=xin[sl], scalar=st[:, c0 + 2:c0 + 3], in1=gmix_b,
                                                               op0=ALU.mult, op1=ALU.mult),
                [("xin", sl), ("st", c0 + 2), "const"], [("hbf", sl)])
            k = i % 2
            for kc in range(8):
                tp(bankbf(k)[:, kc * 128:(kc + 1) * 128], hbf[sl][:, kc * 128:(kc + 1) * 128], [("hbf", sl)], [bk(k)])
            ACT(lambda e, k=k, i=i: e.activation(out=hT[:, :, i * 128:(i + 1) * 128],
                                                 in_=bankbf(k).rearrange("p (k t) -> p k t", t=128), func=AF.Copy),
                [bk(k)], [("hT", i)])
        tap("hT", hT, [("hT", i) for i in range(NT)])
        for i in range(NT):
            k = 2 + i % 2
            for kc in range(8):
                mm(bank(k), hT[:, kc, i * 128:(i + 1) * 128], W_in[:, kc, 1536:2048], kc == 0, kc == 7,
                   [("hT", i), "W_in"], [bk(k)])
            DVE(lambda e, k=k, i=i: e.tensor_copy(out=V_all[:, i, :], in_=bank(k)), [bk(k)], [("V", i)])
        S.barrier()
        R = Bump(arena, XMARK, ARENA_BYTES)
        sgT = R.take([SEQ], BF16)
        TS = [[R.take([512], F32) for _ in range(5)] + [R.take([516], F32)] for _ in range(2)]
        QK = {(d_, w_): R.take([SEQ], BF16) for d_ in range(2) for w_ in "qk"}
        Zbf = [R.take([NT, 128], BF16) for _ in range(2)]
        Y = [[R.take([128], F32) for _ in range(2)] for _ in range(2)]
        Xs = [[R.take([128], F32) for _ in range(2)] for _ in range(2)]
        sqo = [TS[s_][0] for s_ in range(2)]
        onb = [TS[s_][1].bitcast(BF16)[:, 0:512] for s_ in range(2)]
        atm = [TS[s_][2].bitcast(BF16).rearrange("p (d n) -> p d n", d=2) for s_ in range(2)]
        ktok = [TS[s_][4].bitcast(BF16)[:, 0:512] for s_ in range(2)]
        Rall = [R.take([NT], F32) for _ in range(2)]
        Eall = [R.take([NT], F32) for _ in range(2)]
        dR = [R.take([3, NT], F32) for _ in range(2)]
        efac = [R.take([3, NT], F32) for _ in range(2)]
        tot = R.take([2, 4], F32)
        carr = R.take([2, 4], F32)
        st2 = R.take([64], F32)
        for s_ in range(2):
            POOL(lambda e: e.memset(TS[s_][5][:, 0:1], 0.0), [], [("T", s_, 5)])
        POOL(lambda e: e.memset(carr, 0.0), [], ["carr"])
        for h in range(4):
            for blk in range(4):
                tl = slice(blk * 512, (blk + 1) * 512)
                hk = [("hT", 4 * blk + j) for j in range(4)] + ["W_in"]
                kg = 6 + blk % 2
                for kc in range(8):
                    mm(bank(kg), W_in[:, kc, 2048 + h * 128:2048 + (h + 1) * 128], hT[:, kc, tl], kc == 0, kc == 7, hk, [bk(kg)])
                ACT(lambda e: e.activation(out=sgT[:, tl], in_=bank(kg), func=AF.Silu), [bk(kg)], [("sgT", blk)])

            def prep_mm(blk):
                tl = slice(blk * 512, (blk + 1) * 512)
                hk = [("hT", 4 * blk + j) for j in range(4)] + ["W_in"]
                for j, c0 in enumerate((0, 512, 1024)):
                    kk = (3 * blk + j) % 6
                    for kc in range(8):
                        mm(bank(kk), W_in[:, kc, c0 + h * 128:c0 + (h + 1) * 128], hT[:, kc, tl], kc == 0, kc == 7, hk, [bk(kk)])

            def front(it):
                blk, d_ = it // 2, it % 2
                T = TS[it % 2]
                tk = [("T", it % 2, j) for j in range(6)]
                kz = (3 * blk + 1 + d_) % 6
                lbc = cols[:, C_LB + d_ * 4 + h:C_LB + d_ * 4 + h + 1]
                ACT(lambda e: e.activation(out=T[0], in_=bank(kz), func=AF.Exp, scale=-1.0), [bk(kz)], [tk[0]])
                ACT(lambda e: e.activation(out=T[1], in_=T[0], func=AF.Ln, scale=lbc, bias=1.0), [tk[0], "const"], [tk[1]])
                ACT(lambda e: e.activation(out=T[2], in_=T[0], func=AF.Ln, scale=1.0, bias=1.0), [tk[0]], [tk[2]])
                DVE(lambda e: e.tensor_tensor(out=T[4], in0=bank(kz), in1=T[2], op=ALU.add), [bk(kz), tk[2]], [tk[4]])
                DVE(lambda e: e.tensor_tensor_scan(out=T[5][:, 1:513], data0=T[1], data1=T[2], initial=0.0, op0=ALU.add, op1=ALU.subtract),
                    [tk[1], tk[2], tk[5]], [tk[5]])
                DVE(lambda e: e.tensor_copy(out=tot[:, d_, blk:blk + 1], in_=T[5][:, 512:513]), [tk[5]], [("tot", d_)])
                Bx = T[5][:, 1:513] if d_ == 0 else T[5][:, 0:512]
                kB = tk[5]
                DVE(lambda e: e.tensor_copy(out=Rall[d_][:, blk * 4:(blk + 1) * 4], in_=Bx[:, 63:512:128]), [kB], [("Rall", d_)])
                eo_ = 127 if d_ == 0 else 0
                DVE(lambda e: e.tensor_copy(out=Eall[d_][:, blk * 4:(blk + 1) * 4], in_=Bx[:, eo_:512:128]), [kB], [("Eall", d_)])
                DVE(lambda e: e.tensor_tensor(out=T[0].rearrange("p (c j) -> p c j", j=128), in0=Bx.rearrange("p (c j) -> p c j", j=128),
                                              in1=Rall[d_][:, blk * 4:(blk + 1) * 4].unsqueeze(2).to_broadcast([128, 4, 128]),
                                              op=ALU.subtract), [kB, ("Rall", d_)], [tk[0]])

            def back(it):
                blk, d_ = it // 2, it % 2
                tl = slice(blk * 512, (blk + 1) * 512)
                T = TS[it % 2]
                tk = [("T", it % 2, j) for j in range(6)]
                kq = (3 * blk) % 6
                lno = cols[:, C_LNOML + d_ * 4 + h:C_LNOML + d_ * 4 + h + 1]
                sgn = 1.0 if d_ == 0 else -1.0
                ACT(lambda e: e.activation(out=T[2], in_=T[0], func=AF.Exp, scale=sgn), [tk[0]], [tk[2]])
                POOL(lambda e: e.tensor_tensor(out=T[3], in0=T[4], in1=T[0], op=(ALU.add if d_ == 0 else ALU.subtract)),
                     [tk[4], tk[0]], [tk[3]])
                ACT(lambda e: e.activation(out=QK[(d_, "k")][:, tl], in_=T[3], func=AF.Exp, scale=-1.0, bias=lno),
                    [tk[3], "const"], [("QK", d_, "k", blk)])
                DVE(lambda e: e.tensor_tensor(out=QK[(d_, "q")][:, tl], in0=bank(kq), in1=T[2], op=ALU.mult),
                    [bk(kq), tk[2]], [("QK", d_, "q", blk)])

            prep_mm(0)
            front(0)
            for it in range(8):
                if it % 2 == 0 and it // 2 + 1 < 4:
                    prep_mm(it // 2 + 1)
                if it + 1 < 8:
                    front(it + 1)
                back(it)
            for d_ in range(2):
                DVE(lambda e: e.tensor_copy(out=carr[:, d_, 1:2], in_=tot[:, d_, 0:1]), [("tot", d_), "carr"], ["carr"])
                DVE(lambda e: e.tensor_tensor(out=carr[:, d_, 2:3], in0=carr[:, d_, 1:2], in1=tot[:, d_, 1:2], op=ALU.add),
                    [("tot", d_), "carr"], ["carr"])
                DVE(lambda e: e.tensor_tensor(out=carr[:, d_, 3:4], in0=carr[:, d_, 2:3], in1=tot[:, d_, 2:3], op=ALU.add),
                    [("tot", d_), "carr"], ["carr"])
                for arr, key in ((Rall, "Rall"), (Eall, "Eall")):
                    DVE(lambda e, arr=arr: e.tensor_tensor(out=arr[d_].rearrange("p (b c) -> p b c", c=4),
                                                           in0=arr[d_].rearrange("p (b c) -> p b c", c=4),
                                                           in1=carr[:, d_, :].unsqueeze(2).to_broadcast([128, 4, 4]), op=ALU.add),
                        [(key, d_), "carr"], [(key, d_)])
            DVE(lambda e: e.tensor_tensor(out=dR[0][:, 0, 1:16], in0=Eall[0][:, 1:16], in1=Eall[0][:, 0:15], op=ALU.subtract),
                [("Eall", 0)], [("dR", 0)])
            DVE(lambda e: e.tensor_tensor(out=dR[0][:, 2, 1:16], in0=Rall[0][:, 1:16], in1=Eall[0][:, 0:15], op=ALU.subtract),
                [("Eall", 0), ("Rall", 0), ("dR", 0)], [("dR", 0)])
            DVE(lambda e: e.memset(dR[0][:, :, 0:1], 0.0), [("dR", 0)], [("dR", 0)])
            DVE(lambda e: e.tensor_tensor(out=dR[0][:, 1, :], in0=Eall[0], in1=Rall[0], op=ALU.subtract),
                [("Eall", 0), ("Rall", 0), ("dR", 0)], [("dR", 0)])
            DVE(lambda e: e.tensor_tensor(out=dR[1][:, 0, 0:15], in0=Eall[1][:, 1:16], in1=Eall[1][:, 0:15], op=ALU.subtract),
                [("Eall", 1)], [("dR", 1)])
            DVE(lambda e: e.tensor_tensor(out=dR[1][:, 2, 0:15], in0=Eall[1][:, 1:16], in1=Rall[1][:, 0:15], op=ALU.subtract),
                [("Eall", 1), ("Rall", 1), ("dR", 1)], [("dR", 1)])
            DVE(lambda e: e.memset(dR[1][:, :, 15:16], 0.0), [("dR", 1)], [("dR", 1)])
            DVE(lambda e: e.tensor_tensor(out=dR[1][:, 1, :], in0=Rall[1], in1=Eall[1], op=ALU.subtract),
                [("Eall", 1), ("Rall", 1), ("dR", 1)], [("dR", 1)])
            for d_ in range(2):
                ACT(lambda e: e.activation(out=efac[d_], in_=dR[d_], func=AF.Exp), [("dR", d_)], [("efac", d_)])
            if h == 0 and b == 0:
                tap("Qf", QK[(0, "q")], [("QK", 0, "q", j) for j in range(4)])
                tap("Kf", QK[(0, "k")], [("QK", 0, "k", j) for j in range(4)])
            order = [list(range(15)), list(range(15, 0, -1))]
            pslot = {}
            ngrp = 0
            for gi in range(4):
                for d_ in range(2):
                    cs_ = order[d_][gi * 4:gi * 4 + 4]
                    kT, kP, ks = ngrp % 2, 2 + ngrp % 4, ngrp % 2
                    ngrp += 1
                    for j, c in enumerate(cs_):
                        tp(bankbf(kT)[:, j * 128:(j + 1) * 128], QK[(d_, "k")][:, c * 128:(c + 1) * 128], [("QK", d_, "k", c // 4)], [bk(kT)])
                    nn = len(cs_) * 128
                    ACT(lambda e: e.activation(out=ktok[ks][:, 0:nn], in_=bankbf(kT)[:, 0:nn], func=AF.Copy), [bk(kT)], [("T", ks, 4)])
                    for j, c in enumerate(cs_):
                        mm(bank(kP)[:, j * 128:(j + 1) * 128], ktok[ks][:, j * 128:(j + 1) * 128], V_all[:, c, h * 128:(h + 1) * 128], True, True,
                           [("T", ks, 4), ("V", c)], [bk(kP)])
                        pslot[(d_, c)] = (kP, j)
                for idx in range(gi * 4, min(gi * 4 + 4, 15)):
                    for d_ in range(2):
                        c = order[d_][idx]
                        kP_, j_ = pslot[(d_, c)]
                        pw = Xs[d_][idx % 2]
                        ACT(lambda e: e.activation(out=pw, in_=bank(kP_)[:, j_ * 128:(j_ + 1) * 128], func=AF.Copy, scale=efac[d_][:, 1, c:c + 1]),
                            [bk(kP_), ("efac", d_)], [("Xs", d_, idx % 2)])
                    for d_ in range(2):
                        c = order[d_][idx]
                        pw = Xs[d_][idx % 2]
                        yc, yp = Y[d_][idx % 2], Y[d_][(idx + 1) % 2]
                        if idx == 0:
                            DVE(lambda e: e.tensor_copy(out=yc, in_=pw), [("Xs", d_, idx % 2)], [("Y", d_, idx % 2)])
                        else:
                            DVE(lambda e: e.scalar_tensor_tensor(out=yc, in0=yp, scalar=efac[d_][:, 0, c:c + 1], in1=pw, op0=ALU.mult, op1=ALU.add),
                                [("Y", d_, (idx + 1) % 2), ("Xs", d_, idx % 2), ("efac", d_)], [("Y", d_, idx % 2)])
                    for d_ in range(2):
                        c = order[d_][idx]
                        yc = Y[d_][idx % 2]
                        nxt = c + 1 if d_ == 0 else c - 1
                        ACT(lambda e: e.activation(out=Zbf[d_][:, nxt, :], in_=yc, func=AF.Copy, scale=efac[d_][:, 2, nxt:nxt + 1]),
                            [("Y", d_, idx % 2), ("efac", d_)], [("Zbf", d_, nxt)])
            for kk in range(4):
                DVE(lambda e: e.memset(bank(kk), 0.0), [], [bk(kk)])
            for s_ in range(2):
                POOL(lambda e: e.memset(atm[s_], 0.0), [], [("T", s_, 2)])

            def at_mm(g):
                ka = (g % 2) * 2
                for d_ in range(2):
                    for j in range(4):
                        c = g * 4 + j
                        q_, k_ = QK[(d_, "q")], QK[(d_, "k")]
                        o_ = j * 128
                        rk = [("QK", d_, "k", g), ("QK", d_, "q", g)]
                        if d_ == 0:
                            mm(bank(ka)[0:64, o_:o_ + 128], k_[:, c * 128:c * 128 + 64], q_[:, c * 128:(c + 1) * 128], True, True, rk, [bk(ka)])
                            mm(bank(ka)[64:128, o_ + 64:o_ + 128], k_[:, c * 128 + 64:(c + 1) * 128], q_[:, c * 128 + 64:(c + 1) * 128],
                               True, True, rk, [bk(ka)])
                        else:
                            mm(bank(ka + 1)[0:64, o_:o_ + 64], k_[:, c * 128:c * 128 + 64], q_[:, c * 128:c * 128 + 64], True, True, rk, [bk(ka + 1)])
                            mm(bank(ka + 1)[64:128, o_:o_ + 128], k_[:, c * 128 + 64:(c + 1) * 128], q_[:, c * 128:(c + 1) * 128],
                               True, True, rk, [bk(ka + 1)])

            def mask_copy(g):
                ka = (g % 2) * 2
                sa = g % 2
                for d_ in range(2):
                    mk = maskfb[:, d_ * 128:(d_ + 1) * 128].unsqueeze(1).to_broadcast([128, 4, 128])
                    DVE(lambda e: e.copy_predicated(out=atm[sa][:, d_, :].rearrange("p (c j) -> p c j", j=128), mask=mk,
                                                    data=bank(ka + d_).rearrange("p (c j) -> p c j", j=128)),
                        [bk(ka + d_), "maskfb", ("T", sa, 2)], [("T", sa, 2)])

            def o_mm(g):
                ko = 4 + g % 2
                sa = g % 2
                for j in range(4):
                    c = g * 4 + j
                    cs = slice(c * 128, (c + 1) * 128)
                    grp = []
                    if c > 0:
                        grp.append((QK[(0, "q")][:, cs], Zbf[0][:, c, :], [("QK", 0, "q", g), ("Zbf", 0, c)]))
                    if c < NT - 1:
                        grp.append((QK[(1, "q")][:, cs], Zbf[1][:, c, :], [("QK", 1, "q", g), ("Zbf", 1, c)]))
                    vv = V_all[:, c, h * 128:(h + 1) * 128]
                    grp.append((atm[sa][:, 0, j * 128:(j + 1) * 128], vv, [("T", sa, 2), ("V", c)]))
                    grp.append((atm[sa][:, 1, j * 128:(j + 1) * 128], vv, [("T", sa, 2), ("V", c)]))
                    for gi_, (l_, r_, rk) in enumerate(grp):
                        mm(bank(ko)[:, j * 128:(j + 1) * 128], l_, r_, gi_ == 0, gi_ == len(grp) - 1, rk, [bk(ko)])

            def epi_a(g):
                ko = 4 + g % 2
                sa = g % 2
                c0 = (g % 4) * 12
                ACT(lambda e: e.activation(out=sqo[sa], in_=bank(ko), func=AF.Square), [bk(ko)], [("T", sa, 0)])
                DVE(lambda e: e.tensor_reduce(out=st2[:, c0:c0 + 4], in_=sqo[sa].rearrange("p (c j) -> p c j", j=128), axis=AX.X, op=ALU.add),
                    [("T", sa, 0)], [("st2", c0)])
                DVE(lambda e: e.tensor_scalar(out=st2[:, c0 + 4:c0 + 8], in0=st2[:, c0:c0 + 4], scalar1=1.0 / 128, scalar2=EPS,
                                              op0=ALU.mult, op1=ALU.add), [("st2", c0)], [("st2", c0 + 4)])
                POOL(lambda e: e.tensor_tensor(out=st2[:, c0 + 8:c0 + 12], in0=st2[:, c0 + 4:c0 + 8], in1=nhalf.to_broadcast([128, 4]),
                                               op=ALU.pow), [("st2", c0 + 4), "nhalf"], [("st2", c0 + 8)])
                DVE(lambda e: e.tensor_tensor(out=onb[sa].rearrange("p (c j) -> p c j", j=128), in0=bank(ko).rearrange("p (c j) -> p c j", j=128),
                                              in1=st2[:, c0 + 8:c0 + 12].unsqueeze(2).to_broadcast([128, 4, 128]), op=ALU.mult),
                    [bk(ko), ("st2", c0 + 8)], [("T", sa, 1)])

            def epi_b(g):
                kt = 6 + g % 2
                sa = g % 2
                for j in range(4):
                    tp(bankbf(kt)[:, j * 128:(j + 1) * 128], onb[sa][:, j * 128:(j + 1) * 128], [("T", sa, 1)], [bk(kt)])
                gs = slice(g * 512, (g + 1) * 512)
                DVE(lambda e: e.scalar_tensor_tensor(out=oT[:, h, gs], in0=bankbf(kt)[:, 0:512], scalar=cols[:, C_GHG + h:C_GHG + h + 1],
                                                     in1=sgT[:, gs], op0=ALU.mult, op1=ALU.mult),
                    [bk(kt), "const", ("sgT", g)], [("oT", h, g)])

            at_mm(0); mask_copy(0); at_mm(1); o_mm(0); mask_copy(1); at_mm(2); epi_a(0); o_mm(1); mask_copy(2); at_mm(3)
            epi_b(0); epi_a(1); o_mm(2); mask_copy(3); epi_b(1); epi_a(2); o_mm(3); epi_b(2); epi_a(3); epi_b(3)
        tap("oTa", oT[:, 0:4, :], [("oT", h, g) for h in range(4) for g in range(4)])
        S.barrier()
        R = Bump(arena, RMARK, ARENA_BYTES)
        KT = R.take([6, SEQ], BF16)
        Vaug = R.take([NT, 4, 130], BF16)
        SMARK = R.cur
        Wq = R.take([3, 768], BF16)
        Wkv = R.take([2, 1024], BF16)
        posi = R.take([NT], I32)
        posf = R.take([NT], F32)
        cosT = R.take([NT, 32], F32)
        sinT = R.take([NT, 32], F32)
        TMARK = R.cur
        ang = R.take([NT, 32], F32)
        kqi = R.take([NT, 32], I32)
        t_a = R.take([NT, 32], F32)
        t_b = R.take([NT, 32], F32)
        for m in range(3):
            S.dma("pool", lambda e, m=m: e.dma_start(out=Wq[:, m, :], in_=wq[m * 128:(m + 1) * 128, :]), "wq", writes=["Wq"])
        for m in range(2):
            S.dma("pool", lambda e, m=m: e.dma_start(out=Wkv[:, m, :], in_=wkv[m * 128:(m + 1) * 128, :]), "wkv", writes=["Wkv"])
        POOL(lambda e: e.memset(Vaug[:, :, :, 128:130], 1.0), [], ["Vones"])
        S.dma("sp", lambda e: e.dma_start(out=posi, in_=pos[b].rearrange("(n p) -> p n", p=128), allow_slow_non_contiguous=True),
              "posi", writes=["posi"])
        DVE(lambda e: e.tensor_copy(out=posf, in_=posi), ["posi"], ["posf"])
        DVE(lambda e: e.tensor_tensor(out=ang, in0=posf.unsqueeze(2).to_broadcast([128, NT, 32]),
                                      in1=invf_b.unsqueeze(1).to_broadcast([128, NT, 32]), op=ALU.mult), ["posf", "const"], ["ang"])
        DVE(lambda e: e.tensor_scalar(out=kqi, in0=ang, scalar1=float(1.0 / (2 * PI)), scalar2=None, op0=ALU.mult), ["ang"], ["kqi"])
        DVE(lambda e: e.tensor_copy(out=t_a, in_=kqi), ["kqi"], ["t_a"])
        C1, C2, C3 = CW1, CW2, CW3
        DVE(lambda e: e.scalar_tensor_tensor(out=t_b, in0=t_a, scalar=-C1, in1=ang, op0=ALU.mult, op1=ALU.add), ["t_a", "ang"], ["t_b"])
        DVE(lambda e: e.scalar_tensor_tensor(out=ang, in0=t_a, scalar=-C2, in1=t_b, op0=ALU.mult, op1=ALU.add), ["t_a", "t_b", "ang"], ["ang"])
        DVE(lambda e: e.scalar_tensor_tensor(out=t_b, in0=t_a, scalar=-C3, in1=ang, op0=ALU.mult, op1=ALU.add), ["t_a", "ang", "t_b"], ["t_b"])
        DVE(lambda e: e.tensor_scalar(out=ang, in0=t_b, scalar1=PI, scalar2=-PI, op0=ALU.min, op1=ALU.max), ["t_b", "ang"], ["ang"])
        ACT(lambda e: e.activation(out=sinT, in_=ang, func=AF.Sin), ["ang"], ["sinT"])
        DVE(lambda e: e.tensor_scalar(out=t_a, in0=t_b, scalar1=PI / 2, scalar2=None, op0=ALU.add), ["t_b", "t_a"], ["t_a"])
        DVE(lambda e: e.tensor_scalar(out=t_b, in0=t_a, scalar1=PI, scalar2=2 * PI, op0=ALU.is_gt, op1=ALU.mult), ["t_a", "t_b"], ["t_b"])
        DVE(lambda e: e.tensor_tensor(out=t_a, in0=t_a, in1=t_b, op=ALU.subtract), ["t_a", "t_b"], ["t_a"])
        DVE(lambda e: e.tensor_scalar(out=t_a, in0=t_a, scalar1=PI, scalar2=-PI, op0=ALU.min, op1=ALU.max), ["t_a"], ["t_a"])
        ACT(lambda e: e.activation(out=cosT, in_=t_a, func=AF.Sin), ["t_a"], ["cosT"])
        if b == 0:
            tap("cosT", cosT, ["cosT"])
            tap("sinT", sinT, ["sinT"])
        S.barrier()
        R = Bump(arena, TMARK, ARENA_BYTES)
        cT = R.take([5, 512], BF16)
        sq = R.take([5, 512], BF16)
        sqq = R.take([768], BF16)
        t1 = R.take([768], F32)
        qf = R.take([1024], BF16)
        kf = R.take([768], BF16)
        krs = R.take([64], F32)
        krr = R.take([64], F32)
        ta = R.take([256], F32)
        tb = R.take([256], F32)
        tak_ = R.take([64], F32)
        tbk_ = R.take([64], F32)
        sqk = R.take([512], F32)
        st3 = R.take([64], F32)
        junk = R.take([64], BF16)
        POOL(lambda e: e.memset(qf[:, 512:1024], 0.0), [], ["qf"])
        pending_tr = []
        for blk in range(4):
            tl = slice(blk * 512, (blk + 1) * 512)
            hk = [("hT", 4 * blk + j) for j in range(4)] + ["W_in"]
            for m in range(5):
                c0 = 2560 + m * 128
                kc_ = (m % 2) * 2
                for kc in range(8):
                    mm(bank(kc_), W_in[:, kc, c0:c0 + 128], hT[:, kc, tl], kc == 0, kc == 7, hk, [bk(kc_)])
                gcol = cols[:, C_GCQ + m:C_GCQ + m + 1]
                ACT(lambda e, m=m, gcol=gcol: e.activation(out=cT[:, m, :], in_=bank(kc_), func=AF.Copy, scale=gcol),
                    [bk(kc_), "const"], [("cT", m)])
                ACT(lambda e, m=m: e.activation(out=sq[:, m, :], in_=bank(kc_), func=AF.Square), [bk(kc_)], [("sq", m)])
            for j in range(4):
                i = blk * 4 + j
                js = slice(j * 128, (j + 1) * 128)
                its = slice(i * 128, (i + 1) * 128)
                c0 = (i % 2) * 32
                for m in range(3):
                    mm(bank(1)[:, 0:1], sq[:, m, js], ones_b[:, 0:1], m == 0, m == 2, [("sq", m), "ones_b"], [bk(1)])
                for m in range(3, 5):
                    mm(bank(1)[:, 2:3], sq[:, m, js], ones_b[:, 0:1], m == 3, m == 4, [("sq", m), "ones_b"], [bk(1)])
                for kc in range(8):
                    mm(bank(1)[:, 64:128], hT[:, kc, its], W_in[:, kc, 3200:3264], kc == 0, kc == 7, [("hT", i), "W_in"], [bk(1)])
                sc = lambda o: st3[:, c0 + o:c0 + o + 1]
                skey = lambda o: ("st3", c0 + o)
                DVE(lambda e, sc=sc: e.tensor_scalar(out=sc(0), in0=bank(1)[:, 0:1], scalar1=1.0 / 384, scalar2=EPS, op0=ALU.mult, op1=ALU.add),
                    [bk(1)], [skey(0)])
                DVE(lambda e, sc=sc: e.tensor_scalar(out=sc(1), in0=bank(1)[:, 2:3], scalar1=1.0 / 256, scalar2=EPS, op0=ALU.mult, op1=ALU.add),
                    [bk(1)], [skey(1)])
                POOL(lambda e, sc=sc: e.tensor_tensor(out=sc(2), in0=sc(0), in1=nhalf, op=ALU.pow), [skey(0), "nhalf"], [skey(2)])
                POOL(lambda e, sc=sc: e.tensor_tensor(out=sc(3), in0=sc(1), in1=nhalf, op=ALU.pow), [skey(1), "nhalf"], [skey(3)])
                for m in range(3):
                    mm(bank(3), cT[:, m, js], Wq[:, m, 0:512], m == 0, m == 2, [("cT", m), "Wq"], [bk(3)])
                for m in range(3):
                    mm(bank(4)[:, 0:256], cT[:, m, js], Wq[:, m, 512:768], m == 0, m == 2, [("cT", m), "Wq"], [bk(4)])
                for m in range(2):
                    mm(bank(5), cT[:, 3 + m, js], Wkv[:, m, 0:512], m == 0, m == 1, [("cT", 3 + m), "Wkv"], [bk(5)])
                for m in range(2):
                    mm(bank(6), cT[:, 3 + m, js], Wkv[:, m, 512:1024], m == 0, m == 1, [("cT", 3 + m), "Wkv"], [bk(6)])
                prev_tr = pending_tr
                pending_tr = []
                for pe_part, _ in prev_tr:
                    pe_part()
                DVE(lambda e: e.tensor_copy(out=krs, in_=bank(1)[:, 64:128]), [bk(1)], ["krs"])
                ACT(lambda e: e.activation(out=sqq[:, 0:512], in_=bank(3), func=AF.Square), [bk(3)], ["sqq"])
                ACT(lambda e: e.activation(out=sqq[:, 512:768], in_=bank(4)[:, 0:256], func=AF.Square), [bk(4), "sqq"], ["sqq"])
                ACT(lambda e: e.activation(out=sqk, in_=bank(5), func=AF.Square), [bk(5)], ["sqk"])
                ACT(lambda e, sc=sc: e.activation(out=junk[:, 0:64], in_=krs, func=AF.Square, accum_out=sc(13)), ["krs"], ["junk", skey(13)])
                for _, act_part in prev_tr:
                    act_part()
                DVE(lambda e, sc=sc: e.tensor_reduce(out=st3[:, c0 + 4:c0 + 8], in_=sqq[:, 0:512].rearrange("p (h j) -> p h j", j=128),
                                                     axis=AX.X, op=ALU.add), ["sqq"], [skey(4)])
                DVE(lambda e, sc=sc: e.tensor_reduce(out=st3[:, c0 + 8:c0 + 12], in_=sqq[:, 512:768].rearrange("p (h j) -> p h j", j=64),
                                                     axis=AX.X, op=ALU.add), ["sqq"], [skey(8)])
                DVE(lambda e: e.tensor_reduce(out=st3[:, c0 + 16:c0 + 20], in_=sqk.rearrange("p (h j) -> p h j", j=128),
                                              axis=AX.X, op=ALU.add), ["sqk"], [skey(16)])
                DVE(lambda e: e.tensor_tensor(out=st3[:, c0 + 4:c0 + 8], in0=st3[:, c0 + 4:c0 + 8], in1=st3[:, c0 + 8:c0 + 12], op=ALU.add),
                    [skey(4), skey(8)], [skey(4)])
                DVE(lambda e, sc=sc: e.tensor_tensor(out=sc(12), in0=sc(2), in1=sc(2), op=ALU.mult), [skey(2)], [skey(12)])
                DVE(lambda e, sc=sc: e.tensor_scalar(out=st3[:, c0 + 4:c0 + 8], in0=st3[:, c0 + 4:c0 + 8], scalar1=sc(12), scalar2=1.0 / 192,
                                                     op0=ALU.mult, op1=ALU.mult), [skey(4), skey(12)], [skey(4)])
                DVE(lambda e: e.tensor_scalar(out=st3[:, c0 + 4:c0 + 8], in0=st3[:, c0 + 4:c0 + 8], scalar1=EPS, scalar2=None, op0=ALU.add),
                    [skey(4)], [skey(4)])
                DVE(lambda e, sc=sc: e.tensor_tensor(out=sc(14), in0=sc(3), in1=sc(3), op=ALU.mult), [skey(3)], [skey(14)])
                DVE(lambda e, sc=sc: e.tensor_scalar(out=st3[:, c0 + 16:c0 + 20], in0=st3[:, c0 + 16:c0 + 20], scalar1=sc(14), scalar2=sc(13),
                                                     op0=ALU.mult, op1=ALU.add), [skey(16), skey(14), skey(13)], [skey(16)])
                DVE(lambda e: e.tensor_scalar(out=st3[:, c0 + 16:c0 + 20], in0=st3[:, c0 + 16:c0 + 20], scalar1=1.0 / 192, scalar2=EPS,
                                              op0=ALU.mult, op1=ALU.add), [skey(16)], [skey(16)])
                POOL(lambda e: e.tensor_tensor(out=st3[:, c0 + 8:c0 + 12], in0=st3[:, c0 + 4:c0 + 8],
                                               in1=nhalf.to_broadcast([128, 4]), op=ALU.pow), [skey(4), "nhalf", skey(8)], [skey(8)])
                POOL(lambda e: e.tensor_tensor(out=st3[:, c0 + 20:c0 + 24], in0=st3[:, c0 + 16:c0 + 20],
                                               in1=nhalf.to_broadcast([128, 4]), op=ALU.pow), [skey(16), "nhalf"], [skey(20)])
                DVE(lambda e, sc=sc: e.tensor_scalar(out=st3[:, c0 + 8:c0 + 12], in0=st3[:, c0 + 8:c0 + 12], scalar1=sc(2), scalar2=None,
                                                     op0=ALU.mult), [skey(8), skey(2)], [skey(8)])
                DVE(lambda e, sc=sc: e.tensor_scalar(out=st3[:, c0 + 24:c0 + 28], in0=st3[:, c0 + 20:c0 + 24], scalar1=sc(3), scalar2=None,
                                                     op0=ALU.mult), [skey(20), skey(3)], [skey(24)])
                fq = st3[:, c0 + 8:c0 + 12]
                rk_ = st3[:, c0 + 20:c0 + 24]
                fkn = st3[:, c0 + 24:c0 + 28]
                DVE(lambda e, fq=fq: e.tensor_tensor(out=t1[:, 0:512].rearrange("p (h j) -> p h j", j=128),
                                                     in0=bank(3).rearrange("p (h j) -> p h j", j=128),
                                                     in1=fq.unsqueeze(2).to_broadcast([128, 4, 128]), op=ALU.mult), [bk(3), skey(8)], ["t1"])
                DVE(lambda e, fq=fq: e.tensor_tensor(out=t1[:, 512:768].rearrange("p (h j) -> p h j", j=64),
                                                     in0=bank(4)[:, 0:256].rearrange("p (h j) -> p h j", j=64),
                                                     in1=fq.unsqueeze(2).to_broadcast([128, 4, 64]), op=ALU.mult), [bk(4), skey(8), "t1"], ["t1"])
                DVE(lambda e, fkn=fkn: e.tensor_tensor(out=sqk.rearrange("p (h j) -> p h j", j=128),
                                                       in0=bank(5).rearrange("p (h j) -> p h j", j=128),
                                                       in1=fkn.unsqueeze(2).to_broadcast([128, 4, 128]), op=ALU.mult),
                    [bk(5), skey(24), "sqk"], ["sqk"])
                ACT(lambda e, sc=sc, i=i: e.activation(out=Vaug[:, i, :, 0:128], in_=bank(6).rearrange("p (h j) -> p h j", j=128),
                                                       func=AF.Copy, scale=sc(3)), [bk(6), skey(3)], [("Vaug", i)])
                DVE(lambda e: e.tensor_tensor(out=qf[:, 0:512], in0=t1[:, 0:512], in1=gq_b[:, 0:512], op=ALU.mult), ["t1", "const"], ["qf"])
                DVE(lambda e: e.tensor_tensor(out=t1[:, 512:768], in0=t1[:, 512:768], in1=gq_b[:, 512:768], op=ALU.mult), ["t1", "const"], ["t1"])

                def rope(E, src, nh_, dst, rk, wk, ta, tb, tak, tbk):
                    s4 = src.rearrange("p (h a r) -> p h a r", a=2, r=32)
                    a4 = ta[:, 0:nh_ * 64].rearrange("p (h a r) -> p h a r", a=2, r=32)
                    b4 = tb[:, 0:nh_ * 64].rearrange("p (h a r) -> p h a r", a=2, r=32)
                    cb = cosT[:, i, :].unsqueeze(1).unsqueeze(1).to_broadcast([128, nh_, 2, 32])
                    sb_ = sinT[:, i, :].unsqueeze(1).to_broadcast([128, nh_, 32])
                    E(lambda e: e.tensor_tensor(out=a4, in0=s4, in1=cb, op=ALU.mult), rk + ["cosT"], [tak])
                    if isinstance(dst, list):
                        E(lambda e: e.scalar_tensor_tensor(out=b4[:, :, 0, :], in0=s4[:, :, 1, :], scalar=-1.0, in1=sb_, op0=ALU.mult, op1=ALU.mult),
                          rk + ["sinT"], [tbk])
                        E(lambda e: e.tensor_tensor(out=b4[:, :, 1, :], in0=s4[:, :, 0, :], in1=sb_, op=ALU.mult), rk + ["sinT", tbk], [tbk])
                        a3 = ta[:, 0:nh_ * 64].rearrange("p (h j) -> p h j", j=64)
                        b3 = tb[:, 0:nh_ * 64].rearrange("p (h j) -> p h j", j=64)
                        for par, dv in enumerate(dst):
                            E(lambda e, par=par, dv=dv: e.tensor_tensor(out=dv, in0=a3[:, par::2, :], in1=b3[:, par::2, :], op=ALU.add),
                              [tak, tbk] + wk, wk)
                    else:
                        E(lambda e: e.tensor_tensor(out=b4[:, :, 0, :], in0=s4[:, :, 1, :], in1=sb_, op=ALU.mult), rk + ["sinT"], [tbk])
                        E(lambda e: e.tensor_tensor(out=b4[:, :, 1, :], in0=s4[:, :, 0, :], in1=sb_, op=ALU.mult), rk + ["sinT", tbk], [tbk])
                        E(lambda e: e.tensor_tensor(out=dst[:, 0:32], in0=ta[:, 0:32], in1=tb[:, 0:32], op=ALU.subtract), [tak, tbk] + wk, wk)
                        E(lambda e: e.tensor_tensor(out=dst[:, 32:64], in0=ta[:, 32:64], in1=tb[:, 32:64], op=ALU.add), [tak, tbk] + wk, wk)

                qz = qf[:, 512:1024].rearrange("p (i r) -> p i r", r=256)
                rope(DVE, t1[:, 512:768], 4, [qz[:, :, 0:64], qz[:, :, 192:256]], ["t1"], ["qf"], ta, tb, "ta", "tb")
                POOL(lambda e: e.tensor_tensor(out=kf[:, 0:512], in0=sqk, in1=gk_b[:, 0:512], op=ALU.mult), ["sqk", "const"], ["kf"])
                POOL(lambda e: e.tensor_tensor(out=krs, in0=krs, in1=gk_b[:, 512:576], op=ALU.mult), ["krs", "const"], ["krs"])
                rope(POOL, krs, 1, krr, ["krs"], ["krr"], tak_, tbk_, "tak", "tbk")
                POOL(lambda e, rk_=rk_: e.tensor_tensor(out=kf[:, 512:768].rearrange("p (h j) -> p h j", j=64),
                                                        in0=krr.unsqueeze(1).to_broadcast([128, 4, 64]),
                                                        in1=rk_.unsqueeze(2).to_broadcast([128, 4, 64]), op=ALU.mult),
                     ["krr", skey(20), "kf"], ["kf"])
                def mk_tr(i=i, its=its):
                    def pe_part():
                        for m in range(8):
                            tp(bankbf(7)[:, m * 128:(m + 1) * 128], qf[:, m * 128:(m + 1) * 128], ["qf"], [bk(7)])
                        for m in range(6):
                            tp(bankbf(0)[:, m * 128:(m + 1) * 128], kf[:, m * 128:(m + 1) * 128], ["kf"], [bk(0)])

                    def act_part():
                        ACT(lambda e: e.activation(out=hT[:, :, its], in_=bankbf(7).rearrange("p (m t) -> p m t", t=128), func=AF.Copy),
                            [bk(7)], [("hT", i)])
                        ACT(lambda e: e.activation(out=KT[:, :, its], in_=bankbf(0)[:, 0:768].rearrange("p (m t) -> p m t", t=128), func=AF.Copy),
                            [bk(0)], [("KT", i)])
                    return pe_part, act_part
                pending_tr.append(mk_tr())
        for pe_part, act_part in pending_tr:
            pe_part()
            act_part()
        pending_tr = []
        if b == 0:
            tap("QT", hT[:, 0:6, :], [("hT", i) for i in range(NT)])
            tap("KT", KT, [("KT", i) for i in range(NT)])
            tap("Vaug", Vaug, [("Vaug", i) for i in range(NT)] + ["Vones"])
        S.barrier()
        R = Bump(arena, SMARK, ARENA_BYTES)
        PT = [R.take([512], BF16) for _ in range(3)]
        obuf = R.take([4, 512], F32)
        obn = [R.take([512], BF16) for _ in range(2)]
        st4 = R.take([64], F32)
        junk = R.take([512], BF16)
        QT = hT
        it = 0
        npt = 0
        for qb in range(4):
            qs = slice(qb * 512, (qb + 1) * 512)
            qkeys = [("hT", 4 * qb + j) for j in range(4)]
            for h in range(4):
                ko = 2 + (it % 2) * 2
                it += 1
                DVE(lambda e, ko=ko: e.memset(pp[ko // 2][:, :], 0.0), [], [bk(ko), bk(ko + 1)])
                rp = slice((h % 2) * 64, (h % 2) * 64 + 64)
                rc = 4 + h // 2
                def qk(kc):
                    ksl = slice(kc * 128, (kc + 1) * 128)
                    ks_ = kc % 2
                    mm(bank(ks_), KT[:, h, ksl], QT[:, h, qs], True, False, [("KT", kc)] + qkeys, [bk(ks_)])
                    mm(bank(ks_), KT[:, rc, ksl], QT[:, 4 + h, qs], False, True, [("KT", kc)] + qkeys, [bk(ks_)])

                qk(0)
                for kc in range(NT):
                    ks_ = kc % 2
                    ps_ = npt % 3
                    npt += 1
                    ACT(lambda e, ks_=ks_, ps_=ps_: e.activation(out=PT[ps_], in_=bank(ks_), func=AF.Exp), [bk(ks_)], [("PT", ps_)])
                    if kc + 1 < NT:
                        qk(kc + 1)
                    for j in range(4):
                        ob = ko + j // 2
                        mm(bank(ob)[:, (j % 2) * 256:(j % 2) * 256 + 129], PT[ps_][:, j * 128:(j + 1) * 128], Vaug[:, kc, h, 0:129],
                           False, False, [("PT", ps_), ("Vaug", kc), "Vones"], [bk(ob)], skip=True)
                for j in range(4):
                    ob = ko + j // 2
                    o0 = (j % 2) * 256
                    c0 = ((it * 4 + j) % 16) * 2
                    DVE(lambda e, ob=ob, o0=o0, c0=c0: e.reciprocal(out=st4[:, c0:c0 + 1], in_=bank(ob)[:, o0 + 128:o0 + 129]),
                        [bk(ob)], [("st4", c0)])
                    DVE(lambda e, ob=ob, o0=o0, c0=c0, j=j, h=h: e.tensor_scalar(out=obuf[:, j, h * 128:(h + 1) * 128],
                                                                                 in0=bank(ob)[:, o0:o0 + 128], scalar1=st4[:, c0:c0 + 1],
                                                                                 scalar2=None, op0=ALU.mult),
                        [bk(ob), ("st4", c0)], [("obuf", j)])
            for j in range(4):
                i = qb * 4 + j
                c0 = 32 + (i % 8) * 4
                sj = i % 2
                ACT(lambda e, j=j, c0=c0: e.activation(out=junk[:, 0:512], in_=obuf[:, j, :], func=AF.Square, accum_out=st4[:, c0:c0 + 1]),
                    [("obuf", j)], ["junk", ("st4", c0)])
                DVE(lambda e, c0=c0: e.tensor_scalar(out=st4[:, c0 + 1:c0 + 2], in0=st4[:, c0:c0 + 1], scalar1=1.0 / 512, scalar2=EPS,
                                                     op0=ALU.mult, op1=ALU.add), [("st4", c0)], [("st4", c0 + 1)])
                POOL(lambda e, c0=c0: e.tensor_tensor(out=st4[:, c0 + 2:c0 + 3], in0=st4[:, c0 + 1:c0 + 2], in1=nhalf, op=ALU.pow),
                     [("st4", c0 + 1), "nhalf"], [("st4", c0 + 2)])
                DVE(lambda e, j=j, c0=c0, sj=sj: e.scalar_tensor_tensor(out=obn[sj], in0=obuf[:, j, :], scalar=st4[:, c0 + 2:c0 + 3],
                                                                        in1=gmo_b, op0=ALU.mult, op1=ALU.mult),
                    [("obuf", j), ("st4", c0 + 2), "const"], [("obn", sj)])
                kt = 6 + i % 2
                for m in range(4):
                    tp(bankbf(kt)[:, m * 128:(m + 1) * 128], obn[sj][:, m * 128:(m + 1) * 128], [("obn", sj)], [bk(kt)])
                ACT(lambda e, kt=kt, i=i: e.activation(out=oT[:, 4:8, i * 128:(i + 1) * 128],
                                                       in_=bankbf(kt)[:, 0:512].rearrange("p (m t) -> p m t", t=128), func=AF.Copy),
                    [bk(kt)], [("oT", 4, i)])
        tap("oT", oT, [("oT", 4, i) for i in range(NT)])
        S.barrier()
        R = Bump(arena, RMARK, ARENA_BYTES)
        W_o = R.take([8, D], BF16)
        xin = [R.take([D], F32) for _ in range(2)]
        x1o = [R.take([D], F32) for _ in range(2)]
        for kc in range(8):
            S.dma("pool", lambda e, kc=kc: e.dma_start(out=W_o[:, kc, :], in_=w_out[kc * 128:(kc + 1) * 128, :]), "w_o", writes=["W_o"])
        for i in range(NT):
            sl = i % 2
            its = slice(i * 128, (i + 1) * 128)
            S.dma("sp", lambda e, sl=sl, i=i: e.dma_start(out=xin[sl], in_=x[b, i * 128:(i + 1) * 128, :]), f"xin{sl}",
                  writes=[("xin", sl)])
            kp = (i % 2) * 2
            for half in range(2):
                for m in range(8):
                    mm(bank(kp + half), oT[:, m, its], W_o[:, m, half * 512:(half + 1) * 512], m == 0, m == 7, ["W_o"], [bk(kp + half)])
            DVE(lambda e, sl=sl, kp=kp: e.tensor_tensor(out=x1o[sl], in0=pp[kp // 2][:, :], in1=xin[sl], op=ALU.add),
                [bk(kp), bk(kp + 1), ("xin", sl), ("x1o", sl)], [("x1o", sl)])
            S.dma("sp", lambda e, sl=sl, i=i: e.dma_start(out=x1s[b, i * 128:(i + 1) * 128, :], in_=x1o[sl]), f"x1o{sl}",
                  reads=[("x1o", sl)])
        S.barrier()

    Bm = Bump(arena, MARK0, ARENA_BYTES)
    W_up = Bm.take([8, DFF], BF16)
    W_dn = Bm.take([32, D], BF16)
    gffn_b = Bm.take([D], F32)
    TB = 256
    NJ = TB // 128
    x1t = [[Bm.take([D], F32) for _ in range(NJ)] for _ in range(2)]
    hbf = [Bm.take([D], BF16) for _ in range(2)]
    h2T = Bm.take([8, TB], BF16)
    aT = Bm.take([32, TB], BF16)
    rl = [Bm.take([512], F32) for _ in range(2)]
    yo = [Bm.take([D], F32) for _ in range(2)]
    junk = Bm.take([D], BF16)
    st5 = Bm.take([64], F32)
    S.dma("sp", lambda e: e.dma_start(out=gffn_b, in_=g_ffn.partition_broadcast(128)), "const2", writes=["gffn"])
    for kc in range(8):
        for q4 in range(4):
            S.dma("pool", lambda e, kc=kc, q4=q4: e.dma_start(out=W_up[:, kc, q4 * 1024:(q4 + 1) * 1024],
                                                             in_=w_up[kc * 128:(kc + 1) * 128, q4 * 1024:(q4 + 1) * 1024]),
                  "w_up", writes=["W_up"])
    for c in range(32):
        S.dma("pool", lambda e, c=c: e.dma_start(out=W_dn[:, c, :], in_=w_down[c * 128:(c + 1) * 128, :]), "w_dn", writes=["W_dn"])
    x1f = x1s.rearrange("b s d -> (b s) d")
    yf = y.rearrange("b s d -> (b s) d")
    nblk = nseq * SEQ // TB
    nyo = 0
    for blk in range(nblk):
        xs = blk % 2
        for j in range(NJ):
            t = blk * NJ + j
            S.dma("sp", lambda e, xs=xs, j=j, t=t: e.dma_start(out=x1t[xs][j], in_=x1f[t * 128:(t + 1) * 128, :]), f"x1t{xs}{j}",
                  writes=[("x1t", xs, j)])
            c0 = (t % 8) * 4
            sl = t % 2
            ACT(lambda e, xs=xs, j=j, c0=c0: e.activation(out=junk, in_=x1t[xs][j], func=AF.Square, accum_out=st5[:, c0:c0 + 1]),
                [("x1t", xs, j)], ["junk", ("st5", c0)])
            DVE(lambda e, c0=c0: e.tensor_scalar(out=st5[:, c0 + 1:c0 + 2], in0=st5[:, c0:c0 + 1], scalar1=1.0 / D, scalar2=EPS,
                                                 op0=ALU.mult, op1=ALU.add), [("st5", c0)], [("st5", c0 + 1)])
            POOL(lambda e, c0=c0: e.tensor_tensor(out=st5[:, c0 + 2:c0 + 3], in0=st5[:, c0 + 1:c0 + 2], in1=nhalf, op=ALU.pow),
                 [("st5", c0 + 1), "nhalf"], [("st5", c0 + 2)])
            DVE(lambda e, xs=xs, j=j, c0=c0, sl=sl: e.scalar_tensor_tensor(out=hbf[sl], in0=x1t[xs][j], scalar=st5[:, c0 + 2:c0 + 3],
                                                                           in1=gffn_b, op0=ALU.mult, op1=ALU.mult),
                [("x1t", xs, j), ("st5", c0 + 2), "gffn"], [("hbf", sl)])
            for kc in range(8):
                tp(bankbf(0)[:, kc * 128:(kc + 1) * 128], hbf[sl][:, kc * 128:(kc + 1) * 128], [("hbf", sl)], [bk(0)])
            ACT(lambda e, j=j: e.activation(out=h2T[:, :, j * 128:(j + 1) * 128], in_=bankbf(0).rearrange("p (k t) -> p k t", t=128),
                                            func=AF.Copy), [bk(0)], [("h2T", j)])
        hk2 = [("h2T", j) for j in range(NJ)] + ["W_up"]
        for cp in range(16):
            ku = 1 + cp % 2
            for c in range(2):
                cc = cp * 2 + c
                for kc in range(8):
                    mm(bank(ku)[:, c * TB:(c + 1) * TB], W_up[:, kc, cc * 128:(cc + 1) * 128], h2T[:, kc, :], kc == 0, kc == 7, hk2, [bk(ku)])
            rs = cp % 2
            ACT(lambda e, ku=ku, rs=rs: e.activation(out=rl[rs], in_=bank(ku), func=AF.Relu), [bk(ku)], [("rl", rs)])
            POOL(lambda e, rs=rs, cp=cp: e.tensor_tensor(out=aT[:, 2 * cp:2 * cp + 2, :], in0=rl[rs].rearrange("p (c t) -> p c t", t=TB),
                                                         in1=rl[rs].rearrange("p (c t) -> p c t", t=TB), op=ALU.mult),
                 [("rl", rs)], [("aT", cp)])
        for j in range(NJ):
            t = blk * NJ + j
            kd = 4 + (t % 2) * 2
            for half in range(2):
                for c in range(32):
                    mm(bank(kd + half), aT[:, c, j * 128:(j + 1) * 128], W_dn[:, c, half * 512:(half + 1) * 512], c == 0, c == 31,
                       [("aT", c // 2), "W_dn"], [bk(kd + half)])
            ys = nyo % 2
            nyo += 1
            DVE(lambda e, ys=ys, kd=kd, xs=xs, j=j: e.tensor_tensor(out=yo[ys], in0=pp[kd // 2][:, :], in1=x1t[xs][j], op=ALU.add),
                [bk(kd), bk(kd + 1), ("x1t", xs, j), ("yo", ys)], [("yo", ys)])
            S.dma("sp", lambda e, ys=ys, t=t: e.dma_start(out=yf[t * 128:(t + 1) * 128, :], in_=yo[ys]), f"yo{ys}", reads=[("yo", ys)])
    S.barrier()

    sems = {n: es.enter_context(nc.semaphore(n)) for n in sorted(S.sem_names)}
    with nc.Block() as block:
        S.emit(block, sems)
    es.close()
    return nc, dbg_out


def _prep_inputs(inputs, nseq, ncores):
    f32 = np.float32
    g = lambda k: np.ascontiguousarray(np.asarray(inputs[k]))
    wq_ = g("w_q_up")[0].reshape(384, 4, 192)
    wq_p = np.concatenate([wq_[:, :, :128].reshape(384, 512), wq_[:, :, 128:].reshape(384, 256)], axis=1)
    wkv_ = g("w_kv_up")[0].reshape(256, 4, 256)
    wkv_p = np.concatenate([wkv_[:, :, :128].reshape(256, 512), wkv_[:, :, 128:].reshape(256, 512)], axis=1)
    invf = (10000.0 ** (-(np.arange(0, 64, 2, dtype=f32)) / f32(64))).astype(f32).reshape(1, 32)
    shared = {
        "g_mix": g("g_mix_norm").reshape(1, D), "w_in": g("w_in")[0], "lb_param": g("lb_param")[:, 0:2, :],
        "g_hg": g("g_hgrn_out")[0], "g_cq": g("g_cq").reshape(1, 384), "wq": np.ascontiguousarray(wq_p),
        "g_ckv": g("g_ckv").reshape(1, 256), "wkv": np.ascontiguousarray(wkv_p), "g_q": g("g_q_norm").reshape(1, 192),
        "g_k": g("g_k_norm").reshape(1, 192), "g_mo": g("g_mla_out").reshape(1, 512), "w_out": g("w_out")[0],
        "g_ffn": g("g_ffn_norm").reshape(1, D), "w_up": g("w_up")[0], "w_down": g("w_down")[0], "invf": invf,
    }
    x = g("x")
    pos = g("positions").astype(np.int32)
    maps = []
    for c in range(ncores):
        m = dict(shared)
        m["x"] = np.ascontiguousarray(x[c * nseq:(c + 1) * nseq])
        m["pos"] = np.ascontiguousarray(pos[c * nseq:(c + 1) * nseq])
        maps.append(m)
    return maps


def kernel(**inputs):
    nseq = 4
    nc, _ = build(nseq)
    maps = _prep_inputs(inputs, nseq, NCORES)
    res = run_bass_kernel_spmd(nc, maps, core_ids=list(range(NCORES)))
    out = np.concatenate([np.asarray(r["y"]) for r in res.results], axis=0)
    return out.astype(np.float32, copy=False)
```

```python
import numpy as np
from contextlib import ExitStack
import concourse.bass as bass
import concourse.mybir as mybir
from concourse.bass_utils import run_bass_kernel_spmd

F32 = mybir.dt.float32
BF16 = mybir.dt.bfloat16
I32 = mybir.dt.int32
AF = mybir.ActivationFunctionType
ALU = mybir.AluOpType
AX = mybir.AxisListType

NCORES = 8
SEQ = 2048
NT = SEQ // 128
D = 1024
DIN = 3264
DFF = 4096
EPS = 1e-6
PI = float(np.pi)
ARENA_BYTES = 212480

ENGS = ("pe", "act", "dve", "pool", "sp")


def _cody_waite():
    two_pi = 2.0 * np.pi
    c1 = 6.28125
    r1 = two_pi - c1
    m, e = np.frexp(r1)
    c2 = float(np.ldexp(np.round(m * 2 ** 11) / 2 ** 11, e))
    c3 = float(np.float32(two_pi - c1 - c2))
    return c1, c2, c3


CW1, CW2, CW3 = _cody_waite()


class _Rec:
    def __getattr__(self, name):
        def f(*a, **k):
            self.call = (name, a, k)
            return self
        return f


class Sched:
    def __init__(self):
        self.q = {e: [] for e in ENGS}
        self.cnt = {e: 0 for e in ENGS}
        self.seen = {e: {} for e in ENGS}
        self.bufs = {}
        self.dma_tot = {}
        self.sem_names = set(ENGS)

    def _st(self, k):
        st = self.bufs.get(k)
        if st is None:
            st = self.bufs[k] = {"w": None, "r": {}}
        return st

    def _deps(self, eng, reads, writes):
        toks = []
        for k in reads:
            st = self._st(k)
            if st["w"] is not None:
                toks.append(st["w"])
        for k in writes:
            st = self._st(k)
            if st["w"] is not None:
                toks.append(st["w"])
            toks.extend(st["r"].items())
        waits = {}
        for (s, v) in toks:
            if s == "pe" and eng == "pe":
                continue
            if self.seen[eng].get(s, 0) >= v:
                continue
            if waits.get(s, 0) < v:
                waits[s] = v
        for s, v in waits.items():
            self.seen[eng][s] = v
        return list(waits.items())

    def _commit(self, tok, reads, writes):
        for k in reads:
            r = self._st(k)["r"]
            if r.get(tok[0], 0) < tok[1]:
                r[tok[0]] = tok[1]
        for k in writes:
            st = self._st(k)
            st["w"] = tok
            st["r"] = {}

    def op(self, eng, fn, reads=(), writes=()):
        rec = _Rec()
        fn(rec)
        waits = self._deps(eng, reads, writes)
        self.cnt[eng] += 1
        tok = (eng, self.cnt[eng])
        self.q[eng].append((rec.call, waits, (eng, 1)))
        self._commit(tok, reads, writes)

    def dma(self, eng, fn, sem, reads=(), writes=()):
        rec = _Rec()
        fn(rec)
        self.sem_names.add(sem)
        waits = self._deps(eng, reads, writes)
        self.dma_tot[sem] = self.dma_tot.get(sem, 0) + 16
        tok = (sem, self.dma_tot[sem])
        self.q[eng].append((rec.call, waits, (sem, 16)))
        self._commit(tok, reads, writes)

    def barrier(self):
        for e in ENGS:
            waits = []
            for e2 in ENGS:
                if self.cnt[e2] > self.seen[e].get(e2, 0):
                    waits.append((e2, self.cnt[e2]))
                    self.seen[e][e2] = self.cnt[e2]
            for s, tot in self.dma_tot.items():
                if tot > self.seen[e].get(s, 0):
                    waits.append((s, tot))
                    self.seen[e][s] = tot
            self.q[e].append((None, waits, None))
        self.bufs = {}

    def emit(self, block, sems):
        handles = {"pe": block.tensor, "act": block.scalar, "dve": block.vector,
                   "pool": block.gpsimd, "sp": block.sync}

        def mk(e):
            ops = self.q[e]

            def body(engine):
                for fn, waits, inc in ops:
                    for s, v in waits:
                        engine.wait_ge(sems[s], v)
                    if fn is not None:
                        name, a, k = fn
                        getattr(engine, name)(*a, **k).then_inc(sems[inc[0]], inc[1])
            return body

        for e in ENGS:
            handles[e](mk(e))


def _dsize(dt):
    return 2 if dt == BF16 else 4


class Bump:
    def __init__(self, arena, start, end):
        self.arena, self.cur, self.end = arena, start, end

    def take(self, shape, dt):
        n = int(np.prod(shape)) * _dsize(dt)
        off = (self.cur + 63) // 64 * 64
        n4 = (n + 3) // 4 * 4
        self.cur = off + n4
        assert self.cur <= self.end, (self.cur, self.end)
        ap = self.arena[:, off // 4:(off + n4) // 4]
        if dt != F32:
            ap = ap.bitcast(dt)
        if n4 != n:
            ap = ap[:, 0:int(np.prod(shape))]
        if len(shape) == 2:
            ap = ap.rearrange("p (a b) -> p a b", b=shape[1])
        elif len(shape) == 3:
            ap = ap.rearrange("p (a b c) -> p a b c", b=shape[1], c=shape[2])
        return ap


def build(nseq=4, dbg=None):
    nc = bass.Bass("TRN2", target_bir_lowering=False)
    S = Sched()
    dbg_out = {}

    def din(name, shape, dt=F32):
        return nc.dram_tensor(name, list(shape), dt, kind="ExternalInput").ap()

    x = din("x", [nseq, SEQ, D])
    pos = din("pos", [nseq, SEQ], I32)
    g_mix = din("g_mix", [1, D])
    w_in = din("w_in", [D, DIN])
    lb_param = din("lb_param", [2, 2, 512])
    g_hg = din("g_hg", [4, 128])
    g_cq = din("g_cq", [1, 384])
    wq = din("wq", [384, 768])
    g_ckv = din("g_ckv", [1, 256])
    wkv = din("wkv", [256, 1024])
    g_q = din("g_q", [1, 192])
    g_k = din("g_k", [1, 192])
    g_mo = din("g_mo", [1, 512])
    w_out = din("w_out", [D, D])
    g_ffn = din("g_ffn", [1, D])
    w_up = din("w_up", [D, DFF])
    w_down = din("w_down", [DFF, D])
    invf = din("invf", [1, 32])
    y = nc.dram_tensor("y", [nseq, SEQ, D], F32, kind="ExternalOutput").ap()
    x1s = nc.dram_tensor("x1s", [nseq, SEQ, D], F32).ap()

    es = ExitStack()
    arena = es.enter_context(nc.sbuf_tensor("arena", [128, ARENA_BYTES // 4], F32))[:]
    pp = [es.enter_context(nc.psum_tensor(f"pp{i}", [128, 1024], F32)) for i in range(4)]

    def bank(k):
        return pp[k // 2][:, (k % 2) * 512:(k % 2) * 512 + 512]

    def bankbf(k):
        return bank(k).bitcast(BF16)

    def bk(k):
        return ("bank", k)

    def PE(fn, r, w):
        S.op("pe", fn, r, w)

    def ACT(fn, r, w):
        S.op("act", fn, r, w)

    def DVE(fn, r, w):
        S.op("dve", fn, r, w)

    def POOL(fn, r, w):
        S.op("pool", fn, r, w)

    def mm(out, lhsT, rhs, start, stop, r, w, skip=False):
        if skip:
            PE(lambda e: e.matmul(out, lhsT=lhsT, rhs=rhs, start=start, stop=stop, skip_group_check=True), r, w)
        else:
            PE(lambda e: e.matmul(out, lhsT=lhsT, rhs=rhs, start=start, stop=stop), r, w)

    def tp(out, in_, r, w):
        PE(lambda e: e.transpose(out=out, in_=in_, identity=ident), list(r) + ["ident"], w)

    def tap(name, ap, key):
        if dbg is None or name not in dbg:
            return
        shp = list(ap.shape)
        t = nc.dram_tensor("dbg_" + name, shp, ap.dtype, kind="ExternalOutput").ap()
        dbg_out[name] = t
        S.dma("sp", lambda e: e.dma_start(out=t, in_=ap), "dbg", reads=key)

    P = Bump(arena, 0, ARENA_BYTES)
    ident = P.take([128], BF16)
    maskfb = P.take([256], I32)
    cols = P.take([64], F32)
    ones_b = P.take([2], BF16)
    nhalf = P.take([1], F32)
    ones_f = P.take([512], F32)
    MARK0 = P.cur
    C_GCQ, C_GCKV, C_GHG, C_LB, C_LNOML, C_LBP = 0, 3, 5, 9, 17, 25

    A = Bump(arena, MARK0, ARENA_BYTES)
    W_in = A.take([8, DIN], BF16)
    gmix_b = A.take([D], F32)
    gq_b = A.take([768], F32)
    gk_b = A.take([768], F32)
    gmo_b = A.take([512], F32)
    invf_b = A.take([32], F32)
    hT = A.take([8, SEQ], BF16)
    oT = A.take([8, SEQ], BF16)
    RMARK = A.cur

    R0 = Bump(arena, RMARK, ARENA_BYTES)
    identf = R0.take([128], F32)
    lbt = R0.take([8], F32)

    def cdma(out, in_, slow=False):
        if slow:
            S.dma("sp", lambda e: e.dma_start(out=out, in_=in_, allow_slow_non_contiguous=True), "const", writes=["const"])
        else:
            S.dma("sp", lambda e: e.dma_start(out=out, in_=in_), "const", writes=["const"])

    cdma(gmix_b, g_mix.partition_broadcast(128))
    for h in range(4):
        cdma(gq_b[:, h * 128:(h + 1) * 128], g_q[:, 0:128].partition_broadcast(128))
        cdma(gq_b[:, 512 + h * 64:512 + (h + 1) * 64], g_q[:, 128:192].partition_broadcast(128))
        cdma(gk_b[:, h * 128:(h + 1) * 128], g_k[:, 0:128].partition_broadcast(128))
        cdma(gk_b[:, 512 + h * 64:512 + (h + 1) * 64], g_k[:, 128:192].partition_broadcast(128))
    cdma(gmo_b, g_mo.partition_broadcast(128))
    cdma(invf_b, invf.partition_broadcast(128))
    cdma(cols[:, C_GCQ:C_GCQ + 3], g_cq[0].rearrange("(m p) -> p m", p=128), slow=True)
    cdma(cols[:, C_GCKV:C_GCKV + 2], g_ckv[0].rearrange("(m p) -> p m", p=128), slow=True)
    cdma(cols[:, C_GHG:C_GHG + 4], g_hg.rearrange("h e -> e h"), slow=True)
    for d_ in range(2):
        for s_ in range(2):
            o = C_LBP + (d_ * 2 + s_) * 4
            cdma(cols[:, o:o + 4], lb_param[d_, s_].rearrange("(h p) -> p h", p=128), slow=True)
    for kc in range(8):
        S.dma("pool", lambda e, kc=kc: e.dma_start(out=W_in[:, kc, :], in_=w_in[kc * 128:(kc + 1) * 128, :], max_dma_last_dim=4096),
              "w_in", writes=["W_in"])

    POOL(lambda e: e.memset(identf, 0.0), [], ["identf"])
    POOL(lambda e: e.affine_select(out=identf, in_=identf, pattern=[[-1, 128]], compare_op=ALU.not_equal, fill=1.0,
                                   base=0, channel_multiplier=1), ["identf"], ["identf"])
    DVE(lambda e: e.tensor_copy(out=ident, in_=identf), ["identf"], ["ident"])
    POOL(lambda e: e.iota(maskfb[:, 0:128], pattern=[[1, 128]], base=0, channel_multiplier=-1), [], ["maskfb"])
    POOL(lambda e: e.iota(maskfb[:, 128:256], pattern=[[-1, 128]], base=0, channel_multiplier=1), ["maskfb"], ["maskfb"])
    DVE(lambda e: e.tensor_single_scalar(out=maskfb, in_=maskfb, scalar=0, op=ALU.is_ge), ["maskfb"], ["maskfb"])
    POOL(lambda e: e.memset(ones_b, 1.0), [], ["ones_b"])
    POOL(lambda e: e.memset(nhalf, -0.5), [], ["nhalf"])
    POOL(lambda e: e.memset(ones_f, 1.0), [], ["ones_f"])
    DVE(lambda e: e.tensor_scalar(out=gq_b, in0=gq_b, scalar1=float(192 ** -0.5), scalar2=None, op0=ALU.mult), ["const"], ["const"])
    lbp = cols[:, C_LBP:C_LBP + 16].rearrange("p (d s h) -> p d s h", s=2, h=4)
    lbv = cols[:, C_LB:C_LB + 8]
    DVE(lambda e: e.tensor_tensor(out=lbt.rearrange("p (d h) -> p d h", h=4), in0=lbp[:, :, 1, :], in1=lbp[:, :, 0, :], op=ALU.subtract),
        ["const"], ["lbt"])
    ACT(lambda e: e.activation(out=lbt, in_=lbt, func=AF.Exp), ["lbt"], ["lbt"])
    DVE(lambda e: e.tensor_scalar(out=lbt, in0=lbt, scalar1=1.0, scalar2=None, op0=ALU.add), ["lbt"], ["lbt"])
    DVE(lambda e: e.reciprocal(out=lbv, in_=lbt), ["lbt", "const"], ["const"])
    ACT(lambda e: e.activation(out=cols[:, C_LNOML:C_LNOML + 8], in_=lbv, func=AF.Ln, scale=-1.0, bias=1.0), ["const"], ["const"])
    S.barrier()

    for b in range(nseq):
        R = Bump(arena, RMARK, ARENA_BYTES)
        V_all = R.take([NT, 512], BF16)
        XMARK = R.cur
        xin = [R.take([D], F32) for _ in range(2)]
        hbf = [R.take([D], BF16) for _ in range(2)]
        junk = R.take([D], BF16)
        st = R.take([64], F32)
        for i in range(NT):
            sl = i % 2
            S.dma("sp", lambda e, sl=sl, i=i: e.dma_start(out=xin[sl], in_=x[b, i * 128:(i + 1) * 128, :]), f"xin{sl}",
                  writes=[("xin", sl)])
            c0 = (i % 8) * 4
            ACT(lambda e, sl=sl, c0=c0: e.activation(out=junk, in_=xin[sl], func=AF.Square, accum_out=st[:, c0:c0 + 1]),
                [("xin", sl)], ["junk", ("st", c0)])
            DVE(lambda e, c0=c0: e.tensor_scalar(out=st[:, c0 + 1:c0 + 2], in0=st[:, c0:c0 + 1], scalar1=1.0 / D, scalar2=EPS,
                                                 op0=ALU.mult, op1=ALU.add), [("st", c0)], [("st", c0 + 1)])
            POOL(lambda e, c0=c0: e.tensor_tensor(out=st[:, c0 + 2:c0 + 3], in0=st[:, c0 + 1:c0 + 2], in1=nhalf, op=ALU.pow),
                 [("st", c0 + 1), "nhalf"], [("st", c0 + 2)])
            DVE(lambda e, sl=sl, c0=c0: e.scalar_tensor_tensor(out=hbf[sl], in0=xin[sl], scalar=st[:, c0 + 2:c0 + 3], in1=gmix_b,
                                                               op0=ALU.mult, op1=ALU.mult),
                [("xin", sl), ("st", c0 + 2), "const"], [("hbf", sl)])
            k = i % 2
            for kc in range(8):
                tp(bankbf(k)[:, kc * 128:(kc + 1) * 128], hbf[sl][:, kc * 128:(kc + 1) * 128], [("hbf", sl)], [bk(k)])
            ACT(lambda e, k=k, i=i: e.activation(out=hT[:, :, i * 128:(i + 1) * 128],
                                                 in_=bankbf(k).rearrange("p (k t) -> p k t", t=128), func=AF.Copy),
                [bk(k)], [("hT", i)])
        tap("hT", hT, [("hT", i) for i in range(NT)])
        for i in range(NT):
            k = 2 + i % 2
            for kc in range(8):
                mm(bank(k), hT[:, kc, i * 128:(i + 1) * 128], W_in[:, kc, 1536:2048], kc == 0, kc == 7,
                   [("hT", i), "W_in"], [bk(k)])
            DVE(lambda e, k=k, i=i: e.tensor_copy(out=V_all[:, i, :], in_=bank(k)), [bk(k)], [("V", i)])
        S.barrier()
        R = Bump(arena, XMARK, ARENA_BYTES)
        sgTs = [R.take([SEQ], BF16) for _ in range(2)]
        TS = [[R.take([512], F32) for _ in range(5)] + [R.take([516], F32)] for _ in range(2)]
        QK = {(d_, w_): R.take([SEQ], BF16) for d_ in range(2) for w_ in "qk"}
        Zbf = [R.take([NT, 128], BF16) for _ in range(2)]
        Y = [[R.take([128], F32) for _ in range(2)] for _ in range(2)]
        sqo = [TS[s_][0] for s_ in range(2)]
        onb = [TS[s_][1].bitcast(BF16)[:, 0:512] for s_ in range(2)]
        atm = [TS[s_][2].bitcast(BF16).rearrange("p (d n) -> p d n", d=2) for s_ in range(2)]
        ktok = [TS[s_][4].bitcast(BF16)[:, 0:512] for s_ in range(2)]
        Xs = [[TS[i_][3][:, d_ * 128:(d_ + 1) * 128] for i_ in range(2)] for d_ in range(2)]
        Rall = [R.take([NT], F32) for _ in range(2)]
        Eall = [R.take([NT], F32) for _ in range(2)]
        dR = [R.take([3, NT], F32) for _ in range(2)]
        efac = [R.take([3, NT], F32) for _ in range(2)]
        tot = R.take([2, 4], F32)
        carr = R.take([2, 4], F32)
        st2 = R.take([64], F32)
        for s_ in range(2):
            POOL(lambda e: e.memset(TS[s_][5][:, 0:1], 0.0), [], [("T", s_, 5)])
        POOL(lambda e: e.memset(carr, 0.0), [], ["carr"])
        for h in range(4):
            def gates(hh):
                for blk in range(4):
                    tl = slice(blk * 512, (blk + 1) * 512)
                    hk = [("hT", 4 * blk + j) for j in range(4)] + ["W_in"]
                    kg = 6 + blk % 2
                    for kc in range(8):
                        mm(bank(kg), W_in[:, kc, 2048 + hh * 128:2048 + (hh + 1) * 128], hT[:, kc, tl], kc == 0, kc == 7, hk, [bk(kg)])
                    ACT(lambda e: e.activation(out=sgTs[hh % 2][:, tl], in_=bank(kg), func=AF.Silu), [bk(kg)], [("sgT", hh % 2, blk)])

            if h == 0:
                gates(0)
            sgT = sgTs[h % 2]

            def prep_mm(blk):
                tl = slice(blk * 512, (blk + 1) * 512)
                hk = [("hT", 4 * blk + j) for j in range(4)] + ["W_in"]
                for j, c0 in enumerate((0, 512, 1024)):
                    kk = (3 * blk + j) % 6
                    for kc in range(8):
                        mm(bank(kk), W_in[:, kc, c0 + h * 128:c0 + (h + 1) * 128], hT[:, kc, tl], kc == 0, kc == 7, hk, [bk(kk)])

            def front(it):
                blk, d_ = it // 2, it % 2
                T = TS[it % 2]
                tk = [("T", it % 2, j) for j in range(6)]
                kz = (3 * blk + 1 + d_) % 6
                lbc = cols[:, C_LB + d_ * 4 + h:C_LB + d_ * 4 + h + 1]
                ACT(lambda e: e.activation(out=T[0], in_=bank(kz), func=AF.Exp, scale=-1.0), [bk(kz)], [tk[0]])
                ACT(lambda e: e.activation(out=T[1], in_=T[0], func=AF.Ln, scale=lbc, bias=1.0), [tk[0], "const"], [tk[1]])
                ACT(lambda e: e.activation(out=T[2], in_=T[0], func=AF.Ln, scale=1.0, bias=1.0), [tk[0]], [tk[2]])
                DVE(lambda e: e.tensor_tensor(out=T[4], in0=bank(kz), in1=T[2], op=ALU.add), [bk(kz), tk[2]], [tk[4]])
                DVE(lambda e: e.tensor_tensor_scan(out=T[5][:, 1:513], data0=T[1], data1=T[2], initial=0.0, op0=ALU.add, op1=ALU.subtract),
                    [tk[1], tk[2], tk[5]], [tk[5]])
                DVE(lambda e: e.tensor_copy(out=tot[:, d_, blk:blk + 1], in_=T[5][:, 512:513]), [tk[5]], [("tot", d_)])
                Bx = T[5][:, 1:513] if d_ == 0 else T[5][:, 0:512]
                kB = tk[5]
                DVE(lambda e: e.tensor_copy(out=Rall[d_][:, blk * 4:(blk + 1) * 4], in_=Bx[:, 63:512:128]), [kB], [("Rall", d_)])
                eo_ = 127 if d_ == 0 else 0
                DVE(lambda e: e.tensor_copy(out=Eall[d_][:, blk * 4:(blk + 1) * 4], in_=Bx[:, eo_:512:128]), [kB], [("Eall", d_)])
                DVE(lambda e: e.tensor_tensor(out=T[0].rearrange("p (c j) -> p c j", j=128), in0=Bx.rearrange("p (c j) -> p c j", j=128),
                                              in1=Rall[d_][:, blk * 4:(blk + 1) * 4].unsqueeze(2).to_broadcast([128, 4, 128]),
                                              op=ALU.subtract), [kB, ("Rall", d_)], [tk[0]])

            def back(it):
                blk, d_ = it // 2, it % 2
                tl = slice(blk * 512, (blk + 1) * 512)
                T = TS[it % 2]
                tk = [("T", it % 2, j) for j in range(6)]
                kq = (3 * blk) % 6
                lno = cols[:, C_LNOML + d_ * 4 + h:C_LNOML + d_ * 4 + h + 1]
                sgn = 1.0 if d_ == 0 else -1.0
                ACT(lambda e: e.activation(out=T[2], in_=T[0], func=AF.Exp, scale=sgn), [tk[0]], [tk[2]])
                POOL(lambda e: e.tensor_tensor(out=T[3], in0=T[4], in1=T[0], op=(ALU.add if d_ == 0 else ALU.subtract)),
                     [tk[4], tk[0]], [tk[3]])
                ACT(lambda e: e.activation(out=QK[(d_, "k")][:, tl], in_=T[3], func=AF.Exp, scale=-1.0, bias=lno),
                    [tk[3], "const"], [("QK", d_, "k", blk)])
                DVE(lambda e: e.tensor_tensor(out=QK[(d_, "q")][:, tl], in0=bank(kq), in1=T[2], op=ALU.mult),
                    [bk(kq), tk[2]], [("QK", d_, "q", blk)])

            prep_mm(0)
            front(0)
            for it in range(8):
                if it % 2 == 0 and it // 2 + 1 < 4:
                    prep_mm(it // 2 + 1)
                if it + 1 < 8:
                    front(it + 1)
                if it == 1 and h + 1 < 4:
                    gates(h + 1)
                back(it)
            for d_ in range(2):
                DVE(lambda e: e.tensor_copy(out=carr[:, d_, 1:2], in_=tot[:, d_, 0:1]), [("tot", d_), "carr"], ["carr"])
                DVE(lambda e: e.tensor_tensor(out=carr[:, d_, 2:3], in0=carr[:, d_, 1:2], in1=tot[:, d_, 1:2], op=ALU.add),
                    [("tot", d_), "carr"], ["carr"])
                DVE(lambda e: e.tensor_tensor(out=carr[:, d_, 3:4], in0=carr[:, d_, 2:3], in1=tot[:, d_, 2:3], op=ALU.add),
                    [("tot", d_), "carr"], ["carr"])
                for arr, key in ((Rall, "Rall"), (Eall, "Eall")):
                    DVE(lambda e, arr=arr: e.tensor_tensor(out=arr[d_].rearrange("p (b c) -> p b c", c=4),
                                                           in0=arr[d_].rearrange("p (b c) -> p b c", c=4),
                                                           in1=carr[:, d_, :].unsqueeze(2).to_broadcast([128, 4, 4]), op=ALU.add),
                        [(key, d_), "carr"], [(key, d_)])
            DVE(lambda e: e.tensor_tensor(out=dR[0][:, 0, 1:16], in0=Eall[0][:, 1:16], in1=Eall[0][:, 0:15], op=ALU.subtract),
                [("Eall", 0)], [("dR", 0)])
            DVE(lambda e: e.tensor_tensor(out=dR[0][:, 2, 1:16], in0=Rall[0][:, 1:16], in1=Eall[0][:, 0:15], op=ALU.subtract),
                [("Eall", 0), ("Rall", 0), ("dR", 0)], [("dR", 0)])
            DVE(lambda e: e.memset(dR[0][:, :, 0:1], 0.0), [("dR", 0)], [("dR", 0)])
            DVE(lambda e: e.tensor_tensor(out=dR[0][:, 1, :], in0=Eall[0], in1=Rall[0], op=ALU.subtract),
                [("Eall", 0), ("Rall", 0), ("dR", 0)], [("dR", 0)])
            DVE(lambda e: e.tensor_tensor(out=dR[1][:, 0, 0:15], in0=Eall[1][:, 1:16], in1=Eall[1][:, 0:15], op=ALU.subtract),
                [("Eall", 1)], [("dR", 1)])
            DVE(lambda e: e.tensor_tensor(out=dR[1][:, 2, 0:15], in0=Eall[1][:, 1:16], in1=Rall[1][:, 0:15], op=ALU.subtract),
                [("Eall", 1), ("Rall", 1), ("dR", 1)], [("dR", 1)])
            DVE(lambda e: e.memset(dR[1][:, :, 15:16], 0.0), [("dR", 1)], [("dR", 1)])
            DVE(lambda e: e.tensor_tensor(out=dR[1][:, 1, :], in0=Rall[1], in1=Eall[1], op=ALU.subtract),
                [("Eall", 1), ("Rall", 1), ("dR", 1)], [("dR", 1)])
            for d_ in range(2):
                ACT(lambda e: e.activation(out=efac[d_], in_=dR[d_], func=AF.Exp), [("dR", d_)], [("efac", d_)])
            if h == 0 and b == 0:
                tap("Qf", QK[(0, "q")], [("QK", 0, "q", j) for j in range(4)])
                tap("Kf", QK[(0, "k")], [("QK", 0, "k", j) for j in range(4)])
            order = [list(range(15)), list(range(15, 0, -1))]
            pslot = {}

            def grp_pe(gi, d_):
                k_ = gi * 2 + d_
                cs_ = order[d_][gi * 4:gi * 4 + 4]
                kT, kP, ks = k_ % 2, 2 + k_ % 4, k_ % 2
                for j, c in enumerate(cs_):
                    tp(bankbf(kT)[:, j * 128:(j + 1) * 128], QK[(d_, "k")][:, c * 128:(c + 1) * 128], [("QK", d_, "k", c // 4)], [bk(kT)])
                nn = len(cs_) * 128
                ACT(lambda e: e.activation(out=ktok[ks][:, 0:nn], in_=bankbf(kT)[:, 0:nn], func=AF.Copy), [bk(kT)], [("T", ks, 4)])
                for j, c in enumerate(cs_):
                    mm(bank(kP)[:, j * 128:(j + 1) * 128], ktok[ks][:, j * 128:(j + 1) * 128], V_all[:, c, h * 128:(h + 1) * 128], True, True,
                       [("T", ks, 4), ("V", c)], [bk(kP)])
                    pslot[(d_, c)] = (kP, j)

            grp_pe(0, 0)
            grp_pe(0, 1)
            for gi in range(4):
                if gi + 1 < 4:
                    grp_pe(gi + 1, 0)
                    grp_pe(gi + 1, 1)
                for idx in range(gi * 4, min(gi * 4 + 4, 15)):
                    for d_ in range(2):
                        c = order[d_][idx]
                        kP_, j_ = pslot[(d_, c)]
                        pw = Xs[d_][idx % 2]
                        ACT(lambda e: e.activation(out=pw, in_=bank(kP_)[:, j_ * 128:(j_ + 1) * 128], func=AF.Copy, scale=efac[d_][:, 1, c:c + 1]),
                            [bk(kP_), ("efac", d_), ("T", idx % 2, 3)], [("Xs", d_, idx % 2)])
                    for d_ in range(2):
                        c = order[d_][idx]
                        pw = Xs[d_][idx % 2]
                        yc, yp = Y[d_][idx % 2], Y[d_][(idx + 1) % 2]
                        if idx == 0:
                            DVE(lambda e: e.tensor_copy(out=yc, in_=pw), [("Xs", d_, idx % 2)], [("Y", d_, idx % 2)])
                        else:
                            DVE(lambda e: e.scalar_tensor_tensor(out=yc, in0=yp, scalar=efac[d_][:, 0, c:c + 1], in1=pw, op0=ALU.mult, op1=ALU.add),
                                [("Y", d_, (idx + 1) % 2), ("Xs", d_, idx % 2), ("efac", d_)], [("Y", d_, idx % 2)])
                    for d_ in range(2):
                        c = order[d_][idx]
                        yc = Y[d_][idx % 2]
                        nxt = c + 1 if d_ == 0 else c - 1
                        DVE(lambda e: e.tensor_scalar(out=Zbf[d_][:, nxt, :], in0=yc, scalar1=efac[d_][:, 2, nxt:nxt + 1], scalar2=None, op0=ALU.mult),
                            [("Y", d_, idx % 2), ("efac", d_)], [("Zbf", d_, nxt)])
            for kk in range(4):
                DVE(lambda e: e.memset(bank(kk), 0.0), [], [bk(kk)])
            for s_ in range(2):
                POOL(lambda e: e.memset(atm[s_], 0.0), [], [("T", s_, 2)])

            def at_mm(g):
                ka = (g % 2) * 2
                for d_ in range(2):
                    for j in range(4):
                        c = g * 4 + j
                        q_, k_ = QK[(d_, "q")], QK[(d_, "k")]
                        o_ = j * 128
                        rk = [("QK", d_, "k", g), ("QK", d_, "q", g)]
                        if d_ == 0:
                            mm(bank(ka)[0:64, o_:o_ + 128], k_[:, c * 128:c * 128 + 64], q_[:, c * 128:(c + 1) * 128], True, True, rk, [bk(ka)])
                            mm(bank(ka)[64:128, o_ + 64:o_ + 128], k_[:, c * 128 + 64:(c + 1) * 128], q_[:, c * 128 + 64:(c + 1) * 128],
                               True, True, rk, [bk(ka)])
                        else:
                            mm(bank(ka + 1)[0:64, o_:o_ + 64], k_[:, c * 128:c * 128 + 64], q_[:, c * 128:c * 128 + 64], True, True, rk, [bk(ka + 1)])
                            mm(bank(ka + 1)[64:128, o_:o_ + 128], k_[:, c * 128 + 64:(c + 1) * 128], q_[:, c * 128:(c + 1) * 128],
                               True, True, rk, [bk(ka + 1)])

            def mask_copy(g):
                ka = (g % 2) * 2
                sa = g % 2
                for d_ in range(2):
                    mk = maskfb[:, d_ * 128:(d_ + 1) * 128].unsqueeze(1).to_broadcast([128, 4, 128])
                    DVE(lambda e: e.copy_predicated(out=atm[sa][:, d_, :].rearrange("p (c j) -> p c j", j=128), mask=mk,
                                                    data=bank(ka + d_).rearrange("p (c j) -> p c j", j=128)),
                        [bk(ka + d_), "maskfb", ("T", sa, 2)], [("T", sa, 2)])

            def o_mm(g):
                ko = 4 + g % 2
                sa = g % 2
                for j in range(4):
                    c = g * 4 + j
                    cs = slice(c * 128, (c + 1) * 128)
                    grp = []
                    if c > 0:
                        grp.append((QK[(0, "q")][:, cs], Zbf[0][:, c, :], [("QK", 0, "q", g), ("Zbf", 0, c)]))
                    if c < NT - 1:
                        grp.append((QK[(1, "q")][:, cs], Zbf[1][:, c, :], [("QK", 1, "q", g), ("Zbf", 1, c)]))
                    vv = V_all[:, c, h * 128:(h + 1) * 128]
                    grp.append((atm[sa][:, 0, j * 128:(j + 1) * 128], vv, [("T", sa, 2), ("V", c)]))
                    grp.append((atm[sa][:, 1, j * 128:(j + 1) * 128], vv, [("T", sa, 2), ("V", c)]))
                    for gi_, (l_, r_, rk) in enumerate(grp):
                        mm(bank(ko)[:, j * 128:(j + 1) * 128], l_, r_, gi_ == 0, gi_ == len(grp) - 1, rk, [bk(ko)])

            def epi_a(g):
                ko = 4 + g % 2
                sa = g % 2
                c0 = (g % 4) * 12
                ACT(lambda e: e.activation(out=sqo[sa], in_=bank(ko), func=AF.Square), [bk(ko)], [("T", sa, 0)])
                DVE(lambda e: e.tensor_reduce(out=st2[:, c0:c0 + 4], in_=sqo[sa].rearrange("p (c j) -> p c j", j=128), axis=AX.X, op=ALU.add),
                    [("T", sa, 0)], [("st2", c0)])
                DVE(lambda e: e.tensor_scalar(out=st2[:, c0 + 4:c0 + 8], in0=st2[:, c0:c0 + 4], scalar1=1.0 / 128, scalar2=EPS,
                                              op0=ALU.mult, op1=ALU.add), [("st2", c0)], [("st2", c0 + 4)])
                POOL(lambda e: e.tensor_tensor(out=st2[:, c0 + 8:c0 + 12], in0=st2[:, c0 + 4:c0 + 8], in1=nhalf.to_broadcast([128, 4]),
                                               op=ALU.pow), [("st2", c0 + 4), "nhalf"], [("st2", c0 + 8)])
                DVE(lambda e: e.tensor_tensor(out=onb[sa].rearrange("p (c j) -> p c j", j=128), in0=bank(ko).rearrange("p (c j) -> p c j", j=128),
                                              in1=st2[:, c0 + 8:c0 + 12].unsqueeze(2).to_broadcast([128, 4, 128]), op=ALU.mult),
                    [bk(ko), ("st2", c0 + 8)], [("T", sa, 1)])

            def epi_b(g):
                kt = 6 + g % 2
                sa = g % 2
                for j in range(4):
                    tp(bankbf(kt)[:, j * 128:(j + 1) * 128], onb[sa][:, j * 128:(j + 1) * 128], [("T", sa, 1)], [bk(kt)])
                gs = slice(g * 512, (g + 1) * 512)
                DVE(lambda e: e.scalar_tensor_tensor(out=oT[:, h, gs], in0=bankbf(kt)[:, 0:512], scalar=cols[:, C_GHG + h:C_GHG + h + 1],
                                                     in1=sgT[:, gs], op0=ALU.mult, op1=ALU.mult),
                    [bk(kt), "const", ("sgT", h % 2, g)], [("oT", h, g)])

            at_mm(0); mask_copy(0); at_mm(1); o_mm(0); mask_copy(1); at_mm(2); epi_a(0); o_mm(1); mask_copy(2); at_mm(3)
            epi_b(0); epi_a(1); o_mm(2); mask_copy(3); epi_b(1); epi_a(2); o_mm(3); epi_b(2); epi_a(3); epi_b(3)
        tap("oTa", oT[:, 0:4, :], [("oT", h, g) for h in range(4) for g in range(4)])
        S.barrier()
        R = Bump(arena, RMARK, ARENA_BYTES)
        KT = R.take([6, SEQ], BF16)
        Vaug = R.take([NT, 4, 130], BF16)
        SMARK = R.cur
        Wq = R.take([3, 768], BF16)
        Wkv = R.take([2, 1024], BF16)
        posi = R.take([NT], I32)
        posf = R.take([NT], F32)
        cosT = R.take([NT, 32], F32)
        sinT = R.take([NT, 32], F32)
        TMARK = R.cur
        ang = R.take([NT, 32], F32)
        kqi = R.take([NT, 32], I32)
        t_a = R.take([NT, 32], F32)
        t_b = R.take([NT, 32], F32)
        for m in range(3):
            S.dma("pool", lambda e, m=m: e.dma_start(out=Wq[:, m, :], in_=wq[m * 128:(m + 1) * 128, :]), "wq", writes=["Wq"])
        for m in range(2):
            S.dma("pool", lambda e, m=m: e.dma_start(out=Wkv[:, m, :], in_=wkv[m * 128:(m + 1) * 128, :]), "wkv", writes=["Wkv"])
        POOL(lambda e: e.memset(Vaug[:, :, :, 128:130], 1.0), [], ["Vones"])
        S.dma("sp", lambda e: e.dma_start(out=posi, in_=pos[b].rearrange("(n p) -> p n", p=128), allow_slow_non_contiguous=True),
              "posi", writes=["posi"])
        DVE(lambda e: e.tensor_copy(out=posf, in_=posi), ["posi"], ["posf"])
        DVE(lambda e: e.tensor_tensor(out=ang, in0=posf.unsqueeze(2).to_broadcast([128, NT, 32]),
                                      in1=invf_b.unsqueeze(1).to_broadcast([128, NT, 32]), op=ALU.mult), ["posf", "const"], ["ang"])
        DVE(lambda e: e.tensor_scalar(out=kqi, in0=ang, scalar1=float(1.0 / (2 * PI)), scalar2=None, op0=ALU.mult), ["ang"], ["kqi"])
        DVE(lambda e: e.tensor_copy(out=t_a, in_=kqi), ["kqi"], ["t_a"])
        C1, C2, C3 = CW1, CW2, CW3
        DVE(lambda e: e.scalar_tensor_tensor(out=t_b, in0=t_a, scalar=-C1, in1=ang, op0=ALU.mult, op1=ALU.add), ["t_a", "ang"], ["t_b"])
        DVE(lambda e: e.scalar_tensor_tensor(out=ang, in0=t_a, scalar=-C2, in1=t_b, op0=ALU.mult, op1=ALU.add), ["t_a", "t_b", "ang"], ["ang"])
        DVE(lambda e: e.scalar_tensor_tensor(out=t_b, in0=t_a, scalar=-C3, in1=ang, op0=ALU.mult, op1=ALU.add), ["t_a", "ang", "t_b"], ["t_b"])
        DVE(lambda e: e.tensor_scalar(out=ang, in0=t_b, scalar1=PI, scalar2=-PI, op0=ALU.min, op1=ALU.max), ["t_b", "ang"], ["ang"])
        ACT(lambda e: e.activation(out=sinT, in_=ang, func=AF.Sin), ["ang"], ["sinT"])
        DVE(lambda e: e.tensor_scalar(out=t_a, in0=t_b, scalar1=PI / 2, scalar2=None, op0=ALU.add), ["t_b", "t_a"], ["t_a"])
        DVE(lambda e: e.tensor_scalar(out=t_b, in0=t_a, scalar1=PI, scalar2=2 * PI, op0=ALU.is_gt, op1=ALU.mult), ["t_a", "t_b"], ["t_b"])
        DVE(lambda e: e.tensor_tensor(out=t_a, in0=t_a, in1=t_b, op=ALU.subtract), ["t_a", "t_b"], ["t_a"])
        DVE(lambda e: e.tensor_scalar(out=t_a, in0=t_a, scalar1=PI, scalar2=-PI, op0=ALU.min, op1=ALU.max), ["t_a"], ["t_a"])
        ACT(lambda e: e.activation(out=cosT, in_=t_a, func=AF.Sin), ["t_a"], ["cosT"])
        if b == 0:
            tap("cosT", cosT, ["cosT"])
            tap("sinT", sinT, ["sinT"])
        S.barrier()
        R = Bump(arena, TMARK, ARENA_BYTES)
        cT = R.take([5, 512], BF16)
        sq = R.take([5, 512], BF16)
        sqq = R.take([768], BF16)
        t1 = R.take([768], F32)
        qf = R.take([1024], BF16)
        kf = R.take([768], BF16)
        krs = R.take([64], F32)
        krr = R.take([64], F32)
        ta = R.take([256], F32)
        tb = R.take([256], F32)
        tak_ = R.take([64], F32)
        tbk_ = R.take([64], F32)
        sqk = R.take([512], F32)
        st3 = R.take([64], F32)
        junk = R.take([64], BF16)
        POOL(lambda e: e.memset(qf[:, 512:1024], 0.0), [], ["qf"])
        pending_tr = []
        for blk in range(4):
            tl = slice(blk * 512, (blk + 1) * 512)
            hk = [("hT", 4 * blk + j) for j in range(4)] + ["W_in"]
            for m in range(5):
                c0 = 2560 + m * 128
                kc_ = (m % 2) * 2
                for kc in range(8):
                    mm(bank(kc_), W_in[:, kc, c0:c0 + 128], hT[:, kc, tl], kc == 0, kc == 7, hk, [bk(kc_)])
                gcol = cols[:, C_GCQ + m:C_GCQ + m + 1]
                ACT(lambda e, m=m, gcol=gcol: e.activation(out=cT[:, m, :], in_=bank(kc_), func=AF.Copy, scale=gcol),
                    [bk(kc_), "const"], [("cT", m)])
                ACT(lambda e, m=m: e.activation(out=sq[:, m, :], in_=bank(kc_), func=AF.Square), [bk(kc_)], [("sq", m)])
            for j in range(4):
                i = blk * 4 + j
                js = slice(j * 128, (j + 1) * 128)
                its = slice(i * 128, (i + 1) * 128)
                c0 = (i % 2) * 32
                for m in range(3):
                    mm(bank(1)[:, 0:1], sq[:, m, js], ones_b[:, 0:1], m == 0, m == 2, [("sq", m), "ones_b"], [bk(1)])
                for m in range(3, 5):
                    mm(bank(1)[:, 2:3], sq[:, m, js], ones_b[:, 0:1], m == 3, m == 4, [("sq", m), "ones_b"], [bk(1)])
                for kc in range(8):
                    mm(bank(1)[:, 64:128], hT[:, kc, its], W_in[:, kc, 3200:3264], kc == 0, kc == 7, [("hT", i), "W_in"], [bk(1)])
                sc = lambda o: st3[:, c0 + o:c0 + o + 1]
                skey = lambda o: ("st3", c0 + o)
                DVE(lambda e, sc=sc: e.tensor_scalar(out=sc(0), in0=bank(1)[:, 0:1], scalar1=1.0 / 384, scalar2=EPS, op0=ALU.mult, op1=ALU.add),
                    [bk(1)], [skey(0)])
                DVE(lambda e, sc=sc: e.tensor_scalar(out=sc(1), in0=bank(1)[:, 2:3], scalar1=1.0 / 256, scalar2=EPS, op0=ALU.mult, op1=ALU.add),
                    [bk(1)], [skey(1)])
                POOL(lambda e, sc=sc: e.tensor_tensor(out=sc(2), in0=sc(0), in1=nhalf, op=ALU.pow), [skey(0), "nhalf"], [skey(2)])
                POOL(lambda e, sc=sc: e.tensor_tensor(out=sc(3), in0=sc(1), in1=nhalf, op=ALU.pow), [skey(1), "nhalf"], [skey(3)])
                for m in range(3):
                    mm(bank(3), cT[:, m, js], Wq[:, m, 0:512], m == 0, m == 2, [("cT", m), "Wq"], [bk(3)])
                for m in range(3):
                    mm(bank(4)[:, 0:256], cT[:, m, js], Wq[:, m, 512:768], m == 0, m == 2, [("cT", m), "Wq"], [bk(4)])
                for m in range(2):
                    mm(bank(5), cT[:, 3 + m, js], Wkv[:, m, 0:512], m == 0, m == 1, [("cT", 3 + m), "Wkv"], [bk(5)])
                for m in range(2):
                    mm(bank(6), cT[:, 3 + m, js], Wkv[:, m, 512:1024], m == 0, m == 1, [("cT", 3 + m), "Wkv"], [bk(6)])
                prev_tr = pending_tr
                pending_tr = []
                for pe_part, _ in prev_tr:
                    pe_part()
                DVE(lambda e: e.tensor_copy(out=krs, in_=bank(1)[:, 64:128]), [bk(1)], ["krs"])
                ACT(lambda e: e.activation(out=sqq[:, 0:512], in_=bank(3), func=AF.Square), [bk(3)], ["sqq"])
                ACT(lambda e: e.activation(out=sqq[:, 512:768], in_=bank(4)[:, 0:256], func=AF.Square), [bk(4), "sqq"], ["sqq"])
                ACT(lambda e: e.activation(out=sqk, in_=bank(5), func=AF.Square), [bk(5)], ["sqk"])
                ACT(lambda e, sc=sc: e.activation(out=junk[:, 0:64], in_=krs, func=AF.Square, accum_out=sc(13)), ["krs"], ["junk", skey(13)])
                for _, act_part in prev_tr:
                    act_part()
                DVE(lambda e, sc=sc: e.tensor_reduce(out=st3[:, c0 + 4:c0 + 8], in_=sqq[:, 0:512].rearrange("p (h j) -> p h j", j=128),
                                                     axis=AX.X, op=ALU.add), ["sqq"], [skey(4)])
                DVE(lambda e, sc=sc: e.tensor_reduce(out=st3[:, c0 + 8:c0 + 12], in_=sqq[:, 512:768].rearrange("p (h j) -> p h j", j=64),
                                                     axis=AX.X, op=ALU.add), ["sqq"], [skey(8)])
                DVE(lambda e: e.tensor_reduce(out=st3[:, c0 + 16:c0 + 20], in_=sqk.rearrange("p (h j) -> p h j", j=128),
                                              axis=AX.X, op=ALU.add), ["sqk"], [skey(16)])
                DVE(lambda e: e.tensor_tensor(out=st3[:, c0 + 4:c0 + 8], in0=st3[:, c0 + 4:c0 + 8], in1=st3[:, c0 + 8:c0 + 12], op=ALU.add),
                    [skey(4), skey(8)], [skey(4)])
                DVE(lambda e, sc=sc: e.tensor_tensor(out=sc(12), in0=sc(2), in1=sc(2), op=ALU.mult), [skey(2)], [skey(12)])
                DVE(lambda e, sc=sc: e.tensor_scalar(out=st3[:, c0 + 4:c0 + 8], in0=st3[:, c0 + 4:c0 + 8], scalar1=sc(12), scalar2=1.0 / 192,
                                                     op0=ALU.mult, op1=ALU.mult), [skey(4), skey(12)], [skey(4)])
                DVE(lambda e: e.tensor_scalar(out=st3[:, c0 + 4:c0 + 8], in0=st3[:, c0 + 4:c0 + 8], scalar1=EPS, scalar2=None, op0=ALU.add),
                    [skey(4)], [skey(4)])
                DVE(lambda e, sc=sc: e.tensor_tensor(out=sc(14), in0=sc(3), in1=sc(3), op=ALU.mult), [skey(3)], [skey(14)])
                DVE(lambda e, sc=sc: e.tensor_scalar(out=st3[:, c0 + 16:c0 + 20], in0=st3[:, c0 + 16:c0 + 20], scalar1=sc(14), scalar2=sc(13),
                                                     op0=ALU.mult, op1=ALU.add), [skey(16), skey(14), skey(13)], [skey(16)])
                DVE(lambda e: e.tensor_scalar(out=st3[:, c0 + 16:c0 + 20], in0=st3[:, c0 + 16:c0 + 20], scalar1=1.0 / 192, scalar2=EPS,
                                              op0=ALU.mult, op1=ALU.add), [skey(16)], [skey(16)])
                POOL(lambda e: e.tensor_tensor(out=st3[:, c0 + 8:c0 + 12], in0=st3[:, c0 + 4:c0 + 8],
                                               in1=nhalf.to_broadcast([128, 4]), op=ALU.pow), [skey(4), "nhalf", skey(8)], [skey(8)])
                POOL(lambda e: e.tensor_tensor(out=st3[:, c0 + 20:c0 + 24], in0=st3[:, c0 + 16:c0 + 20],
                                               in1=nhalf.to_broadcast([128, 4]), op=ALU.pow), [skey(16), "nhalf"], [skey(20)])
                DVE(lambda e, sc=sc: e.tensor_scalar(out=st3[:, c0 + 8:c0 + 12], in0=st3[:, c0 + 8:c0 + 12], scalar1=sc(2), scalar2=None,
                                                     op0=ALU.mult), [skey(8), skey(2)], [skey(8)])
                DVE(lambda e, sc=sc: e.tensor_scalar(out=st3[:, c0 + 24:c0 + 28], in0=st3[:, c0 + 20:c0 + 24], scalar1=sc(3), scalar2=None,
                                                     op0=ALU.mult), [skey(20), skey(3)], [skey(24)])
                fq = st3[:, c0 + 8:c0 + 12]
                rk_ = st3[:, c0 + 20:c0 + 24]
                fkn = st3[:, c0 + 24:c0 + 28]
                DVE(lambda e, fq=fq: e.tensor_tensor(out=t1[:, 0:512].rearrange("p (h j) -> p h j", j=128),
                                                     in0=bank(3).rearrange("p (h j) -> p h j", j=128),
                                                     in1=fq.unsqueeze(2).to_broadcast([128, 4, 128]), op=ALU.mult), [bk(3), skey(8)], ["t1"])
                DVE(lambda e, fq=fq: e.tensor_tensor(out=t1[:, 512:768].rearrange("p (h j) -> p h j", j=64),
                                                     in0=bank(4)[:, 0:256].rearrange("p (h j) -> p h j", j=64),
                                                     in1=fq.unsqueeze(2).to_broadcast([128, 4, 64]), op=ALU.mult), [bk(4), skey(8), "t1"], ["t1"])
                DVE(lambda e, fkn=fkn: e.tensor_tensor(out=sqk.rearrange("p (h j) -> p h j", j=128),
                                                       in0=bank(5).rearrange("p (h j) -> p h j", j=128),
                                                       in1=fkn.unsqueeze(2).to_broadcast([128, 4, 128]), op=ALU.mult),
                    [bk(5), skey(24), "sqk"], ["sqk"])
                ACT(lambda e, sc=sc, i=i: e.activation(out=Vaug[:, i, :, 0:128], in_=bank(6).rearrange("p (h j) -> p h j", j=128),
                                                       func=AF.Copy, scale=sc(3)), [bk(6), skey(3)], [("Vaug", i)])
                DVE(lambda e: e.tensor_tensor(out=qf[:, 0:512], in0=t1[:, 0:512], in1=gq_b[:, 0:512], op=ALU.mult), ["t1", "const"], ["qf"])
                DVE(lambda e: e.tensor_tensor(out=t1[:, 512:768], in0=t1[:, 512:768], in1=gq_b[:, 512:768], op=ALU.mult), ["t1", "const"], ["t1"])

                def rope(E, src, nh_, dst, rk, wk, ta, tb, tak, tbk):
                    s4 = src.rearrange("p (h a r) -> p h a r", a=2, r=32)
                    a4 = ta[:, 0:nh_ * 64].rearrange("p (h a r) -> p h a r", a=2, r=32)
                    b4 = tb[:, 0:nh_ * 64].rearrange("p (h a r) -> p h a r", a=2, r=32)
                    cb = cosT[:, i, :].unsqueeze(1).unsqueeze(1).to_broadcast([128, nh_, 2, 32])
                    sb_ = sinT[:, i, :].unsqueeze(1).to_broadcast([128, nh_, 32])
                    E(lambda e: e.tensor_tensor(out=a4, in0=s4, in1=cb, op=ALU.mult), rk + ["cosT"], [tak])
                    if isinstance(dst, list):
                        E(lambda e: e.scalar_tensor_tensor(out=b4[:, :, 0, :], in0=s4[:, :, 1, :], scalar=-1.0, in1=sb_, op0=ALU.mult, op1=ALU.mult),
                          rk + ["sinT"], [tbk])
                        E(lambda e: e.tensor_tensor(out=b4[:, :, 1, :], in0=s4[:, :, 0, :], in1=sb_, op=ALU.mult), rk + ["sinT", tbk], [tbk])
                        a3 = ta[:, 0:nh_ * 64].rearrange("p (h j) -> p h j", j=64)
                        b3 = tb[:, 0:nh_ * 64].rearrange("p (h j) -> p h j", j=64)
                        for par, dv in enumerate(dst):
                            E(lambda e, par=par, dv=dv: e.tensor_tensor(out=dv, in0=a3[:, par::2, :], in1=b3[:, par::2, :], op=ALU.add),
                              [tak, tbk] + wk, wk)
                    else:
                        E(lambda e: e.tensor_tensor(out=b4[:, :, 0, :], in0=s4[:, :, 1, :], in1=sb_, op=ALU.mult), rk + ["sinT"], [tbk])
                        E(lambda e: e.tensor_tensor(out=b4[:, :, 1, :], in0=s4[:, :, 0, :], in1=sb_, op=ALU.mult), rk + ["sinT", tbk], [tbk])
                        E(lambda e: e.tensor_tensor(out=dst[:, 0:32], in0=ta[:, 0:32], in1=tb[:, 0:32], op=ALU.subtract), [tak, tbk] + wk, wk)
                        E(lambda e: e.tensor_tensor(out=dst[:, 32:64], in0=ta[:, 32:64], in1=tb[:, 32:64], op=ALU.add), [tak, tbk] + wk, wk)

                qz = qf[:, 512:1024].rearrange("p (i r) -> p i r", r=256)
                rope(DVE, t1[:, 512:768], 4, [qz[:, :, 0:64], qz[:, :, 192:256]], ["t1"], ["qf"], ta, tb, "ta", "tb")
                POOL(lambda e: e.tensor_tensor(out=kf[:, 0:512], in0=sqk, in1=gk_b[:, 0:512], op=ALU.mult), ["sqk", "const"], ["kf"])
                POOL(lambda e: e.tensor_tensor(out=krs, in0=krs, in1=gk_b[:, 512:576], op=ALU.mult), ["krs", "const"], ["krs"])
                rope(POOL, krs, 1, krr, ["krs"], ["krr"], tak_, tbk_, "tak", "tbk")
                POOL(lambda e, rk_=rk_: e.tensor_tensor(out=kf[:, 512:768].rearrange("p (h j) -> p h j", j=64),
                                                        in0=krr.unsqueeze(1).to_broadcast([128, 4, 64]),
                                                        in1=rk_.unsqueeze(2).to_broadcast([128, 4, 64]), op=ALU.mult),
                     ["krr", skey(20), "kf"], ["kf"])
                def mk_tr(i=i, its=its):
                    def pe_part():
                        for m in range(8):
                            tp(bankbf(7)[:, m * 128:(m + 1) * 128], qf[:, m * 128:(m + 1) * 128], ["qf"], [bk(7)])
                        for m in range(6):
                            tp(bankbf(0)[:, m * 128:(m + 1) * 128], kf[:, m * 128:(m + 1) * 128], ["kf"], [bk(0)])

                    def act_part():
                        ACT(lambda e: e.activation(out=hT[:, :, its], in_=bankbf(7).rearrange("p (m t) -> p m t", t=128), func=AF.Copy),
                            [bk(7)], [("hT", i)])
                        ACT(lambda e: e.activation(out=KT[:, :, its], in_=bankbf(0)[:, 0:768].rearrange("p (m t) -> p m t", t=128), func=AF.Copy),
                            [bk(0)], [("KT", i)])
                    return pe_part, act_part
                pending_tr.append(mk_tr())
        for pe_part, act_part in pending_tr:
            pe_part()
            act_part()
        pending_tr = []
        if b == 0:
            tap("QT", hT[:, 0:6, :], [("hT", i) for i in range(NT)])
            tap("KT", KT, [("KT", i) for i in range(NT)])
            tap("Vaug", Vaug, [("Vaug", i) for i in range(NT)] + ["Vones"])
        S.barrier()
        R = Bump(arena, SMARK, ARENA_BYTES)
        PT = [R.take([512], BF16) for _ in range(3)]
        obuf = R.take([4, 512], F32)
        obn = [R.take([512], BF16) for _ in range(2)]
        st4 = R.take([64], F32)
        junk = R.take([512], BF16)
        QT = hT
        it = 0
        npt = 0
        for qb in range(4):
            qs = slice(qb * 512, (qb + 1) * 512)
            qkeys = [("hT", 4 * qb + j) for j in range(4)]
            for h in range(4):
                ko = 2 + (it % 2) * 2
                it += 1
                DVE(lambda e, ko=ko: e.memset(pp[ko // 2][:, :], 0.0), [], [bk(ko), bk(ko + 1)])
                rp = slice((h % 2) * 64, (h % 2) * 64 + 64)
                rc = 4 + h // 2
                def qk(kc):
                    ksl = slice(kc * 128, (kc + 1) * 128)
                    ks_ = kc % 2
                    mm(bank(ks_), KT[:, h, ksl], QT[:, h, qs], True, False, [("KT", kc)] + qkeys, [bk(ks_)])
                    mm(bank(ks_), KT[:, rc, ksl], QT[:, 4 + h, qs], False, True, [("KT", kc)] + qkeys, [bk(ks_)])

                qk(0)
                for kc in range(NT):
                    ks_ = kc % 2
                    ps_ = npt % 3
                    npt += 1
                    ACT(lambda e, ks_=ks_, ps_=ps_: e.activation(out=PT[ps_], in_=bank(ks_), func=AF.Exp), [bk(ks_)], [("PT", ps_)])
                    if kc + 1 < NT:
                        qk(kc + 1)
                    for j in range(4):
                        ob = ko + j // 2
                        mm(bank(ob)[:, (j % 2) * 256:(j % 2) * 256 + 129], PT[ps_][:, j * 128:(j + 1) * 128], Vaug[:, kc, h, 0:129],
                           False, False, [("PT", ps_), ("Vaug", kc), "Vones"], [bk(ob)], skip=True)
                for j in range(4):
                    ob = ko + j // 2
                    o0 = (j % 2) * 256
                    c0 = ((it * 4 + j) % 16) * 2
                    DVE(lambda e, ob=ob, o0=o0, c0=c0: e.reciprocal(out=st4[:, c0:c0 + 1], in_=bank(ob)[:, o0 + 128:o0 + 129]),
                        [bk(ob)], [("st4", c0)])
                    DVE(lambda e, ob=ob, o0=o0, c0=c0, j=j, h=h: e.tensor_scalar(out=obuf[:, j, h * 128:(h + 1) * 128],
                                                                                 in0=bank(ob)[:, o0:o0 + 128], scalar1=st4[:, c0:c0 + 1],
                                                                                 scalar2=None, op0=ALU.mult),
                        [bk(ob), ("st4", c0)], [("obuf", j)])
            for j in range(4):
                i = qb * 4 + j
                c0 = 32 + (i % 8) * 4
                sj = i % 2
                ACT(lambda e, j=j, c0=c0: e.activation(out=junk[:, 0:512], in_=obuf[:, j, :], func=AF.Square, accum_out=st4[:, c0:c0 + 1]),
                    [("obuf", j)], ["junk", ("st4", c0)])
                DVE(lambda e, c0=c0: e.tensor_scalar(out=st4[:, c0 + 1:c0 + 2], in0=st4[:, c0:c0 + 1], scalar1=1.0 / 512, scalar2=EPS,
                                                     op0=ALU.mult, op1=ALU.add), [("st4", c0)], [("st4", c0 + 1)])
                POOL(lambda e, c0=c0: e.tensor_tensor(out=st4[:, c0 + 2:c0 + 3], in0=st4[:, c0 + 1:c0 + 2], in1=nhalf, op=ALU.pow),
                     [("st4", c0 + 1), "nhalf"], [("st4", c0 + 2)])
                DVE(lambda e, j=j, c0=c0, sj=sj: e.scalar_tensor_tensor(out=obn[sj], in0=obuf[:, j, :], scalar=st4[:, c0 + 2:c0 + 3],
                                                                        in1=gmo_b, op0=ALU.mult, op1=ALU.mult),
                    [("obuf", j), ("st4", c0 + 2), "const"], [("obn", sj)])
                kt = 6 + i % 2
                for m in range(4):
                    tp(bankbf(kt)[:, m * 128:(m + 1) * 128], obn[sj][:, m * 128:(m + 1) * 128], [("obn", sj)], [bk(kt)])
                ACT(lambda e, kt=kt, i=i: e.activation(out=oT[:, 4:8, i * 128:(i + 1) * 128],
                                                       in_=bankbf(kt)[:, 0:512].rearrange("p (m t) -> p m t", t=128), func=AF.Copy),
                    [bk(kt)], [("oT", 4, i)])
        tap("oT", oT, [("oT", 4, i) for i in range(NT)])
        S.barrier()
        R = Bump(arena, RMARK, ARENA_BYTES)
        W_o = R.take([8, D], BF16)
        xin = [R.take([D], F32) for _ in range(2)]
        x1o = [R.take([D], F32) for _ in range(2)]
        for kc in range(8):
            S.dma("pool", lambda e, kc=kc: e.dma_start(out=W_o[:, kc, :], in_=w_out[kc * 128:(kc + 1) * 128, :]), "w_o", writes=["W_o"])
        for i in range(NT):
            sl = i % 2
            its = slice(i * 128, (i + 1) * 128)
            S.dma("sp", lambda e, sl=sl, i=i: e.dma_start(out=xin[sl], in_=x[b, i * 128:(i + 1) * 128, :]), f"xin{sl}",
                  writes=[("xin", sl)])
            kp = (i % 2) * 2
            for half in range(2):
                for m in range(8):
                    mm(bank(kp + half), oT[:, m, its], W_o[:, m, half * 512:(half + 1) * 512], m == 0, m == 7, ["W_o"], [bk(kp + half)])
            DVE(lambda e, sl=sl, kp=kp: e.tensor_tensor(out=x1o[sl], in0=pp[kp // 2][:, :], in1=xin[sl], op=ALU.add),
                [bk(kp), bk(kp + 1), ("xin", sl), ("x1o", sl)], [("x1o", sl)])
            S.dma("sp", lambda e, sl=sl, i=i: e.dma_start(out=x1s[b, i * 128:(i + 1) * 128, :], in_=x1o[sl]), f"x1o{sl}",
                  reads=[("x1o", sl)])
        S.barrier()

    Bm = Bump(arena, MARK0, ARENA_BYTES)
    W_up = Bm.take([8, DFF], BF16)
    W_dn = Bm.take([32, D], BF16)
    gffn_b = Bm.take([D], F32)
    TB = 256
    NJ = TB // 128
    x1t = [[Bm.take([D], F32) for _ in range(NJ)] for _ in range(2)]
    hbf = [Bm.take([D], BF16) for _ in range(2)]
    h2T = Bm.take([8, TB], BF16)
    aT = Bm.take([32, TB], BF16)
    rl = [Bm.take([512], F32) for _ in range(2)]
    yo = [Bm.take([D], F32) for _ in range(2)]
    junk = Bm.take([D], BF16)
    st5 = Bm.take([64], F32)
    S.dma("sp", lambda e: e.dma_start(out=gffn_b, in_=g_ffn.partition_broadcast(128)), "const2", writes=["gffn"])
    for kc in range(8):
        for q4 in range(4):
            S.dma("pool", lambda e, kc=kc, q4=q4: e.dma_start(out=W_up[:, kc, q4 * 1024:(q4 + 1) * 1024],
                                                             in_=w_up[kc * 128:(kc + 1) * 128, q4 * 1024:(q4 + 1) * 1024]),
                  "w_up", writes=["W_up"])
    for c in range(32):
        S.dma("pool", lambda e, c=c: e.dma_start(out=W_dn[:, c, :], in_=w_down[c * 128:(c + 1) * 128, :]), "w_dn", writes=["W_dn"])
    x1f = x1s.rearrange("b s d -> (b s) d")
    yf = y.rearrange("b s d -> (b s) d")
    nblk = nseq * SEQ // TB
    nyo = 0
    for blk in range(nblk):
        xs = blk % 2
        for j in range(NJ):
            t = blk * NJ + j
            S.dma("sp", lambda e, xs=xs, j=j, t=t: e.dma_start(out=x1t[xs][j], in_=x1f[t * 128:(t + 1) * 128, :]), f"x1t{xs}{j}",
                  writes=[("x1t", xs, j)])
            c0 = (t % 8) * 4
            sl = t % 2
            ACT(lambda e, xs=xs, j=j, c0=c0: e.activation(out=junk, in_=x1t[xs][j], func=AF.Square, accum_out=st5[:, c0:c0 + 1]),
                [("x1t", xs, j)], ["junk", ("st5", c0)])
            DVE(lambda e, c0=c0: e.tensor_scalar(out=st5[:, c0 + 1:c0 + 2], in0=st5[:, c0:c0 + 1], scalar1=1.0 / D, scalar2=EPS,
                                                 op0=ALU.mult, op1=ALU.add), [("st5", c0)], [("st5", c0 + 1)])
            POOL(lambda e, c0=c0: e.tensor_tensor(out=st5[:, c0 + 2:c0 + 3], in0=st5[:, c0 + 1:c0 + 2], in1=nhalf, op=ALU.pow),
                 [("st5", c0 + 1), "nhalf"], [("st5", c0 + 2)])
            DVE(lambda e, xs=xs, j=j, c0=c0, sl=sl: e.scalar_tensor_tensor(out=hbf[sl], in0=x1t[xs][j], scalar=st5[:, c0 + 2:c0 + 3],
                                                                           in1=gffn_b, op0=ALU.mult, op1=ALU.mult),
                [("x1t", xs, j), ("st5", c0 + 2), "gffn"], [("hbf", sl)])
            for kc in range(8):
                tp(bankbf(0)[:, kc * 128:(kc + 1) * 128], hbf[sl][:, kc * 128:(kc + 1) * 128], [("hbf", sl)], [bk(0)])
            ACT(lambda e, j=j: e.activation(out=h2T[:, :, j * 128:(j + 1) * 128], in_=bankbf(0).rearrange("p (k t) -> p k t", t=128),
                                            func=AF.Copy), [bk(0)], [("h2T", j)])
        hk2 = [("h2T", j) for j in range(NJ)] + ["W_up"]
        for cp in range(16):
            ku = 1 + cp % 2
            for c in range(2):
                cc = cp * 2 + c
                for kc in range(8):
                    mm(bank(ku)[:, c * TB:(c + 1) * TB], W_up[:, kc, cc * 128:(cc + 1) * 128], h2T[:, kc, :], kc == 0, kc == 7, hk2, [bk(ku)])
            rs = cp % 2
            ACT(lambda e, ku=ku, rs=rs: e.activation(out=rl[rs], in_=bank(ku), func=AF.Relu), [bk(ku)], [("rl", rs)])
            POOL(lambda e, rs=rs, cp=cp: e.tensor_tensor(out=aT[:, 2 * cp:2 * cp + 2, :], in0=rl[rs].rearrange("p (c t) -> p c t", t=TB),
                                                         in1=rl[rs].rearrange("p (c t) -> p c t", t=TB), op=ALU.mult),
                 [("rl", rs)], [("aT", cp)])
        for j in range(NJ):
            t = blk * NJ + j
            kd = 4 + (t % 2) * 2
            for half in range(2):
                for c in range(32):
                    mm(bank(kd + half), aT[:, c, j * 128:(j + 1) * 128], W_dn[:, c, half * 512:(half + 1) * 512], c == 0, c == 31,
                       [("aT", c // 2), "W_dn"], [bk(kd + half)])
            ys = nyo % 2
            nyo += 1
            DVE(lambda e, ys=ys, kd=kd, xs=xs, j=j: e.tensor_tensor(out=yo[ys], in0=pp[kd // 2][:, :], in1=x1t[xs][j], op=ALU.add),
                [bk(kd), bk(kd + 1), ("x1t", xs, j), ("yo", ys)], [("yo", ys)])
            S.dma("sp", lambda e, ys=ys, t=t: e.dma_start(out=yf[t * 128:(t + 1) * 128, :], in_=yo[ys]), f"yo{ys}", reads=[("yo", ys)])
    S.barrier()

    sems = {n: es.enter_context(nc.semaphore(n)) for n in sorted(S.sem_names)}
    with nc.Block() as block:
        S.emit(block, sems)
    es.close()
    return nc, dbg_out


def _prep_inputs(inputs, nseq, ncores):
    f32 = np.float32
    g = lambda k: np.ascontiguousarray(np.asarray(inputs[k]))
    wq_ = g("w_q_up")[0].reshape(384, 4, 192)
    wq_p = np.concatenate([wq_[:, :, :128].reshape(384, 512), wq_[:, :, 128:].reshape(384, 256)], axis=1)
    wkv_ = g("w_kv_up")[0].reshape(256, 4, 256)
    wkv_p = np.concatenate([wkv_[:, :, :128].reshape(256, 512), wkv_[:, :, 128:].reshape(256, 512)], axis=1)
    invf = (10000.0 ** (-(np.arange(0, 64, 2, dtype=f32)) / f32(64))).astype(f32).reshape(1, 32)
    shared = {
        "g_mix": g("g_mix_norm").reshape(1, D), "w_in": g("w_in")[0], "lb_param": g("lb_param")[:, 0:2, :],
        "g_hg": g("g_hgrn_out")[0], "g_cq": g("g_cq").reshape(1, 384), "wq": np.ascontiguousarray(wq_p),
        "g_ckv": g("g_ckv").reshape(1, 256), "wkv": np.ascontiguousarray(wkv_p), "g_q": g("g_q_norm").reshape(1, 192),
        "g_k": g("g_k_norm").reshape(1, 192), "g_mo": g("g_mla_out").reshape(1, 512), "w_out": g("w_out")[0],
        "g_ffn": g("g_ffn_norm").reshape(1, D), "w_up": g("w_up")[0], "w_down": g("w_down")[0], "invf": invf,
    }
    x = g("x")
    pos = g("positions").astype(np.int32)
    maps = []
    for c in range(ncores):
        m = dict(shared)
        m["x"] = np.ascontiguousarray(x[c * nseq:(c + 1) * nseq])
        m["pos"] = np.ascontiguousarray(pos[c * nseq:(c + 1) * nseq])
        maps.append(m)
    return maps


def kernel(**inputs):
    nseq = 4
    nc, _ = build(nseq)
    maps = _prep_inputs(inputs, nseq, NCORES)
    res = run_bass_kernel_spmd(nc, maps, core_ids=list(range(NCORES)))
    out = np.concatenate([np.asarray(r["y"]) for r in res.results], axis=0)
    return out.astype(np.float32, copy=False)
```

```python
import numpy as np
from contextlib import ExitStack
import concourse.bass as bass
import concourse.mybir as mybir
from concourse.bass_utils import run_bass_kernel_spmd

F32 = mybir.dt.float32
BF16 = mybir.dt.bfloat16
I32 = mybir.dt.int32
AF = mybir.ActivationFunctionType
ALU = mybir.AluOpType
AX = mybir.AxisListType

NCORES = 8
SEQ = 2048
NT = SEQ // 128
D = 1024
DIN = 3264
DFF = 4096
EPS = 1e-6
PI = float(np.pi)
ARENA_BYTES = 212480

ENGS = ("pe", "act", "dve", "pool", "sp")


def _cody_waite():
    two_pi = 2.0 * np.pi
    c1 = 6.28125
    r1 = two_pi - c1
    m, e = np.frexp(r1)
    c2 = float(np.ldexp(np.round(m * 2 ** 11) / 2 ** 11, e))
    c3 = float(np.float32(two_pi - c1 - c2))
    return c1, c2, c3


CW1, CW2, CW3 = _cody_waite()


class _Rec:
    def __getattr__(self, name):
        def f(*a, **k):
            self.call = (name, a, k)
            return self
        return f


class Sched:
    def __init__(self):
        self.q = {e: [] for e in ENGS}
        self.cnt = {e: 0 for e in ENGS}
        self.seen = {e: {} for e in ENGS}
        self.bufs = {}
        self.dma_tot = {}
        self.sem_names = set(ENGS)

    def _st(self, k):
        st = self.bufs.get(k)
        if st is None:
            st = self.bufs[k] = {"w": None, "r": {}}
        return st

    def _deps(self, eng, reads, writes):
        toks = []
        for k in reads:
            st = self._st(k)
            if st["w"] is not None:
                toks.append(st["w"])
        for k in writes:
            st = self._st(k)
            if st["w"] is not None:
                toks.append(st["w"])
            toks.extend(st["r"].items())
        waits = {}
        for (s, v) in toks:
            if s == "pe" and eng == "pe":
                continue
            if self.seen[eng].get(s, 0) >= v:
                continue
            if waits.get(s, 0) < v:
                waits[s] = v
        for s, v in waits.items():
            self.seen[eng][s] = v
        return list(waits.items())

    def _commit(self, tok, reads, writes):
        for k in reads:
            r = self._st(k)["r"]
            if r.get(tok[0], 0) < tok[1]:
                r[tok[0]] = tok[1]
        for k in writes:
            st = self._st(k)
            st["w"] = tok
            st["r"] = {}

    def op(self, eng, fn, reads=(), writes=()):
        rec = _Rec()
        fn(rec)
        waits = self._deps(eng, reads, writes)
        self.cnt[eng] += 1
        tok = (eng, self.cnt[eng])
        self.q[eng].append((rec.call, waits, (eng, 1)))
        self._commit(tok, reads, writes)

    def dma(self, eng, fn, sem, reads=(), writes=()):
        rec = _Rec()
        fn(rec)
        self.sem_names.add(sem)
        waits = self._deps(eng, reads, writes)
        self.dma_tot[sem] = self.dma_tot.get(sem, 0) + 16
        tok = (sem, self.dma_tot[sem])
        self.q[eng].append((rec.call, waits, (sem, 16)))
        self._commit(tok, reads, writes)

    def barrier(self):
        for e in ENGS:
            waits = []
            for e2 in ENGS:
                if self.cnt[e2] > self.seen[e].get(e2, 0):
                    waits.append((e2, self.cnt[e2]))
                    self.seen[e][e2] = self.cnt[e2]
            for s, tot in self.dma_tot.items():
                if tot > self.seen[e].get(s, 0):
                    waits.append((s, tot))
                    self.seen[e][s] = tot
            self.q[e].append((None, waits, None))
        self.bufs = {}

    def emit(self, block, sems):
        handles = {"pe": block.tensor, "act": block.scalar, "dve": block.vector,
                   "pool": block.gpsimd, "sp": block.sync}

        def mk(e):
            ops = self.q[e]

            def body(engine):
                for fn, waits, inc in ops:
                    for s, v in waits:
                        engine.wait_ge(sems[s], v)
                    if fn is not None:
                        name, a, k = fn
                        getattr(engine, name)(*a, **k).then_inc(sems[inc[0]], inc[1])
            return body

        for e in ENGS:
            handles[e](mk(e))


def _dsize(dt):
    return 2 if dt == BF16 else 4


class Bump:
    def __init__(self, arena, start, end):
        self.arena, self.cur, self.end = arena, start, end

    def take(self, shape, dt):
        n = int(np.prod(shape)) * _dsize(dt)
        off = (self.cur + 63) // 64 * 64
        n4 = (n + 3) // 4 * 4
        self.cur = off + n4
        assert self.cur <= self.end, (self.cur, self.end)
        ap = self.arena[:, off // 4:(off + n4) // 4]
        if dt != F32:
            ap = ap.bitcast(dt)
        if n4 != n:
            ap = ap[:, 0:int(np.prod(shape))]
        if len(shape) == 2:
            ap = ap.rearrange("p (a b) -> p a b", b=shape[1])
        elif len(shape) == 3:
            ap = ap.rearrange("p (a b c) -> p a b c", b=shape[1], c=shape[2])
        return ap


def build(nseq=4, dbg=None):
    nc = bass.Bass("TRN2", target_bir_lowering=False)
    S = Sched()
    dbg_out = {}

    def din(name, shape, dt=F32):
        return nc.dram_tensor(name, list(shape), dt, kind="ExternalInput").ap()

    x = din("x", [nseq, SEQ, D])
    pos = din("pos", [nseq, SEQ], I32)
    g_mix = din("g_mix", [1, D])
    w_in = din("w_in", [D, DIN])
    lb_param = din("lb_param", [2, 2, 512])
    g_hg = din("g_hg", [4, 128])
    g_cq = din("g_cq", [1, 384])
    wq = din("wq", [384, 768])
    g_ckv = din("g_ckv", [1, 256])
    wkv = din("wkv", [256, 1024])
    g_q = din("g_q", [1, 192])
    g_k = din("g_k", [1, 192])
    g_mo = din("g_mo", [1, 512])
    w_out = din("w_out", [D, D])
    g_ffn = din("g_ffn", [1, D])
    w_up = din("w_up", [D, DFF])
    w_down = din("w_down", [DFF, D])
    invf = din("invf", [1, 32])
    y = nc.dram_tensor("y", [nseq, SEQ, D], F32, kind="ExternalOutput").ap()
    x1s = nc.dram_tensor("x1s", [nseq, SEQ, D], F32).ap()

    es = ExitStack()
    arena = es.enter_context(nc.sbuf_tensor("arena", [128, ARENA_BYTES // 4], F32))[:]
    pp = [es.enter_context(nc.psum_tensor(f"pp{i}", [128, 1024], F32)) for i in range(4)]

    def bank(k):
        return pp[k // 2][:, (k % 2) * 512:(k % 2) * 512 + 512]

    def bankbf(k):
        return bank(k).bitcast(BF16)

    def bk(k):
        return ("bank", k)

    def PE(fn, r, w):
        S.op("pe", fn, r, w)

    def ACT(fn, r, w):
        S.op("act", fn, r, w)

    def DVE(fn, r, w):
        S.op("dve", fn, r, w)

    def POOL(fn, r, w):
        S.op("pool", fn, r, w)

    def mm(out, lhsT, rhs, start, stop, r, w, skip=False):
        if skip:
            PE(lambda e: e.matmul(out, lhsT=lhsT, rhs=rhs, start=start, stop=stop, skip_group_check=True), r, w)
        else:
            PE(lambda e: e.matmul(out, lhsT=lhsT, rhs=rhs, start=start, stop=stop), r, w)

    def tp(out, in_, r, w):
        PE(lambda e: e.transpose(out=out, in_=in_, identity=ident), list(r) + ["ident"], w)

    def tap(name, ap, key):
        if dbg is None or name not in dbg:
            return
        shp = list(ap.shape)
        t = nc.dram_tensor("dbg_" + name, shp, ap.dtype, kind="ExternalOutput").ap()
        dbg_out[name] = t
        S.dma("sp", lambda e: e.dma_start(out=t, in_=ap), "dbg", reads=key)

    P = Bump(arena, 0, ARENA_BYTES)
    ident = P.take([128], BF16)
    maskfb = P.take([256], I32)
    cols = P.take([64], F32)
    ones_b = P.take([2], BF16)
    nhalf = P.take([1], F32)
    ones_f = P.take([512], F32)
    MARK0 = P.cur
    C_GCQ, C_GCKV, C_GHG, C_LB, C_LNOML, C_LBP = 0, 3, 5, 9, 17, 25

    A = Bump(arena, MARK0, ARENA_BYTES)
    W_in = A.take([8, DIN], BF16)
    gmix_b = A.take([D], F32)
    gq_b = A.take([768], F32)
    gk_b = A.take([768], F32)
    gmo_b = A.take([512], F32)
    invf_b = A.take([32], F32)
    hT = A.take([8, SEQ], BF16)
    oT = A.take([8, SEQ], BF16)
    RMARK = A.cur

    R0 = Bump(arena, RMARK, ARENA_BYTES)
    identf = R0.take([128], F32)
    lbt = R0.take([8], F32)

    def cdma(out, in_, slow=False):
        if slow:
            S.dma("sp", lambda e: e.dma_start(out=out, in_=in_, allow_slow_non_contiguous=True), "const", writes=["const"])
        else:
            S.dma("sp", lambda e: e.dma_start(out=out, in_=in_), "const", writes=["const"])

    cdma(gmix_b, g_mix.partition_broadcast(128))
    for h in range(4):
        cdma(gq_b[:, h * 128:(h + 1) * 128], g_q[:, 0:128].partition_broadcast(128))
        cdma(gq_b[:, 512 + h * 64:512 + (h + 1) * 64], g_q[:, 128:192].partition_broadcast(128))
        cdma(gk_b[:, h * 128:(h + 1) * 128], g_k[:, 0:128].partition_broadcast(128))
        cdma(gk_b[:, 512 + h * 64:512 + (h + 1) * 64], g_k[:, 128:192].partition_broadcast(128))
    cdma(gmo_b, g_mo.partition_broadcast(128))
    cdma(invf_b, invf.partition_broadcast(128))
    cdma(cols[:, C_GCQ:C_GCQ + 3], g_cq[0].rearrange("(m p) -> p m", p=128), slow=True)
    cdma(cols[:, C_GCKV:C_GCKV + 2], g_ckv[0].rearrange("(m p) -> p m", p=128), slow=True)
    cdma(cols[:, C_GHG:C_GHG + 4], g_hg.rearrange("h e -> e h"), slow=True)
    for d_ in range(2):
        for s_ in range(2):
            o = C_LBP + (d_ * 2 + s_) * 4
            cdma(cols[:, o:o + 4], lb_param[d_, s_].rearrange("(h p) -> p h", p=128), slow=True)
    for kc in range(8):
        S.dma("pool", lambda e, kc=kc: e.dma_start(out=W_in[:, kc, :], in_=w_in[kc * 128:(kc + 1) * 128, :], max_dma_last_dim=4096),
              "w_in", writes=["W_in"])

    POOL(lambda e: e.memset(identf, 0.0), [], ["identf"])
    POOL(lambda e: e.affine_select(out=identf, in_=identf, pattern=[[-1, 128]], compare_op=ALU.not_equal, fill=1.0,
                                   base=0, channel_multiplier=1), ["identf"], ["identf"])
    DVE(lambda e: e.tensor_copy(out=ident, in_=identf), ["identf"], ["ident"])
    POOL(lambda e: e.iota(maskfb[:, 0:128], pattern=[[1, 128]], base=0, channel_multiplier=-1), [], ["maskfb"])
    POOL(lambda e: e.iota(maskfb[:, 128:256], pattern=[[-1, 128]], base=0, channel_multiplier=1), ["maskfb"], ["maskfb"])
    DVE(lambda e: e.tensor_single_scalar(out=maskfb, in_=maskfb, scalar=0, op=ALU.is_ge), ["maskfb"], ["maskfb"])
    POOL(lambda e: e.memset(ones_b, 1.0), [], ["ones_b"])
    POOL(lambda e: e.memset(nhalf, -0.5), [], ["nhalf"])
    POOL(lambda e: e.memset(ones_f, 1.0), [], ["ones_f"])
    DVE(lambda e: e.tensor_scalar(out=gq_b, in0=gq_b, scalar1=float(192 ** -0.5), scalar2=None, op0=ALU.mult), ["const"], ["const"])
    lbp = cols[:, C_LBP:C_LBP + 16].rearrange("p (d s h) -> p d s h", s=2, h=4)
    lbv = cols[:, C_LB:C_LB + 8]
    DVE(lambda e: e.tensor_tensor(out=lbt.rearrange("p (d h) -> p d h", h=4), in0=lbp[:, :, 1, :], in1=lbp[:, :, 0, :], op=ALU.subtract),
        ["const"], ["lbt"])
    ACT(lambda e: e.activation(out=lbt, in_=lbt, func=AF.Exp), ["lbt"], ["lbt"])
    DVE(lambda e: e.tensor_scalar(out=lbt, in0=lbt, scalar1=1.0, scalar2=None, op0=ALU.add), ["lbt"], ["lbt"])
    DVE(lambda e: e.reciprocal(out=lbv, in_=lbt), ["lbt", "const"], ["const"])
    ACT(lambda e: e.activation(out=cols[:, C_LNOML:C_LNOML + 8], in_=lbv, func=AF.Ln, scale=-1.0, bias=1.0), ["const"], ["const"])
    S.barrier()

    for b in range(nseq):
        def a0_tile(bb, i, xin, hbf, junk, st, kb0):
            sl = i % 2
            S.dma("sp", lambda e: e.dma_start(out=xin[sl], in_=x[bb, i * 128:(i + 1) * 128, :]), f"xinA{sl}", writes=[("xinA", sl)])
            c0 = (i % 8) * 4
            ACT(lambda e: e.activation(out=junk, in_=xin[sl], func=AF.Square, accum_out=st[:, c0:c0 + 1]), [("xinA", sl)], ["junkA", ("stA", c0)])
            DVE(lambda e: e.tensor_scalar(out=st[:, c0 + 1:c0 + 2], in0=st[:, c0:c0 + 1], scalar1=1.0 / D, scalar2=EPS, op0=ALU.mult, op1=ALU.add),
                [("stA", c0)], [("stA", c0 + 1)])
            POOL(lambda e: e.tensor_tensor(out=st[:, c0 + 2:c0 + 3], in0=st[:, c0 + 1:c0 + 2], in1=nhalf, op=ALU.pow),
                 [("stA", c0 + 1), "nhalf"], [("stA", c0 + 2)])
            DVE(lambda e: e.scalar_tensor_tensor(out=hbf[sl], in0=xin[sl], scalar=st[:, c0 + 2:c0 + 3], in1=gmix_b, op0=ALU.mult, op1=ALU.mult),
                [("xinA", sl), ("stA", c0 + 2), "const"], [("hbfA", sl)])
            k = kb0 + i % 2
            for kc in range(8):
                tp(bankbf(k)[:, kc * 128:(kc + 1) * 128], hbf[sl][:, kc * 128:(kc + 1) * 128], [("hbfA", sl)], [bk(k)])
            ACT(lambda e: e.activation(out=hT[:, :, i * 128:(i + 1) * 128], in_=bankbf(k).rearrange("p (k t) -> p k t", t=128), func=AF.Copy),
                [bk(k)], [("hT", i)])

        R = Bump(arena, RMARK, ARENA_BYTES)
        V_all = R.take([NT, 512], BF16)
        XMARK = R.cur
        if b == 0:
            xinA = [R.take([D], F32) for _ in range(2)]
            hbfA = [R.take([D], BF16) for _ in range(2)]
            junkA = R.take([D], BF16)
            stA = R.take([64], F32)
            for i in range(NT):
                a0_tile(0, i, xinA, hbfA, junkA, stA, 0)
        tap("hT", hT, [("hT", i) for i in range(NT)])
        for i in range(NT):
            k = 2 + i % 2
            for kc in range(8):
                mm(bank(k), hT[:, kc, i * 128:(i + 1) * 128], W_in[:, kc, 1536:2048], kc == 0, kc == 7,
                   [("hT", i), "W_in"], [bk(k)])
            DVE(lambda e, k=k, i=i: e.tensor_copy(out=V_all[:, i, :], in_=bank(k)), [bk(k)], [("V", i)])
        S.barrier()
        R = Bump(arena, XMARK, ARENA_BYTES)
        sgTs = [R.take([SEQ], BF16) for _ in range(2)]
        TS = [[R.take([512], F32) for _ in range(5)] + [R.take([516], F32)] for _ in range(2)]
        QK = {(d_, w_): R.take([SEQ], BF16) for d_ in range(2) for w_ in "qk"}
        Zbf = [R.take([NT, 128], BF16) for _ in range(2)]
        Y = [[R.take([128], F32) for _ in range(2)] for _ in range(2)]
        sqo = [TS[s_][0] for s_ in range(2)]
        onb = [TS[s_][1].bitcast(BF16)[:, 0:512] for s_ in range(2)]
        atm = [TS[s_][2].bitcast(BF16).rearrange("p (d n) -> p d n", d=2) for s_ in range(2)]
        ktok = [TS[s_][4].bitcast(BF16)[:, 0:512] for s_ in range(2)]
        Xs = [[TS[i_][3][:, d_ * 128:(d_ + 1) * 128] for i_ in range(2)] for d_ in range(2)]
        Rall = [R.take([NT], F32) for _ in range(2)]
        Eall = [R.take([NT], F32) for _ in range(2)]
        dR = [R.take([3, NT], F32) for _ in range(2)]
        efac = [R.take([3, NT], F32) for _ in range(2)]
        tot = R.take([2, 4], F32)
        carr = R.take([2, 4], F32)
        st2 = R.take([64], F32)
        for s_ in range(2):
            POOL(lambda e: e.memset(TS[s_][5][:, 0:1], 0.0), [], [("T", s_, 5)])
        POOL(lambda e: e.memset(carr, 0.0), [], ["carr"])
        for h in range(4):
            def gates(hh):
                for blk in range(4):
                    tl = slice(blk * 512, (blk + 1) * 512)
                    hk = [("hT", 4 * blk + j) for j in range(4)] + ["W_in"]
                    kg = 6 + blk % 2
                    for kc in range(8):
                        mm(bank(kg), W_in[:, kc, 2048 + hh * 128:2048 + (hh + 1) * 128], hT[:, kc, tl], kc == 0, kc == 7, hk, [bk(kg)])
                    ACT(lambda e: e.activation(out=sgTs[hh % 2][:, tl], in_=bank(kg), func=AF.Silu), [bk(kg)], [("sgT", hh % 2, blk)])

            if h == 0:
                gates(0)
            sgT = sgTs[h % 2]

            def prep_mm(blk):
                tl = slice(blk * 512, (blk + 1) * 512)
                hk = [("hT", 4 * blk + j) for j in range(4)] + ["W_in"]
                for j, c0 in enumerate((0, 512, 1024)):
                    kk = (3 * blk + j) % 6
                    for kc in range(8):
                        mm(bank(kk), W_in[:, kc, c0 + h * 128:c0 + (h + 1) * 128], hT[:, kc, tl], kc == 0, kc == 7, hk, [bk(kk)])

            def front(it):
                blk, d_ = it // 2, it % 2
                T = TS[it % 2]
                tk = [("T", it % 2, j) for j in range(6)]
                kz = (3 * blk + 1 + d_) % 6
                lbc = cols[:, C_LB + d_ * 4 + h:C_LB + d_ * 4 + h + 1]
                ACT(lambda e: e.activation(out=T[0], in_=bank(kz), func=AF.Exp, scale=-1.0), [bk(kz)], [tk[0]])
                ACT(lambda e: e.activation(out=T[1], in_=T[0], func=AF.Ln, scale=lbc, bias=1.0), [tk[0], "const"], [tk[1]])
                ACT(lambda e: e.activation(out=T[2], in_=T[0], func=AF.Ln, scale=1.0, bias=1.0), [tk[0]], [tk[2]])
                DVE(lambda e: e.tensor_tensor(out=T[4], in0=bank(kz), in1=T[2], op=ALU.add), [bk(kz), tk[2]], [tk[4]])
                DVE(lambda e: e.tensor_tensor_scan(out=T[5][:, 1:513], data0=T[1], data1=T[2], initial=0.0, op0=ALU.add, op1=ALU.subtract),
                    [tk[1], tk[2], tk[5]], [tk[5]])
                DVE(lambda e: e.tensor_copy(out=tot[:, d_, blk:blk + 1], in_=T[5][:, 512:513]), [tk[5]], [("tot", d_)])
                Bx = T[5][:, 1:513] if d_ == 0 else T[5][:, 0:512]
                kB = tk[5]
                DVE(lambda e: e.tensor_copy(out=Rall[d_][:, blk * 4:(blk + 1) * 4], in_=Bx[:, 63:512:128]), [kB], [("Rall", d_)])
                eo_ = 127 if d_ == 0 else 0
                DVE(lambda e: e.tensor_copy(out=Eall[d_][:, blk * 4:(blk + 1) * 4], in_=Bx[:, eo_:512:128]), [kB], [("Eall", d_)])
                DVE(lambda e: e.tensor_tensor(out=T[0].rearrange("p (c j) -> p c j", j=128), in0=Bx.rearrange("p (c j) -> p c j", j=128),
                                              in1=Rall[d_][:, blk * 4:(blk + 1) * 4].unsqueeze(2).to_broadcast([128, 4, 128]),
                                              op=ALU.subtract), [kB, ("Rall", d_)], [tk[0]])

            def back(it):
                blk, d_ = it // 2, it % 2
                tl = slice(blk * 512, (blk + 1) * 512)
                T = TS[it % 2]
                tk = [("T", it % 2, j) for j in range(6)]
                kq = (3 * blk) % 6
                lno = cols[:, C_LNOML + d_ * 4 + h:C_LNOML + d_ * 4 + h + 1]
                sgn = 1.0 if d_ == 0 else -1.0
                ACT(lambda e: e.activation(out=T[2], in_=T[0], func=AF.Exp, scale=sgn), [tk[0]], [tk[2]])
                POOL(lambda e: e.tensor_tensor(out=T[3], in0=T[4], in1=T[0], op=(ALU.add if d_ == 0 else ALU.subtract)),
                     [tk[4], tk[0]], [tk[3]])
                ACT(lambda e: e.activation(out=QK[(d_, "k")][:, tl], in_=T[3], func=AF.Exp, scale=-1.0, bias=lno),
                    [tk[3], "const"], [("QK", d_, "k", blk)])
                DVE(lambda e: e.tensor_tensor(out=QK[(d_, "q")][:, tl], in0=bank(kq), in1=T[2], op=ALU.mult),
                    [bk(kq), tk[2]], [("QK", d_, "q", blk)])

            prep_mm(0)
            front(0)
            for it in range(8):
                if it % 2 == 0 and it // 2 + 1 < 4:
                    prep_mm(it // 2 + 1)
                if it + 1 < 8:
                    front(it + 1)
                if it == 1 and h + 1 < 4:
                    gates(h + 1)
                back(it)
            for d_ in range(2):
                DVE(lambda e: e.tensor_copy(out=carr[:, d_, 1:2], in_=tot[:, d_, 0:1]), [("tot", d_), "carr"], ["carr"])
                DVE(lambda e: e.tensor_tensor(out=carr[:, d_, 2:3], in0=carr[:, d_, 1:2], in1=tot[:, d_, 1:2], op=ALU.add),
                    [("tot", d_), "carr"], ["carr"])
                DVE(lambda e: e.tensor_tensor(out=carr[:, d_, 3:4], in0=carr[:, d_, 2:3], in1=tot[:, d_, 2:3], op=ALU.add),
                    [("tot", d_), "carr"], ["carr"])
                for arr, key in ((Rall, "Rall"), (Eall, "Eall")):
                    DVE(lambda e, arr=arr: e.tensor_tensor(out=arr[d_].rearrange("p (b c) -> p b c", c=4),
                                                           in0=arr[d_].rearrange("p (b c) -> p b c", c=4),
                                                           in1=carr[:, d_, :].unsqueeze(2).to_broadcast([128, 4, 4]), op=ALU.add),
                        [(key, d_), "carr"], [(key, d_)])
            DVE(lambda e: e.tensor_tensor(out=dR[0][:, 0, 1:16], in0=Eall[0][:, 1:16], in1=Eall[0][:, 0:15], op=ALU.subtract),
                [("Eall", 0)], [("dR", 0)])
            DVE(lambda e: e.tensor_tensor(out=dR[0][:, 2, 1:16], in0=Rall[0][:, 1:16], in1=Eall[0][:, 0:15], op=ALU.subtract),
                [("Eall", 0), ("Rall", 0), ("dR", 0)], [("dR", 0)])
            DVE(lambda e: e.memset(dR[0][:, :, 0:1], 0.0), [("dR", 0)], [("dR", 0)])
            DVE(lambda e: e.tensor_tensor(out=dR[0][:, 1, :], in0=Eall[0], in1=Rall[0], op=ALU.subtract),
                [("Eall", 0), ("Rall", 0), ("dR", 0)], [("dR", 0)])
            DVE(lambda e: e.tensor_tensor(out=dR[1][:, 0, 0:15], in0=Eall[1][:, 1:16], in1=Eall[1][:, 0:15], op=ALU.subtract),
                [("Eall", 1)], [("dR", 1)])
            DVE(lambda e: e.tensor_tensor(out=dR[1][:, 2, 0:15], in0=Eall[1][:, 1:16], in1=Rall[1][:, 0:15], op=ALU.subtract),
                [("Eall", 1), ("Rall", 1), ("dR", 1)], [("dR", 1)])
            DVE(lambda e: e.memset(dR[1][:, :, 15:16], 0.0), [("dR", 1)], [("dR", 1)])
            DVE(lambda e: e.tensor_tensor(out=dR[1][:, 1, :], in0=Rall[1], in1=Eall[1], op=ALU.subtract),
                [("Eall", 1), ("Rall", 1), ("dR", 1)], [("dR", 1)])
            for d_ in range(2):
                ACT(lambda e: e.activation(out=efac[d_], in_=dR[d_], func=AF.Exp), [("dR", d_)], [("efac", d_)])
            if h == 0 and b == 0:
                tap("Qf", QK[(0, "q")], [("QK", 0, "q", j) for j in range(4)])
                tap("Kf", QK[(0, "k")], [("QK", 0, "k", j) for j in range(4)])
            order = [list(range(15)), list(range(15, 0, -1))]
            pslot = {}

            def grp_pe(gi, d_):
                k_ = gi * 2 + d_
                cs_ = order[d_][gi * 4:gi * 4 + 4]
                kT, kP, ks = k_ % 2, 2 + k_ % 4, k_ % 2
                for j, c in enumerate(cs_):
                    tp(bankbf(kT)[:, j * 128:(j + 1) * 128], QK[(d_, "k")][:, c * 128:(c + 1) * 128], [("QK", d_, "k", c // 4)], [bk(kT)])
                nn = len(cs_) * 128
                ACT(lambda e: e.activation(out=ktok[ks][:, 0:nn], in_=bankbf(kT)[:, 0:nn], func=AF.Copy), [bk(kT)], [("T", ks, 4)])
                for j, c in enumerate(cs_):
                    mm(bank(kP)[:, j * 128:(j + 1) * 128], ktok[ks][:, j * 128:(j + 1) * 128], V_all[:, c, h * 128:(h + 1) * 128], True, True,
                       [("T", ks, 4), ("V", c)], [bk(kP)])
                    pslot[(d_, c)] = (kP, j)

            grp_pe(0, 0)
            grp_pe(0, 1)
            for gi in range(4):
                if gi + 1 < 4:
                    grp_pe(gi + 1, 0)
                    grp_pe(gi + 1, 1)
                for idx in range(gi * 4, min(gi * 4 + 4, 15)):
                    for d_ in range(2):
                        c = order[d_][idx]
                        kP_, j_ = pslot[(d_, c)]
                        pw = Xs[d_][idx % 2]
                        ACT(lambda e: e.activation(out=pw, in_=bank(kP_)[:, j_ * 128:(j_ + 1) * 128], func=AF.Copy, scale=efac[d_][:, 1, c:c + 1]),
                            [bk(kP_), ("efac", d_), ("T", idx % 2, 3)], [("Xs", d_, idx % 2)])
                    for d_ in range(2):
                        c = order[d_][idx]
                        pw = Xs[d_][idx % 2]
                        yc, yp = Y[d_][idx % 2], Y[d_][(idx + 1) % 2]
                        if idx == 0:
                            DVE(lambda e: e.tensor_copy(out=yc, in_=pw), [("Xs", d_, idx % 2)], [("Y", d_, idx % 2)])
                        else:
                            DVE(lambda e: e.scalar_tensor_tensor(out=yc, in0=yp, scalar=efac[d_][:, 0, c:c + 1], in1=pw, op0=ALU.mult, op1=ALU.add),
                                [("Y", d_, (idx + 1) % 2), ("Xs", d_, idx % 2), ("efac", d_)], [("Y", d_, idx % 2)])
                    for d_ in range(2):
                        c = order[d_][idx]
                        yc = Y[d_][idx % 2]
                        nxt = c + 1 if d_ == 0 else c - 1
                        DVE(lambda e: e.tensor_scalar(out=Zbf[d_][:, nxt, :], in0=yc, scalar1=efac[d_][:, 2, nxt:nxt + 1], scalar2=None, op0=ALU.mult),
                            [("Y", d_, idx % 2), ("efac", d_)], [("Zbf", d_, nxt)])
            for kk in range(4):
                DVE(lambda e: e.memset(bank(kk), 0.0), [], [bk(kk)])
            for s_ in range(2):
                POOL(lambda e: e.memset(atm[s_], 0.0), [], [("T", s_, 2)])

            def at_mm(g):
                ka = (g % 2) * 2
                for d_ in range(2):
                    for j in range(4):
                        c = g * 4 + j
                        q_, k_ = QK[(d_, "q")], QK[(d_, "k")]
                        o_ = j * 128
                        rk = [("QK", d_, "k", g), ("QK", d_, "q", g)]
                        if d_ == 0:
                            mm(bank(ka)[0:64, o_:o_ + 128], k_[:, c * 128:c * 128 + 64], q_[:, c * 128:(c + 1) * 128], True, True, rk, [bk(ka)])
                            mm(bank(ka)[64:128, o_ + 64:o_ + 128], k_[:, c * 128 + 64:(c + 1) * 128], q_[:, c * 128 + 64:(c + 1) * 128],
                               True, True, rk, [bk(ka)])
                        else:
                            mm(bank(ka + 1)[0:64, o_:o_ + 64], k_[:, c * 128:c * 128 + 64], q_[:, c * 128:c * 128 + 64], True, True, rk, [bk(ka + 1)])
                            mm(bank(ka + 1)[64:128, o_:o_ + 128], k_[:, c * 128 + 64:(c + 1) * 128], q_[:, c * 128:(c + 1) * 128],
                               True, True, rk, [bk(ka + 1)])

            def mask_copy(g):
                ka = (g % 2) * 2
                sa = g % 2
                for d_ in range(2):
                    mk = maskfb[:, d_ * 128:(d_ + 1) * 128].unsqueeze(1).to_broadcast([128, 4, 128])
                    DVE(lambda e: e.copy_predicated(out=atm[sa][:, d_, :].rearrange("p (c j) -> p c j", j=128), mask=mk,
                                                    data=bank(ka + d_).rearrange("p (c j) -> p c j", j=128)),
                        [bk(ka + d_), "maskfb", ("T", sa, 2)], [("T", sa, 2)])

            def o_mm(g):
                ko = 4 + g % 2
                sa = g % 2
                for j in range(4):
                    c = g * 4 + j
                    cs = slice(c * 128, (c + 1) * 128)
                    grp = []
                    if c > 0:
                        grp.append((QK[(0, "q")][:, cs], Zbf[0][:, c, :], [("QK", 0, "q", g), ("Zbf", 0, c)]))
                    if c < NT - 1:
                        grp.append((QK[(1, "q")][:, cs], Zbf[1][:, c, :], [("QK", 1, "q", g), ("Zbf", 1, c)]))
                    vv = V_all[:, c, h * 128:(h + 1) * 128]
                    grp.append((atm[sa][:, 0, j * 128:(j + 1) * 128], vv, [("T", sa, 2), ("V", c)]))
                    grp.append((atm[sa][:, 1, j * 128:(j + 1) * 128], vv, [("T", sa, 2), ("V", c)]))
                    for gi_, (l_, r_, rk) in enumerate(grp):
                        mm(bank(ko)[:, j * 128:(j + 1) * 128], l_, r_, gi_ == 0, gi_ == len(grp) - 1, rk, [bk(ko)])

            def epi_a(g):
                ko = 4 + g % 2
                sa = g % 2
                c0 = (g % 4) * 12
                ACT(lambda e: e.activation(out=sqo[sa], in_=bank(ko), func=AF.Square), [bk(ko)], [("T", sa, 0)])
                DVE(lambda e: e.tensor_reduce(out=st2[:, c0:c0 + 4], in_=sqo[sa].rearrange("p (c j) -> p c j", j=128), axis=AX.X, op=ALU.add),
                    [("T", sa, 0)], [("st2", c0)])
                DVE(lambda e: e.tensor_scalar(out=st2[:, c0 + 4:c0 + 8], in0=st2[:, c0:c0 + 4], scalar1=1.0 / 128, scalar2=EPS,
                                              op0=ALU.mult, op1=ALU.add), [("st2", c0)], [("st2", c0 + 4)])
                POOL(lambda e: e.tensor_tensor(out=st2[:, c0 + 8:c0 + 12], in0=st2[:, c0 + 4:c0 + 8], in1=nhalf.to_broadcast([128, 4]),
                                               op=ALU.pow), [("st2", c0 + 4), "nhalf"], [("st2", c0 + 8)])
                DVE(lambda e: e.tensor_tensor(out=onb[sa].rearrange("p (c j) -> p c j", j=128), in0=bank(ko).rearrange("p (c j) -> p c j", j=128),
                                              in1=st2[:, c0 + 8:c0 + 12].unsqueeze(2).to_broadcast([128, 4, 128]), op=ALU.mult),
                    [bk(ko), ("st2", c0 + 8)], [("T", sa, 1)])

            def epi_b(g):
                kt = 6 + g % 2
                sa = g % 2
                for j in range(4):
                    tp(bankbf(kt)[:, j * 128:(j + 1) * 128], onb[sa][:, j * 128:(j + 1) * 128], [("T", sa, 1)], [bk(kt)])
                gs = slice(g * 512, (g + 1) * 512)
                DVE(lambda e: e.scalar_tensor_tensor(out=oT[:, h, gs], in0=bankbf(kt)[:, 0:512], scalar=cols[:, C_GHG + h:C_GHG + h + 1],
                                                     in1=sgT[:, gs], op0=ALU.mult, op1=ALU.mult),
                    [bk(kt), "const", ("sgT", h % 2, g)], [("oT", h, g)])

            at_mm(0); mask_copy(0); at_mm(1); o_mm(0); mask_copy(1); at_mm(2); epi_a(0); o_mm(1); mask_copy(2); at_mm(3)
            epi_b(0); epi_a(1); o_mm(2); mask_copy(3); epi_b(1); epi_a(2); o_mm(3); epi_b(2); epi_a(3); epi_b(3)
        tap("oTa", oT[:, 0:4, :], [("oT", h, g) for h in range(4) for g in range(4)])
        S.barrier()
        R = Bump(arena, RMARK, ARENA_BYTES)
        KT = R.take([6, SEQ], BF16)
        Vaug = R.take([NT, 4, 130], BF16)
        SMARK = R.cur
        Wq = R.take([3, 768], BF16)
        Wkv = R.take([2, 1024], BF16)
        posi = R.take([NT], I32)
        posf = R.take([NT], F32)
        cosT = R.take([NT, 32], F32)
        sinT = R.take([NT, 32], F32)
        TMARK = R.cur
        ang = R.take([NT, 32], F32)
        kqi = R.take([NT, 32], I32)
        t_a = R.take([NT, 32], F32)
        t_b = R.take([NT, 32], F32)
        for m in range(3):
            S.dma("pool", lambda e, m=m: e.dma_start(out=Wq[:, m, :], in_=wq[m * 128:(m + 1) * 128, :]), "wq", writes=["Wq"])
        for m in range(2):
            S.dma("pool", lambda e, m=m: e.dma_start(out=Wkv[:, m, :], in_=wkv[m * 128:(m + 1) * 128, :]), "wkv", writes=["Wkv"])
        POOL(lambda e: e.memset(Vaug[:, :, :, 128:130], 1.0), [], ["Vones"])
        S.dma("sp", lambda e: e.dma_start(out=posi, in_=pos[b].rearrange("(n p) -> p n", p=128), allow_slow_non_contiguous=True),
              "posi", writes=["posi"])
        DVE(lambda e: e.tensor_copy(out=posf, in_=posi), ["posi"], ["posf"])
        DVE(lambda e: e.tensor_tensor(out=ang, in0=posf.unsqueeze(2).to_broadcast([128, NT, 32]),
                                      in1=invf_b.unsqueeze(1).to_broadcast([128, NT, 32]), op=ALU.mult), ["posf", "const"], ["ang"])
        DVE(lambda e: e.tensor_scalar(out=kqi, in0=ang, scalar1=float(1.0 / (2 * PI)), scalar2=None, op0=ALU.mult), ["ang"], ["kqi"])
        DVE(lambda e: e.tensor_copy(out=t_a, in_=kqi), ["kqi"], ["t_a"])
        C1, C2, C3 = CW1, CW2, CW3
        DVE(lambda e: e.scalar_tensor_tensor(out=t_b, in0=t_a, scalar=-C1, in1=ang, op0=ALU.mult, op1=ALU.add), ["t_a", "ang"], ["t_b"])
        DVE(lambda e: e.scalar_tensor_tensor(out=ang, in0=t_a, scalar=-C2, in1=t_b, op0=ALU.mult, op1=ALU.add), ["t_a", "t_b", "ang"], ["ang"])
        DVE(lambda e: e.scalar_tensor_tensor(out=t_b, in0=t_a, scalar=-C3, in1=ang, op0=ALU.mult, op1=ALU.add), ["t_a", "ang", "t_b"], ["t_b"])
        DVE(lambda e: e.tensor_scalar(out=ang, in0=t_b, scalar1=PI, scalar2=-PI, op0=ALU.min, op1=ALU.max), ["t_b", "ang"], ["ang"])
        ACT(lambda e: e.activation(out=sinT, in_=ang, func=AF.Sin), ["ang"], ["sinT"])
        DVE(lambda e: e.tensor_scalar(out=t_a, in0=t_b, scalar1=PI / 2, scalar2=None, op0=ALU.add), ["t_b", "t_a"], ["t_a"])
        DVE(lambda e: e.tensor_scalar(out=t_b, in0=t_a, scalar1=PI, scalar2=2 * PI, op0=ALU.is_gt, op1=ALU.mult), ["t_a", "t_b"], ["t_b"])
        DVE(lambda e: e.tensor_tensor(out=t_a, in0=t_a, in1=t_b, op=ALU.subtract), ["t_a", "t_b"], ["t_a"])
        DVE(lambda e: e.tensor_scalar(out=t_a, in0=t_a, scalar1=PI, scalar2=-PI, op0=ALU.min, op1=ALU.max), ["t_a"], ["t_a"])
        ACT(lambda e: e.activation(out=cosT, in_=t_a, func=AF.Sin), ["t_a"], ["cosT"])
        if b == 0:
            tap("cosT", cosT, ["cosT"])
            tap("sinT", sinT, ["sinT"])
        S.barrier()
        R = Bump(arena, TMARK, ARENA_BYTES)
        cT = R.take([5, 512], BF16)
        sq = R.take([5, 512], BF16)
        sqq = R.take([768], BF16)
        t1 = R.take([768], F32)
        qf = R.take([1024], BF16)
        kf = R.take([768], BF16)
        krs = R.take([64], F32)
        krr = R.take([64], F32)
        ta = R.take([256], F32)
        tb = R.take([256], F32)
        tak_ = R.take([64], F32)
        tbk_ = R.take([64], F32)
        sqk = R.take([512], F32)
        st3 = R.take([64], F32)
        junk = R.take([64], BF16)
        POOL(lambda e: e.memset(qf[:, 512:1024], 0.0), [], ["qf"])
        pending_tr = []
        for blk in range(4):
            tl = slice(blk * 512, (blk + 1) * 512)
            hk = [("hT", 4 * blk + j) for j in range(4)] + ["W_in"]
            for m in range(5):
                c0 = 2560 + m * 128
                kc_ = (m % 2) * 2
                for kc in range(8):
                    mm(bank(kc_), W_in[:, kc, c0:c0 + 128], hT[:, kc, tl], kc == 0, kc == 7, hk, [bk(kc_)])
                gcol = cols[:, C_GCQ + m:C_GCQ + m + 1]
                ACT(lambda e, m=m, gcol=gcol: e.activation(out=cT[:, m, :], in_=bank(kc_), func=AF.Copy, scale=gcol),
                    [bk(kc_), "const"], [("cT", m)])
                ACT(lambda e, m=m: e.activation(out=sq[:, m, :], in_=bank(kc_), func=AF.Square), [bk(kc_)], [("sq", m)])
            for j in range(4):
                i = blk * 4 + j
                js = slice(j * 128, (j + 1) * 128)
                its = slice(i * 128, (i + 1) * 128)
                c0 = (i % 2) * 32
                for m in range(3):
                    mm(bank(1)[:, 0:1], sq[:, m, js], ones_b[:, 0:1], m == 0, m == 2, [("sq", m), "ones_b"], [bk(1)])
                for m in range(3, 5):
                    mm(bank(1)[:, 2:3], sq[:, m, js], ones_b[:, 0:1], m == 3, m == 4, [("sq", m), "ones_b"], [bk(1)])
                for kc in range(8):
                    mm(bank(1)[:, 64:128], hT[:, kc, its], W_in[:, kc, 3200:3264], kc == 0, kc == 7, [("hT", i), "W_in"], [bk(1)])
                sc = lambda o: st3[:, c0 + o:c0 + o + 1]
                skey = lambda o: ("st3", c0 + o)
                DVE(lambda e, sc=sc: e.tensor_scalar(out=sc(0), in0=bank(1)[:, 0:1], scalar1=1.0 / 384, scalar2=EPS, op0=ALU.mult, op1=ALU.add),
                    [bk(1)], [skey(0)])
                DVE(lambda e, sc=sc: e.tensor_scalar(out=sc(1), in0=bank(1)[:, 2:3], scalar1=1.0 / 256, scalar2=EPS, op0=ALU.mult, op1=ALU.add),
                    [bk(1)], [skey(1)])
                POOL(lambda e, sc=sc: e.tensor_tensor(out=sc(2), in0=sc(0), in1=nhalf, op=ALU.pow), [skey(0), "nhalf"], [skey(2)])
                POOL(lambda e, sc=sc: e.tensor_tensor(out=sc(3), in0=sc(1), in1=nhalf, op=ALU.pow), [skey(1), "nhalf"], [skey(3)])
                for m in range(3):
                    mm(bank(3), cT[:, m, js], Wq[:, m, 0:512], m == 0, m == 2, [("cT", m), "Wq"], [bk(3)])
                for m in range(3):
                    mm(bank(4)[:, 0:256], cT[:, m, js], Wq[:, m, 512:768], m == 0, m == 2, [("cT", m), "Wq"], [bk(4)])
                for m in range(2):
                    mm(bank(5), cT[:, 3 + m, js], Wkv[:, m, 0:512], m == 0, m == 1, [("cT", 3 + m), "Wkv"], [bk(5)])
                for m in range(2):
                    mm(bank(6), cT[:, 3 + m, js], Wkv[:, m, 512:1024], m == 0, m == 1, [("cT", 3 + m), "Wkv"], [bk(6)])
                prev_tr = pending_tr
                pending_tr = []
                for pe_part, _ in prev_tr:
                    pe_part()
                DVE(lambda e: e.tensor_copy(out=krs, in_=bank(1)[:, 64:128]), [bk(1)], ["krs"])
                ACT(lambda e: e.activation(out=sqq[:, 0:512], in_=bank(3), func=AF.Square), [bk(3)], ["sqq"])
                ACT(lambda e: e.activation(out=sqq[:, 512:768], in_=bank(4)[:, 0:256], func=AF.Square), [bk(4), "sqq"], ["sqq"])
                ACT(lambda e: e.activation(out=sqk, in_=bank(5), func=AF.Square), [bk(5)], ["sqk"])
                ACT(lambda e, sc=sc: e.activation(out=junk[:, 0:64], in_=krs, func=AF.Square, accum_out=sc(13)), ["krs"], ["junk", skey(13)])
                for _, act_part in prev_tr:
                    act_part()
                DVE(lambda e, sc=sc: e.tensor_reduce(out=st3[:, c0 + 4:c0 + 8], in_=sqq[:, 0:512].rearrange("p (h j) -> p h j", j=128),
                                                     axis=AX.X, op=ALU.add), ["sqq"], [skey(4)])
                DVE(lambda e, sc=sc: e.tensor_reduce(out=st3[:, c0 + 8:c0 + 12], in_=sqq[:, 512:768].rearrange("p (h j) -> p h j", j=64),
                                                     axis=AX.X, op=ALU.add), ["sqq"], [skey(8)])
                DVE(lambda e: e.tensor_reduce(out=st3[:, c0 + 16:c0 + 20], in_=sqk.rearrange("p (h j) -> p h j", j=128),
                                              axis=AX.X, op=ALU.add), ["sqk"], [skey(16)])
                DVE(lambda e: e.tensor_tensor(out=st3[:, c0 + 4:c0 + 8], in0=st3[:, c0 + 4:c0 + 8], in1=st3[:, c0 + 8:c0 + 12], op=ALU.add),
                    [skey(4), skey(8)], [skey(4)])
                DVE(lambda e, sc=sc: e.tensor_tensor(out=sc(12), in0=sc(2), in1=sc(2), op=ALU.mult), [skey(2)], [skey(12)])
                DVE(lambda e, sc=sc: e.tensor_scalar(out=st3[:, c0 + 4:c0 + 8], in0=st3[:, c0 + 4:c0 + 8], scalar1=sc(12), scalar2=1.0 / 192,
                                                     op0=ALU.mult, op1=ALU.mult), [skey(4), skey(12)], [skey(4)])
                DVE(lambda e: e.tensor_scalar(out=st3[:, c0 + 4:c0 + 8], in0=st3[:, c0 + 4:c0 + 8], scalar1=EPS, scalar2=None, op0=ALU.add),
                    [skey(4)], [skey(4)])
                DVE(lambda e, sc=sc: e.tensor_tensor(out=sc(14), in0=sc(3), in1=sc(3), op=ALU.mult), [skey(3)], [skey(14)])
                DVE(lambda e, sc=sc: e.tensor_scalar(out=st3[:, c0 + 16:c0 + 20], in0=st3[:, c0 + 16:c0 + 20], scalar1=sc(14), scalar2=sc(13),
                                                     op0=ALU.mult, op1=ALU.add), [skey(16), skey(14), skey(13)], [skey(16)])
                DVE(lambda e: e.tensor_scalar(out=st3[:, c0 + 16:c0 + 20], in0=st3[:, c0 + 16:c0 + 20], scalar1=1.0 / 192, scalar2=EPS,
                                              op0=ALU.mult, op1=ALU.add), [skey(16)], [skey(16)])
                POOL(lambda e: e.tensor_tensor(out=st3[:, c0 + 8:c0 + 12], in0=st3[:, c0 + 4:c0 + 8],
                                               in1=nhalf.to_broadcast([128, 4]), op=ALU.pow), [skey(4), "nhalf", skey(8)], [skey(8)])
                POOL(lambda e: e.tensor_tensor(out=st3[:, c0 + 20:c0 + 24], in0=st3[:, c0 + 16:c0 + 20],
                                               in1=nhalf.to_broadcast([128, 4]), op=ALU.pow), [skey(16), "nhalf"], [skey(20)])
                DVE(lambda e, sc=sc: e.tensor_scalar(out=st3[:, c0 + 8:c0 + 12], in0=st3[:, c0 + 8:c0 + 12], scalar1=sc(2), scalar2=None,
                                                     op0=ALU.mult), [skey(8), skey(2)], [skey(8)])
                DVE(lambda e, sc=sc: e.tensor_scalar(out=st3[:, c0 + 24:c0 + 28], in0=st3[:, c0 + 20:c0 + 24], scalar1=sc(3), scalar2=None,
                                                     op0=ALU.mult), [skey(20), skey(3)], [skey(24)])
                fq = st3[:, c0 + 8:c0 + 12]
                rk_ = st3[:, c0 + 20:c0 + 24]
                fkn = st3[:, c0 + 24:c0 + 28]
                DVE(lambda e, fq=fq: e.tensor_tensor(out=t1[:, 0:512].rearrange("p (h j) -> p h j", j=128),
                                                     in0=bank(3).rearrange("p (h j) -> p h j", j=128),
                                                     in1=fq.unsqueeze(2).to_broadcast([128, 4, 128]), op=ALU.mult), [bk(3), skey(8)], ["t1"])
                DVE(lambda e, fq=fq: e.tensor_tensor(out=t1[:, 512:768].rearrange("p (h j) -> p h j", j=64),
                                                     in0=bank(4)[:, 0:256].rearrange("p (h j) -> p h j", j=64),
                                                     in1=fq.unsqueeze(2).to_broadcast([128, 4, 64]), op=ALU.mult), [bk(4), skey(8), "t1"], ["t1"])
                DVE(lambda e, fkn=fkn: e.tensor_tensor(out=sqk.rearrange("p (h j) -> p h j", j=128),
                                                       in0=bank(5).rearrange("p (h j) -> p h j", j=128),
                                                       in1=fkn.unsqueeze(2).to_broadcast([128, 4, 128]), op=ALU.mult),
                    [bk(5), skey(24), "sqk"], ["sqk"])
                ACT(lambda e, sc=sc, i=i: e.activation(out=Vaug[:, i, :, 0:128], in_=bank(6).rearrange("p (h j) -> p h j", j=128),
                                                       func=AF.Copy, scale=sc(3)), [bk(6), skey(3)], [("Vaug", i)])
                DVE(lambda e: e.tensor_tensor(out=qf[:, 0:512], in0=t1[:, 0:512], in1=gq_b[:, 0:512], op=ALU.mult), ["t1", "const"], ["qf"])
                DVE(lambda e: e.tensor_tensor(out=t1[:, 512:768], in0=t1[:, 512:768], in1=gq_b[:, 512:768], op=ALU.mult), ["t1", "const"], ["t1"])

                def rope(E, src, nh_, dst, rk, wk, ta, tb, tak, tbk):
                    s4 = src.rearrange("p (h a r) -> p h a r", a=2, r=32)
                    a4 = ta[:, 0:nh_ * 64].rearrange("p (h a r) -> p h a r", a=2, r=32)
                    b4 = tb[:, 0:nh_ * 64].rearrange("p (h a r) -> p h a r", a=2, r=32)
                    cb = cosT[:, i, :].unsqueeze(1).unsqueeze(1).to_broadcast([128, nh_, 2, 32])
                    sb_ = sinT[:, i, :].unsqueeze(1).to_broadcast([128, nh_, 32])
                    E(lambda e: e.tensor_tensor(out=a4, in0=s4, in1=cb, op=ALU.mult), rk + ["cosT"], [tak])
                    if isinstance(dst, list):
                        E(lambda e: e.scalar_tensor_tensor(out=b4[:, :, 0, :], in0=s4[:, :, 1, :], scalar=-1.0, in1=sb_, op0=ALU.mult, op1=ALU.mult),
                          rk + ["sinT"], [tbk])
                        E(lambda e: e.tensor_tensor(out=b4[:, :, 1, :], in0=s4[:, :, 0, :], in1=sb_, op=ALU.mult), rk + ["sinT", tbk], [tbk])
                        a3 = ta[:, 0:nh_ * 64].rearrange("p (h j) -> p h j", j=64)
                        b3 = tb[:, 0:nh_ * 64].rearrange("p (h j) -> p h j", j=64)
                        for par, dv in enumerate(dst):
                            E(lambda e, par=par, dv=dv: e.tensor_tensor(out=dv, in0=a3[:, par::2, :], in1=b3[:, par::2, :], op=ALU.add),
                              [tak, tbk] + wk, wk)
                    else:
                        E(lambda e: e.tensor_tensor(out=b4[:, :, 0, :], in0=s4[:, :, 1, :], in1=sb_, op=ALU.mult), rk + ["sinT"], [tbk])
                        E(lambda e: e.tensor_tensor(out=b4[:, :, 1, :], in0=s4[:, :, 0, :], in1=sb_, op=ALU.mult), rk + ["sinT", tbk], [tbk])
                        E(lambda e: e.tensor_tensor(out=dst[:, 0:32], in0=ta[:, 0:32], in1=tb[:, 0:32], op=ALU.subtract), [tak, tbk] + wk, wk)
                        E(lambda e: e.tensor_tensor(out=dst[:, 32:64], in0=ta[:, 32:64], in1=tb[:, 32:64], op=ALU.add), [tak, tbk] + wk, wk)

                qz = qf[:, 512:1024].rearrange("p (i r) -> p i r", r=256)
                rope(DVE, t1[:, 512:768], 4, [qz[:, :, 0:64], qz[:, :, 192:256]], ["t1"], ["qf"], ta, tb, "ta", "tb")
                POOL(lambda e: e.tensor_tensor(out=kf[:, 0:512], in0=sqk, in1=gk_b[:, 0:512], op=ALU.mult), ["sqk", "const"], ["kf"])
                POOL(lambda e: e.tensor_tensor(out=krs, in0=krs, in1=gk_b[:, 512:576], op=ALU.mult), ["krs", "const"], ["krs"])
                rope(POOL, krs, 1, krr, ["krs"], ["krr"], tak_, tbk_, "tak", "tbk")
                POOL(lambda e, rk_=rk_: e.tensor_tensor(out=kf[:, 512:768].rearrange("p (h j) -> p h j", j=64),
                                                        in0=krr.unsqueeze(1).to_broadcast([128, 4, 64]),
                                                        in1=rk_.unsqueeze(2).to_broadcast([128, 4, 64]), op=ALU.mult),
                     ["krr", skey(20), "kf"], ["kf"])
                def mk_tr(i=i, its=its):
                    def pe_part():
                        for m in range(8):
                            tp(bankbf(7)[:, m * 128:(m + 1) * 128], qf[:, m * 128:(m + 1) * 128], ["qf"], [bk(7)])
                        for m in range(6):
                            tp(bankbf(0)[:, m * 128:(m + 1) * 128], kf[:, m * 128:(m + 1) * 128], ["kf"], [bk(0)])

                    def act_part():
                        ACT(lambda e: e.activation(out=hT[:, :, its], in_=bankbf(7).rearrange("p (m t) -> p m t", t=128), func=AF.Copy),
                            [bk(7)], [("hT", i)])
                        ACT(lambda e: e.activation(out=KT[:, :, its], in_=bankbf(0)[:, 0:768].rearrange("p (m t) -> p m t", t=128), func=AF.Copy),
                            [bk(0)], [("KT", i)])
                    return pe_part, act_part
                pending_tr.append(mk_tr())
        for pe_part, act_part in pending_tr:
            pe_part()
            act_part()
        pending_tr = []
        if b == 0:
            tap("QT", hT[:, 0:6, :], [("hT", i) for i in range(NT)])
            tap("KT", KT, [("KT", i) for i in range(NT)])
            tap("Vaug", Vaug, [("Vaug", i) for i in range(NT)] + ["Vones"])
        S.barrier()
        R = Bump(arena, SMARK, ARENA_BYTES)
        W_o = R.take([8, D], BF16)
        for kc in range(8):
            S.dma("pool", lambda e, kc=kc: e.dma_start(out=W_o[:, kc, :], in_=w_out[kc * 128:(kc + 1) * 128, :]), "w_o", writes=["W_o"])
        PT = [R.take([512], BF16) for _ in range(3)]
        obuf = R.take([4, 512], F32)
        obn = [R.take([512], BF16) for _ in range(2)]
        st4 = R.take([64], F32)
        junk = R.take([512], BF16)
        QT = hT
        it = 0
        npt = 0
        for qb in range(4):
            qs = slice(qb * 512, (qb + 1) * 512)
            qkeys = [("hT", 4 * qb + j) for j in range(4)]
            for h in range(4):
                ko = 2 + (it % 2) * 2
                it += 1
                DVE(lambda e, ko=ko: e.memset(pp[ko // 2][:, :], 0.0), [], [bk(ko), bk(ko + 1)])
                rp = slice((h % 2) * 64, (h % 2) * 64 + 64)
                rc = 4 + h // 2
                def qk(kc):
                    ksl = slice(kc * 128, (kc + 1) * 128)
                    ks_ = kc % 2
                    mm(bank(ks_), KT[:, h, ksl], QT[:, h, qs], True, False, [("KT", kc)] + qkeys, [bk(ks_)])
                    mm(bank(ks_), KT[:, rc, ksl], QT[:, 4 + h, qs], False, True, [("KT", kc)] + qkeys, [bk(ks_)])

                qk(0)
                for kc in range(NT):
                    ks_ = kc % 2
                    ps_ = npt % 3
                    npt += 1
                    ACT(lambda e, ks_=ks_, ps_=ps_: e.activation(out=PT[ps_], in_=bank(ks_), func=AF.Exp), [bk(ks_)], [("PT", ps_)])
                    if kc + 1 < NT:
                        qk(kc + 1)
                    for j in range(4):
                        ob = ko + j // 2
                        mm(bank(ob)[:, (j % 2) * 256:(j % 2) * 256 + 129], PT[ps_][:, j * 128:(j + 1) * 128], Vaug[:, kc, h, 0:129],
                           False, False, [("PT", ps_), ("Vaug", kc), "Vones"], [bk(ob)], skip=True)
                for j in range(4):
                    ob = ko + j // 2
                    o0 = (j % 2) * 256
                    c0 = ((it * 4 + j) % 16) * 2
                    DVE(lambda e, ob=ob, o0=o0, c0=c0: e.reciprocal(out=st4[:, c0:c0 + 1], in_=bank(ob)[:, o0 + 128:o0 + 129]),
                        [bk(ob)], [("st4", c0)])
                    DVE(lambda e, ob=ob, o0=o0, c0=c0, j=j, h=h: e.tensor_scalar(out=obuf[:, j, h * 128:(h + 1) * 128],
                                                                                 in0=bank(ob)[:, o0:o0 + 128], scalar1=st4[:, c0:c0 + 1],
                                                                                 scalar2=None, op0=ALU.mult),
                        [bk(ob), ("st4", c0)], [("obuf", j)])
            for j in range(4):
                i = qb * 4 + j
                c0 = 32 + (i % 8) * 4
                sj = i % 2
                ACT(lambda e, j=j, c0=c0: e.activation(out=junk[:, 0:512], in_=obuf[:, j, :], func=AF.Square, accum_out=st4[:, c0:c0 + 1]),
                    [("obuf", j)], ["junk", ("st4", c0)])
                DVE(lambda e, c0=c0: e.tensor_scalar(out=st4[:, c0 + 1:c0 + 2], in0=st4[:, c0:c0 + 1], scalar1=1.0 / 512, scalar2=EPS,
                                                     op0=ALU.mult, op1=ALU.add), [("st4", c0)], [("st4", c0 + 1)])
                POOL(lambda e, c0=c0: e.tensor_tensor(out=st4[:, c0 + 2:c0 + 3], in0=st4[:, c0 + 1:c0 + 2], in1=nhalf, op=ALU.pow),
                     [("st4", c0 + 1), "nhalf"], [("st4", c0 + 2)])
                DVE(lambda e, j=j, c0=c0, sj=sj: e.scalar_tensor_tensor(out=obn[sj], in0=obuf[:, j, :], scalar=st4[:, c0 + 2:c0 + 3],
                                                                        in1=gmo_b, op0=ALU.mult, op1=ALU.mult),
                    [("obuf", j), ("st4", c0 + 2), "const"], [("obn", sj)])
                kt = 6 + i % 2
                for m in range(4):
                    tp(bankbf(kt)[:, m * 128:(m + 1) * 128], obn[sj][:, m * 128:(m + 1) * 128], [("obn", sj)], [bk(kt)])
                ACT(lambda e, kt=kt, i=i: e.activation(out=oT[:, 4:8, i * 128:(i + 1) * 128],
                                                       in_=bankbf(kt)[:, 0:512].rearrange("p (m t) -> p m t", t=128), func=AF.Copy),
                    [bk(kt)], [("oT", 4, i)])
        tap("oT", oT, [("oT", 4, i) for i in range(NT)])
        S.barrier()
        R = Bump(arena, RMARK, SMARK)
        xin = [R.take([D], F32) for _ in range(2)]
        x1o = [R.take([D], F32) for _ in range(2)]
        nxt_seq = b + 1 < nseq
        if nxt_seq:
            xinA = [R.take([D], F32) for _ in range(2)]
            hbfA = [R.take([D], BF16) for _ in range(2)]
            junkA = R.take([D], BF16)
            stA = R.take([64], F32)
        for i in range(NT):
            sl = i % 2
            its = slice(i * 128, (i + 1) * 128)
            S.dma("sp", lambda e, sl=sl, i=i: e.dma_start(out=xin[sl], in_=x[b, i * 128:(i + 1) * 128, :]), f"xin{sl}",
                  writes=[("xin", sl)])
            kp = (i % 2) * 2
            for half in range(2):
                for m in range(8):
                    mm(bank(kp + half), oT[:, m, its], W_o[:, m, half * 512:(half + 1) * 512], m == 0, m == 7, [], [bk(kp + half)])
            DVE(lambda e, sl=sl, kp=kp: e.tensor_tensor(out=x1o[sl], in0=pp[kp // 2][:, :], in1=xin[sl], op=ALU.add),
                [bk(kp), bk(kp + 1), ("xin", sl), ("x1o", sl)], [("x1o", sl)])
            S.dma("sp", lambda e, sl=sl, i=i: e.dma_start(out=x1s[b, i * 128:(i + 1) * 128, :], in_=x1o[sl]), f"x1o{sl}",
                  reads=[("x1o", sl)])
            if nxt_seq:
                a0_tile(b + 1, i, xinA, hbfA, junkA, stA, 4)
        S.barrier()

    Bm = Bump(arena, MARK0, ARENA_BYTES)
    W_up = Bm.take([8, DFF], BF16)
    W_dn = Bm.take([32, D], BF16)
    gffn_b = Bm.take([D], F32)
    TB = 256
    NJ = TB // 128
    x1t = [[Bm.take([D], F32) for _ in range(NJ)] for _ in range(2)]
    hbf = [Bm.take([D], BF16) for _ in range(2)]
    h2T = Bm.take([8, TB], BF16)
    aT = Bm.take([32, TB], BF16)
    rl = [Bm.take([512], F32) for _ in range(2)]
    yo = [Bm.take([D], F32) for _ in range(2)]
    junk = Bm.take([D], BF16)
    st5 = Bm.take([64], F32)
    S.dma("sp", lambda e: e.dma_start(out=gffn_b, in_=g_ffn.partition_broadcast(128)), "const2", writes=["gffn"])
    for kc in range(8):
        for q4 in range(4):
            S.dma("pool", lambda e, kc=kc, q4=q4: e.dma_start(out=W_up[:, kc, q4 * 1024:(q4 + 1) * 1024],
                                                             in_=w_up[kc * 128:(kc + 1) * 128, q4 * 1024:(q4 + 1) * 1024]),
                  "w_up", writes=["W_up"])
    for c in range(32):
        S.dma("pool", lambda e, c=c: e.dma_start(out=W_dn[:, c, :], in_=w_down[c * 128:(c + 1) * 128, :]), "w_dn", writes=["W_dn"])
    x1f = x1s.rearrange("b s d -> (b s) d")
    yf = y.rearrange("b s d -> (b s) d")
    nblk = nseq * SEQ // TB
    nyo = 0
    for blk in range(nblk):
        xs = blk % 2
        for j in range(NJ):
            t = blk * NJ + j
            S.dma("sp", lambda e, xs=xs, j=j, t=t: e.dma_start(out=x1t[xs][j], in_=x1f[t * 128:(t + 1) * 128, :]), f"x1t{xs}{j}",
                  writes=[("x1t", xs, j)])
            c0 = (t % 8) * 4
            sl = t % 2
            ACT(lambda e, xs=xs, j=j, c0=c0: e.activation(out=junk, in_=x1t[xs][j], func=AF.Square, accum_out=st5[:, c0:c0 + 1]),
                [("x1t", xs, j)], ["junk", ("st5", c0)])
            DVE(lambda e, c0=c0: e.tensor_scalar(out=st5[:, c0 + 1:c0 + 2], in0=st5[:, c0:c0 + 1], scalar1=1.0 / D, scalar2=EPS,
                                                 op0=ALU.mult, op1=ALU.add), [("st5", c0)], [("st5", c0 + 1)])
            POOL(lambda e, c0=c0: e.tensor_tensor(out=st5[:, c0 + 2:c0 + 3], in0=st5[:, c0 + 1:c0 + 2], in1=nhalf, op=ALU.pow),
                 [("st5", c0 + 1), "nhalf"], [("st5", c0 + 2)])
            DVE(lambda e, xs=xs, j=j, c0=c0, sl=sl: e.scalar_tensor_tensor(out=hbf[sl], in0=x1t[xs][j], scalar=st5[:, c0 + 2:c0 + 3],
                                                                           in1=gffn_b, op0=ALU.mult, op1=ALU.mult),
                [("x1t", xs, j), ("st5", c0 + 2), "gffn"], [("hbf", sl)])
            for kc in range(8):
                tp(bankbf(0)[:, kc * 128:(kc + 1) * 128], hbf[sl][:, kc * 128:(kc + 1) * 128], [("hbf", sl)], [bk(0)])
            ACT(lambda e, j=j: e.activation(out=h2T[:, :, j * 128:(j + 1) * 128], in_=bankbf(0).rearrange("p (k t) -> p k t", t=128),
                                            func=AF.Copy), [bk(0)], [("h2T", j)])
        hk2 = [("h2T", j) for j in range(NJ)] + ["W_up"]
        for cp in range(16):
            ku = 1 + cp % 2
            for c in range(2):
                cc = cp * 2 + c
                for kc in range(8):
                    mm(bank(ku)[:, c * TB:(c + 1) * TB], W_up[:, kc, cc * 128:(cc + 1) * 128], h2T[:, kc, :], kc == 0, kc == 7, hk2, [bk(ku)])
            rs = cp % 2
            ACT(lambda e, ku=ku, rs=rs: e.activation(out=rl[rs], in_=bank(ku), func=AF.Relu), [bk(ku)], [("rl", rs)])
            POOL(lambda e, rs=rs, cp=cp: e.tensor_tensor(out=aT[:, 2 * cp:2 * cp + 2, :], in0=rl[rs].rearrange("p (c t) -> p c t", t=TB),
                                                         in1=rl[rs].rearrange("p (c t) -> p c t", t=TB), op=ALU.mult),
                 [("rl", rs)], [("aT", cp)])
        for j in range(NJ):
            t = blk * NJ + j
            kd = 4 + (t % 2) * 2
            for half in range(2):
                for c in range(32):
                    mm(bank(kd + half), aT[:, c, j * 128:(j + 1) * 128], W_dn[:, c, half * 512:(half + 1) * 512], c == 0, c == 31,
                       [("aT", c // 2), "W_dn"], [bk(kd + half)])
            ys = nyo % 2
            nyo += 1
            DVE(lambda e, ys=ys, kd=kd, xs=xs, j=j: e.tensor_tensor(out=yo[ys], in0=pp[kd // 2][:, :], in1=x1t[xs][j], op=ALU.add),
                [bk(kd), bk(kd + 1), ("x1t", xs, j), ("yo", ys)], [("yo", ys)])
            S.dma("sp", lambda e, ys=ys, t=t: e.dma_start(out=yf[t * 128:(t + 1) * 128, :], in_=yo[ys]), f"yo{ys}", reads=[("yo", ys)])
    S.barrier()

    sems = {n: es.enter_context(nc.semaphore(n)) for n in sorted(S.sem_names)}
    with nc.Block() as block:
        S.emit(block, sems)
    es.close()
    return nc, dbg_out


def _prep_inputs(inputs, nseq, ncores):
    f32 = np.float32
    g = lambda k: np.ascontiguousarray(np.asarray(inputs[k]))
    wq_ = g("w_q_up")[0].reshape(384, 4, 192)
    wq_p = np.concatenate([wq_[:, :, :128].reshape(384, 512), wq_[:, :, 128:].reshape(384, 256)], axis=1)
    wkv_ = g("w_kv_up")[0].reshape(256, 4, 256)
    wkv_p = np.concatenate([wkv_[:, :, :128].reshape(256, 512), wkv_[:, :, 128:].reshape(256, 512)], axis=1)
    invf = (10000.0 ** (-(np.arange(0, 64, 2, dtype=f32)) / f32(64))).astype(f32).reshape(1, 32)
    shared = {
        "g_mix": g("g_mix_norm").reshape(1, D), "w_in": g("w_in")[0], "lb_param": g("lb_param")[:, 0:2, :],
        "g_hg": g("g_hgrn_out")[0], "g_cq": g("g_cq").reshape(1, 384), "wq": np.ascontiguousarray(wq_p),
        "g_ckv": g("g_ckv").reshape(1, 256), "wkv": np.ascontiguousarray(wkv_p), "g_q": g("g_q_norm").reshape(1, 192),
        "g_k": g("g_k_norm").reshape(1, 192), "g_mo": g("g_mla_out").reshape(1, 512), "w_out": g("w_out")[0],
        "g_ffn": g("g_ffn_norm").reshape(1, D), "w_up": g("w_up")[0], "w_down": g("w_down")[0], "invf": invf,
    }
    x = g("x")
    pos = g("positions").astype(np.int32)
    maps = []
    for c in range(ncores):
        m = dict(shared)
        m["x"] = np.ascontiguousarray(x[c * nseq:(c + 1) * nseq])
        m["pos"] = np.ascontiguousarray(pos[c * nseq:(c + 1) * nseq])
        maps.append(m)
    return maps


def kernel(**inputs):
    nseq = 4
    nc, _ = build(nseq)
    maps = _prep_inputs(inputs, nseq, NCORES)
    res = run_bass_kernel_spmd(nc, maps, core_ids=list(range(NCORES)))
    out = np.concatenate([np.asarray(r["y"]) for r in res.results], axis=0)
    return out.astype(np.float32, copy=False)
```

```python
import numpy as np
from contextlib import ExitStack
import concourse.bass as bass
import concourse.mybir as mybir
from concourse.bass_utils import run_bass_kernel_spmd

F32 = mybir.dt.float32
BF16 = mybir.dt.bfloat16
I32 = mybir.dt.int32
AF = mybir.ActivationFunctionType
ALU = mybir.AluOpType
AX = mybir.AxisListType

NCORES = 8
SEQ = 2048
NT = SEQ // 128
D = 1024
DIN = 3264
DFF = 4096
EPS = 1e-6
PI = float(np.pi)
ARENA_BYTES = 212480

ENGS = ("pe", "act", "dve", "pool", "sp")


def _cody_waite():
    two_pi = 2.0 * np.pi
    c1 = 6.28125
    r1 = two_pi - c1
    m, e = np.frexp(r1)
    c2 = float(np.ldexp(np.round(m * 2 ** 11) / 2 ** 11, e))
    c3 = float(np.float32(two_pi - c1 - c2))
    return c1, c2, c3


CW1, CW2, CW3 = _cody_waite()


class _Rec:
    def __getattr__(self, name):
        def f(*a, **k):
            self.call = (name, a, k)
            return self
        return f


class Sched:
    def __init__(self):
        self.q = {e: [] for e in ENGS}
        self.cnt = {e: 0 for e in ENGS}
        self.seen = {e: {} for e in ENGS}
        self.bufs = {}
        self.dma_tot = {}
        self.sem_names = set(ENGS)

    def _st(self, k):
        st = self.bufs.get(k)
        if st is None:
            st = self.bufs[k] = {"w": None, "r": {}}
        return st

    def _deps(self, eng, reads, writes):
        toks = []
        for k in reads:
            st = self._st(k)
            if st["w"] is not None:
                toks.append(st["w"])
        for k in writes:
            st = self._st(k)
            if st["w"] is not None:
                toks.append(st["w"])
            toks.extend(st["r"].items())
        waits = {}
        for (s, v) in toks:
            if s == "pe" and eng == "pe":
                continue
            if self.seen[eng].get(s, 0) >= v:
                continue
            if waits.get(s, 0) < v:
                waits[s] = v
        for s, v in waits.items():
            self.seen[eng][s] = v
        return list(waits.items())

    def _commit(self, tok, reads, writes):
        for k in reads:
            r = self._st(k)["r"]
            if r.get(tok[0], 0) < tok[1]:
                r[tok[0]] = tok[1]
        for k in writes:
            st = self._st(k)
            st["w"] = tok
            st["r"] = {}

    def op(self, eng, fn, reads=(), writes=()):
        rec = _Rec()
        fn(rec)
        waits = self._deps(eng, reads, writes)
        self.cnt[eng] += 1
        tok = (eng, self.cnt[eng])
        self.q[eng].append((rec.call, waits, (eng, 1)))
        self._commit(tok, reads, writes)

    def dma(self, eng, fn, sem, reads=(), writes=()):
        rec = _Rec()
        fn(rec)
        self.sem_names.add(sem)
        waits = self._deps(eng, reads, writes)
        self.dma_tot[sem] = self.dma_tot.get(sem, 0) + 16
        tok = (sem, self.dma_tot[sem])
        self.q[eng].append((rec.call, waits, (sem, 16)))
        self._commit(tok, reads, writes)

    def barrier(self):
        for e in ENGS:
            waits = []
            for e2 in ENGS:
                if self.cnt[e2] > self.seen[e].get(e2, 0):
                    waits.append((e2, self.cnt[e2]))
                    self.seen[e][e2] = self.cnt[e2]
            for s, tot in self.dma_tot.items():
                if tot > self.seen[e].get(s, 0):
                    waits.append((s, tot))
                    self.seen[e][s] = tot
            self.q[e].append((None, waits, None))
        self.bufs = {}

    def emit(self, block, sems):
        handles = {"pe": block.tensor, "act": block.scalar, "dve": block.vector,
                   "pool": block.gpsimd, "sp": block.sync}

        def mk(e):
            ops = self.q[e]

            def body(engine):
                for fn, waits, inc in ops:
                    for s, v in waits:
                        engine.wait_ge(sems[s], v)
                    if fn is not None:
                        name, a, k = fn
                        getattr(engine, name)(*a, **k).then_inc(sems[inc[0]], inc[1])
            return body

        for e in ENGS:
            handles[e](mk(e))


def _dsize(dt):
    return 2 if dt == BF16 else 4


class Bump:
    def __init__(self, arena, start, end):
        self.arena, self.cur, self.end = arena, start, end

    def take(self, shape, dt):
        n = int(np.prod(shape)) * _dsize(dt)
        off = (self.cur + 63) // 64 * 64
        n4 = (n + 3) // 4 * 4
        self.cur = off + n4
        assert self.cur <= self.end, (self.cur, self.end)
        ap = self.arena[:, off // 4:(off + n4) // 4]
        if dt != F32:
            ap = ap.bitcast(dt)
        if n4 != n:
            ap = ap[:, 0:int(np.prod(shape))]
        if len(shape) == 2:
            ap = ap.rearrange("p (a b) -> p a b", b=shape[1])
        elif len(shape) == 3:
            ap = ap.rearrange("p (a b c) -> p a b c", b=shape[1], c=shape[2])
        return ap


def build(nseq=4, dbg=None):
    nc = bass.Bass("TRN2", target_bir_lowering=False)
    S = Sched()
    dbg_out = {}

    def din(name, shape, dt=F32):
        return nc.dram_tensor(name, list(shape), dt, kind="ExternalInput").ap()

    x = din("x", [nseq, SEQ, D])
    pos = din("pos", [nseq, SEQ], I32)
    g_mix = din("g_mix", [1, D])
    w_in = din("w_in", [D, DIN])
    lb_param = din("lb_param", [2, 2, 512])
    g_hg = din("g_hg", [4, 128])
    g_cq = din("g_cq", [1, 384])
    wq = din("wq", [384, 768])
    g_ckv = din("g_ckv", [1, 256])
    wkv = din("wkv", [256, 1024])
    g_q = din("g_q", [1, 192])
    g_k = din("g_k", [1, 192])
    g_mo = din("g_mo", [1, 512])
    w_out = din("w_out", [D, D])
    g_ffn = din("g_ffn", [1, D])
    w_up = din("w_up", [D, DFF])
    w_down = din("w_down", [DFF, D])
    invf = din("invf", [1, 32])
    y = nc.dram_tensor("y", [nseq, SEQ, D], F32, kind="ExternalOutput").ap()
    x1s = nc.dram_tensor("x1s", [nseq, SEQ, D], F32).ap()

    es = ExitStack()
    arena = es.enter_context(nc.sbuf_tensor("arena", [128, ARENA_BYTES // 4], F32))[:]
    pp = [es.enter_context(nc.psum_tensor(f"pp{i}", [128, 1024], F32)) for i in range(4)]

    def bank(k):
        return pp[k // 2][:, (k % 2) * 512:(k % 2) * 512 + 512]

    def bankbf(k):
        return bank(k).bitcast(BF16)

    def bk(k):
        return ("bank", k)

    def PE(fn, r, w):
        S.op("pe", fn, r, w)

    def ACT(fn, r, w):
        S.op("act", fn, r, w)

    def DVE(fn, r, w):
        S.op("dve", fn, r, w)

    def POOL(fn, r, w):
        S.op("pool", fn, r, w)

    def mm(out, lhsT, rhs, start, stop, r, w, skip=False):
        if skip:
            PE(lambda e: e.matmul(out, lhsT=lhsT, rhs=rhs, start=start, stop=stop, skip_group_check=True), r, w)
        else:
            PE(lambda e: e.matmul(out, lhsT=lhsT, rhs=rhs, start=start, stop=stop), r, w)

    def tp(out, in_, r, w):
        PE(lambda e: e.transpose(out=out, in_=in_, identity=ident), list(r) + ["ident"], w)

    def tap(name, ap, key):
        if dbg is None or name not in dbg:
            return
        shp = list(ap.shape)
        t = nc.dram_tensor("dbg_" + name, shp, ap.dtype, kind="ExternalOutput").ap()
        dbg_out[name] = t
        S.dma("sp", lambda e: e.dma_start(out=t, in_=ap), "dbg", reads=key)

    P = Bump(arena, 0, ARENA_BYTES)
    ident = P.take([128], BF16)
    maskfb = P.take([256], I32)
    cols = P.take([64], F32)
    ones_b = P.take([2], BF16)
    nhalf = P.take([1], F32)
    ones_f = P.take([512], F32)
    MARK0 = P.cur
    C_GCQ, C_GCKV, C_GHG, C_LB, C_LNOML, C_LBP = 0, 3, 5, 9, 17, 25

    A = Bump(arena, MARK0, ARENA_BYTES)
    W_in = A.take([8, DIN], BF16)
    gmix_b = A.take([D], F32)
    gq_b = A.take([768], F32)
    gk_b = A.take([768], F32)
    gmo_b = A.take([512], F32)
    invf_b = A.take([32], F32)
    hT = A.take([8, SEQ], BF16)
    oT = A.take([8, SEQ], BF16)
    RMARK = A.cur

    R0 = Bump(arena, RMARK, ARENA_BYTES)
    identf = R0.take([128], F32)
    lbt = R0.take([8], F32)

    def cdma(out, in_, slow=False):
        if slow:
            S.dma("sp", lambda e: e.dma_start(out=out, in_=in_, allow_slow_non_contiguous=True), "const", writes=["const"])
        else:
            S.dma("sp", lambda e: e.dma_start(out=out, in_=in_), "const", writes=["const"])

    cdma(gmix_b, g_mix.partition_broadcast(128))
    for h in range(4):
        cdma(gq_b[:, h * 128:(h + 1) * 128], g_q[:, 0:128].partition_broadcast(128))
        cdma(gq_b[:, 512 + h * 64:512 + (h + 1) * 64], g_q[:, 128:192].partition_broadcast(128))
        cdma(gk_b[:, h * 128:(h + 1) * 128], g_k[:, 0:128].partition_broadcast(128))
        cdma(gk_b[:, 512 + h * 64:512 + (h + 1) * 64], g_k[:, 128:192].partition_broadcast(128))
    cdma(gmo_b, g_mo.partition_broadcast(128))
    cdma(invf_b, invf.partition_broadcast(128))
    cdma(cols[:, C_GCQ:C_GCQ + 3], g_cq[0].rearrange("(m p) -> p m", p=128), slow=True)
    cdma(cols[:, C_GCKV:C_GCKV + 2], g_ckv[0].rearrange("(m p) -> p m", p=128), slow=True)
    cdma(cols[:, C_GHG:C_GHG + 4], g_hg.rearrange("h e -> e h"), slow=True)
    for d_ in range(2):
        for s_ in range(2):
            o = C_LBP + (d_ * 2 + s_) * 4
            cdma(cols[:, o:o + 4], lb_param[d_, s_].rearrange("(h p) -> p h", p=128), slow=True)
    for kc in range(8):
        S.dma("pool", lambda e, kc=kc: e.dma_start(out=W_in[:, kc, :], in_=w_in[kc * 128:(kc + 1) * 128, :], max_dma_last_dim=4096),
              "w_in", writes=["W_in"])

    POOL(lambda e: e.memset(identf, 0.0), [], ["identf"])
    POOL(lambda e: e.affine_select(out=identf, in_=identf, pattern=[[-1, 128]], compare_op=ALU.not_equal, fill=1.0,
                                   base=0, channel_multiplier=1), ["identf"], ["identf"])
    DVE(lambda e: e.tensor_copy(out=ident, in_=identf), ["identf"], ["ident"])
    POOL(lambda e: e.iota(maskfb[:, 0:128], pattern=[[1, 128]], base=0, channel_multiplier=-1), [], ["maskfb"])
    POOL(lambda e: e.iota(maskfb[:, 128:256], pattern=[[-1, 128]], base=0, channel_multiplier=1), ["maskfb"], ["maskfb"])
    DVE(lambda e: e.tensor_single_scalar(out=maskfb, in_=maskfb, scalar=0, op=ALU.is_ge), ["maskfb"], ["maskfb"])
    POOL(lambda e: e.memset(ones_b, 1.0), [], ["ones_b"])
    POOL(lambda e: e.memset(nhalf, -0.5), [], ["nhalf"])
    POOL(lambda e: e.memset(ones_f, 1.0), [], ["ones_f"])
    DVE(lambda e: e.tensor_scalar(out=gq_b, in0=gq_b, scalar1=float(192 ** -0.5), scalar2=None, op0=ALU.mult), ["const"], ["const"])
    lbp = cols[:, C_LBP:C_LBP + 16].rearrange("p (d s h) -> p d s h", s=2, h=4)
    lbv = cols[:, C_LB:C_LB + 8]
    DVE(lambda e: e.tensor_tensor(out=lbt.rearrange("p (d h) -> p d h", h=4), in0=lbp[:, :, 1, :], in1=lbp[:, :, 0, :], op=ALU.subtract),
        ["const"], ["lbt"])
    ACT(lambda e: e.activation(out=lbt, in_=lbt, func=AF.Exp), ["lbt"], ["lbt"])
    DVE(lambda e: e.tensor_scalar(out=lbt, in0=lbt, scalar1=1.0, scalar2=None, op0=ALU.add), ["lbt"], ["lbt"])
    DVE(lambda e: e.reciprocal(out=lbv, in_=lbt), ["lbt", "const"], ["const"])
    ACT(lambda e: e.activation(out=cols[:, C_LNOML:C_LNOML + 8], in_=lbv, func=AF.Ln, scale=-1.0, bias=1.0), ["const"], ["const"])
    S.barrier()

    for b in range(nseq):
        def a0_tile(bb, i, xin, hbf, junk, st, kb0):
            sl = i % 2
            S.dma("sp", lambda e: e.dma_start(out=xin[sl], in_=x[bb, i * 128:(i + 1) * 128, :]), f"xinA{sl}", writes=[("xinA", sl)])
            c0 = (i % 8) * 4
            ACT(lambda e: e.activation(out=junk, in_=xin[sl], func=AF.Square, accum_out=st[:, c0:c0 + 1]), [("xinA", sl)], ["junkA", ("stA", c0)])
            DVE(lambda e: e.tensor_scalar(out=st[:, c0 + 1:c0 + 2], in0=st[:, c0:c0 + 1], scalar1=1.0 / D, scalar2=EPS, op0=ALU.mult, op1=ALU.add),
                [("stA", c0)], [("stA", c0 + 1)])
            POOL(lambda e: e.tensor_tensor(out=st[:, c0 + 2:c0 + 3], in0=st[:, c0 + 1:c0 + 2], in1=nhalf, op=ALU.pow),
                 [("stA", c0 + 1), "nhalf"], [("stA", c0 + 2)])
            DVE(lambda e: e.scalar_tensor_tensor(out=hbf[sl], in0=xin[sl], scalar=st[:, c0 + 2:c0 + 3], in1=gmix_b, op0=ALU.mult, op1=ALU.mult),
                [("xinA", sl), ("stA", c0 + 2), "const"], [("hbfA", sl)])
            k = kb0 + i % 2
            for kc in range(8):
                tp(bankbf(k)[:, kc * 128:(kc + 1) * 128], hbf[sl][:, kc * 128:(kc + 1) * 128], [("hbfA", sl)], [bk(k)])
            ACT(lambda e: e.activation(out=hT[:, :, i * 128:(i + 1) * 128], in_=bankbf(k).rearrange("p (k t) -> p k t", t=128), func=AF.Copy),
                [bk(k)], [("hT", i)])

        R = Bump(arena, RMARK, ARENA_BYTES)
        V_all = R.take([NT, 512], BF16)
        XMARK = R.cur
        if b == 0:
            xinA = [R.take([D], F32) for _ in range(2)]
            hbfA = [R.take([D], BF16) for _ in range(2)]
            junkA = R.take([D], BF16)
            stA = R.take([64], F32)
            for i in range(NT):
                a0_tile(0, i, xinA, hbfA, junkA, stA, 0)
        tap("hT", hT, [("hT", i) for i in range(NT)])
        for i in range(NT):
            k = 2 + i % 2
            for kc in range(8):
                mm(bank(k), hT[:, kc, i * 128:(i + 1) * 128], W_in[:, kc, 1536:2048], kc == 0, kc == 7,
                   [("hT", i), "W_in"], [bk(k)])
            DVE(lambda e, k=k, i=i: e.tensor_copy(out=V_all[:, i, :], in_=bank(k)), [bk(k)], [("V", i)])
        S.barrier()
        R = Bump(arena, XMARK, ARENA_BYTES)
        sgTs = [R.take([SEQ], BF16) for _ in range(2)]
        TS = [[R.take([512], F32) for _ in range(5)] + [R.take([516], F32)] for _ in range(2)]
        QK = {(d_, w_): R.take([SEQ], BF16) for d_ in range(2) for w_ in "qk"}
        Zbf = [R.take([NT, 128], BF16) for _ in range(2)]
        Y = [[R.take([128], F32) for _ in range(2)] for _ in range(2)]
        sqo = [TS[s_][0] for s_ in range(2)]
        onb = [TS[s_][1].bitcast(BF16)[:, 0:512] for s_ in range(2)]
        atm = [TS[s_][2].bitcast(BF16).rearrange("p (d n) -> p d n", d=2) for s_ in range(2)]
        ktok = [TS[s_][4].bitcast(BF16)[:, 0:512] for s_ in range(2)]
        Xs = [[TS[i_][3][:, d_ * 128:(d_ + 1) * 128] for i_ in range(2)] for d_ in range(2)]
        Rall = [R.take([NT], F32) for _ in range(2)]
        Eall = [R.take([NT], F32) for _ in range(2)]
        dR = [R.take([3, NT], F32) for _ in range(2)]
        efac = [R.take([3, NT], F32) for _ in range(2)]
        tot = R.take([2, 4], F32)
        carr = R.take([2, 4], F32)
        st2 = R.take([64], F32)
        for s_ in range(2):
            POOL(lambda e: e.memset(TS[s_][5][:, 0:1], 0.0), [], [("T", s_, 5)])
        POOL(lambda e: e.memset(carr, 0.0), [], ["carr"])
        for h in range(4):
            def gates(hh):
                for blk in range(4):
                    tl = slice(blk * 512, (blk + 1) * 512)
                    hk = [("hT", 4 * blk + j) for j in range(4)] + ["W_in"]
                    kg = 6 + blk % 2
                    for kc in range(8):
                        mm(bank(kg), W_in[:, kc, 2048 + hh * 128:2048 + (hh + 1) * 128], hT[:, kc, tl], kc == 0, kc == 7, hk, [bk(kg)])
                    ACT(lambda e: e.activation(out=sgTs[hh % 2][:, tl], in_=bank(kg), func=AF.Silu), [bk(kg)], [("sgT", hh % 2, blk)])

            if h == 0:
                gates(0)
            sgT = sgTs[h % 2]

            def prep_mm(blk):
                tl = slice(blk * 512, (blk + 1) * 512)
                hk = [("hT", 4 * blk + j) for j in range(4)] + ["W_in"]
                for j, c0 in enumerate((0, 512, 1024)):
                    kk = (3 * blk + j) % 6
                    for kc in range(8):
                        mm(bank(kk), W_in[:, kc, c0 + h * 128:c0 + (h + 1) * 128], hT[:, kc, tl], kc == 0, kc == 7, hk, [bk(kk)])

            def front(it):
                blk, d_ = it // 2, it % 2
                T = TS[it % 2]
                tk = [("T", it % 2, j) for j in range(6)]
                kz = (3 * blk + 1 + d_) % 6
                lbc = cols[:, C_LB + d_ * 4 + h:C_LB + d_ * 4 + h + 1]
                ACT(lambda e: e.activation(out=T[0], in_=bank(kz), func=AF.Exp, scale=-1.0), [bk(kz)], [tk[0]])
                ACT(lambda e: e.activation(out=T[1], in_=T[0], func=AF.Ln, scale=lbc, bias=1.0), [tk[0], "const"], [tk[1]])
                ACT(lambda e: e.activation(out=T[2], in_=T[0], func=AF.Ln, scale=1.0, bias=1.0), [tk[0]], [tk[2]])
                DVE(lambda e: e.tensor_tensor(out=T[4], in0=bank(kz), in1=T[2], op=ALU.add), [bk(kz), tk[2]], [tk[4]])
                DVE(lambda e: e.tensor_tensor_scan(out=T[5][:, 1:513], data0=T[1], data1=T[2], initial=0.0, op0=ALU.add, op1=ALU.subtract),
                    [tk[1], tk[2], tk[5]], [tk[5]])
                DVE(lambda e: e.tensor_copy(out=tot[:, d_, blk:blk + 1], in_=T[5][:, 512:513]), [tk[5]], [("tot", d_)])
                Bx = T[5][:, 1:513] if d_ == 0 else T[5][:, 0:512]
                kB = tk[5]
                DVE(lambda e: e.tensor_copy(out=Rall[d_][:, blk * 4:(blk + 1) * 4], in_=Bx[:, 63:512:128]), [kB], [("Rall", d_)])
                eo_ = 127 if d_ == 0 else 0
                DVE(lambda e: e.tensor_copy(out=Eall[d_][:, blk * 4:(blk + 1) * 4], in_=Bx[:, eo_:512:128]), [kB], [("Eall", d_)])
                DVE(lambda e: e.tensor_tensor(out=T[0].rearrange("p (c j) -> p c j", j=128), in0=Bx.rearrange("p (c j) -> p c j", j=128),
                                              in1=Rall[d_][:, blk * 4:(blk + 1) * 4].unsqueeze(2).to_broadcast([128, 4, 128]),
                                              op=ALU.subtract), [kB, ("Rall", d_)], [tk[0]])

            def back(it):
                blk, d_ = it // 2, it % 2
                tl = slice(blk * 512, (blk + 1) * 512)
                T = TS[it % 2]
                tk = [("T", it % 2, j) for j in range(6)]
                kq = (3 * blk) % 6
                lno = cols[:, C_LNOML + d_ * 4 + h:C_LNOML + d_ * 4 + h + 1]
                sgn = 1.0 if d_ == 0 else -1.0
                ACT(lambda e: e.activation(out=T[2], in_=T[0], func=AF.Exp, scale=sgn), [tk[0]], [tk[2]])
                POOL(lambda e: e.tensor_tensor(out=T[3], in0=T[4], in1=T[0], op=(ALU.add if d_ == 0 else ALU.subtract)),
                     [tk[4], tk[0]], [tk[3]])
                ACT(lambda e: e.activation(out=QK[(d_, "k")][:, tl], in_=T[3], func=AF.Exp, scale=-1.0, bias=lno),
                    [tk[3], "const"], [("QK", d_, "k", blk)])
                DVE(lambda e: e.tensor_tensor(out=QK[(d_, "q")][:, tl], in0=bank(kq), in1=T[2], op=ALU.mult),
                    [bk(kq), tk[2]], [("QK", d_, "q", blk)])

            prep_mm(0)
            front(0)
            for it in range(8):
                if it % 2 == 0 and it // 2 + 1 < 4:
                    prep_mm(it // 2 + 1)
                if it + 1 < 8:
                    front(it + 1)
                if it == 1 and h + 1 < 4:
                    gates(h + 1)
                back(it)
            for d_ in range(2):
                DVE(lambda e: e.tensor_copy(out=carr[:, d_, 1:2], in_=tot[:, d_, 0:1]), [("tot", d_), "carr"], ["carr"])
                DVE(lambda e: e.tensor_tensor(out=carr[:, d_, 2:3], in0=carr[:, d_, 1:2], in1=tot[:, d_, 1:2], op=ALU.add),
                    [("tot", d_), "carr"], ["carr"])
                DVE(lambda e: e.tensor_tensor(out=carr[:, d_, 3:4], in0=carr[:, d_, 2:3], in1=tot[:, d_, 2:3], op=ALU.add),
                    [("tot", d_), "carr"], ["carr"])
                for arr, key in ((Rall, "Rall"), (Eall, "Eall")):
                    DVE(lambda e, arr=arr: e.tensor_tensor(out=arr[d_].rearrange("p (b c) -> p b c", c=4),
                                                           in0=arr[d_].rearrange("p (b c) -> p b c", c=4),
                                                           in1=carr[:, d_, :].unsqueeze(2).to_broadcast([128, 4, 4]), op=ALU.add),
                        [(key, d_), "carr"], [(key, d_)])
            DVE(lambda e: e.tensor_tensor(out=dR[0][:, 0, 1:16], in0=Eall[0][:, 1:16], in1=Eall[0][:, 0:15], op=ALU.subtract),
                [("Eall", 0)], [("dR", 0)])
            DVE(lambda e: e.tensor_tensor(out=dR[0][:, 2, 1:16], in0=Rall[0][:, 1:16], in1=Eall[0][:, 0:15], op=ALU.subtract),
                [("Eall", 0), ("Rall", 0), ("dR", 0)], [("dR", 0)])
            DVE(lambda e: e.memset(dR[0][:, :, 0:1], 0.0), [("dR", 0)], [("dR", 0)])
            DVE(lambda e: e.tensor_tensor(out=dR[0][:, 1, :], in0=Eall[0], in1=Rall[0], op=ALU.subtract),
                [("Eall", 0), ("Rall", 0), ("dR", 0)], [("dR", 0)])
            DVE(lambda e: e.tensor_tensor(out=dR[1][:, 0, 0:15], in0=Eall[1][:, 1:16], in1=Eall[1][:, 0:15], op=ALU.subtract),
                [("Eall", 1)], [("dR", 1)])
            DVE(lambda e: e.tensor_tensor(out=dR[1][:, 2, 0:15], in0=Eall[1][:, 1:16], in1=Rall[1][:, 0:15], op=ALU.subtract),
                [("Eall", 1), ("Rall", 1), ("dR", 1)], [("dR", 1)])
            DVE(lambda e: e.memset(dR[1][:, :, 15:16], 0.0), [("dR", 1)], [("dR", 1)])
            DVE(lambda e: e.tensor_tensor(out=dR[1][:, 1, :], in0=Rall[1], in1=Eall[1], op=ALU.subtract),
                [("Eall", 1), ("Rall", 1), ("dR", 1)], [("dR", 1)])
            for d_ in range(2):
                ACT(lambda e: e.activation(out=efac[d_], in_=dR[d_], func=AF.Exp), [("dR", d_)], [("efac", d_)])
            if h == 0 and b == 0:
                tap("Qf", QK[(0, "q")], [("QK", 0, "q", j) for j in range(4)])
                tap("Kf", QK[(0, "k")], [("QK", 0, "k", j) for j in range(4)])
            order = [list(range(15)), list(range(15, 0, -1))]
            pslot = {}

            def grp_pe(gi, d_):
                k_ = gi * 2 + d_
                cs_ = order[d_][gi * 4:gi * 4 + 4]
                kT, kP, ks = k_ % 2, 2 + k_ % 4, k_ % 2
                for j, c in enumerate(cs_):
                    tp(bankbf(kT)[:, j * 128:(j + 1) * 128], QK[(d_, "k")][:, c * 128:(c + 1) * 128], [("QK", d_, "k", c // 4)], [bk(kT)])
                nn = len(cs_) * 128
                ACT(lambda e: e.activation(out=ktok[ks][:, 0:nn], in_=bankbf(kT)[:, 0:nn], func=AF.Copy), [bk(kT)], [("T", ks, 4)])
                for j, c in enumerate(cs_):
                    mm(bank(kP)[:, j * 128:(j + 1) * 128], ktok[ks][:, j * 128:(j + 1) * 128], V_all[:, c, h * 128:(h + 1) * 128], True, True,
                       [("T", ks, 4), ("V", c)], [bk(kP)])
                    pslot[(d_, c)] = (kP, j)

            grp_pe(0, 0)
            grp_pe(0, 1)
            for gi in range(4):
                if gi + 1 < 4:
                    grp_pe(gi + 1, 0)
                    grp_pe(gi + 1, 1)
                for idx in range(gi * 4, min(gi * 4 + 4, 15)):
                    for d_ in range(2):
                        c = order[d_][idx]
                        kP_, j_ = pslot[(d_, c)]
                        pw = Xs[d_][idx % 2]
                        ACT(lambda e: e.activation(out=pw, in_=bank(kP_)[:, j_ * 128:(j_ + 1) * 128], func=AF.Copy, scale=efac[d_][:, 1, c:c + 1]),
                            [bk(kP_), ("efac", d_), ("T", idx % 2, 3)], [("Xs", d_, idx % 2)])
                    for d_ in range(2):
                        c = order[d_][idx]
                        pw = Xs[d_][idx % 2]
                        yc, yp = Y[d_][idx % 2], Y[d_][(idx + 1) % 2]
                        if idx == 0:
                            DVE(lambda e: e.tensor_copy(out=yc, in_=pw), [("Xs", d_, idx % 2)], [("Y", d_, idx % 2)])
                        else:
                            DVE(lambda e: e.scalar_tensor_tensor(out=yc, in0=yp, scalar=efac[d_][:, 0, c:c + 1], in1=pw, op0=ALU.mult, op1=ALU.add),
                                [("Y", d_, (idx + 1) % 2), ("Xs", d_, idx % 2), ("efac", d_)], [("Y", d_, idx % 2)])
                    for d_ in range(2):
                        c = order[d_][idx]
                        yc = Y[d_][idx % 2]
                        nxt = c + 1 if d_ == 0 else c - 1
                        DVE(lambda e: e.tensor_scalar(out=Zbf[d_][:, nxt, :], in0=yc, scalar1=efac[d_][:, 2, nxt:nxt + 1], scalar2=None, op0=ALU.mult),
                            [("Y", d_, idx % 2), ("efac", d_)], [("Zbf", d_, nxt)])
            for kk in range(4):
                DVE(lambda e: e.memset(bank(kk), 0.0), [], [bk(kk)])
            for s_ in range(2):
                POOL(lambda e: e.memset(atm[s_], 0.0), [], [("T", s_, 2)])

            def at_mm(g):
                ka = (g % 2) * 2
                for d_ in range(2):
                    for j in range(4):
                        c = g * 4 + j
                        q_, k_ = QK[(d_, "q")], QK[(d_, "k")]
                        o_ = j * 128
                        rk = [("QK", d_, "k", g), ("QK", d_, "q", g)]
                        if d_ == 0:
                            mm(bank(ka)[0:64, o_:o_ + 128], k_[:, c * 128:c * 128 + 64], q_[:, c * 128:(c + 1) * 128], True, True, rk, [bk(ka)])
                            mm(bank(ka)[64:128, o_ + 64:o_ + 128], k_[:, c * 128 + 64:(c + 1) * 128], q_[:, c * 128 + 64:(c + 1) * 128],
                               True, True, rk, [bk(ka)])
                        else:
                            mm(bank(ka + 1)[0:64, o_:o_ + 64], k_[:, c * 128:c * 128 + 64], q_[:, c * 128:c * 128 + 64], True, True, rk, [bk(ka + 1)])
                            mm(bank(ka + 1)[64:128, o_:o_ + 128], k_[:, c * 128 + 64:(c + 1) * 128], q_[:, c * 128:(c + 1) * 128],
                               True, True, rk, [bk(ka + 1)])

            def mask_copy(g):
                ka = (g % 2) * 2
                sa = g % 2
                for d_ in range(2):
                    mk = maskfb[:, d_ * 128:(d_ + 1) * 128].unsqueeze(1).to_broadcast([128, 4, 128])
                    DVE(lambda e: e.copy_predicated(out=atm[sa][:, d_, :].rearrange("p (c j) -> p c j", j=128), mask=mk,
                                                    data=bank(ka + d_).rearrange("p (c j) -> p c j", j=128)),
                        [bk(ka + d_), "maskfb", ("T", sa, 2)], [("T", sa, 2)])

            def o_mm(g):
                ko = 4 + g % 2
                sa = g % 2
                for j in range(4):
                    c = g * 4 + j
                    cs = slice(c * 128, (c + 1) * 128)
                    grp = []
                    if c > 0:
                        grp.append((QK[(0, "q")][:, cs], Zbf[0][:, c, :], [("QK", 0, "q", g), ("Zbf", 0, c)]))
                    if c < NT - 1:
                        grp.append((QK[(1, "q")][:, cs], Zbf[1][:, c, :], [("QK", 1, "q", g), ("Zbf", 1, c)]))
                    vv = V_all[:, c, h * 128:(h + 1) * 128]
                    grp.append((atm[sa][:, 0, j * 128:(j + 1) * 128], vv, [("T", sa, 2), ("V", c)]))
                    grp.append((atm[sa][:, 1, j * 128:(j + 1) * 128], vv, [("T", sa, 2), ("V", c)]))
                    for gi_, (l_, r_, rk) in enumerate(grp):
                        mm(bank(ko)[:, j * 128:(j + 1) * 128], l_, r_, gi_ == 0, gi_ == len(grp) - 1, rk, [bk(ko)])

            def epi_a(g):
                ko = 4 + g % 2
                sa = g % 2
                c0 = (g % 4) * 12
                ACT(lambda e: e.activation(out=sqo[sa], in_=bank(ko), func=AF.Square), [bk(ko)], [("T", sa, 0)])
                DVE(lambda e: e.tensor_reduce(out=st2[:, c0:c0 + 4], in_=sqo[sa].rearrange("p (c j) -> p c j", j=128), axis=AX.X, op=ALU.add),
                    [("T", sa, 0)], [("st2", c0)])
                DVE(lambda e: e.tensor_scalar(out=st2[:, c0 + 4:c0 + 8], in0=st2[:, c0:c0 + 4], scalar1=1.0 / 128, scalar2=EPS,
                                              op0=ALU.mult, op1=ALU.add), [("st2", c0)], [("st2", c0 + 4)])
                POOL(lambda e: e.tensor_tensor(out=st2[:, c0 + 8:c0 + 12], in0=st2[:, c0 + 4:c0 + 8], in1=nhalf.to_broadcast([128, 4]),
                                               op=ALU.pow), [("st2", c0 + 4), "nhalf"], [("st2", c0 + 8)])
                DVE(lambda e: e.tensor_tensor(out=onb[sa].rearrange("p (c j) -> p c j", j=128), in0=bank(ko).rearrange("p (c j) -> p c j", j=128),
                                              in1=st2[:, c0 + 8:c0 + 12].unsqueeze(2).to_broadcast([128, 4, 128]), op=ALU.mult),
                    [bk(ko), ("st2", c0 + 8)], [("T", sa, 1)])

            def epi_b(g):
                kt = 6 + g % 2
                sa = g % 2
                for j in range(4):
                    tp(bankbf(kt)[:, j * 128:(j + 1) * 128], onb[sa][:, j * 128:(j + 1) * 128], [("T", sa, 1)], [bk(kt)])
                gs = slice(g * 512, (g + 1) * 512)
                DVE(lambda e: e.scalar_tensor_tensor(out=oT[:, h, gs], in0=bankbf(kt)[:, 0:512], scalar=cols[:, C_GHG + h:C_GHG + h + 1],
                                                     in1=sgT[:, gs], op0=ALU.mult, op1=ALU.mult),
                    [bk(kt), "const", ("sgT", h % 2, g)], [("oT", h, g)])

            at_mm(0); mask_copy(0); at_mm(1); o_mm(0); mask_copy(1); at_mm(2); epi_a(0); o_mm(1); mask_copy(2); at_mm(3)
            epi_b(0); epi_a(1); o_mm(2); mask_copy(3); epi_b(1); epi_a(2); o_mm(3); epi_b(2); epi_a(3); epi_b(3)
        tap("oTa", oT[:, 0:4, :], [("oT", h, g) for h in range(4) for g in range(4)])
        S.barrier()
        R = Bump(arena, RMARK, ARENA_BYTES)
        KT = R.take([6, SEQ], BF16)
        Vaug = R.take([NT, 4, 130], BF16)
        SMARK = R.cur
        Wq = R.take([3, 768], BF16)
        Wkv = R.take([2, 1024], BF16)
        posi = R.take([NT], I32)
        posf = R.take([NT], F32)
        cosT = R.take([NT, 32], F32)
        sinT = R.take([NT, 32], F32)
        TMARK = R.cur
        ang = R.take([NT, 32], F32)
        kqi = R.take([NT, 32], I32)
        t_a = R.take([NT, 32], F32)
        t_b = R.take([NT, 32], F32)
        for m in range(3):
            S.dma("pool", lambda e, m=m: e.dma_start(out=Wq[:, m, :], in_=wq[m * 128:(m + 1) * 128, :]), "wq", writes=["Wq"])
        for m in range(2):
            S.dma("pool", lambda e, m=m: e.dma_start(out=Wkv[:, m, :], in_=wkv[m * 128:(m + 1) * 128, :]), "wkv", writes=["Wkv"])
        POOL(lambda e: e.memset(Vaug[:, :, :, 128:130], 1.0), [], ["Vones"])
        S.dma("sp", lambda e: e.dma_start(out=posi, in_=pos[b].rearrange("(n p) -> p n", p=128), allow_slow_non_contiguous=True),
              "posi", writes=["posi"])
        DVE(lambda e: e.tensor_copy(out=posf, in_=posi), ["posi"], ["posf"])
        DVE(lambda e: e.tensor_tensor(out=ang, in0=posf.unsqueeze(2).to_broadcast([128, NT, 32]),
                                      in1=invf_b.unsqueeze(1).to_broadcast([128, NT, 32]), op=ALU.mult), ["posf", "const"], ["ang"])
        DVE(lambda e: e.tensor_scalar(out=kqi, in0=ang, scalar1=float(1.0 / (2 * PI)), scalar2=None, op0=ALU.mult), ["ang"], ["kqi"])
        DVE(lambda e: e.tensor_copy(out=t_a, in_=kqi), ["kqi"], ["t_a"])
        C1, C2, C3 = CW1, CW2, CW3
        DVE(lambda e: e.scalar_tensor_tensor(out=t_b, in0=t_a, scalar=-C1, in1=ang, op0=ALU.mult, op1=ALU.add), ["t_a", "ang"], ["t_b"])
        DVE(lambda e: e.scalar_tensor_tensor(out=ang, in0=t_a, scalar=-C2, in1=t_b, op0=ALU.mult, op1=ALU.add), ["t_a", "t_b", "ang"], ["ang"])
        DVE(lambda e: e.scalar_tensor_tensor(out=t_b, in0=t_a, scalar=-C3, in1=ang, op0=ALU.mult, op1=ALU.add), ["t_a", "ang", "t_b"], ["t_b"])
        DVE(lambda e: e.tensor_scalar(out=ang, in0=t_b, scalar1=PI, scalar2=-PI, op0=ALU.min, op1=ALU.max), ["t_b", "ang"], ["ang"])
        ACT(lambda e: e.activation(out=sinT, in_=ang, func=AF.Sin), ["ang"], ["sinT"])
        DVE(lambda e: e.tensor_scalar(out=t_a, in0=t_b, scalar1=PI / 2, scalar2=None, op0=ALU.add), ["t_b", "t_a"], ["t_a"])
        DVE(lambda e: e.tensor_scalar(out=t_b, in0=t_a, scalar1=PI, scalar2=2 * PI, op0=ALU.is_gt, op1=ALU.mult), ["t_a", "t_b"], ["t_b"])
        DVE(lambda e: e.tensor_tensor(out=t_a, in0=t_a, in1=t_b, op=ALU.subtract), ["t_a", "t_b"], ["t_a"])
        DVE(lambda e: e.tensor_scalar(out=t_a, in0=t_a, scalar1=PI, scalar2=-PI, op0=ALU.min, op1=ALU.max), ["t_a"], ["t_a"])
        ACT(lambda e: e.activation(out=cosT, in_=t_a, func=AF.Sin), ["t_a"], ["cosT"])
        if b == 0:
            tap("cosT", cosT, ["cosT"])
            tap("sinT", sinT, ["sinT"])
        S.barrier()
        R = Bump(arena, TMARK, ARENA_BYTES)
        cT = R.take([5, 512], BF16)
        sq = R.take([5, 512], BF16)
        sqq = R.take([768], BF16)
        t1 = R.take([768], F32)
        qf = R.take([1024], BF16)
        kf = R.take([768], BF16)
        krs = R.take([64], F32)
        krr = R.take([64], F32)
        ta = R.take([256], F32)
        tb = R.take([256], F32)
        tak_ = R.take([64], F32)
        tbk_ = R.take([64], F32)
        sqk = R.take([512], F32)
        st3 = R.take([64], F32)
        junk = R.take([64], BF16)
        POOL(lambda e: e.memset(qf[:, 512:1024], 0.0), [], ["qf"])
        pending_tr = []
        for blk in range(4):
            tl = slice(blk * 512, (blk + 1) * 512)
            hk = [("hT", 4 * blk + j) for j in range(4)] + ["W_in"]
            for m in range(5):
                c0 = 2560 + m * 128
                kc_ = (m % 2) * 2
                for kc in range(8):
                    mm(bank(kc_), W_in[:, kc, c0:c0 + 128], hT[:, kc, tl], kc == 0, kc == 7, hk, [bk(kc_)])
                gcol = cols[:, C_GCQ + m:C_GCQ + m + 1]
                ACT(lambda e, m=m, gcol=gcol: e.activation(out=cT[:, m, :], in_=bank(kc_), func=AF.Copy, scale=gcol),
                    [bk(kc_), "const"], [("cT", m)])
                ACT(lambda e, m=m: e.activation(out=sq[:, m, :], in_=bank(kc_), func=AF.Square), [bk(kc_)], [("sq", m)])
            for j in range(4):
                i = blk * 4 + j
                js = slice(j * 128, (j + 1) * 128)
                its = slice(i * 128, (i + 1) * 128)
                c0 = (i % 2) * 32
                for m in range(3):
                    mm(bank(1)[:, 0:1], sq[:, m, js], ones_b[:, 0:1], m == 0, m == 2, [("sq", m), "ones_b"], [bk(1)])
                for m in range(3, 5):
                    mm(bank(1)[:, 2:3], sq[:, m, js], ones_b[:, 0:1], m == 3, m == 4, [("sq", m), "ones_b"], [bk(1)])
                for kc in range(8):
                    mm(bank(1)[:, 64:128], hT[:, kc, its], W_in[:, kc, 3200:3264], kc == 0, kc == 7, [("hT", i), "W_in"], [bk(1)])
                sc = lambda o: st3[:, c0 + o:c0 + o + 1]
                skey = lambda o: ("st3", c0 + o)
                DVE(lambda e, sc=sc: e.tensor_scalar(out=sc(0), in0=bank(1)[:, 0:1], scalar1=1.0 / 384, scalar2=EPS, op0=ALU.mult, op1=ALU.add),
                    [bk(1)], [skey(0)])
                DVE(lambda e, sc=sc: e.tensor_scalar(out=sc(1), in0=bank(1)[:, 2:3], scalar1=1.0 / 256, scalar2=EPS, op0=ALU.mult, op1=ALU.add),
                    [bk(1)], [skey(1)])
                POOL(lambda e, sc=sc: e.tensor_tensor(out=sc(2), in0=sc(0), in1=nhalf, op=ALU.pow), [skey(0), "nhalf"], [skey(2)])
                POOL(lambda e, sc=sc: e.tensor_tensor(out=sc(3), in0=sc(1), in1=nhalf, op=ALU.pow), [skey(1), "nhalf"], [skey(3)])
                for m in range(3):
                    mm(bank(3), cT[:, m, js], Wq[:, m, 0:512], m == 0, m == 2, [("cT", m), "Wq"], [bk(3)])
                for m in range(3):
                    mm(bank(4)[:, 0:256], cT[:, m, js], Wq[:, m, 512:768], m == 0, m == 2, [("cT", m), "Wq"], [bk(4)])
                for m in range(2):
                    mm(bank(5), cT[:, 3 + m, js], Wkv[:, m, 0:512], m == 0, m == 1, [("cT", 3 + m), "Wkv"], [bk(5)])
                for m in range(2):
                    mm(bank(6), cT[:, 3 + m, js], Wkv[:, m, 512:1024], m == 0, m == 1, [("cT", 3 + m), "Wkv"], [bk(6)])
                prev_tr = pending_tr
                pending_tr = []
                for pe_part, _ in prev_tr:
                    pe_part()
                DVE(lambda e: e.tensor_copy(out=krs, in_=bank(1)[:, 64:128]), [bk(1)], ["krs"])
                ACT(lambda e: e.activation(out=sqq[:, 0:512], in_=bank(3), func=AF.Square), [bk(3)], ["sqq"])
                ACT(lambda e: e.activation(out=sqq[:, 512:768], in_=bank(4)[:, 0:256], func=AF.Square), [bk(4), "sqq"], ["sqq"])
                ACT(lambda e: e.activation(out=sqk, in_=bank(5), func=AF.Square), [bk(5)], ["sqk"])
                ACT(lambda e, sc=sc: e.activation(out=junk[:, 0:64], in_=krs, func=AF.Square, accum_out=sc(13)), ["krs"], ["junk", skey(13)])
                for _, act_part in prev_tr:
                    act_part()
                DVE(lambda e, sc=sc: e.tensor_reduce(out=st3[:, c0 + 4:c0 + 8], in_=sqq[:, 0:512].rearrange("p (h j) -> p h j", j=128),
                                                     axis=AX.X, op=ALU.add), ["sqq"], [skey(4)])
                DVE(lambda e, sc=sc: e.tensor_reduce(out=st3[:, c0 + 8:c0 + 12], in_=sqq[:, 512:768].rearrange("p (h j) -> p h j", j=64),
                                                     axis=AX.X, op=ALU.add), ["sqq"], [skey(8)])
                DVE(lambda e: e.tensor_reduce(out=st3[:, c0 + 16:c0 + 20], in_=sqk.rearrange("p (h j) -> p h j", j=128),
                                              axis=AX.X, op=ALU.add), ["sqk"], [skey(16)])
                DVE(lambda e: e.tensor_tensor(out=st3[:, c0 + 4:c0 + 8], in0=st3[:, c0 + 4:c0 + 8], in1=st3[:, c0 + 8:c0 + 12], op=ALU.add),
                    [skey(4), skey(8)], [skey(4)])
                DVE(lambda e, sc=sc: e.tensor_tensor(out=sc(12), in0=sc(2), in1=sc(2), op=ALU.mult), [skey(2)], [skey(12)])
                DVE(lambda e, sc=sc: e.tensor_scalar(out=st3[:, c0 + 4:c0 + 8], in0=st3[:, c0 + 4:c0 + 8], scalar1=sc(12), scalar2=1.0 / 192,
                                                     op0=ALU.mult, op1=ALU.mult), [skey(4), skey(12)], [skey(4)])
                DVE(lambda e: e.tensor_scalar(out=st3[:, c0 + 4:c0 + 8], in0=st3[:, c0 + 4:c0 + 8], scalar1=EPS, scalar2=None, op0=ALU.add),
                    [skey(4)], [skey(4)])
                DVE(lambda e, sc=sc: e.tensor_tensor(out=sc(14), in0=sc(3), in1=sc(3), op=ALU.mult), [skey(3)], [skey(14)])
                DVE(lambda e, sc=sc: e.tensor_scalar(out=st3[:, c0 + 16:c0 + 20], in0=st3[:, c0 + 16:c0 + 20], scalar1=sc(14), scalar2=sc(13),
                                                     op0=ALU.mult, op1=ALU.add), [skey(16), skey(14), skey(13)], [skey(16)])
                DVE(lambda e: e.tensor_scalar(out=st3[:, c0 + 16:c0 + 20], in0=st3[:, c0 + 16:c0 + 20], scalar1=1.0 / 192, scalar2=EPS,
                                              op0=ALU.mult, op1=ALU.add), [skey(16)], [skey(16)])
                POOL(lambda e: e.tensor_tensor(out=st3[:, c0 + 8:c0 + 12], in0=st3[:, c0 + 4:c0 + 8],
                                               in1=nhalf.to_broadcast([128, 4]), op=ALU.pow), [skey(4), "nhalf", skey(8)], [skey(8)])
                POOL(lambda e: e.tensor_tensor(out=st3[:, c0 + 20:c0 + 24], in0=st3[:, c0 + 16:c0 + 20],
                                               in1=nhalf.to_broadcast([128, 4]), op=ALU.pow), [skey(16), "nhalf"], [skey(20)])
                DVE(lambda e, sc=sc: e.tensor_scalar(out=st3[:, c0 + 8:c0 + 12], in0=st3[:, c0 + 8:c0 + 12], scalar1=sc(2), scalar2=None,
                                                     op0=ALU.mult), [skey(8), skey(2)], [skey(8)])
                DVE(lambda e, sc=sc: e.tensor_scalar(out=st3[:, c0 + 24:c0 + 28], in0=st3[:, c0 + 20:c0 + 24], scalar1=sc(3), scalar2=None,
                                                     op0=ALU.mult), [skey(20), skey(3)], [skey(24)])
                fq = st3[:, c0 + 8:c0 + 12]
                rk_ = st3[:, c0 + 20:c0 + 24]
                fkn = st3[:, c0 + 24:c0 + 28]
                DVE(lambda e, fq=fq: e.tensor_tensor(out=t1[:, 0:512].rearrange("p (h j) -> p h j", j=128),
                                                     in0=bank(3).rearrange("p (h j) -> p h j", j=128),
                                                     in1=fq.unsqueeze(2).to_broadcast([128, 4, 128]), op=ALU.mult), [bk(3), skey(8)], ["t1"])
                DVE(lambda e, fq=fq: e.tensor_tensor(out=t1[:, 512:768].rearrange("p (h j) -> p h j", j=64),
                                                     in0=bank(4)[:, 0:256].rearrange("p (h j) -> p h j", j=64),
                                                     in1=fq.unsqueeze(2).to_broadcast([128, 4, 64]), op=ALU.mult), [bk(4), skey(8), "t1"], ["t1"])
                DVE(lambda e, fkn=fkn: e.tensor_tensor(out=sqk.rearrange("p (h j) -> p h j", j=128),
                                                       in0=bank(5).rearrange("p (h j) -> p h j", j=128),
                                                       in1=fkn.unsqueeze(2).to_broadcast([128, 4, 128]), op=ALU.mult),
                    [bk(5), skey(24), "sqk"], ["sqk"])
                ACT(lambda e, sc=sc, i=i: e.activation(out=Vaug[:, i, :, 0:128], in_=bank(6).rearrange("p (h j) -> p h j", j=128),
                                                       func=AF.Copy, scale=sc(3)), [bk(6), skey(3)], [("Vaug", i)])
                DVE(lambda e: e.tensor_tensor(out=qf[:, 0:512], in0=t1[:, 0:512], in1=gq_b[:, 0:512], op=ALU.mult), ["t1", "const"], ["qf"])
                DVE(lambda e: e.tensor_tensor(out=t1[:, 512:768], in0=t1[:, 512:768], in1=gq_b[:, 512:768], op=ALU.mult), ["t1", "const"], ["t1"])

                def rope(E, src, nh_, dst, rk, wk, ta, tb, tak, tbk):
                    s4 = src.rearrange("p (h a r) -> p h a r", a=2, r=32)
                    a4 = ta[:, 0:nh_ * 64].rearrange("p (h a r) -> p h a r", a=2, r=32)
                    b4 = tb[:, 0:nh_ * 64].rearrange("p (h a r) -> p h a r", a=2, r=32)
                    cb = cosT[:, i, :].unsqueeze(1).unsqueeze(1).to_broadcast([128, nh_, 2, 32])
                    sb_ = sinT[:, i, :].unsqueeze(1).to_broadcast([128, nh_, 32])
                    E(lambda e: e.tensor_tensor(out=a4, in0=s4, in1=cb, op=ALU.mult), rk + ["cosT"], [tak])
                    if isinstance(dst, list):
                        E(lambda e: e.scalar_tensor_tensor(out=b4[:, :, 0, :], in0=s4[:, :, 1, :], scalar=-1.0, in1=sb_, op0=ALU.mult, op1=ALU.mult),
                          rk + ["sinT"], [tbk])
                        E(lambda e: e.tensor_tensor(out=b4[:, :, 1, :], in0=s4[:, :, 0, :], in1=sb_, op=ALU.mult), rk + ["sinT", tbk], [tbk])
                        a3 = ta[:, 0:nh_ * 64].rearrange("p (h j) -> p h j", j=64)
                        b3 = tb[:, 0:nh_ * 64].rearrange("p (h j) -> p h j", j=64)
                        for par, dv in enumerate(dst):
                            E(lambda e, par=par, dv=dv: e.tensor_tensor(out=dv, in0=a3[:, par::2, :], in1=b3[:, par::2, :], op=ALU.add),
                              [tak, tbk] + wk, wk)
                    else:
                        E(lambda e: e.tensor_tensor(out=b4[:, :, 0, :], in0=s4[:, :, 1, :], in1=sb_, op=ALU.mult), rk + ["sinT"], [tbk])
                        E(lambda e: e.tensor_tensor(out=b4[:, :, 1, :], in0=s4[:, :, 0, :], in1=sb_, op=ALU.mult), rk + ["sinT", tbk], [tbk])
                        E(lambda e: e.tensor_tensor(out=dst[:, 0:32], in0=ta[:, 0:32], in1=tb[:, 0:32], op=ALU.subtract), [tak, tbk] + wk, wk)
                        E(lambda e: e.tensor_tensor(out=dst[:, 32:64], in0=ta[:, 32:64], in1=tb[:, 32:64], op=ALU.add), [tak, tbk] + wk, wk)

                qz = qf[:, 512:1024].rearrange("p (i r) -> p i r", r=256)
                rope(DVE, t1[:, 512:768], 4, [qz[:, :, 0:64], qz[:, :, 192:256]], ["t1"], ["qf"], ta, tb, "ta", "tb")
                POOL(lambda e: e.tensor_tensor(out=kf[:, 0:512], in0=sqk, in1=gk_b[:, 0:512], op=ALU.mult), ["sqk", "const"], ["kf"])
                POOL(lambda e: e.tensor_tensor(out=krs, in0=krs, in1=gk_b[:, 512:576], op=ALU.mult), ["krs", "const"], ["krs"])
                rope(POOL, krs, 1, krr, ["krs"], ["krr"], tak_, tbk_, "tak", "tbk")
                POOL(lambda e, rk_=rk_: e.tensor_tensor(out=kf[:, 512:768].rearrange("p (h j) -> p h j", j=64),
                                                        in0=krr.unsqueeze(1).to_broadcast([128, 4, 64]),
                                                        in1=rk_.unsqueeze(2).to_broadcast([128, 4, 64]), op=ALU.mult),
                     ["krr", skey(20), "kf"], ["kf"])
                def mk_tr(i=i, its=its):
                    def pe_part():
                        for m in range(8):
                            tp(bankbf(7)[:, m * 128:(m + 1) * 128], qf[:, m * 128:(m + 1) * 128], ["qf"], [bk(7)])
                        for m in range(6):
                            tp(bankbf(0)[:, m * 128:(m + 1) * 128], kf[:, m * 128:(m + 1) * 128], ["kf"], [bk(0)])

                    def act_part():
                        ACT(lambda e: e.activation(out=hT[:, :, its], in_=bankbf(7).rearrange("p (m t) -> p m t", t=128), func=AF.Copy),
                            [bk(7)], [("hT", i)])
                        ACT(lambda e: e.activation(out=KT[:, :, its], in_=bankbf(0)[:, 0:768].rearrange("p (m t) -> p m t", t=128), func=AF.Copy),
                            [bk(0)], [("KT", i)])
                    return pe_part, act_part
                pending_tr.append(mk_tr())
        for pe_part, act_part in pending_tr:
            pe_part()
            act_part()
        pending_tr = []
        if b == 0:
            tap("QT", hT[:, 0:6, :], [("hT", i) for i in range(NT)])
            tap("KT", KT, [("KT", i) for i in range(NT)])
            tap("Vaug", Vaug, [("Vaug", i) for i in range(NT)] + ["Vones"])
        S.barrier()
        R = Bump(arena, SMARK, ARENA_BYTES)
        W_o = R.take([8, D], BF16)
        for kc in range(8):
            S.dma("pool", lambda e, kc=kc: e.dma_start(out=W_o[:, kc, :], in_=w_out[kc * 128:(kc + 1) * 128, :]), "w_o", writes=["W_o"])
        PT = [R.take([512], BF16) for _ in range(3)]
        obuf = R.take([4, 512], F32)
        obn = [R.take([512], BF16) for _ in range(2)]
        st4 = R.take([64], F32)
        junk = R.take([512], BF16)
        QT = hT
        it = 0
        npt = 0
        for qb in range(4):
            qs = slice(qb * 512, (qb + 1) * 512)
            qkeys = [("hT", 4 * qb + j) for j in range(4)]
            for h in range(4):
                ko = 2 + (it % 2) * 2
                it += 1
                DVE(lambda e, ko=ko: e.memset(pp[ko // 2][:, :], 0.0), [], [bk(ko), bk(ko + 1)])
                rp = slice((h % 2) * 64, (h % 2) * 64 + 64)
                rc = 4 + h // 2
                def qk(kc):
                    ksl = slice(kc * 128, (kc + 1) * 128)
                    ks_ = kc % 2
                    mm(bank(ks_), KT[:, h, ksl], QT[:, h, qs], True, False, [("KT", kc)] + qkeys, [bk(ks_)])
                    mm(bank(ks_), KT[:, rc, ksl], QT[:, 4 + h, qs], False, True, [("KT", kc)] + qkeys, [bk(ks_)])

                qk(0)
                for kc in range(NT):
                    ks_ = kc % 2
                    ps_ = npt % 3
                    npt += 1
                    ACT(lambda e, ks_=ks_, ps_=ps_: e.activation(out=PT[ps_], in_=bank(ks_), func=AF.Exp), [bk(ks_)], [("PT", ps_)])
                    if kc + 1 < NT:
                        qk(kc + 1)
                    for j in range(4):
                        ob = ko + j // 2
                        mm(bank(ob)[:, (j % 2) * 256:(j % 2) * 256 + 129], PT[ps_][:, j * 128:(j + 1) * 128], Vaug[:, kc, h, 0:129],
                           False, False, [("PT", ps_), ("Vaug", kc), "Vones"], [bk(ob)], skip=True)
                for j in range(4):
                    ob = ko + j // 2
                    o0 = (j % 2) * 256
                    c0 = ((it * 4 + j) % 16) * 2
                    DVE(lambda e, ob=ob, o0=o0, c0=c0: e.reciprocal(out=st4[:, c0:c0 + 1], in_=bank(ob)[:, o0 + 128:o0 + 129]),
                        [bk(ob)], [("st4", c0)])
                    DVE(lambda e, ob=ob, o0=o0, c0=c0, j=j, h=h: e.tensor_scalar(out=obuf[:, j, h * 128:(h + 1) * 128],
                                                                                 in0=bank(ob)[:, o0:o0 + 128], scalar1=st4[:, c0:c0 + 1],
                                                                                 scalar2=None, op0=ALU.mult),
                        [bk(ob), ("st4", c0)], [("obuf", j)])
            for j in range(4):
                i = qb * 4 + j
                c0 = 32 + (i % 8) * 4
                sj = i % 2
                ACT(lambda e, j=j, c0=c0: e.activation(out=junk[:, 0:512], in_=obuf[:, j, :], func=AF.Square, accum_out=st4[:, c0:c0 + 1]),
                    [("obuf", j)], ["junk", ("st4", c0)])
                DVE(lambda e, c0=c0: e.tensor_scalar(out=st4[:, c0 + 1:c0 + 2], in0=st4[:, c0:c0 + 1], scalar1=1.0 / 512, scalar2=EPS,
                                                     op0=ALU.mult, op1=ALU.add), [("st4", c0)], [("st4", c0 + 1)])
                POOL(lambda e, c0=c0: e.tensor_tensor(out=st4[:, c0 + 2:c0 + 3], in0=st4[:, c0 + 1:c0 + 2], in1=nhalf, op=ALU.pow),
                     [("st4", c0 + 1), "nhalf"], [("st4", c0 + 2)])
                DVE(lambda e, j=j, c0=c0, sj=sj: e.scalar_tensor_tensor(out=obn[sj], in0=obuf[:, j, :], scalar=st4[:, c0 + 2:c0 + 3],
                                                                        in1=gmo_b, op0=ALU.mult, op1=ALU.mult),
                    [("obuf", j), ("st4", c0 + 2), "const"], [("obn", sj)])
                kt = 6 + i % 2
                for m in range(4):
                    tp(bankbf(kt)[:, m * 128:(m + 1) * 128], obn[sj][:, m * 128:(m + 1) * 128], [("obn", sj)], [bk(kt)])
                ACT(lambda e, kt=kt, i=i: e.activation(out=oT[:, 4:8, i * 128:(i + 1) * 128],
                                                       in_=bankbf(kt)[:, 0:512].rearrange("p (m t) -> p m t", t=128), func=AF.Copy),
                    [bk(kt)], [("oT", 4, i)])
        tap("oT", oT, [("oT", 4, i) for i in range(NT)])
        S.barrier()
        R = Bump(arena, RMARK, SMARK)
        xin = [R.take([D], F32) for _ in range(2)]
        x1o = [R.take([D], F32) for _ in range(2)]
        nxt_seq = b + 1 < nseq
        if nxt_seq:
            xinA = [R.take([D], F32) for _ in range(2)]
            hbfA = [R.take([D], BF16) for _ in range(2)]
            junkA = R.take([D], BF16)
            stA = R.take([64], F32)
        for i in range(NT):
            sl = i % 2
            its = slice(i * 128, (i + 1) * 128)
            S.dma("sp", lambda e, sl=sl, i=i: e.dma_start(out=xin[sl], in_=x[b, i * 128:(i + 1) * 128, :]), f"xin{sl}",
                  writes=[("xin", sl)])
            kp = (i % 2) * 2
            for half in range(2):
                for m in range(8):
                    mm(bank(kp + half), oT[:, m, its], W_o[:, m, half * 512:(half + 1) * 512], m == 0, m == 7, [], [bk(kp + half)])
            DVE(lambda e, sl=sl, kp=kp: e.tensor_tensor(out=x1o[sl], in0=pp[kp // 2][:, :], in1=xin[sl], op=ALU.add),
                [bk(kp), bk(kp + 1), ("xin", sl), ("x1o", sl)], [("x1o", sl)])
            S.dma("pool", lambda e, sl=sl, i=i: e.dma_start(out=x1s[b, i * 128:(i + 1) * 128, :], in_=x1o[sl]), f"x1o{sl}",
                  reads=[("x1o", sl)])
            if nxt_seq:
                a0_tile(b + 1, i, xinA, hbfA, junkA, stA, 4)
        S.barrier()

    Bm = Bump(arena, MARK0, ARENA_BYTES)
    W_up = Bm.take([8, DFF], BF16)
    W_dn = Bm.take([32, D], BF16)
    gffn_b = Bm.take([D], F32)
    TB = 256
    NJ = TB // 128
    x1t = [[Bm.take([D], F32) for _ in range(NJ)] for _ in range(2)]
    hbf = [Bm.take([D], BF16) for _ in range(2)]
    h2T = Bm.take([8, TB], BF16)
    aT = Bm.take([32, TB], BF16)
    rl = [Bm.take([512], F32) for _ in range(2)]
    yo = [Bm.take([D], F32) for _ in range(2)]
    junk = Bm.take([D], BF16)
    st5 = Bm.take([64], F32)
    S.dma("sp", lambda e: e.dma_start(out=gffn_b, in_=g_ffn.partition_broadcast(128)), "const2", writes=["gffn"])
    for kc in range(8):
        for q4 in range(4):
            S.dma("pool", lambda e, kc=kc, q4=q4: e.dma_start(out=W_up[:, kc, q4 * 1024:(q4 + 1) * 1024],
                                                             in_=w_up[kc * 128:(kc + 1) * 128, q4 * 1024:(q4 + 1) * 1024]),
                  "w_up", writes=["W_up"])
    for c in range(32):
        S.dma("pool", lambda e, c=c: e.dma_start(out=W_dn[:, c, :], in_=w_down[c * 128:(c + 1) * 128, :]), "w_dn", writes=["W_dn"])
    x1f = x1s.rearrange("b s d -> (b s) d")
    yf = y.rearrange("b s d -> (b s) d")
    nblk = nseq * SEQ // TB
    nyo = 0
    for blk in range(nblk):
        xs = blk % 2
        for j in range(NJ):
            t = blk * NJ + j
            S.dma("sp", lambda e, xs=xs, j=j, t=t: e.dma_start(out=x1t[xs][j], in_=x1f[t * 128:(t + 1) * 128, :]), f"x1t{xs}{j}",
                  writes=[("x1t", xs, j)])
            c0 = (t % 8) * 4
            sl = t % 2
            ACT(lambda e, xs=xs, j=j, c0=c0: e.activation(out=junk, in_=x1t[xs][j], func=AF.Square, accum_out=st5[:, c0:c0 + 1]),
                [("x1t", xs, j)], ["junk", ("st5", c0)])
            DVE(lambda e, c0=c0: e.tensor_scalar(out=st5[:, c0 + 1:c0 + 2], in0=st5[:, c0:c0 + 1], scalar1=1.0 / D, scalar2=EPS,
                                                 op0=ALU.mult, op1=ALU.add), [("st5", c0)], [("st5", c0 + 1)])
            POOL(lambda e, c0=c0: e.tensor_tensor(out=st5[:, c0 + 2:c0 + 3], in0=st5[:, c0 + 1:c0 + 2], in1=nhalf, op=ALU.pow),
                 [("st5", c0 + 1), "nhalf"], [("st5", c0 + 2)])
            DVE(lambda e, xs=xs, j=j, c0=c0, sl=sl: e.scalar_tensor_tensor(out=hbf[sl], in0=x1t[xs][j], scalar=st5[:, c0 + 2:c0 + 3],
                                                                           in1=gffn_b, op0=ALU.mult, op1=ALU.mult),
                [("x1t", xs, j), ("st5", c0 + 2), "gffn"], [("hbf", sl)])
            for kc in range(8):
                tp(bankbf(0)[:, kc * 128:(kc + 1) * 128], hbf[sl][:, kc * 128:(kc + 1) * 128], [("hbf", sl)], [bk(0)])
            ACT(lambda e, j=j: e.activation(out=h2T[:, :, j * 128:(j + 1) * 128], in_=bankbf(0).rearrange("p (k t) -> p k t", t=128),
                                            func=AF.Copy), [bk(0)], [("h2T", j)])
        hk2 = [("h2T", j) for j in range(NJ)] + ["W_up"]
        for cp in range(16):
            ku = 1 + cp % 2
            for c in range(2):
                cc = cp * 2 + c
                for kc in range(8):
                    mm(bank(ku)[:, c * TB:(c + 1) * TB], W_up[:, kc, cc * 128:(cc + 1) * 128], h2T[:, kc, :], kc == 0, kc == 7, hk2, [bk(ku)])
            rs = cp % 2
            ACT(lambda e, ku=ku, rs=rs: e.activation(out=rl[rs], in_=bank(ku), func=AF.Relu), [bk(ku)], [("rl", rs)])
            POOL(lambda e, rs=rs, cp=cp: e.tensor_tensor(out=aT[:, 2 * cp:2 * cp + 2, :], in0=rl[rs].rearrange("p (c t) -> p c t", t=TB),
                                                         in1=rl[rs].rearrange("p (c t) -> p c t", t=TB), op=ALU.mult),
                 [("rl", rs)], [("aT", cp)])
        for j in range(NJ):
            t = blk * NJ + j
            kd = 4 + (t % 2) * 2
            for half in range(2):
                for c in range(32):
                    mm(bank(kd + half), aT[:, c, j * 128:(j + 1) * 128], W_dn[:, c, half * 512:(half + 1) * 512], c == 0, c == 31,
                       [("aT", c // 2), "W_dn"], [bk(kd + half)])
            ys = nyo % 2
            nyo += 1
            DVE(lambda e, ys=ys, kd=kd, xs=xs, j=j: e.tensor_tensor(out=yo[ys], in0=pp[kd // 2][:, :], in1=x1t[xs][j], op=ALU.add),
                [bk(kd), bk(kd + 1), ("x1t", xs, j), ("yo", ys)], [("yo", ys)])
            S.dma("pool", lambda e, ys=ys, t=t: e.dma_start(out=yf[t * 128:(t + 1) * 128, :], in_=yo[ys]), f"yo{ys}", reads=[("yo", ys)])
    S.barrier()

    sems = {n: es.enter_context(nc.semaphore(n)) for n in sorted(S.sem_names)}
    with nc.Block() as block:
        S.emit(block, sems)
    es.close()
    return nc, dbg_out


def _prep_inputs(inputs, nseq, ncores):
    f32 = np.float32
    g = lambda k: np.ascontiguousarray(np.asarray(inputs[k]))
    wq_ = g("w_q_up")[0].reshape(384, 4, 192)
    wq_p = np.concatenate([wq_[:, :, :128].reshape(384, 512), wq_[:, :, 128:].reshape(384, 256)], axis=1)
    wkv_ = g("w_kv_up")[0].reshape(256, 4, 256)
    wkv_p = np.concatenate([wkv_[:, :, :128].reshape(256, 512), wkv_[:, :, 128:].reshape(256, 512)], axis=1)
    invf = (10000.0 ** (-(np.arange(0, 64, 2, dtype=f32)) / f32(64))).astype(f32).reshape(1, 32)
    shared = {
        "g_mix": g("g_mix_norm").reshape(1, D), "w_in": g("w_in")[0], "lb_param": g("lb_param")[:, 0:2, :],
        "g_hg": g("g_hgrn_out")[0], "g_cq": g("g_cq").reshape(1, 384), "wq": np.ascontiguousarray(wq_p),
        "g_ckv": g("g_ckv").reshape(1, 256), "wkv": np.ascontiguousarray(wkv_p), "g_q": g("g_q_norm").reshape(1, 192),
        "g_k": g("g_k_norm").reshape(1, 192), "g_mo": g("g_mla_out").reshape(1, 512), "w_out": g("w_out")[0],
        "g_ffn": g("g_ffn_norm").reshape(1, D), "w_up": g("w_up")[0], "w_down": g("w_down")[0], "invf": invf,
    }
    x = g("x")
    pos = g("positions").astype(np.int32)
    maps = []
    for c in range(ncores):
        m = dict(shared)
        m["x"] = np.ascontiguousarray(x[c * nseq:(c + 1) * nseq])
        m["pos"] = np.ascontiguousarray(pos[c * nseq:(c + 1) * nseq])
        maps.append(m)
    return maps


def kernel(**inputs):
    nseq = 4
    nc, _ = build(nseq)
    maps = _prep_inputs(inputs, nseq, NCORES)
    res = run_bass_kernel_spmd(nc, maps, core_ids=list(range(NCORES)))
    out = np.concatenate([np.asarray(r["y"]) for r in res.results], axis=0)
    return out.astype(np.float32, copy=False)
```

```python
import numpy as np
from contextlib import ExitStack
import concourse.bass as bass
import concourse.mybir as mybir
from concourse.bass_utils import run_bass_kernel_spmd

F32 = mybir.dt.float32
BF16 = mybir.dt.bfloat16
I32 = mybir.dt.int32
AF = mybir.ActivationFunctionType
ALU = mybir.AluOpType
AX = mybir.AxisListType

NCORES = 8
SEQ = 2048
NT = SEQ // 128
D = 1024
DIN = 3264
DFF = 4096
EPS = 1e-6
PI = float(np.pi)
ARENA_BYTES = 212480

ENGS = ("pe", "act", "dve", "pool", "sp")


def _cody_waite():
    two_pi = 2.0 * np.pi
    c1 = 6.28125
    r1 = two_pi - c1
    m, e = np.frexp(r1)
    c2 = float(np.ldexp(np.round(m * 2 ** 11) / 2 ** 11, e))
    c3 = float(np.float32(two_pi - c1 - c2))
    return c1, c2, c3


CW1, CW2, CW3 = _cody_waite()


class _Rec:
    def __getattr__(self, name):
        def f(*a, **k):
            self.call = (name, a, k)
            return self
        return f


class Sched:
    def __init__(self):
        self.q = {e: [] for e in ENGS}
        self.cnt = {e: 0 for e in ENGS}
        self.seen = {e: {} for e in ENGS}
        self.bufs = {}
        self.dma_tot = {}
        self.sem_names = set(ENGS)

    def _st(self, k):
        st = self.bufs.get(k)
        if st is None:
            st = self.bufs[k] = {"w": None, "r": {}}
        return st

    def _deps(self, eng, reads, writes):
        toks = []
        for k in reads:
            st = self._st(k)
            if st["w"] is not None:
                toks.append(st["w"])
        for k in writes:
            st = self._st(k)
            if st["w"] is not None:
                toks.append(st["w"])
            toks.extend(st["r"].items())
        waits = {}
        for (s, v) in toks:
            if s == "pe" and eng == "pe":
                continue
            if self.seen[eng].get(s, 0) >= v:
                continue
            if waits.get(s, 0) < v:
                waits[s] = v
        for s, v in waits.items():
            self.seen[eng][s] = v
        return list(waits.items())

    def _commit(self, tok, reads, writes):
        for k in reads:
            r = self._st(k)["r"]
            if r.get(tok[0], 0) < tok[1]:
                r[tok[0]] = tok[1]
        for k in writes:
            st = self._st(k)
            st["w"] = tok
            st["r"] = {}

    def op(self, eng, fn, reads=(), writes=()):
        rec = _Rec()
        fn(rec)
        waits = self._deps(eng, reads, writes)
        self.cnt[eng] += 1
        tok = (eng, self.cnt[eng])
        self.q[eng].append((rec.call, waits, (eng, 1)))
        self._commit(tok, reads, writes)

    def dma(self, eng, fn, sem, reads=(), writes=(), nodeps=False):
        rec = _Rec()
        fn(rec)
        self.sem_names.add(sem)
        waits = [] if nodeps else self._deps(eng, reads, writes)
        self.dma_tot[sem] = self.dma_tot.get(sem, 0) + 16
        tok = (sem, self.dma_tot[sem])
        self.q[eng].append((rec.call, waits, (sem, 16)))
        self._commit(tok, reads, writes)

    def barrier(self):
        for e in ENGS:
            waits = []
            for e2 in ENGS:
                if self.cnt[e2] > self.seen[e].get(e2, 0):
                    waits.append((e2, self.cnt[e2]))
                    self.seen[e][e2] = self.cnt[e2]
            for s, tot in self.dma_tot.items():
                if tot > self.seen[e].get(s, 0):
                    waits.append((s, tot))
                    self.seen[e][s] = tot
            self.q[e].append((None, waits, None))
        self.bufs = {}

    def emit(self, block, sems):
        handles = {"pe": block.tensor, "act": block.scalar, "dve": block.vector,
                   "pool": block.gpsimd, "sp": block.sync}

        def mk(e):
            ops = self.q[e]

            def body(engine):
                for fn, waits, inc in ops:
                    for s, v in waits:
                        engine.wait_ge(sems[s], v)
                    if fn is not None:
                        name, a, k = fn
                        getattr(engine, name)(*a, **k).then_inc(sems[inc[0]], inc[1])
            return body

        for e in ENGS:
            handles[e](mk(e))


def _dsize(dt):
    return 2 if dt == BF16 else 4


class Bump:
    def __init__(self, arena, start, end):
        self.arena, self.cur, self.end = arena, start, end

    def take(self, shape, dt):
        n = int(np.prod(shape)) * _dsize(dt)
        off = (self.cur + 63) // 64 * 64
        n4 = (n + 3) // 4 * 4
        self.cur = off + n4
        assert self.cur <= self.end, (self.cur, self.end)
        ap = self.arena[:, off // 4:(off + n4) // 4]
        if dt != F32:
            ap = ap.bitcast(dt)
        if n4 != n:
            ap = ap[:, 0:int(np.prod(shape))]
        if len(shape) == 2:
            ap = ap.rearrange("p (a b) -> p a b", b=shape[1])
        elif len(shape) == 3:
            ap = ap.rearrange("p (a b c) -> p a b c", b=shape[1], c=shape[2])
        return ap


def build(nseq=4, dbg=None):
    nc = bass.Bass("TRN2", target_bir_lowering=False)
    S = Sched()
    dbg_out = {}

    def din(name, shape, dt=F32):
        return nc.dram_tensor(name, list(shape), dt, kind="ExternalInput").ap()

    x = din("x", [nseq, SEQ, D])
    pos = din("pos", [nseq, SEQ], I32)
    g_mix = din("g_mix", [1, D])
    w_in = din("w_in", [D, DIN])
    lb_param = din("lb_param", [2, 2, 512])
    g_hg = din("g_hg", [4, 128])
    g_cq = din("g_cq", [1, 384])
    wq = din("wq", [384, 768])
    g_ckv = din("g_ckv", [1, 256])
    wkv = din("wkv", [256, 1024])
    g_q = din("g_q", [1, 192])
    g_k = din("g_k", [1, 192])
    g_mo = din("g_mo", [1, 512])
    w_out = din("w_out", [D, D])
    g_ffn = din("g_ffn", [1, D])
    w_up = din("w_up", [D, DFF])
    w_down = din("w_down", [DFF, D])
    invf = din("invf", [1, 32])
    y = nc.dram_tensor("y", [nseq, SEQ, D], F32, kind="ExternalOutput").ap()
    x1s = nc.dram_tensor("x1s", [nseq, SEQ, D], F32).ap()

    es = ExitStack()
    arena = es.enter_context(nc.sbuf_tensor("arena", [128, ARENA_BYTES // 4], F32))[:]
    pp = [es.enter_context(nc.psum_tensor(f"pp{i}", [128, 1024], F32)) for i in range(4)]

    def bank(k):
        return pp[k // 2][:, (k % 2) * 512:(k % 2) * 512 + 512]

    def bankbf(k):
        return bank(k).bitcast(BF16)

    def bk(k):
        return ("bank", k)

    def PE(fn, r, w):
        S.op("pe", fn, r, w)

    def ACT(fn, r, w):
        S.op("act", fn, r, w)

    def DVE(fn, r, w):
        S.op("dve", fn, r, w)

    def POOL(fn, r, w):
        S.op("pool", fn, r, w)

    def mm(out, lhsT, rhs, start, stop, r, w, skip=False):
        if skip:
            PE(lambda e: e.matmul(out, lhsT=lhsT, rhs=rhs, start=start, stop=stop, skip_group_check=True), r, w)
        else:
            PE(lambda e: e.matmul(out, lhsT=lhsT, rhs=rhs, start=start, stop=stop), r, w)

    def tp(out, in_, r, w):
        PE(lambda e: e.transpose(out=out, in_=in_, identity=ident), list(r) + ["ident"], w)

    def tap(name, ap, key):
        if dbg is None or name not in dbg:
            return
        shp = list(ap.shape)
        t = nc.dram_tensor("dbg_" + name, shp, ap.dtype, kind="ExternalOutput").ap()
        dbg_out[name] = t
        S.dma("sp", lambda e: e.dma_start(out=t, in_=ap), "dbg", reads=key)

    P = Bump(arena, 0, ARENA_BYTES)
    ident = P.take([128], BF16)
    maskfb = P.take([256], I32)
    cols = P.take([64], F32)
    ones_b = P.take([2], BF16)
    nhalf = P.take([1], F32)
    ones_f = P.take([512], F32)
    MARK0 = P.cur
    C_GCQ, C_GCKV, C_GHG, C_LB, C_LNOML, C_LBP, C_GQN, C_GKN = 0, 3, 5, 9, 17, 25, 41, 42

    A = Bump(arena, MARK0, ARENA_BYTES)
    W_in = A.take([8, DIN], BF16)
    gmix_b = A.take([D], F32)
    gq_b = A.take([768], F32)
    gk_b = A.take([768], F32)
    gmo_b = A.take([512], F32)
    invf_b = A.take([32], F32)
    hT = A.take([8, SEQ], BF16)
    oT = A.take([8, SEQ], BF16)
    RMARK = A.cur

    R0 = Bump(arena, RMARK, ARENA_BYTES)
    identf = R0.take([128], F32)
    lbt = R0.take([8], F32)

    def cdma(out, in_, slow=False):
        if slow:
            S.dma("sp", lambda e: e.dma_start(out=out, in_=in_, allow_slow_non_contiguous=True), "const", writes=["const"])
        else:
            S.dma("sp", lambda e: e.dma_start(out=out, in_=in_), "const", writes=["const"])

    cdma(gmix_b, g_mix.partition_broadcast(128))
    for h in range(4):
        cdma(gq_b[:, h * 128:(h + 1) * 128], g_q[:, 0:128].partition_broadcast(128))
        cdma(gq_b[:, 512 + h * 64:512 + (h + 1) * 64], g_q[:, 128:192].partition_broadcast(128))
        cdma(gk_b[:, h * 128:(h + 1) * 128], g_k[:, 0:128].partition_broadcast(128))
        cdma(gk_b[:, 512 + h * 64:512 + (h + 1) * 64], g_k[:, 128:192].partition_broadcast(128))
    cdma(gmo_b, g_mo.partition_broadcast(128))
    cdma(invf_b, invf.partition_broadcast(128))
    cdma(cols[:, C_GCQ:C_GCQ + 3], g_cq[0].rearrange("(m p) -> p m", p=128), slow=True)
    cdma(cols[:, C_GCKV:C_GCKV + 2], g_ckv[0].rearrange("(m p) -> p m", p=128), slow=True)
    cdma(cols[:, C_GHG:C_GHG + 4], g_hg.rearrange("h e -> e h"), slow=True)
    cdma(cols[:, C_GQN:C_GQN + 1], g_q[0:1, 0:128].rearrange("o p -> p o"), slow=True)
    cdma(cols[:, C_GKN:C_GKN + 1], g_k[0:1, 0:128].rearrange("o p -> p o"), slow=True)
    for d_ in range(2):
        for s_ in range(2):
            o = C_LBP + (d_ * 2 + s_) * 4
            cdma(cols[:, o:o + 4], lb_param[d_, s_].rearrange("(h p) -> p h", p=128), slow=True)
    for kc in range(8):
        S.dma("pool", lambda e, kc=kc: e.dma_start(out=W_in[:, kc, :], in_=w_in[kc * 128:(kc + 1) * 128, :], max_dma_last_dim=4096),
              "w_in", writes=["W_in"])

    POOL(lambda e: e.memset(identf, 0.0), [], ["identf"])
    POOL(lambda e: e.affine_select(out=identf, in_=identf, pattern=[[-1, 128]], compare_op=ALU.not_equal, fill=1.0,
                                   base=0, channel_multiplier=1), ["identf"], ["identf"])
    DVE(lambda e: e.tensor_copy(out=ident, in_=identf), ["identf"], ["ident"])
    POOL(lambda e: e.iota(maskfb[:, 0:128], pattern=[[1, 128]], base=0, channel_multiplier=-1), [], ["maskfb"])
    POOL(lambda e: e.iota(maskfb[:, 128:256], pattern=[[-1, 128]], base=0, channel_multiplier=1), ["maskfb"], ["maskfb"])
    DVE(lambda e: e.tensor_single_scalar(out=maskfb, in_=maskfb, scalar=0, op=ALU.is_ge), ["maskfb"], ["maskfb"])
    POOL(lambda e: e.memset(ones_b, 1.0), [], ["ones_b"])
    POOL(lambda e: e.memset(nhalf, -0.5), [], ["nhalf"])
    POOL(lambda e: e.memset(ones_f, 1.0), [], ["ones_f"])
    DVE(lambda e: e.tensor_scalar(out=gq_b, in0=gq_b, scalar1=float(192 ** -0.5), scalar2=None, op0=ALU.mult), ["const"], ["const"])
    DVE(lambda e: e.tensor_scalar(out=cols[:, C_GQN:C_GQN + 1], in0=cols[:, C_GQN:C_GQN + 1], scalar1=float(192 ** -0.5), scalar2=None,
                                  op0=ALU.mult), ["const"], ["const"])
    lbp = cols[:, C_LBP:C_LBP + 16].rearrange("p (d s h) -> p d s h", s=2, h=4)
    lbv = cols[:, C_LB:C_LB + 8]
    DVE(lambda e: e.tensor_tensor(out=lbt.rearrange("p (d h) -> p d h", h=4), in0=lbp[:, :, 1, :], in1=lbp[:, :, 0, :], op=ALU.subtract),
        ["const"], ["lbt"])
    ACT(lambda e: e.activation(out=lbt, in_=lbt, func=AF.Exp), ["lbt"], ["lbt"])
    DVE(lambda e: e.tensor_scalar(out=lbt, in0=lbt, scalar1=1.0, scalar2=None, op0=ALU.add), ["lbt"], ["lbt"])
    DVE(lambda e: e.reciprocal(out=lbv, in_=lbt), ["lbt", "const"], ["const"])
    ACT(lambda e: e.activation(out=cols[:, C_LNOML:C_LNOML + 8], in_=lbv, func=AF.Ln, scale=-1.0, bias=1.0), ["const"], ["const"])
    S.barrier()

    for b in range(nseq):
        def a0_tile(bb, i, xin, hbf, junk, st, kb0):
            sl = i % 2
            S.dma("sp", lambda e: e.dma_start(out=xin[sl], in_=x[bb, i * 128:(i + 1) * 128, :]), f"xinA{sl}", writes=[("xinA", sl)])
            c0 = (i % 8) * 4
            ACT(lambda e: e.activation(out=junk, in_=xin[sl], func=AF.Square, accum_out=st[:, c0:c0 + 1]), [("xinA", sl)], ["junkA", ("stA", c0)])
            DVE(lambda e: e.tensor_scalar(out=st[:, c0 + 1:c0 + 2], in0=st[:, c0:c0 + 1], scalar1=1.0 / D, scalar2=EPS, op0=ALU.mult, op1=ALU.add),
                [("stA", c0)], [("stA", c0 + 1)])
            POOL(lambda e: e.tensor_tensor(out=st[:, c0 + 2:c0 + 3], in0=st[:, c0 + 1:c0 + 2], in1=nhalf, op=ALU.pow),
                 [("stA", c0 + 1), "nhalf"], [("stA", c0 + 2)])
            DVE(lambda e: e.scalar_tensor_tensor(out=hbf[sl], in0=xin[sl], scalar=st[:, c0 + 2:c0 + 3], in1=gmix_b, op0=ALU.mult, op1=ALU.mult),
                [("xinA", sl), ("stA", c0 + 2), "const"], [("hbfA", sl)])
            k = kb0 + i % 2
            for kc in range(8):
                tp(bankbf(k)[:, kc * 128:(kc + 1) * 128], hbf[sl][:, kc * 128:(kc + 1) * 128], [("hbfA", sl)], [bk(k)])
            ACT(lambda e: e.activation(out=hT[:, :, i * 128:(i + 1) * 128], in_=bankbf(k).rearrange("p (k t) -> p k t", t=128), func=AF.Copy),
                [bk(k)], [("hT", i)])

        R = Bump(arena, RMARK, ARENA_BYTES)
        V_all = R.take([NT, 512], BF16)
        XMARK = R.cur
        if b == 0:
            xinA = [R.take([D], F32) for _ in range(2)]
            hbfA = [R.take([D], BF16) for _ in range(2)]
            junkA = R.take([D], BF16)
            stA = R.take([64], F32)
            for i in range(NT):
                a0_tile(0, i, xinA, hbfA, junkA, stA, 0)
        tap("hT", hT, [("hT", i) for i in range(NT)])
        for i in range(NT):
            k = 2 + i % 2
            for kc in range(8):
                mm(bank(k), hT[:, kc, i * 128:(i + 1) * 128], W_in[:, kc, 1536:2048], kc == 0, kc == 7,
                   [("hT", i), "W_in"], [bk(k)])
            DVE(lambda e, k=k, i=i: e.tensor_copy(out=V_all[:, i, :], in_=bank(k)), [bk(k)], [("V", i)])
        S.barrier()
        R = Bump(arena, XMARK, ARENA_BYTES)
        sgTs = [R.take([SEQ], BF16) for _ in range(2)]
        TS = [[R.take([512], F32) for _ in range(5)] + [R.take([516], F32)] for _ in range(2)]
        QK = {(d_, w_): R.take([SEQ], BF16) for d_ in range(2) for w_ in "qk"}
        Zbf = [R.take([NT, 128], BF16) for _ in range(2)]
        Y = [[R.take([128], F32) for _ in range(2)] for _ in range(2)]
        sqo = [TS[s_][0] for s_ in range(2)]
        onb = [TS[s_][1].bitcast(BF16)[:, 0:512] for s_ in range(2)]
        atm = [TS[s_][2].bitcast(BF16).rearrange("p (d n) -> p d n", d=2) for s_ in range(2)]
        ktok = [TS[s_][4].bitcast(BF16)[:, 0:512] for s_ in range(2)]
        Xs = [[TS[i_][3][:, d_ * 128:(d_ + 1) * 128] for i_ in range(2)] for d_ in range(2)]
        Rall = [R.take([NT], F32) for _ in range(2)]
        Eall = [R.take([NT], F32) for _ in range(2)]
        dR = [R.take([3, NT], F32) for _ in range(2)]
        efac = [R.take([3, NT], F32) for _ in range(2)]
        tot = R.take([2, 4], F32)
        carr = R.take([2, 4], F32)
        st2 = R.take([64], F32)
        for s_ in range(2):
            POOL(lambda e: e.memset(TS[s_][5][:, 0:1], 0.0), [], [("T", s_, 5)])
        POOL(lambda e: e.memset(carr, 0.0), [], ["carr"])
        for h in range(4):
            def gates(hh):
                for blk in range(4):
                    tl = slice(blk * 512, (blk + 1) * 512)
                    hk = [("hT", 4 * blk + j) for j in range(4)] + ["W_in"]
                    kg = 6 + blk % 2
                    for kc in range(8):
                        mm(bank(kg), W_in[:, kc, 2048 + hh * 128:2048 + (hh + 1) * 128], hT[:, kc, tl], kc == 0, kc == 7, hk, [bk(kg)])
                    ACT(lambda e: e.activation(out=sgTs[hh % 2][:, tl], in_=bank(kg), func=AF.Silu), [bk(kg)], [("sgT", hh % 2, blk)])

            if h == 0:
                gates(0)
            sgT = sgTs[h % 2]

            def prep_mm(blk):
                tl = slice(blk * 512, (blk + 1) * 512)
                hk = [("hT", 4 * blk + j) for j in range(4)] + ["W_in"]
                for j, c0 in enumerate((0, 512, 1024)):
                    kk = (3 * blk + j) % 6
                    for kc in range(8):
                        mm(bank(kk), W_in[:, kc, c0 + h * 128:c0 + (h + 1) * 128], hT[:, kc, tl], kc == 0, kc == 7, hk, [bk(kk)])

            def front(it):
                blk, d_ = it // 2, it % 2
                T = TS[it % 2]
                tk = [("T", it % 2, j) for j in range(6)]
                kz = (3 * blk + 1 + d_) % 6
                lbc = cols[:, C_LB + d_ * 4 + h:C_LB + d_ * 4 + h + 1]
                ACT(lambda e: e.activation(out=T[0], in_=bank(kz), func=AF.Exp, scale=-1.0), [bk(kz)], [tk[0]])
                ACT(lambda e: e.activation(out=T[1], in_=T[0], func=AF.Ln, scale=lbc, bias=1.0), [tk[0], "const"], [tk[1]])
                ACT(lambda e: e.activation(out=T[2], in_=T[0], func=AF.Ln, scale=1.0, bias=1.0), [tk[0]], [tk[2]])
                DVE(lambda e: e.tensor_tensor(out=T[4], in0=bank(kz), in1=T[2], op=ALU.add), [bk(kz), tk[2]], [tk[4]])
                DVE(lambda e: e.tensor_tensor_scan(out=T[5][:, 1:513], data0=T[1], data1=T[2], initial=0.0, op0=ALU.add, op1=ALU.subtract),
                    [tk[1], tk[2], tk[5]], [tk[5]])
                DVE(lambda e: e.tensor_copy(out=tot[:, d_, blk:blk + 1], in_=T[5][:, 512:513]), [tk[5]], [("tot", d_)])
                Bx = T[5][:, 1:513] if d_ == 0 else T[5][:, 0:512]
                kB = tk[5]
                DVE(lambda e: e.tensor_copy(out=Rall[d_][:, blk * 4:(blk + 1) * 4], in_=Bx[:, 63:512:128]), [kB], [("Rall", d_)])
                eo_ = 127 if d_ == 0 else 0
                DVE(lambda e: e.tensor_copy(out=Eall[d_][:, blk * 4:(blk + 1) * 4], in_=Bx[:, eo_:512:128]), [kB], [("Eall", d_)])
                DVE(lambda e: e.tensor_tensor(out=T[0].rearrange("p (c j) -> p c j", j=128), in0=Bx.rearrange("p (c j) -> p c j", j=128),
                                              in1=Rall[d_][:, blk * 4:(blk + 1) * 4].unsqueeze(2).to_broadcast([128, 4, 128]),
                                              op=ALU.subtract), [kB, ("Rall", d_)], [tk[0]])

            def back(it):
                blk, d_ = it // 2, it % 2
                tl = slice(blk * 512, (blk + 1) * 512)
                T = TS[it % 2]
                tk = [("T", it % 2, j) for j in range(6)]
                kq = (3 * blk) % 6
                lno = cols[:, C_LNOML + d_ * 4 + h:C_LNOML + d_ * 4 + h + 1]
                sgn = 1.0 if d_ == 0 else -1.0
                ACT(lambda e: e.activation(out=T[2], in_=T[0], func=AF.Exp, scale=sgn), [tk[0]], [tk[2]])
                POOL(lambda e: e.tensor_tensor(out=T[3], in0=T[4], in1=T[0], op=(ALU.add if d_ == 0 else ALU.subtract)),
                     [tk[4], tk[0]], [tk[3]])
                ACT(lambda e: e.activation(out=QK[(d_, "k")][:, tl], in_=T[3], func=AF.Exp, scale=-1.0, bias=lno),
                    [tk[3], "const"], [("QK", d_, "k", blk)])
                DVE(lambda e: e.tensor_tensor(out=QK[(d_, "q")][:, tl], in0=bank(kq), in1=T[2], op=ALU.mult),
                    [bk(kq), tk[2]], [("QK", d_, "q", blk)])

            prep_mm(0)
            front(0)
            for it in range(8):
                if it % 2 == 0 and it // 2 + 1 < 4:
                    prep_mm(it // 2 + 1)
                if it + 1 < 8:
                    front(it + 1)
                if it == 1 and h + 1 < 4:
                    gates(h + 1)
                back(it)
            for d_ in range(2):
                DVE(lambda e: e.tensor_copy(out=carr[:, d_, 1:2], in_=tot[:, d_, 0:1]), [("tot", d_), "carr"], ["carr"])
                DVE(lambda e: e.tensor_tensor(out=carr[:, d_, 2:3], in0=carr[:, d_, 1:2], in1=tot[:, d_, 1:2], op=ALU.add),
                    [("tot", d_), "carr"], ["carr"])
                DVE(lambda e: e.tensor_tensor(out=carr[:, d_, 3:4], in0=carr[:, d_, 2:3], in1=tot[:, d_, 2:3], op=ALU.add),
                    [("tot", d_), "carr"], ["carr"])
                for arr, key in ((Rall, "Rall"), (Eall, "Eall")):
                    DVE(lambda e, arr=arr: e.tensor_tensor(out=arr[d_].rearrange("p (b c) -> p b c", c=4),
                                                           in0=arr[d_].rearrange("p (b c) -> p b c", c=4),
                                                           in1=carr[:, d_, :].unsqueeze(2).to_broadcast([128, 4, 4]), op=ALU.add),
                        [(key, d_), "carr"], [(key, d_)])
            DVE(lambda e: e.tensor_tensor(out=dR[0][:, 0, 1:16], in0=Eall[0][:, 1:16], in1=Eall[0][:, 0:15], op=ALU.subtract),
                [("Eall", 0)], [("dR", 0)])
            DVE(lambda e: e.tensor_tensor(out=dR[0][:, 2, 1:16], in0=Rall[0][:, 1:16], in1=Eall[0][:, 0:15], op=ALU.subtract),
                [("Eall", 0), ("Rall", 0), ("dR", 0)], [("dR", 0)])
            DVE(lambda e: e.memset(dR[0][:, :, 0:1], 0.0), [("dR", 0)], [("dR", 0)])
            DVE(lambda e: e.tensor_tensor(out=dR[0][:, 1, :], in0=Eall[0], in1=Rall[0], op=ALU.subtract),
                [("Eall", 0), ("Rall", 0), ("dR", 0)], [("dR", 0)])
            DVE(lambda e: e.tensor_tensor(out=dR[1][:, 0, 0:15], in0=Eall[1][:, 1:16], in1=Eall[1][:, 0:15], op=ALU.subtract),
                [("Eall", 1)], [("dR", 1)])
            DVE(lambda e: e.tensor_tensor(out=dR[1][:, 2, 0:15], in0=Eall[1][:, 1:16], in1=Rall[1][:, 0:15], op=ALU.subtract),
                [("Eall", 1), ("Rall", 1), ("dR", 1)], [("dR", 1)])
            DVE(lambda e: e.memset(dR[1][:, :, 15:16], 0.0), [("dR", 1)], [("dR", 1)])
            DVE(lambda e: e.tensor_tensor(out=dR[1][:, 1, :], in0=Rall[1], in1=Eall[1], op=ALU.subtract),
                [("Eall", 1), ("Rall", 1), ("dR", 1)], [("dR", 1)])
            for d_ in range(2):
                ACT(lambda e: e.activation(out=efac[d_], in_=dR[d_], func=AF.Exp), [("dR", d_)], [("efac", d_)])
            if h == 0 and b == 0:
                tap("Qf", QK[(0, "q")], [("QK", 0, "q", j) for j in range(4)])
                tap("Kf", QK[(0, "k")], [("QK", 0, "k", j) for j in range(4)])
            order = [list(range(15)), list(range(15, 0, -1))]
            pslot = {}

            def grp_pe(gi, d_):
                k_ = gi * 2 + d_
                cs_ = order[d_][gi * 4:gi * 4 + 4]
                kT, kP, ks = k_ % 2, 2 + k_ % 4, k_ % 2
                for j, c in enumerate(cs_):
                    tp(bankbf(kT)[:, j * 128:(j + 1) * 128], QK[(d_, "k")][:, c * 128:(c + 1) * 128], [("QK", d_, "k", c // 4)], [bk(kT)])
                nn = len(cs_) * 128
                ACT(lambda e: e.activation(out=ktok[ks][:, 0:nn], in_=bankbf(kT)[:, 0:nn], func=AF.Copy), [bk(kT)], [("T", ks, 4)])
                for j, c in enumerate(cs_):
                    mm(bank(kP)[:, j * 128:(j + 1) * 128], ktok[ks][:, j * 128:(j + 1) * 128], V_all[:, c, h * 128:(h + 1) * 128], True, True,
                       [("T", ks, 4), ("V", c)], [bk(kP)])
                    pslot[(d_, c)] = (kP, j)

            grp_pe(0, 0)
            grp_pe(0, 1)
            for gi in range(4):
                if gi + 1 < 4:
                    grp_pe(gi + 1, 0)
                    grp_pe(gi + 1, 1)
                for idx in range(gi * 4, min(gi * 4 + 4, 15)):
                    for d_ in range(2):
                        c = order[d_][idx]
                        kP_, j_ = pslot[(d_, c)]
                        pw = Xs[d_][idx % 2]
                        ACT(lambda e: e.activation(out=pw, in_=bank(kP_)[:, j_ * 128:(j_ + 1) * 128], func=AF.Copy, scale=efac[d_][:, 1, c:c + 1]),
                            [bk(kP_), ("efac", d_), ("T", idx % 2, 3)], [("Xs", d_, idx % 2)])
                    for d_ in range(2):
                        c = order[d_][idx]
                        pw = Xs[d_][idx % 2]
                        yc, yp = Y[d_][idx % 2], Y[d_][(idx + 1) % 2]
                        if idx == 0:
                            DVE(lambda e: e.tensor_copy(out=yc, in_=pw), [("Xs", d_, idx % 2)], [("Y", d_, idx % 2)])
                        else:
                            DVE(lambda e: e.scalar_tensor_tensor(out=yc, in0=yp, scalar=efac[d_][:, 0, c:c + 1], in1=pw, op0=ALU.mult, op1=ALU.add),
                                [("Y", d_, (idx + 1) % 2), ("Xs", d_, idx % 2), ("efac", d_)], [("Y", d_, idx % 2)])
                    for d_ in range(2):
                        c = order[d_][idx]
                        yc = Y[d_][idx % 2]
                        nxt = c + 1 if d_ == 0 else c - 1
                        DVE(lambda e: e.tensor_scalar(out=Zbf[d_][:, nxt, :], in0=yc, scalar1=efac[d_][:, 2, nxt:nxt + 1], scalar2=None, op0=ALU.mult),
                            [("Y", d_, idx % 2), ("efac", d_)], [("Zbf", d_, nxt)])
            for kk in range(4):
                DVE(lambda e: e.memset(bank(kk), 0.0), [], [bk(kk)])
            for s_ in range(2):
                POOL(lambda e: e.memset(atm[s_], 0.0), [], [("T", s_, 2)])

            def at_mm(g):
                ka = (g % 2) * 2
                for d_ in range(2):
                    for j in range(4):
                        c = g * 4 + j
                        q_, k_ = QK[(d_, "q")], QK[(d_, "k")]
                        o_ = j * 128
                        rk = [("QK", d_, "k", g), ("QK", d_, "q", g)]
                        if d_ == 0:
                            mm(bank(ka)[0:64, o_:o_ + 128], k_[:, c * 128:c * 128 + 64], q_[:, c * 128:(c + 1) * 128], True, True, rk, [bk(ka)])
                            mm(bank(ka)[64:128, o_ + 64:o_ + 128], k_[:, c * 128 + 64:(c + 1) * 128], q_[:, c * 128 + 64:(c + 1) * 128],
                               True, True, rk, [bk(ka)])
                        else:
                            mm(bank(ka + 1)[0:64, o_:o_ + 64], k_[:, c * 128:c * 128 + 64], q_[:, c * 128:c * 128 + 64], True, True, rk, [bk(ka + 1)])
                            mm(bank(ka + 1)[64:128, o_:o_ + 128], k_[:, c * 128 + 64:(c + 1) * 128], q_[:, c * 128:(c + 1) * 128],
                               True, True, rk, [bk(ka + 1)])

            def mask_copy(g):
                ka = (g % 2) * 2
                sa = g % 2
                for d_ in range(2):
                    mk = maskfb[:, d_ * 128:(d_ + 1) * 128].unsqueeze(1).to_broadcast([128, 4, 128])
                    DVE(lambda e: e.copy_predicated(out=atm[sa][:, d_, :].rearrange("p (c j) -> p c j", j=128), mask=mk,
                                                    data=bank(ka + d_).rearrange("p (c j) -> p c j", j=128)),
                        [bk(ka + d_), "maskfb", ("T", sa, 2)], [("T", sa, 2)])

            def o_mm(g):
                ko = 4 + g % 2
                sa = g % 2
                for j in range(4):
                    c = g * 4 + j
                    cs = slice(c * 128, (c + 1) * 128)
                    grp = []
                    if c > 0:
                        grp.append((QK[(0, "q")][:, cs], Zbf[0][:, c, :], [("QK", 0, "q", g), ("Zbf", 0, c)]))
                    if c < NT - 1:
                        grp.append((QK[(1, "q")][:, cs], Zbf[1][:, c, :], [("QK", 1, "q", g), ("Zbf", 1, c)]))
                    vv = V_all[:, c, h * 128:(h + 1) * 128]
                    grp.append((atm[sa][:, 0, j * 128:(j + 1) * 128], vv, [("T", sa, 2), ("V", c)]))
                    grp.append((atm[sa][:, 1, j * 128:(j + 1) * 128], vv, [("T", sa, 2), ("V", c)]))
                    for gi_, (l_, r_, rk) in enumerate(grp):
                        mm(bank(ko)[:, j * 128:(j + 1) * 128], l_, r_, gi_ == 0, gi_ == len(grp) - 1, rk, [bk(ko)])

            def epi_a(g):
                ko = 4 + g % 2
                sa = g % 2
                c0 = (g % 4) * 12
                ACT(lambda e: e.activation(out=sqo[sa], in_=bank(ko), func=AF.Square), [bk(ko)], [("T", sa, 0)])
                DVE(lambda e: e.tensor_reduce(out=st2[:, c0:c0 + 4], in_=sqo[sa].rearrange("p (c j) -> p c j", j=128), axis=AX.X, op=ALU.add),
                    [("T", sa, 0)], [("st2", c0)])
                DVE(lambda e: e.tensor_scalar(out=st2[:, c0 + 4:c0 + 8], in0=st2[:, c0:c0 + 4], scalar1=1.0 / 128, scalar2=EPS,
                                              op0=ALU.mult, op1=ALU.add), [("st2", c0)], [("st2", c0 + 4)])
                POOL(lambda e: e.tensor_tensor(out=st2[:, c0 + 8:c0 + 12], in0=st2[:, c0 + 4:c0 + 8], in1=nhalf.to_broadcast([128, 4]),
                                               op=ALU.pow), [("st2", c0 + 4), "nhalf"], [("st2", c0 + 8)])
                DVE(lambda e: e.tensor_tensor(out=onb[sa].rearrange("p (c j) -> p c j", j=128), in0=bank(ko).rearrange("p (c j) -> p c j", j=128),
                                              in1=st2[:, c0 + 8:c0 + 12].unsqueeze(2).to_broadcast([128, 4, 128]), op=ALU.mult),
                    [bk(ko), ("st2", c0 + 8)], [("T", sa, 1)])

            def epi_b(g):
                kt = 6 + g % 2
                sa = g % 2
                for j in range(4):
                    tp(bankbf(kt)[:, j * 128:(j + 1) * 128], onb[sa][:, j * 128:(j + 1) * 128], [("T", sa, 1)], [bk(kt)])
                gs = slice(g * 512, (g + 1) * 512)
                DVE(lambda e: e.scalar_tensor_tensor(out=oT[:, h, gs], in0=bankbf(kt)[:, 0:512], scalar=cols[:, C_GHG + h:C_GHG + h + 1],
                                                     in1=sgT[:, gs], op0=ALU.mult, op1=ALU.mult),
                    [bk(kt), "const", ("sgT", h % 2, g)], [("oT", h, g)])

            at_mm(0); mask_copy(0); at_mm(1); o_mm(0); mask_copy(1); at_mm(2); epi_a(0); o_mm(1); mask_copy(2); at_mm(3)
            epi_b(0); epi_a(1); o_mm(2); mask_copy(3); epi_b(1); epi_a(2); o_mm(3); epi_b(2); epi_a(3); epi_b(3)
        tap("oTa", oT[:, 0:4, :], [("oT", h, g) for h in range(4) for g in range(4)])
        S.barrier()
        R = Bump(arena, RMARK, ARENA_BYTES)
        KT = R.take([6, SEQ], BF16)
        Vaug = R.take([NT, 4, 130], BF16)
        SMARK = R.cur
        Wq = R.take([3, 768], BF16)
        Wkv = R.take([2, 1024], BF16)
        posi = R.take([NT], I32)
        posf = R.take([NT], F32)
        cosT = R.take([NT, 32], F32)
        sinT = R.take([NT, 32], F32)
        TMARK = R.cur
        ang = R.take([NT, 32], F32)
        kqi = R.take([NT, 32], I32)
        t_a = R.take([NT, 32], F32)
        t_b = R.take([NT, 32], F32)
        for m in range(3):
            S.dma("pool", lambda e, m=m: e.dma_start(out=Wq[:, m, :], in_=wq[m * 128:(m + 1) * 128, :]), "wq", writes=["Wq"])
        for m in range(2):
            S.dma("pool", lambda e, m=m: e.dma_start(out=Wkv[:, m, :], in_=wkv[m * 128:(m + 1) * 128, :]), "wkv", writes=["Wkv"])
        POOL(lambda e: e.memset(Vaug[:, :, :, 128:130], 1.0), [], ["Vones"])
        S.dma("sp", lambda e: e.dma_start(out=posi, in_=pos[b].rearrange("(n p) -> p n", p=128), allow_slow_non_contiguous=True),
              "posi", writes=["posi"])
        DVE(lambda e: e.tensor_copy(out=posf, in_=posi), ["posi"], ["posf"])
        DVE(lambda e: e.tensor_tensor(out=ang, in0=posf.unsqueeze(2).to_broadcast([128, NT, 32]),
                                      in1=invf_b.unsqueeze(1).to_broadcast([128, NT, 32]), op=ALU.mult), ["posf", "const"], ["ang"])
        DVE(lambda e: e.tensor_scalar(out=kqi, in0=ang, scalar1=float(1.0 / (2 * PI)), scalar2=None, op0=ALU.mult), ["ang"], ["kqi"])
        DVE(lambda e: e.tensor_copy(out=t_a, in_=kqi), ["kqi"], ["t_a"])
        C1, C2, C3 = CW1, CW2, CW3
        DVE(lambda e: e.scalar_tensor_tensor(out=t_b, in0=t_a, scalar=-C1, in1=ang, op0=ALU.mult, op1=ALU.add), ["t_a", "ang"], ["t_b"])
        DVE(lambda e: e.scalar_tensor_tensor(out=ang, in0=t_a, scalar=-C2, in1=t_b, op0=ALU.mult, op1=ALU.add), ["t_a", "t_b", "ang"], ["ang"])
        DVE(lambda e: e.scalar_tensor_tensor(out=t_b, in0=t_a, scalar=-C3, in1=ang, op0=ALU.mult, op1=ALU.add), ["t_a", "ang", "t_b"], ["t_b"])
        DVE(lambda e: e.tensor_scalar(out=ang, in0=t_b, scalar1=PI, scalar2=-PI, op0=ALU.min, op1=ALU.max), ["t_b", "ang"], ["ang"])
        ACT(lambda e: e.activation(out=sinT, in_=ang, func=AF.Sin), ["ang"], ["sinT"])
        DVE(lambda e: e.tensor_scalar(out=t_a, in0=t_b, scalar1=PI / 2, scalar2=None, op0=ALU.add), ["t_b", "t_a"], ["t_a"])
        DVE(lambda e: e.tensor_scalar(out=t_b, in0=t_a, scalar1=PI, scalar2=2 * PI, op0=ALU.is_gt, op1=ALU.mult), ["t_a", "t_b"], ["t_b"])
        DVE(lambda e: e.tensor_tensor(out=t_a, in0=t_a, in1=t_b, op=ALU.subtract), ["t_a", "t_b"], ["t_a"])
        DVE(lambda e: e.tensor_scalar(out=t_a, in0=t_a, scalar1=PI, scalar2=-PI, op0=ALU.min, op1=ALU.max), ["t_a"], ["t_a"])
        ACT(lambda e: e.activation(out=cosT, in_=t_a, func=AF.Sin), ["t_a"], ["cosT"])
        if b == 0:
            tap("cosT", cosT, ["cosT"])
            tap("sinT", sinT, ["sinT"])
        S.barrier()
        R = Bump(arena, TMARK, ARENA_BYTES)
        cT = R.take([5, 512], BF16)
        sq = R.take([5, 512], BF16)
        sqq = R.take([768], BF16)
        t1 = R.take([768], F32)
        qf = R.take([1024], BF16)
        kf = R.take([768], BF16)
        krs = R.take([64], F32)
        krr = R.take([64], F32)
        ta = R.take([256], F32)
        tb = R.take([256], F32)
        tak_ = R.take([64], F32)
        tbk_ = R.take([64], F32)
        sqk = R.take([512], F32)
        st3 = R.take([64], F32)
        junk = R.take([64], BF16)
        POOL(lambda e: e.memset(qf[:, 512:1024], 0.0), [], ["qf"])
        pending_tr = []
        for blk in range(4):
            tl = slice(blk * 512, (blk + 1) * 512)
            hk = [("hT", 4 * blk + j) for j in range(4)] + ["W_in"]
            for m in range(5):
                c0 = 2560 + m * 128
                kc_ = (m % 2) * 2
                for kc in range(8):
                    mm(bank(kc_), W_in[:, kc, c0:c0 + 128], hT[:, kc, tl], kc == 0, kc == 7, hk, [bk(kc_)])
                gcol = cols[:, C_GCQ + m:C_GCQ + m + 1]
                ACT(lambda e, m=m, gcol=gcol: e.activation(out=cT[:, m, :], in_=bank(kc_), func=AF.Copy, scale=gcol),
                    [bk(kc_), "const"], [("cT", m)])
                ACT(lambda e, m=m: e.activation(out=sq[:, m, :], in_=bank(kc_), func=AF.Square), [bk(kc_)], [("sq", m)])
            for j in range(4):
                i = blk * 4 + j
                js = slice(j * 128, (j + 1) * 128)
                its = slice(i * 128, (i + 1) * 128)
                c0 = (i % 2) * 32
                for m in range(3):
                    mm(bank(1)[:, 0:1], sq[:, m, js], ones_b[:, 0:1], m == 0, m == 2, [("sq", m), "ones_b"], [bk(1)])
                for m in range(3, 5):
                    mm(bank(1)[:, 2:3], sq[:, m, js], ones_b[:, 0:1], m == 3, m == 4, [("sq", m), "ones_b"], [bk(1)])
                for kc in range(8):
                    mm(bank(1)[:, 64:128], hT[:, kc, its], W_in[:, kc, 3200:3264], kc == 0, kc == 7, [("hT", i), "W_in"], [bk(1)])
                sc = lambda o: st3[:, c0 + o:c0 + o + 1]
                skey = lambda o: ("st3", c0 + o)
                DVE(lambda e, sc=sc: e.tensor_scalar(out=sc(0), in0=bank(1)[:, 0:1], scalar1=1.0 / 384, scalar2=EPS, op0=ALU.mult, op1=ALU.add),
                    [bk(1)], [skey(0)])
                DVE(lambda e, sc=sc: e.tensor_scalar(out=sc(1), in0=bank(1)[:, 2:3], scalar1=1.0 / 256, scalar2=EPS, op0=ALU.mult, op1=ALU.add),
                    [bk(1)], [skey(1)])
                POOL(lambda e, sc=sc: e.tensor_tensor(out=sc(2), in0=sc(0), in1=nhalf, op=ALU.pow), [skey(0), "nhalf"], [skey(2)])
                POOL(lambda e, sc=sc: e.tensor_tensor(out=sc(3), in0=sc(1), in1=nhalf, op=ALU.pow), [skey(1), "nhalf"], [skey(3)])
                for m in range(3):
                    mm(bank(3), cT[:, m, js], Wq[:, m, 0:512], m == 0, m == 2, [("cT", m), "Wq"], [bk(3)])
                for m in range(3):
                    mm(bank(4)[:, 0:256], cT[:, m, js], Wq[:, m, 512:768], m == 0, m == 2, [("cT", m), "Wq"], [bk(4)])
                for m in range(2):
                    mm(bank(5), cT[:, 3 + m, js], Wkv[:, m, 0:512], m == 0, m == 1, [("cT", 3 + m), "Wkv"], [bk(5)])
                for m in range(2):
                    mm(bank(6), cT[:, 3 + m, js], Wkv[:, m, 512:1024], m == 0, m == 1, [("cT", 3 + m), "Wkv"], [bk(6)])
                prev_tr = pending_tr
                pending_tr = []
                for pe_part, _ in prev_tr:
                    pe_part()
                DVE(lambda e: e.tensor_copy(out=krs, in_=bank(1)[:, 64:128]), [bk(1)], ["krs"])
                ACT(lambda e: e.activation(out=sqq[:, 0:512], in_=bank(3), func=AF.Square), [bk(3)], ["sqq"])
                ACT(lambda e: e.activation(out=sqq[:, 512:768], in_=bank(4)[:, 0:256], func=AF.Square), [bk(4), "sqq"], ["sqq"])
                ACT(lambda e: e.activation(out=sqk, in_=bank(5), func=AF.Square), [bk(5)], ["sqk"])
                ACT(lambda e, sc=sc: e.activation(out=junk[:, 0:64], in_=krs, func=AF.Square, accum_out=sc(13)), ["krs"], ["junk", skey(13)])
                for _, act_part in prev_tr:
                    act_part()
                DVE(lambda e, sc=sc: e.tensor_reduce(out=st3[:, c0 + 4:c0 + 8], in_=sqq[:, 0:512].rearrange("p (h j) -> p h j", j=128),
                                                     axis=AX.X, op=ALU.add), ["sqq"], [skey(4)])
                DVE(lambda e, sc=sc: e.tensor_reduce(out=st3[:, c0 + 8:c0 + 12], in_=sqq[:, 512:768].rearrange("p (h j) -> p h j", j=64),
                                                     axis=AX.X, op=ALU.add), ["sqq"], [skey(8)])
                DVE(lambda e: e.tensor_reduce(out=st3[:, c0 + 16:c0 + 20], in_=sqk.rearrange("p (h j) -> p h j", j=128),
                                              axis=AX.X, op=ALU.add), ["sqk"], [skey(16)])
                DVE(lambda e: e.tensor_tensor(out=st3[:, c0 + 4:c0 + 8], in0=st3[:, c0 + 4:c0 + 8], in1=st3[:, c0 + 8:c0 + 12], op=ALU.add),
                    [skey(4), skey(8)], [skey(4)])
                DVE(lambda e, sc=sc: e.tensor_tensor(out=sc(12), in0=sc(2), in1=sc(2), op=ALU.mult), [skey(2)], [skey(12)])
                DVE(lambda e, sc=sc: e.tensor_scalar(out=st3[:, c0 + 4:c0 + 8], in0=st3[:, c0 + 4:c0 + 8], scalar1=sc(12), scalar2=1.0 / 192,
                                                     op0=ALU.mult, op1=ALU.mult), [skey(4), skey(12)], [skey(4)])
                DVE(lambda e: e.tensor_scalar(out=st3[:, c0 + 4:c0 + 8], in0=st3[:, c0 + 4:c0 + 8], scalar1=EPS, scalar2=None, op0=ALU.add),
                    [skey(4)], [skey(4)])
                DVE(lambda e, sc=sc: e.tensor_tensor(out=sc(14), in0=sc(3), in1=sc(3), op=ALU.mult), [skey(3)], [skey(14)])
                DVE(lambda e, sc=sc: e.tensor_scalar(out=st3[:, c0 + 16:c0 + 20], in0=st3[:, c0 + 16:c0 + 20], scalar1=sc(14), scalar2=sc(13),
                                                     op0=ALU.mult, op1=ALU.add), [skey(16), skey(14), skey(13)], [skey(16)])
                DVE(lambda e: e.tensor_scalar(out=st3[:, c0 + 16:c0 + 20], in0=st3[:, c0 + 16:c0 + 20], scalar1=1.0 / 192, scalar2=EPS,
                                              op0=ALU.mult, op1=ALU.add), [skey(16)], [skey(16)])
                POOL(lambda e: e.tensor_tensor(out=st3[:, c0 + 8:c0 + 12], in0=st3[:, c0 + 4:c0 + 8],
                                               in1=nhalf.to_broadcast([128, 4]), op=ALU.pow), [skey(4), "nhalf", skey(8)], [skey(8)])
                POOL(lambda e: e.tensor_tensor(out=st3[:, c0 + 20:c0 + 24], in0=st3[:, c0 + 16:c0 + 20],
                                               in1=nhalf.to_broadcast([128, 4]), op=ALU.pow), [skey(16), "nhalf"], [skey(20)])
                DVE(lambda e, sc=sc: e.tensor_scalar(out=st3[:, c0 + 8:c0 + 12], in0=st3[:, c0 + 8:c0 + 12], scalar1=sc(2), scalar2=None,
                                                     op0=ALU.mult), [skey(8), skey(2)], [skey(8)])
                DVE(lambda e, sc=sc: e.tensor_scalar(out=st3[:, c0 + 24:c0 + 28], in0=st3[:, c0 + 20:c0 + 24], scalar1=sc(3), scalar2=None,
                                                     op0=ALU.mult), [skey(20), skey(3)], [skey(24)])
                fq = st3[:, c0 + 8:c0 + 12]
                rk_ = st3[:, c0 + 20:c0 + 24]
                fkn = st3[:, c0 + 24:c0 + 28]
                DVE(lambda e, fq=fq: e.tensor_tensor(out=qf[:, 0:512].rearrange("p (h j) -> p h j", j=128),
                                                     in0=bank(3).rearrange("p (h j) -> p h j", j=128),
                                                     in1=fq.unsqueeze(2).to_broadcast([128, 4, 128]), op=ALU.mult), [bk(3), skey(8), "qf"], ["qf"])
                DVE(lambda e, fq=fq: e.tensor_tensor(out=t1[:, 512:768].rearrange("p (h j) -> p h j", j=64),
                                                     in0=bank(4)[:, 0:256].rearrange("p (h j) -> p h j", j=64),
                                                     in1=fq.unsqueeze(2).to_broadcast([128, 4, 64]), op=ALU.mult), [bk(4), skey(8), "t1"], ["t1"])
                DVE(lambda e, fkn=fkn: e.tensor_tensor(out=kf[:, 0:512].rearrange("p (h j) -> p h j", j=128),
                                                       in0=bank(5).rearrange("p (h j) -> p h j", j=128),
                                                       in1=fkn.unsqueeze(2).to_broadcast([128, 4, 128]), op=ALU.mult),
                    [bk(5), skey(24), "kf"], ["kf"])
                ACT(lambda e, sc=sc, i=i: e.activation(out=Vaug[:, i, :, 0:128], in_=bank(6).rearrange("p (h j) -> p h j", j=128),
                                                       func=AF.Copy, scale=sc(3)), [bk(6), skey(3)], [("Vaug", i)])
                DVE(lambda e: e.tensor_tensor(out=t1[:, 512:768], in0=t1[:, 512:768], in1=gq_b[:, 512:768], op=ALU.mult), ["t1", "const"], ["t1"])

                def rope(E, src, nh_, dst, rk, wk, ta, tb, tak, tbk):
                    s4 = src.rearrange("p (h a r) -> p h a r", a=2, r=32)
                    a4 = ta[:, 0:nh_ * 64].rearrange("p (h a r) -> p h a r", a=2, r=32)
                    b4 = tb[:, 0:nh_ * 64].rearrange("p (h a r) -> p h a r", a=2, r=32)
                    cb = cosT[:, i, :].unsqueeze(1).unsqueeze(1).to_broadcast([128, nh_, 2, 32])
                    sb_ = sinT[:, i, :].unsqueeze(1).to_broadcast([128, nh_, 32])
                    E(lambda e: e.tensor_tensor(out=a4, in0=s4, in1=cb, op=ALU.mult), rk + ["cosT"], [tak])
                    if isinstance(dst, list):
                        E(lambda e: e.scalar_tensor_tensor(out=b4[:, :, 0, :], in0=s4[:, :, 1, :], scalar=-1.0, in1=sb_, op0=ALU.mult, op1=ALU.mult),
                          rk + ["sinT"], [tbk])
                        E(lambda e: e.tensor_tensor(out=b4[:, :, 1, :], in0=s4[:, :, 0, :], in1=sb_, op=ALU.mult), rk + ["sinT", tbk], [tbk])
                        a3 = ta[:, 0:nh_ * 64].rearrange("p (h j) -> p h j", j=64)
                        b3 = tb[:, 0:nh_ * 64].rearrange("p (h j) -> p h j", j=64)
                        for par, dv in enumerate(dst):
                            E(lambda e, par=par, dv=dv: e.tensor_tensor(out=dv, in0=a3[:, par::2, :], in1=b3[:, par::2, :], op=ALU.add),
                              [tak, tbk] + wk, wk)
                    else:
                        E(lambda e: e.tensor_tensor(out=b4[:, :, 0, :], in0=s4[:, :, 1, :], in1=sb_, op=ALU.mult), rk + ["sinT"], [tbk])
                        E(lambda e: e.tensor_tensor(out=b4[:, :, 1, :], in0=s4[:, :, 0, :], in1=sb_, op=ALU.mult), rk + ["sinT", tbk], [tbk])
                        E(lambda e: e.tensor_tensor(out=dst[:, 0:32], in0=ta[:, 0:32], in1=tb[:, 0:32], op=ALU.subtract), [tak, tbk] + wk, wk)
                        E(lambda e: e.tensor_tensor(out=dst[:, 32:64], in0=ta[:, 32:64], in1=tb[:, 32:64], op=ALU.add), [tak, tbk] + wk, wk)

                qz = qf[:, 512:1024].rearrange("p (i r) -> p i r", r=256)
                rope(DVE, t1[:, 512:768], 4, [qz[:, :, 0:64], qz[:, :, 192:256]], ["t1"], ["qf"], ta, tb, "ta", "tb")
                POOL(lambda e: e.tensor_tensor(out=krs, in0=krs, in1=gk_b[:, 512:576], op=ALU.mult), ["krs", "const"], ["krs"])
                rope(POOL, krs, 1, krr, ["krs"], ["krr"], tak_, tbk_, "tak", "tbk")
                POOL(lambda e, rk_=rk_: e.tensor_tensor(out=kf[:, 512:768].rearrange("p (h j) -> p h j", j=64),
                                                        in0=krr.unsqueeze(1).to_broadcast([128, 4, 64]),
                                                        in1=rk_.unsqueeze(2).to_broadcast([128, 4, 64]), op=ALU.mult),
                     ["krr", skey(20), "kf"], ["kf"])
                def mk_tr(i=i, its=its):
                    def pe_part():
                        for m in range(8):
                            tp(bankbf(7)[:, m * 128:(m + 1) * 128], qf[:, m * 128:(m + 1) * 128], ["qf"], [bk(7)])
                        for m in range(6):
                            tp(bankbf(0)[:, m * 128:(m + 1) * 128], kf[:, m * 128:(m + 1) * 128], ["kf"], [bk(0)])

                    def act_part():
                        ACT(lambda e: e.activation(out=hT[:, 0:4, its], in_=bankbf(7)[:, 0:512].rearrange("p (m t) -> p m t", t=128), func=AF.Copy,
                                                   scale=cols[:, C_GQN:C_GQN + 1]), [bk(7), "const"], [("hT", i)])
                        ACT(lambda e: e.activation(out=hT[:, 4:8, its], in_=bankbf(7)[:, 512:1024].rearrange("p (m t) -> p m t", t=128), func=AF.Copy),
                            [bk(7), ("hT", i)], [("hT", i)])
                        ACT(lambda e: e.activation(out=KT[:, 0:4, its], in_=bankbf(0)[:, 0:512].rearrange("p (m t) -> p m t", t=128), func=AF.Copy,
                                                   scale=cols[:, C_GKN:C_GKN + 1]), [bk(0), "const"], [("KT", i)])
                        ACT(lambda e: e.activation(out=KT[:, 4:6, its], in_=bankbf(0)[:, 512:768].rearrange("p (m t) -> p m t", t=128), func=AF.Copy),
                            [bk(0), ("KT", i)], [("KT", i)])
                    return pe_part, act_part
                pending_tr.append(mk_tr())
        for pe_part, act_part in pending_tr:
            pe_part()
            act_part()
        pending_tr = []
        if b == 0:
            tap("QT", hT[:, 0:6, :], [("hT", i) for i in range(NT)])
            tap("KT", KT, [("KT", i) for i in range(NT)])
            tap("Vaug", Vaug, [("Vaug", i) for i in range(NT)] + ["Vones"])
        S.barrier()
        R = Bump(arena, SMARK, ARENA_BYTES)
        W_o = R.take([8, D], BF16)
        for kc in range(8):
            S.dma("pool", lambda e, kc=kc: e.dma_start(out=W_o[:, kc, :], in_=w_out[kc * 128:(kc + 1) * 128, :]), "w_o", writes=["W_o"])
        PT = [R.take([512], BF16) for _ in range(3)]
        obuf = R.take([4, 512], F32)
        obn = [R.take([512], BF16) for _ in range(2)]
        st4 = R.take([64], F32)
        junk = R.take([512], BF16)
        QT = hT
        it = 0
        npt = 0
        for qb in range(4):
            qs = slice(qb * 512, (qb + 1) * 512)
            qkeys = [("hT", 4 * qb + j) for j in range(4)]
            for h in range(4):
                ko = 2 + (it % 2) * 2
                it += 1
                DVE(lambda e, ko=ko: e.memset(pp[ko // 2][:, :], 0.0), [], [bk(ko), bk(ko + 1)])
                rp = slice((h % 2) * 64, (h % 2) * 64 + 64)
                rc = 4 + h // 2
                def qk(kc):
                    ksl = slice(kc * 128, (kc + 1) * 128)
                    ks_ = kc % 2
                    mm(bank(ks_), KT[:, h, ksl], QT[:, h, qs], True, False, [("KT", kc)] + qkeys, [bk(ks_)])
                    mm(bank(ks_), KT[:, rc, ksl], QT[:, 4 + h, qs], False, True, [("KT", kc)] + qkeys, [bk(ks_)])

                qk(0)
                for kc in range(NT):
                    ks_ = kc % 2
                    ps_ = npt % 3
                    npt += 1
                    ACT(lambda e, ks_=ks_, ps_=ps_: e.activation(out=PT[ps_], in_=bank(ks_), func=AF.Exp), [bk(ks_)], [("PT", ps_)])
                    if kc + 1 < NT:
                        qk(kc + 1)
                    for j in range(4):
                        ob = ko + j // 2
                        mm(bank(ob)[:, (j % 2) * 256:(j % 2) * 256 + 129], PT[ps_][:, j * 128:(j + 1) * 128], Vaug[:, kc, h, 0:129],
                           False, False, [("PT", ps_), ("Vaug", kc), "Vones"], [bk(ob)], skip=True)
                for j in range(4):
                    ob = ko + j // 2
                    o0 = (j % 2) * 256
                    c0 = ((it * 4 + j) % 16) * 2
                    DVE(lambda e, ob=ob, o0=o0, c0=c0: e.reciprocal(out=st4[:, c0:c0 + 1], in_=bank(ob)[:, o0 + 128:o0 + 129]),
                        [bk(ob)], [("st4", c0)])
                    DVE(lambda e, ob=ob, o0=o0, c0=c0, j=j, h=h: e.tensor_scalar(out=obuf[:, j, h * 128:(h + 1) * 128],
                                                                                 in0=bank(ob)[:, o0:o0 + 128], scalar1=st4[:, c0:c0 + 1],
                                                                                 scalar2=None, op0=ALU.mult),
                        [bk(ob), ("st4", c0)], [("obuf", j)])
            for j in range(4):
                i = qb * 4 + j
                c0 = 32 + (i % 8) * 4
                sj = i % 2
                ACT(lambda e, j=j, c0=c0: e.activation(out=junk[:, 0:512], in_=obuf[:, j, :], func=AF.Square, accum_out=st4[:, c0:c0 + 1]),
                    [("obuf", j)], ["junk", ("st4", c0)])
                DVE(lambda e, c0=c0: e.tensor_scalar(out=st4[:, c0 + 1:c0 + 2], in0=st4[:, c0:c0 + 1], scalar1=1.0 / 512, scalar2=EPS,
                                                     op0=ALU.mult, op1=ALU.add), [("st4", c0)], [("st4", c0 + 1)])
                POOL(lambda e, c0=c0: e.tensor_tensor(out=st4[:, c0 + 2:c0 + 3], in0=st4[:, c0 + 1:c0 + 2], in1=nhalf, op=ALU.pow),
                     [("st4", c0 + 1), "nhalf"], [("st4", c0 + 2)])
                DVE(lambda e, j=j, c0=c0, sj=sj: e.scalar_tensor_tensor(out=obn[sj], in0=obuf[:, j, :], scalar=st4[:, c0 + 2:c0 + 3],
                                                                        in1=gmo_b, op0=ALU.mult, op1=ALU.mult),
                    [("obuf", j), ("st4", c0 + 2), "const"], [("obn", sj)])
                kt = 6 + i % 2
                for m in range(4):
                    tp(bankbf(kt)[:, m * 128:(m + 1) * 128], obn[sj][:, m * 128:(m + 1) * 128], [("obn", sj)], [bk(kt)])
                ACT(lambda e, kt=kt, i=i: e.activation(out=oT[:, 4:8, i * 128:(i + 1) * 128],
                                                       in_=bankbf(kt)[:, 0:512].rearrange("p (m t) -> p m t", t=128), func=AF.Copy),
                    [bk(kt)], [("oT", 4, i)])
        tap("oT", oT, [("oT", 4, i) for i in range(NT)])
        S.barrier()
        R = Bump(arena, RMARK, SMARK)
        xin = [R.take([D], F32) for _ in range(2)]
        x1o = [R.take([D], F32) for _ in range(2)]
        nxt_seq = b + 1 < nseq
        if nxt_seq:
            xinA = [R.take([D], F32) for _ in range(2)]
            hbfA = [R.take([D], BF16) for _ in range(2)]
            junkA = R.take([D], BF16)
            stA = R.take([64], F32)
        for i in range(NT):
            sl = i % 2
            its = slice(i * 128, (i + 1) * 128)
            S.dma("sp", lambda e, sl=sl, i=i: e.dma_start(out=xin[sl], in_=x[b, i * 128:(i + 1) * 128, :]), f"xin{sl}",
                  writes=[("xin", sl)])
            kp = (i % 2) * 2
            for half in range(2):
                for m in range(8):
                    mm(bank(kp + half), oT[:, m, its], W_o[:, m, half * 512:(half + 1) * 512], m == 0, m == 7, [], [bk(kp + half)])
            DVE(lambda e, sl=sl, kp=kp: e.tensor_tensor(out=x1o[sl], in0=pp[kp // 2][:, :], in1=xin[sl], op=ALU.add),
                [bk(kp), bk(kp + 1), ("xin", sl), ("x1o", sl)], [("x1o", sl)])
            S.dma("pool", lambda e, sl=sl, i=i: e.dma_start(out=x1s[b, i * 128:(i + 1) * 128, :], in_=x1o[sl]), f"x1o{sl}",
                  reads=[("x1o", sl)])
            if nxt_seq:
                a0_tile(b + 1, i, xinA, hbfA, junkA, stA, 4)
        S.barrier()

    Bm = Bump(arena, MARK0, ARENA_BYTES)
    W_up = Bm.take([8, DFF], BF16)
    W_dn = Bm.take([32, D], BF16)
    gffn_b = Bm.take([D], F32)
    TB = 256
    NJ = TB // 128
    x1t = [[Bm.take([D], F32) for _ in range(NJ)] for _ in range(2)]
    hbf = [Bm.take([D], BF16) for _ in range(2)]
    h2Ts = [Bm.take([8, TB], BF16) for _ in range(2)]
    aT = Bm.take([32, TB], BF16)
    rl = [Bm.take([512], F32) for _ in range(2)]
    yo = [Bm.take([D], F32) for _ in range(2)]
    st5 = Bm.take([64], F32)
    S.dma("sp", lambda e: e.dma_start(out=gffn_b, in_=g_ffn.partition_broadcast(128)), "const2", writes=["gffn"])
    for kc in range(8):
        for q4 in range(4):
            S.dma("pool", lambda e, kc=kc, q4=q4: e.dma_start(out=W_up[:, kc, q4 * 1024:(q4 + 1) * 1024],
                                                             in_=w_up[kc * 128:(kc + 1) * 128, q4 * 1024:(q4 + 1) * 1024]),
                  "w_up", writes=["W_up"], nodeps=True)
    for c in range(32):
        S.dma("pool", lambda e, c=c: e.dma_start(out=W_dn[:, c, :], in_=w_down[c * 128:(c + 1) * 128, :]), "w_dn", writes=["W_dn"], nodeps=True)
    x1f = x1s.rearrange("b s d -> (b s) d")
    yf = y.rearrange("b s d -> (b s) d")
    nblk = nseq * SEQ // TB
    nyo = 0
    def norm_part(blk):
        xs = blk % 2
        for j in range(NJ):
            t = blk * NJ + j
            S.dma("sp", lambda e: e.dma_start(out=x1t[xs][j], in_=x1f[t * 128:(t + 1) * 128, :]), f"x1t{xs}{j}", writes=[("x1t", xs, j)])
            c0 = (t % 8) * 4
            sl = t % 2
            ACT(lambda e: e.activation(out=hbf[sl], in_=x1t[xs][j], func=AF.Square, accum_out=st5[:, c0:c0 + 1]),
                [("x1t", xs, j)], [("hbf", sl), ("st5", c0)])
            DVE(lambda e: e.tensor_scalar(out=st5[:, c0 + 1:c0 + 2], in0=st5[:, c0:c0 + 1], scalar1=1.0 / D, scalar2=EPS,
                                          op0=ALU.mult, op1=ALU.add), [("st5", c0)], [("st5", c0 + 1)])
            POOL(lambda e: e.tensor_tensor(out=st5[:, c0 + 2:c0 + 3], in0=st5[:, c0 + 1:c0 + 2], in1=nhalf, op=ALU.pow),
                 [("st5", c0 + 1), "nhalf"], [("st5", c0 + 2)])
            DVE(lambda e: e.scalar_tensor_tensor(out=hbf[sl], in0=x1t[xs][j], scalar=st5[:, c0 + 2:c0 + 3], in1=gffn_b,
                                                 op0=ALU.mult, op1=ALU.mult), [("x1t", xs, j), ("st5", c0 + 2), "gffn"], [("hbf", sl)])

    def tr_part(blk):
        h2T_ = h2Ts[blk % 2]
        for j in range(NJ):
            t = blk * NJ + j
            sl = t % 2
            for kc in range(8):
                tp(bankbf(0)[:, kc * 128:(kc + 1) * 128], hbf[sl][:, kc * 128:(kc + 1) * 128], [("hbf", sl)], [bk(0)])
            ACT(lambda e: e.activation(out=h2T_[:, :, j * 128:(j + 1) * 128], in_=bankbf(0).rearrange("p (k t) -> p k t", t=128),
                                       func=AF.Copy), [bk(0)], [("h2T", blk % 2, j)])

    norm_part(0)
    tr_part(0)
    for blk in range(nblk):
        xs = blk % 2
        h2T = h2Ts[blk % 2]
        hk2 = [("h2T", blk % 2, j) for j in range(NJ)] + ["W_up"]
        for cp in range(16):
            ku = 1 + cp % 2
            for c in range(2):
                cc = cp * 2 + c
                for kc in range(8):
                    mm(bank(ku)[:, c * TB:(c + 1) * TB], W_up[:, kc, cc * 128:(cc + 1) * 128], h2T[:, kc, :], kc == 0, kc == 7, hk2, [bk(ku)])
            rs = cp % 2
            ACT(lambda e, ku=ku, rs=rs: e.activation(out=rl[rs], in_=bank(ku), func=AF.Relu), [bk(ku)], [("rl", rs)])
            POOL(lambda e, rs=rs, cp=cp: e.tensor_tensor(out=aT[:, 2 * cp:2 * cp + 2, :], in0=rl[rs].rearrange("p (c t) -> p c t", t=TB),
                                                         in1=rl[rs].rearrange("p (c t) -> p c t", t=TB), op=ALU.mult),
                 [("rl", rs)], [("aT", cp)])
            if cp == 7 and blk + 1 < nblk:
                norm_part(blk + 1)
        if blk + 1 < nblk:
            tr_part(blk + 1)
        for j in range(NJ):
            t = blk * NJ + j
            kd = 4 + (t % 2) * 2
            for half in range(2):
                for c in range(32):
                    mm(bank(kd + half), aT[:, c, j * 128:(j + 1) * 128], W_dn[:, c, half * 512:(half + 1) * 512], c == 0, c == 31,
                       [("aT", c // 2), "W_dn"], [bk(kd + half)])
            ys = nyo % 2
            nyo += 1
            DVE(lambda e, ys=ys, kd=kd, xs=xs, j=j: e.tensor_tensor(out=yo[ys], in0=pp[kd // 2][:, :], in1=x1t[xs][j], op=ALU.add),
                [bk(kd), bk(kd + 1), ("x1t", xs, j), ("yo", ys)], [("yo", ys)])
            S.dma("pool", lambda e, ys=ys, t=t: e.dma_start(out=yf[t * 128:(t + 1) * 128, :], in_=yo[ys]), f"yo{ys}", reads=[("yo", ys)])
    S.barrier()

    sems = {n: es.enter_context(nc.semaphore(n)) for n in sorted(S.sem_names)}
    with nc.Block() as block:
        S.emit(block, sems)
    es.close()
    return nc, dbg_out


def _prep_inputs(inputs, nseq, ncores):
    f32 = np.float32
    g = lambda k: np.ascontiguousarray(np.asarray(inputs[k]))
    wq_ = g("w_q_up")[0].reshape(384, 4, 192)
    wq_p = np.concatenate([wq_[:, :, :128].reshape(384, 512), wq_[:, :, 128:].reshape(384, 256)], axis=1)
    wkv_ = g("w_kv_up")[0].reshape(256, 4, 256)
    wkv_p = np.concatenate([wkv_[:, :, :128].reshape(256, 512), wkv_[:, :, 128:].reshape(256, 512)], axis=1)
    invf = (10000.0 ** (-(np.arange(0, 64, 2, dtype=f32)) / f32(64))).astype(f32).reshape(1, 32)
    shared = {
        "g_mix": g("g_mix_norm").reshape(1, D), "w_in": g("w_in")[0], "lb_param": g("lb_param")[:, 0:2, :],
        "g_hg": g("g_hgrn_out")[0], "g_cq": g("g_cq").reshape(1, 384), "wq": np.ascontiguousarray(wq_p),
        "g_ckv": g("g_ckv").reshape(1, 256), "wkv": np.ascontiguousarray(wkv_p), "g_q": g("g_q_norm").reshape(1, 192),
        "g_k": g("g_k_norm").reshape(1, 192), "g_mo": g("g_mla_out").reshape(1, 512), "w_out": g("w_out")[0],
        "g_ffn": g("g_ffn_norm").reshape(1, D), "w_up": g("w_up")[0], "w_down": g("w_down")[0], "invf": invf,
    }
    x = g("x")
    pos = g("positions").astype(np.int32)
    maps = []
    for c in range(ncores):
        m = dict(shared)
        m["x"] = np.ascontiguousarray(x[c * nseq:(c + 1) * nseq])
        m["pos"] = np.ascontiguousarray(pos[c * nseq:(c + 1) * nseq])
        maps.append(m)
    return maps


def kernel(**inputs):
    nseq = 4
    nc, _ = build(nseq)
    maps = _prep_inputs(inputs, nseq, NCORES)
    res = run_bass_kernel_spmd(nc, maps, core_ids=list(range(NCORES)))
    out = np.concatenate([np.asarray(r["y"]) for r in res.results], axis=0)
    return out.astype(np.float32, copy=False)
```

```python
import numpy as np
from contextlib import ExitStack
import concourse.bass as bass
import concourse.mybir as mybir
from concourse.bass_utils import run_bass_kernel_spmd

F32 = mybir.dt.float32
BF16 = mybir.dt.bfloat16
I32 = mybir.dt.int32
AF = mybir.ActivationFunctionType
ALU = mybir.AluOpType
AX = mybir.AxisListType

NCORES = 8
SEQ = 2048
NT = SEQ // 128
D = 1024
DIN = 3264
DFF = 4096
EPS = 1e-6
PI = float(np.pi)
ARENA_BYTES = 212480

ENGS = ("pe", "act", "dve", "pool", "sp")


def _cody_waite():
    two_pi = 2.0 * np.pi
    c1 = 6.28125
    r1 = two_pi - c1
    m, e = np.frexp(r1)
    c2 = float(np.ldexp(np.round(m * 2 ** 11) / 2 ** 11, e))
    c3 = float(np.float32(two_pi - c1 - c2))
    return c1, c2, c3


CW1, CW2, CW3 = _cody_waite()


class _Rec:
    def __getattr__(self, name):
        def f(*a, **k):
            self.call = (name, a, k)
            return self
        return f


class Sched:
    def __init__(self):
        self.q = {e: [] for e in ENGS}
        self.cnt = {e: 0 for e in ENGS}
        self.seen = {e: {} for e in ENGS}
        self.bufs = {}
        self.dma_tot = {}
        self.sem_names = set(ENGS)

    def _st(self, k):
        st = self.bufs.get(k)
        if st is None:
            st = self.bufs[k] = {"w": None, "r": {}}
        return st

    def _deps(self, eng, reads, writes):
        toks = []
        for k in reads:
            st = self._st(k)
            if st["w"] is not None:
                toks.append(st["w"])
        for k in writes:
            st = self._st(k)
            if st["w"] is not None:
                toks.append(st["w"])
            toks.extend(st["r"].items())
        waits = {}
        for (s, v) in toks:
            if s == "pe" and eng == "pe":
                continue
            if self.seen[eng].get(s, 0) >= v:
                continue
            if waits.get(s, 0) < v:
                waits[s] = v
        for s, v in waits.items():
            self.seen[eng][s] = v
        return list(waits.items())

    def _commit(self, tok, reads, writes):
        for k in reads:
            r = self._st(k)["r"]
            if r.get(tok[0], 0) < tok[1]:
                r[tok[0]] = tok[1]
        for k in writes:
            st = self._st(k)
            st["w"] = tok
            st["r"] = {}

    def op(self, eng, fn, reads=(), writes=()):
        rec = _Rec()
        fn(rec)
        waits = self._deps(eng, reads, writes)
        self.cnt[eng] += 1
        tok = (eng, self.cnt[eng])
        self.q[eng].append((rec.call, waits, (eng, 1)))
        self._commit(tok, reads, writes)

    def dma(self, eng, fn, sem, reads=(), writes=(), nodeps=False):
        rec = _Rec()
        fn(rec)
        self.sem_names.add(sem)
        waits = [] if nodeps else self._deps(eng, reads, writes)
        self.dma_tot[sem] = self.dma_tot.get(sem, 0) + 16
        tok = (sem, self.dma_tot[sem])
        self.q[eng].append((rec.call, waits, (sem, 16)))
        self._commit(tok, reads, writes)

    def barrier(self):
        for e in ENGS:
            waits = []
            for e2 in ENGS:
                if self.cnt[e2] > self.seen[e].get(e2, 0):
                    waits.append((e2, self.cnt[e2]))
                    self.seen[e][e2] = self.cnt[e2]
            for s, tot in self.dma_tot.items():
                if tot > self.seen[e].get(s, 0):
                    waits.append((s, tot))
                    self.seen[e][s] = tot
            self.q[e].append((None, waits, None))
        self.bufs = {}

    def emit(self, block, sems):
        handles = {"pe": block.tensor, "act": block.scalar, "dve": block.vector,
                   "pool": block.gpsimd, "sp": block.sync}

        def mk(e):
            ops = self.q[e]

            def body(engine):
                for fn, waits, inc in ops:
                    for s, v in waits:
                        engine.wait_ge(sems[s], v)
                    if fn is not None:
                        name, a, k = fn
                        getattr(engine, name)(*a, **k).then_inc(sems[inc[0]], inc[1])
            return body

        for e in ENGS:
            handles[e](mk(e))


def _dsize(dt):
    return 2 if dt == BF16 else 4


class Bump:
    def __init__(self, arena, start, end):
        self.arena, self.cur, self.end = arena, start, end

    def take(self, shape, dt):
        n = int(np.prod(shape)) * _dsize(dt)
        off = (self.cur + 63) // 64 * 64
        n4 = (n + 3) // 4 * 4
        self.cur = off + n4
        assert self.cur <= self.end, (self.cur, self.end)
        ap = self.arena[:, off // 4:(off + n4) // 4]
        if dt != F32:
            ap = ap.bitcast(dt)
        if n4 != n:
            ap = ap[:, 0:int(np.prod(shape))]
        if len(shape) == 2:
            ap = ap.rearrange("p (a b) -> p a b", b=shape[1])
        elif len(shape) == 3:
            ap = ap.rearrange("p (a b c) -> p a b c", b=shape[1], c=shape[2])
        return ap


def build(nseq=4, dbg=None):
    nc = bass.Bass("TRN2", target_bir_lowering=False)
    S = Sched()
    dbg_out = {}

    def din(name, shape, dt=F32):
        return nc.dram_tensor(name, list(shape), dt, kind="ExternalInput").ap()

    x = din("x", [nseq, SEQ, D])
    pos = din("pos", [nseq, SEQ], I32)
    g_mix = din("g_mix", [1, D])
    w_in = din("w_in", [D, DIN])
    lb_param = din("lb_param", [2, 2, 512])
    g_hg = din("g_hg", [4, 128])
    g_cq = din("g_cq", [1, 384])
    wq = din("wq", [384, 768])
    g_ckv = din("g_ckv", [1, 256])
    wkv = din("wkv", [256, 1024])
    g_q = din("g_q", [1, 192])
    g_k = din("g_k", [1, 192])
    g_mo = din("g_mo", [1, 512])
    w_out = din("w_out", [D, D])
    g_ffn = din("g_ffn", [1, D])
    w_up = din("w_up", [D, DFF])
    w_down = din("w_down", [DFF, D])
    invf = din("invf", [1, 32])
    y = nc.dram_tensor("y", [nseq, SEQ, D], F32, kind="ExternalOutput").ap()
    x1s = nc.dram_tensor("x1s", [nseq, SEQ, D], F32).ap()

    es = ExitStack()
    arena = es.enter_context(nc.sbuf_tensor("arena", [128, ARENA_BYTES // 4], F32))[:]
    pp = [es.enter_context(nc.psum_tensor(f"pp{i}", [128, 1024], F32)) for i in range(4)]

    def bank(k):
        return pp[k // 2][:, (k % 2) * 512:(k % 2) * 512 + 512]

    def bankbf(k):
        return bank(k).bitcast(BF16)

    def bk(k):
        return ("bank", k)

    def PE(fn, r, w):
        S.op("pe", fn, r, w)

    def ACT(fn, r, w):
        S.op("act", fn, r, w)

    def DVE(fn, r, w):
        S.op("dve", fn, r, w)

    def POOL(fn, r, w):
        S.op("pool", fn, r, w)

    def mm(out, lhsT, rhs, start, stop, r, w, skip=False):
        if skip:
            PE(lambda e: e.matmul(out, lhsT=lhsT, rhs=rhs, start=start, stop=stop, skip_group_check=True), r, w)
        else:
            PE(lambda e: e.matmul(out, lhsT=lhsT, rhs=rhs, start=start, stop=stop), r, w)

    def tp(out, in_, r, w):
        PE(lambda e: e.transpose(out=out, in_=in_, identity=ident), list(r) + ["ident"], w)

    def tap(name, ap, key):
        if dbg is None or name not in dbg:
            return
        shp = list(ap.shape)
        t = nc.dram_tensor("dbg_" + name, shp, ap.dtype, kind="ExternalOutput").ap()
        dbg_out[name] = t
        S.dma("sp", lambda e: e.dma_start(out=t, in_=ap), "dbg", reads=key)

    P = Bump(arena, 0, ARENA_BYTES)
    ident = P.take([128], BF16)
    maskfb = P.take([256], I32)
    cols = P.take([64], F32)
    ones_b = P.take([2], BF16)
    nhalf = P.take([1], F32)
    ones_f = P.take([512], F32)
    MARK0 = P.cur
    C_GCQ, C_GCKV, C_GHG, C_LB, C_LNOML, C_LBP, C_GQN, C_GKN = 0, 3, 5, 9, 17, 25, 41, 42

    A = Bump(arena, MARK0, ARENA_BYTES)
    W_in = A.take([8, DIN], BF16)
    gmix_b = A.take([D], F32)
    gq_b = A.take([768], F32)
    gk_b = A.take([768], F32)
    gmo_b = A.take([512], F32)
    invf_b = A.take([32], F32)
    hT = A.take([8, SEQ], BF16)
    oT = A.take([8, SEQ], BF16)
    RMARK = A.cur

    R0 = Bump(arena, RMARK, ARENA_BYTES)
    identf = R0.take([128], F32)
    lbt = R0.take([8], F32)

    def cdma(out, in_, slow=False):
        if slow:
            S.dma("sp", lambda e: e.dma_start(out=out, in_=in_, allow_slow_non_contiguous=True), "const", writes=["const"])
        else:
            S.dma("sp", lambda e: e.dma_start(out=out, in_=in_), "const", writes=["const"])

    cdma(gmix_b, g_mix.partition_broadcast(128))
    for h in range(4):
        cdma(gq_b[:, h * 128:(h + 1) * 128], g_q[:, 0:128].partition_broadcast(128))
        cdma(gq_b[:, 512 + h * 64:512 + (h + 1) * 64], g_q[:, 128:192].partition_broadcast(128))
        cdma(gk_b[:, h * 128:(h + 1) * 128], g_k[:, 0:128].partition_broadcast(128))
        cdma(gk_b[:, 512 + h * 64:512 + (h + 1) * 64], g_k[:, 128:192].partition_broadcast(128))
    cdma(gmo_b, g_mo.partition_broadcast(128))
    cdma(invf_b, invf.partition_broadcast(128))
    cdma(cols[:, C_GCQ:C_GCQ + 3], g_cq[0].rearrange("(m p) -> p m", p=128), slow=True)
    cdma(cols[:, C_GCKV:C_GCKV + 2], g_ckv[0].rearrange("(m p) -> p m", p=128), slow=True)
    cdma(cols[:, C_GHG:C_GHG + 4], g_hg.rearrange("h e -> e h"), slow=True)
    cdma(cols[:, C_GQN:C_GQN + 1], g_q[0:1, 0:128].rearrange("o p -> p o"), slow=True)
    cdma(cols[:, C_GKN:C_GKN + 1], g_k[0:1, 0:128].rearrange("o p -> p o"), slow=True)
    for d_ in range(2):
        for s_ in range(2):
            o = C_LBP + (d_ * 2 + s_) * 4
            cdma(cols[:, o:o + 4], lb_param[d_, s_].rearrange("(h p) -> p h", p=128), slow=True)
    for kc in range(8):
        S.dma("pool", lambda e, kc=kc: e.dma_start(out=W_in[:, kc, :], in_=w_in[kc * 128:(kc + 1) * 128, :], max_dma_last_dim=4096),
              "w_in", writes=["W_in"], nodeps=True)

    POOL(lambda e: e.memset(identf, 0.0), [], ["identf"])
    POOL(lambda e: e.affine_select(out=identf, in_=identf, pattern=[[-1, 128]], compare_op=ALU.not_equal, fill=1.0,
                                   base=0, channel_multiplier=1), ["identf"], ["identf"])
    DVE(lambda e: e.tensor_copy(out=ident, in_=identf), ["identf"], ["ident"])
    POOL(lambda e: e.iota(maskfb[:, 0:128], pattern=[[1, 128]], base=0, channel_multiplier=-1), [], ["maskfb"])
    POOL(lambda e: e.iota(maskfb[:, 128:256], pattern=[[-1, 128]], base=0, channel_multiplier=1), ["maskfb"], ["maskfb"])
    DVE(lambda e: e.tensor_single_scalar(out=maskfb, in_=maskfb, scalar=0, op=ALU.is_ge), ["maskfb"], ["maskfb"])
    POOL(lambda e: e.memset(ones_b, 1.0), [], ["ones_b"])
    POOL(lambda e: e.memset(nhalf, -0.5), [], ["nhalf"])
    POOL(lambda e: e.memset(ones_f, 1.0), [], ["ones_f"])
    DVE(lambda e: e.tensor_scalar(out=gq_b, in0=gq_b, scalar1=float(192 ** -0.5), scalar2=None, op0=ALU.mult), ["const"], ["const"])
    DVE(lambda e: e.tensor_scalar(out=cols[:, C_GQN:C_GQN + 1], in0=cols[:, C_GQN:C_GQN + 1], scalar1=float(192 ** -0.5), scalar2=None,
                                  op0=ALU.mult), ["const"], ["const"])
    lbp = cols[:, C_LBP:C_LBP + 16].rearrange("p (d s h) -> p d s h", s=2, h=4)
    lbv = cols[:, C_LB:C_LB + 8]
    DVE(lambda e: e.tensor_tensor(out=lbt.rearrange("p (d h) -> p d h", h=4), in0=lbp[:, :, 1, :], in1=lbp[:, :, 0, :], op=ALU.subtract),
        ["const"], ["lbt"])
    ACT(lambda e: e.activation(out=lbt, in_=lbt, func=AF.Exp), ["lbt"], ["lbt"])
    DVE(lambda e: e.tensor_scalar(out=lbt, in0=lbt, scalar1=1.0, scalar2=None, op0=ALU.add), ["lbt"], ["lbt"])
    DVE(lambda e: e.reciprocal(out=lbv, in_=lbt), ["lbt", "const"], ["const"])
    ACT(lambda e: e.activation(out=cols[:, C_LNOML:C_LNOML + 8], in_=lbv, func=AF.Ln, scale=-1.0, bias=1.0), ["const"], ["const"])
    S.barrier()

    for b in range(nseq):
        def a0_tile(bb, i, xin, hbf, junk, st, kb0):
            sl = i % 2
            S.dma("sp", lambda e: e.dma_start(out=xin[sl], in_=x[bb, i * 128:(i + 1) * 128, :]), f"xinA{sl}", writes=[("xinA", sl)])
            c0 = (i % 8) * 4
            ACT(lambda e: e.activation(out=junk, in_=xin[sl], func=AF.Square, accum_out=st[:, c0:c0 + 1]), [("xinA", sl)], ["junkA", ("stA", c0)])
            DVE(lambda e: e.tensor_scalar(out=st[:, c0 + 1:c0 + 2], in0=st[:, c0:c0 + 1], scalar1=1.0 / D, scalar2=EPS, op0=ALU.mult, op1=ALU.add),
                [("stA", c0)], [("stA", c0 + 1)])
            POOL(lambda e: e.tensor_tensor(out=st[:, c0 + 2:c0 + 3], in0=st[:, c0 + 1:c0 + 2], in1=nhalf, op=ALU.pow),
                 [("stA", c0 + 1), "nhalf"], [("stA", c0 + 2)])
            DVE(lambda e: e.scalar_tensor_tensor(out=hbf[sl], in0=xin[sl], scalar=st[:, c0 + 2:c0 + 3], in1=gmix_b, op0=ALU.mult, op1=ALU.mult),
                [("xinA", sl), ("stA", c0 + 2), "const"], [("hbfA", sl)])
            k = kb0 + i % 2
            for kc in range(8):
                tp(bankbf(k)[:, kc * 128:(kc + 1) * 128], hbf[sl][:, kc * 128:(kc + 1) * 128], [("hbfA", sl)], [bk(k)])
            ACT(lambda e: e.activation(out=hT[:, :, i * 128:(i + 1) * 128], in_=bankbf(k).rearrange("p (k t) -> p k t", t=128), func=AF.Copy),
                [bk(k)], [("hT", i)])

        R = Bump(arena, RMARK, ARENA_BYTES)
        V_all = R.take([NT, 512], BF16)
        XMARK = R.cur
        if b == 0:
            xinA = [R.take([D], F32) for _ in range(2)]
            hbfA = [R.take([D], BF16) for _ in range(2)]
            junkA = R.take([D], BF16)
            stA = R.take([64], F32)
            for i in range(NT):
                a0_tile(0, i, xinA, hbfA, junkA, stA, 0)
        tap("hT", hT, [("hT", i) for i in range(NT)])
        for i in range(NT):
            k = 2 + i % 2
            for kc in range(8):
                mm(bank(k), hT[:, kc, i * 128:(i + 1) * 128], W_in[:, kc, 1536:2048], kc == 0, kc == 7,
                   [("hT", i), "W_in"], [bk(k)])
            DVE(lambda e, k=k, i=i: e.tensor_copy(out=V_all[:, i, :], in_=bank(k)), [bk(k)], [("V", i)])
        S.barrier()
        R = Bump(arena, XMARK, ARENA_BYTES)
        sgTs = [R.take([SEQ], BF16) for _ in range(2)]
        TS = [[R.take([512], F32) for _ in range(5)] + [R.take([516], F32)] for _ in range(2)]
        QK = {(d_, w_): R.take([SEQ], BF16) for d_ in range(2) for w_ in "qk"}
        Zbf = [R.take([NT, 128], BF16) for _ in range(2)]
        Y = [[R.take([128], F32) for _ in range(2)] for _ in range(2)]
        sqo = [TS[s_][0] for s_ in range(2)]
        onb = [TS[s_][1].bitcast(BF16)[:, 0:512] for s_ in range(2)]
        atm = [TS[s_][2].bitcast(BF16).rearrange("p (d n) -> p d n", d=2) for s_ in range(2)]
        ktok = [TS[s_][4].bitcast(BF16)[:, 0:512] for s_ in range(2)]
        Xs = [[TS[i_][3][:, d_ * 128:(d_ + 1) * 128] for i_ in range(2)] for d_ in range(2)]
        Rall = [R.take([NT], F32) for _ in range(2)]
        Eall = [R.take([NT], F32) for _ in range(2)]
        dR = [R.take([3, NT], F32) for _ in range(2)]
        efac = [R.take([3, NT], F32) for _ in range(2)]
        tot = R.take([2, 4], F32)
        carr = R.take([2, 4], F32)
        st2 = R.take([64], F32)
        for s_ in range(2):
            POOL(lambda e: e.memset(TS[s_][5][:, 0:1], 0.0), [], [("T", s_, 5)])
        POOL(lambda e: e.memset(carr, 0.0), [], ["carr"])
        for h in range(4):
            def gates(hh):
                for blk in range(4):
                    tl = slice(blk * 512, (blk + 1) * 512)
                    hk = [("hT", 4 * blk + j) for j in range(4)] + ["W_in"]
                    kg = 6 + blk % 2
                    for kc in range(8):
                        mm(bank(kg), W_in[:, kc, 2048 + hh * 128:2048 + (hh + 1) * 128], hT[:, kc, tl], kc == 0, kc == 7, hk, [bk(kg)])
                    ACT(lambda e: e.activation(out=sgTs[hh % 2][:, tl], in_=bank(kg), func=AF.Silu), [bk(kg)], [("sgT", hh % 2, blk)])

            if h == 0:
                gates(0)
            sgT = sgTs[h % 2]

            def prep_mm(blk):
                tl = slice(blk * 512, (blk + 1) * 512)
                hk = [("hT", 4 * blk + j) for j in range(4)] + ["W_in"]
                for j, c0 in enumerate((0, 512, 1024)):
                    kk = (3 * blk + j) % 6
                    for kc in range(8):
                        mm(bank(kk), W_in[:, kc, c0 + h * 128:c0 + (h + 1) * 128], hT[:, kc, tl], kc == 0, kc == 7, hk, [bk(kk)])

            def front(it):
                blk, d_ = it // 2, it % 2
                T = TS[it % 2]
                tk = [("T", it % 2, j) for j in range(6)]
                kz = (3 * blk + 1 + d_) % 6
                lbc = cols[:, C_LB + d_ * 4 + h:C_LB + d_ * 4 + h + 1]
                ACT(lambda e: e.activation(out=T[0], in_=bank(kz), func=AF.Exp, scale=-1.0), [bk(kz)], [tk[0]])
                ACT(lambda e: e.activation(out=T[1], in_=T[0], func=AF.Ln, scale=lbc, bias=1.0), [tk[0], "const"], [tk[1]])
                ACT(lambda e: e.activation(out=T[2], in_=T[0], func=AF.Ln, scale=1.0, bias=1.0), [tk[0]], [tk[2]])
                DVE(lambda e: e.tensor_tensor(out=T[4], in0=bank(kz), in1=T[2], op=ALU.add), [bk(kz), tk[2]], [tk[4]])
                DVE(lambda e: e.tensor_tensor_scan(out=T[5][:, 1:513], data0=T[1], data1=T[2], initial=0.0, op0=ALU.add, op1=ALU.subtract),
                    [tk[1], tk[2], tk[5]], [tk[5]])
                DVE(lambda e: e.tensor_copy(out=tot[:, d_, blk:blk + 1], in_=T[5][:, 512:513]), [tk[5]], [("tot", d_)])
                Bx = T[5][:, 1:513] if d_ == 0 else T[5][:, 0:512]
                kB = tk[5]
                DVE(lambda e: e.tensor_copy(out=Rall[d_][:, blk * 4:(blk + 1) * 4], in_=Bx[:, 63:512:128]), [kB], [("Rall", d_)])
                eo_ = 127 if d_ == 0 else 0
                DVE(lambda e: e.tensor_copy(out=Eall[d_][:, blk * 4:(blk + 1) * 4], in_=Bx[:, eo_:512:128]), [kB], [("Eall", d_)])
                DVE(lambda e: e.tensor_tensor(out=T[0].rearrange("p (c j) -> p c j", j=128), in0=Bx.rearrange("p (c j) -> p c j", j=128),
                                              in1=Rall[d_][:, blk * 4:(blk + 1) * 4].unsqueeze(2).to_broadcast([128, 4, 128]),
                                              op=ALU.subtract), [kB, ("Rall", d_)], [tk[0]])

            def back(it):
                blk, d_ = it // 2, it % 2
                tl = slice(blk * 512, (blk + 1) * 512)
                T = TS[it % 2]
                tk = [("T", it % 2, j) for j in range(6)]
                kq = (3 * blk) % 6
                lno = cols[:, C_LNOML + d_ * 4 + h:C_LNOML + d_ * 4 + h + 1]
                sgn = 1.0 if d_ == 0 else -1.0
                ACT(lambda e: e.activation(out=T[2], in_=T[0], func=AF.Exp, scale=sgn), [tk[0]], [tk[2]])
                POOL(lambda e: e.tensor_tensor(out=T[3], in0=T[4], in1=T[0], op=(ALU.add if d_ == 0 else ALU.subtract)),
                     [tk[4], tk[0]], [tk[3]])
                ACT(lambda e: e.activation(out=QK[(d_, "k")][:, tl], in_=T[3], func=AF.Exp, scale=-1.0, bias=lno),
                    [tk[3], "const"], [("QK", d_, "k", blk)])
                DVE(lambda e: e.tensor_tensor(out=QK[(d_, "q")][:, tl], in0=bank(kq), in1=T[2], op=ALU.mult),
                    [bk(kq), tk[2]], [("QK", d_, "q", blk)])

            prep_mm(0)
            front(0)
            for it in range(8):
                if it % 2 == 0 and it // 2 + 1 < 4:
                    prep_mm(it // 2 + 1)
                if it + 1 < 8:
                    front(it + 1)
                if it == 1 and h + 1 < 4:
                    gates(h + 1)
                back(it)
            for d_ in range(2):
                DVE(lambda e: e.tensor_copy(out=carr[:, d_, 1:2], in_=tot[:, d_, 0:1]), [("tot", d_), "carr"], ["carr"])
                DVE(lambda e: e.tensor_tensor(out=carr[:, d_, 2:3], in0=carr[:, d_, 1:2], in1=tot[:, d_, 1:2], op=ALU.add),
                    [("tot", d_), "carr"], ["carr"])
                DVE(lambda e: e.tensor_tensor(out=carr[:, d_, 3:4], in0=carr[:, d_, 2:3], in1=tot[:, d_, 2:3], op=ALU.add),
                    [("tot", d_), "carr"], ["carr"])
                for arr, key in ((Rall, "Rall"), (Eall, "Eall")):
                    DVE(lambda e, arr=arr: e.tensor_tensor(out=arr[d_].rearrange("p (b c) -> p b c", c=4),
                                                           in0=arr[d_].rearrange("p (b c) -> p b c", c=4),
                                                           in1=carr[:, d_, :].unsqueeze(2).to_broadcast([128, 4, 4]), op=ALU.add),
                        [(key, d_), "carr"], [(key, d_)])
            DVE(lambda e: e.tensor_tensor(out=dR[0][:, 0, 1:16], in0=Eall[0][:, 1:16], in1=Eall[0][:, 0:15], op=ALU.subtract),
                [("Eall", 0)], [("dR", 0)])
            DVE(lambda e: e.tensor_tensor(out=dR[0][:, 2, 1:16], in0=Rall[0][:, 1:16], in1=Eall[0][:, 0:15], op=ALU.subtract),
                [("Eall", 0), ("Rall", 0), ("dR", 0)], [("dR", 0)])
            DVE(lambda e: e.memset(dR[0][:, :, 0:1], 0.0), [("dR", 0)], [("dR", 0)])
            DVE(lambda e: e.tensor_tensor(out=dR[0][:, 1, :], in0=Eall[0], in1=Rall[0], op=ALU.subtract),
                [("Eall", 0), ("Rall", 0), ("dR", 0)], [("dR", 0)])
            DVE(lambda e: e.tensor_tensor(out=dR[1][:, 0, 0:15], in0=Eall[1][:, 1:16], in1=Eall[1][:, 0:15], op=ALU.subtract),
                [("Eall", 1)], [("dR", 1)])
            DVE(lambda e: e.tensor_tensor(out=dR[1][:, 2, 0:15], in0=Eall[1][:, 1:16], in1=Rall[1][:, 0:15], op=ALU.subtract),
                [("Eall", 1), ("Rall", 1), ("dR", 1)], [("dR", 1)])
            DVE(lambda e: e.memset(dR[1][:, :, 15:16], 0.0), [("dR", 1)], [("dR", 1)])
            DVE(lambda e: e.tensor_tensor(out=dR[1][:, 1, :], in0=Rall[1], in1=Eall[1], op=ALU.subtract),
                [("Eall", 1), ("Rall", 1), ("dR", 1)], [("dR", 1)])
            for d_ in range(2):
                ACT(lambda e: e.activation(out=efac[d_], in_=dR[d_], func=AF.Exp), [("dR", d_)], [("efac", d_)])
            if h == 0 and b == 0:
                tap("Qf", QK[(0, "q")], [("QK", 0, "q", j) for j in range(4)])
                tap("Kf", QK[(0, "k")], [("QK", 0, "k", j) for j in range(4)])
            order = [list(range(15)), list(range(15, 0, -1))]
            pslot = {}

            def grp_pe(gi, d_):
                k_ = gi * 2 + d_
                cs_ = order[d_][gi * 4:gi * 4 + 4]
                kT, kP, ks = k_ % 2, 2 + k_ % 4, k_ % 2
                for j, c in enumerate(cs_):
                    tp(bankbf(kT)[:, j * 128:(j + 1) * 128], QK[(d_, "k")][:, c * 128:(c + 1) * 128], [("QK", d_, "k", c // 4)], [bk(kT)])
                nn = len(cs_) * 128
                ACT(lambda e: e.activation(out=ktok[ks][:, 0:nn], in_=bankbf(kT)[:, 0:nn], func=AF.Copy), [bk(kT)], [("T", ks, 4)])
                for j, c in enumerate(cs_):
                    mm(bank(kP)[:, j * 128:(j + 1) * 128], ktok[ks][:, j * 128:(j + 1) * 128], V_all[:, c, h * 128:(h + 1) * 128], True, True,
                       [("T", ks, 4), ("V", c)], [bk(kP)])
                    pslot[(d_, c)] = (kP, j)

            grp_pe(0, 0)
            grp_pe(0, 1)
            for gi in range(4):
                if gi + 1 < 4:
                    grp_pe(gi + 1, 0)
                    grp_pe(gi + 1, 1)
                for idx in range(gi * 4, min(gi * 4 + 4, 15)):
                    for d_ in range(2):
                        c = order[d_][idx]
                        kP_, j_ = pslot[(d_, c)]
                        pw = Xs[d_][idx % 2]
                        ACT(lambda e: e.activation(out=pw, in_=bank(kP_)[:, j_ * 128:(j_ + 1) * 128], func=AF.Copy, scale=efac[d_][:, 1, c:c + 1]),
                            [bk(kP_), ("efac", d_), ("T", idx % 2, 3)], [("Xs", d_, idx % 2)])
                    for d_ in range(2):
                        c = order[d_][idx]
                        pw = Xs[d_][idx % 2]
                        yc, yp = Y[d_][idx % 2], Y[d_][(idx + 1) % 2]
                        if idx == 0:
                            DVE(lambda e: e.tensor_copy(out=yc, in_=pw), [("Xs", d_, idx % 2)], [("Y", d_, idx % 2)])
                        else:
                            DVE(lambda e: e.scalar_tensor_tensor(out=yc, in0=yp, scalar=efac[d_][:, 0, c:c + 1], in1=pw, op0=ALU.mult, op1=ALU.add),
                                [("Y", d_, (idx + 1) % 2), ("Xs", d_, idx % 2), ("efac", d_)], [("Y", d_, idx % 2)])
                    for d_ in range(2):
                        c = order[d_][idx]
                        yc = Y[d_][idx % 2]
                        nxt = c + 1 if d_ == 0 else c - 1
                        DVE(lambda e: e.tensor_scalar(out=Zbf[d_][:, nxt, :], in0=yc, scalar1=efac[d_][:, 2, nxt:nxt + 1], scalar2=None, op0=ALU.mult),
                            [("Y", d_, idx % 2), ("efac", d_)], [("Zbf", d_, nxt)])
            for kk in range(4):
                DVE(lambda e: e.memset(bank(kk), 0.0), [], [bk(kk)])
            for s_ in range(2):
                POOL(lambda e: e.memset(atm[s_], 0.0), [], [("T", s_, 2)])

            def at_mm(g):
                ka = (g % 2) * 2
                for d_ in range(2):
                    for j in range(4):
                        c = g * 4 + j
                        q_, k_ = QK[(d_, "q")], QK[(d_, "k")]
                        o_ = j * 128
                        rk = [("QK", d_, "k", g), ("QK", d_, "q", g)]
                        if d_ == 0:
                            mm(bank(ka)[0:64, o_:o_ + 128], k_[:, c * 128:c * 128 + 64], q_[:, c * 128:(c + 1) * 128], True, True, rk, [bk(ka)])
                            mm(bank(ka)[64:128, o_ + 64:o_ + 128], k_[:, c * 128 + 64:(c + 1) * 128], q_[:, c * 128 + 64:(c + 1) * 128],
                               True, True, rk, [bk(ka)])
                        else:
                            mm(bank(ka + 1)[0:64, o_:o_ + 64], k_[:, c * 128:c * 128 + 64], q_[:, c * 128:c * 128 + 64], True, True, rk, [bk(ka + 1)])
                            mm(bank(ka + 1)[64:128, o_:o_ + 128], k_[:, c * 128 + 64:(c + 1) * 128], q_[:, c * 128:(c + 1) * 128],
                               True, True, rk, [bk(ka + 1)])

            def mask_copy(g):
                ka = (g % 2) * 2
                sa = g % 2
                for d_ in range(2):
                    mk = maskfb[:, d_ * 128:(d_ + 1) * 128].unsqueeze(1).to_broadcast([128, 4, 128])
                    DVE(lambda e: e.copy_predicated(out=atm[sa][:, d_, :].rearrange("p (c j) -> p c j", j=128), mask=mk,
                                                    data=bank(ka + d_).rearrange("p (c j) -> p c j", j=128)),
                        [bk(ka + d_), "maskfb", ("T", sa, 2)], [("T", sa, 2)])

            def o_mm(g):
                ko = 4 + g % 2
                sa = g % 2
                for j in range(4):
                    c = g * 4 + j
                    cs = slice(c * 128, (c + 1) * 128)
                    grp = []
                    if c > 0:
                        grp.append((QK[(0, "q")][:, cs], Zbf[0][:, c, :], [("QK", 0, "q", g), ("Zbf", 0, c)]))
                    if c < NT - 1:
                        grp.append((QK[(1, "q")][:, cs], Zbf[1][:, c, :], [("QK", 1, "q", g), ("Zbf", 1, c)]))
                    vv = V_all[:, c, h * 128:(h + 1) * 128]
                    grp.append((atm[sa][:, 0, j * 128:(j + 1) * 128], vv, [("T", sa, 2), ("V", c)]))
                    grp.append((atm[sa][:, 1, j * 128:(j + 1) * 128], vv, [("T", sa, 2), ("V", c)]))
                    for gi_, (l_, r_, rk) in enumerate(grp):
                        mm(bank(ko)[:, j * 128:(j + 1) * 128], l_, r_, gi_ == 0, gi_ == len(grp) - 1, rk, [bk(ko)])

            def epi_a(g):
                ko = 4 + g % 2
                sa = g % 2
                c0 = (g % 4) * 12
                ACT(lambda e: e.activation(out=sqo[sa], in_=bank(ko), func=AF.Square), [bk(ko)], [("T", sa, 0)])
                DVE(lambda e: e.tensor_reduce(out=st2[:, c0:c0 + 4], in_=sqo[sa].rearrange("p (c j) -> p c j", j=128), axis=AX.X, op=ALU.add),
                    [("T", sa, 0)], [("st2", c0)])
                DVE(lambda e: e.tensor_scalar(out=st2[:, c0 + 4:c0 + 8], in0=st2[:, c0:c0 + 4], scalar1=1.0 / 128, scalar2=EPS,
                                              op0=ALU.mult, op1=ALU.add), [("st2", c0)], [("st2", c0 + 4)])
                POOL(lambda e: e.tensor_tensor(out=st2[:, c0 + 8:c0 + 12], in0=st2[:, c0 + 4:c0 + 8], in1=nhalf.to_broadcast([128, 4]),
                                               op=ALU.pow), [("st2", c0 + 4), "nhalf"], [("st2", c0 + 8)])
                DVE(lambda e: e.tensor_tensor(out=onb[sa].rearrange("p (c j) -> p c j", j=128), in0=bank(ko).rearrange("p (c j) -> p c j", j=128),
                                              in1=st2[:, c0 + 8:c0 + 12].unsqueeze(2).to_broadcast([128, 4, 128]), op=ALU.mult),
                    [bk(ko), ("st2", c0 + 8)], [("T", sa, 1)])

            def epi_b(g):
                kt = 6 + g % 2
                sa = g % 2
                for j in range(4):
                    tp(bankbf(kt)[:, j * 128:(j + 1) * 128], onb[sa][:, j * 128:(j + 1) * 128], [("T", sa, 1)], [bk(kt)])
                gs = slice(g * 512, (g + 1) * 512)
                DVE(lambda e: e.scalar_tensor_tensor(out=oT[:, h, gs], in0=bankbf(kt)[:, 0:512], scalar=cols[:, C_GHG + h:C_GHG + h + 1],
                                                     in1=sgT[:, gs], op0=ALU.mult, op1=ALU.mult),
                    [bk(kt), "const", ("sgT", h % 2, g)], [("oT", h, g)])

            at_mm(0); mask_copy(0); at_mm(1); o_mm(0); mask_copy(1); at_mm(2); epi_a(0); o_mm(1); mask_copy(2); at_mm(3)
            epi_b(0); epi_a(1); o_mm(2); mask_copy(3); epi_b(1); epi_a(2); o_mm(3); epi_b(2); epi_a(3); epi_b(3)
        tap("oTa", oT[:, 0:4, :], [("oT", h, g) for h in range(4) for g in range(4)])
        S.barrier()
        R = Bump(arena, RMARK, ARENA_BYTES)
        KT = R.take([6, SEQ], BF16)
        Vaug = R.take([NT, 4, 130], BF16)
        SMARK = R.cur
        Wq = R.take([3, 768], BF16)
        Wkv = R.take([2, 1024], BF16)
        posi = R.take([NT], I32)
        posf = R.take([NT], F32)
        cosT = R.take([NT, 32], F32)
        sinT = R.take([NT, 32], F32)
        TMARK = R.cur
        ang = R.take([NT, 32], F32)
        kqi = R.take([NT, 32], I32)
        t_a = R.take([NT, 32], F32)
        t_b = R.take([NT, 32], F32)
        for m in range(3):
            S.dma("pool", lambda e, m=m: e.dma_start(out=Wq[:, m, :], in_=wq[m * 128:(m + 1) * 128, :]), "wq", writes=["Wq"], nodeps=True)
        for m in range(2):
            S.dma("pool", lambda e, m=m: e.dma_start(out=Wkv[:, m, :], in_=wkv[m * 128:(m + 1) * 128, :]), "wkv", writes=["Wkv"], nodeps=True)
        POOL(lambda e: e.memset(Vaug[:, :, :, 128:130], 1.0), [], ["Vones"])
        S.dma("sp", lambda e: e.dma_start(out=posi, in_=pos[b].rearrange("(n p) -> p n", p=128), allow_slow_non_contiguous=True),
              "posi", writes=["posi"])
        DVE(lambda e: e.tensor_copy(out=posf, in_=posi), ["posi"], ["posf"])
        DVE(lambda e: e.tensor_tensor(out=ang, in0=posf.unsqueeze(2).to_broadcast([128, NT, 32]),
                                      in1=invf_b.unsqueeze(1).to_broadcast([128, NT, 32]), op=ALU.mult), ["posf", "const"], ["ang"])
        DVE(lambda e: e.tensor_scalar(out=kqi, in0=ang, scalar1=float(1.0 / (2 * PI)), scalar2=None, op0=ALU.mult), ["ang"], ["kqi"])
        DVE(lambda e: e.tensor_copy(out=t_a, in_=kqi), ["kqi"], ["t_a"])
        C1, C2, C3 = CW1, CW2, CW3
        DVE(lambda e: e.scalar_tensor_tensor(out=t_b, in0=t_a, scalar=-C1, in1=ang, op0=ALU.mult, op1=ALU.add), ["t_a", "ang"], ["t_b"])
        DVE(lambda e: e.scalar_tensor_tensor(out=ang, in0=t_a, scalar=-C2, in1=t_b, op0=ALU.mult, op1=ALU.add), ["t_a", "t_b", "ang"], ["ang"])
        DVE(lambda e: e.scalar_tensor_tensor(out=t_b, in0=t_a, scalar=-C3, in1=ang, op0=ALU.mult, op1=ALU.add), ["t_a", "ang", "t_b"], ["t_b"])
        DVE(lambda e: e.tensor_scalar(out=ang, in0=t_b, scalar1=PI, scalar2=-PI, op0=ALU.min, op1=ALU.max), ["t_b", "ang"], ["ang"])
        ACT(lambda e: e.activation(out=sinT, in_=ang, func=AF.Sin), ["ang"], ["sinT"])
        DVE(lambda e: e.tensor_scalar(out=t_a, in0=t_b, scalar1=PI / 2, scalar2=None, op0=ALU.add), ["t_b", "t_a"], ["t_a"])
        DVE(lambda e: e.tensor_scalar(out=t_b, in0=t_a, scalar1=PI, scalar2=2 * PI, op0=ALU.is_gt, op1=ALU.mult), ["t_a", "t_b"], ["t_b"])
        DVE(lambda e: e.tensor_tensor(out=t_a, in0=t_a, in1=t_b, op=ALU.subtract), ["t_a", "t_b"], ["t_a"])
        DVE(lambda e: e.tensor_scalar(out=t_a, in0=t_a, scalar1=PI, scalar2=-PI, op0=ALU.min, op1=ALU.max), ["t_a"], ["t_a"])
        ACT(lambda e: e.activation(out=cosT, in_=t_a, func=AF.Sin), ["t_a"], ["cosT"])
        if b == 0:
            tap("cosT", cosT, ["cosT"])
            tap("sinT", sinT, ["sinT"])
        S.barrier()
        R = Bump(arena, TMARK, ARENA_BYTES)
        cT = R.take([5, 512], BF16)
        sq = R.take([5, 512], BF16)
        sqq = R.take([768], BF16)
        t1 = R.take([768], F32)
        qf = R.take([1024], BF16)
        kf = R.take([768], BF16)
        krs = R.take([64], F32)
        krr = R.take([64], F32)
        ta = R.take([256], F32)
        tb = R.take([256], F32)
        tak_ = R.take([64], F32)
        tbk_ = R.take([64], F32)
        sqk = R.take([512], F32)
        st3 = R.take([64], F32)
        junk = R.take([64], BF16)
        POOL(lambda e: e.memset(qf[:, 512:1024], 0.0), [], ["qf"])
        pending_tr = []
        for blk in range(4):
            tl = slice(blk * 512, (blk + 1) * 512)
            hk = [("hT", 4 * blk + j) for j in range(4)] + ["W_in"]
            for m in range(5):
                c0 = 2560 + m * 128
                kc_ = (m % 2) * 2
                for kc in range(8):
                    mm(bank(kc_), W_in[:, kc, c0:c0 + 128], hT[:, kc, tl], kc == 0, kc == 7, hk, [bk(kc_)])
                gcol = cols[:, C_GCQ + m:C_GCQ + m + 1]
                ACT(lambda e, m=m, gcol=gcol: e.activation(out=cT[:, m, :], in_=bank(kc_), func=AF.Copy, scale=gcol),
                    [bk(kc_), "const"], [("cT", m)])
                ACT(lambda e, m=m: e.activation(out=sq[:, m, :], in_=bank(kc_), func=AF.Square), [bk(kc_)], [("sq", m)])
            for j in range(4):
                i = blk * 4 + j
                js = slice(j * 128, (j + 1) * 128)
                its = slice(i * 128, (i + 1) * 128)
                c0 = (i % 2) * 32
                for m in range(3):
                    mm(bank(1)[:, 0:1], sq[:, m, js], ones_b[:, 0:1], m == 0, m == 2, [("sq", m), "ones_b"], [bk(1)])
                for m in range(3, 5):
                    mm(bank(1)[:, 2:3], sq[:, m, js], ones_b[:, 0:1], m == 3, m == 4, [("sq", m), "ones_b"], [bk(1)])
                for kc in range(8):
                    mm(bank(1)[:, 64:128], hT[:, kc, its], W_in[:, kc, 3200:3264], kc == 0, kc == 7, [("hT", i), "W_in"], [bk(1)])
                sc = lambda o: st3[:, c0 + o:c0 + o + 1]
                skey = lambda o: ("st3", c0 + o)
                DVE(lambda e, sc=sc: e.tensor_scalar(out=sc(0), in0=bank(1)[:, 0:1], scalar1=1.0 / 384, scalar2=EPS, op0=ALU.mult, op1=ALU.add),
                    [bk(1)], [skey(0)])
                DVE(lambda e, sc=sc: e.tensor_scalar(out=sc(1), in0=bank(1)[:, 2:3], scalar1=1.0 / 256, scalar2=EPS, op0=ALU.mult, op1=ALU.add),
                    [bk(1)], [skey(1)])
                POOL(lambda e, sc=sc: e.tensor_tensor(out=sc(2), in0=sc(0), in1=nhalf, op=ALU.pow), [skey(0), "nhalf"], [skey(2)])
                POOL(lambda e, sc=sc: e.tensor_tensor(out=sc(3), in0=sc(1), in1=nhalf, op=ALU.pow), [skey(1), "nhalf"], [skey(3)])
                for m in range(3):
                    mm(bank(3), cT[:, m, js], Wq[:, m, 0:512], m == 0, m == 2, [("cT", m), "Wq"], [bk(3)])
                for m in range(3):
                    mm(bank(4)[:, 0:256], cT[:, m, js], Wq[:, m, 512:768], m == 0, m == 2, [("cT", m), "Wq"], [bk(4)])
                for m in range(2):
                    mm(bank(5), cT[:, 3 + m, js], Wkv[:, m, 0:512], m == 0, m == 1, [("cT", 3 + m), "Wkv"], [bk(5)])
                for m in range(2):
                    mm(bank(6), cT[:, 3 + m, js], Wkv[:, m, 512:1024], m == 0, m == 1, [("cT", 3 + m), "Wkv"], [bk(6)])
                prev_tr = pending_tr
                pending_tr = []
                for pe_part, _ in prev_tr:
                    pe_part()
                DVE(lambda e: e.tensor_copy(out=krs, in_=bank(1)[:, 64:128]), [bk(1)], ["krs"])
                ACT(lambda e: e.activation(out=sqq[:, 0:512], in_=bank(3), func=AF.Square), [bk(3)], ["sqq"])
                ACT(lambda e: e.activation(out=sqq[:, 512:768], in_=bank(4)[:, 0:256], func=AF.Square), [bk(4), "sqq"], ["sqq"])
                ACT(lambda e: e.activation(out=sqk, in_=bank(5), func=AF.Square), [bk(5)], ["sqk"])
                ACT(lambda e, sc=sc: e.activation(out=junk[:, 0:64], in_=krs, func=AF.Square, accum_out=sc(13)), ["krs"], ["junk", skey(13)])
                for _, act_part in prev_tr:
                    act_part()
                DVE(lambda e, sc=sc: e.tensor_reduce(out=st3[:, c0 + 4:c0 + 8], in_=sqq[:, 0:512].rearrange("p (h j) -> p h j", j=128),
                                                     axis=AX.X, op=ALU.add), ["sqq"], [skey(4)])
                DVE(lambda e, sc=sc: e.tensor_reduce(out=st3[:, c0 + 8:c0 + 12], in_=sqq[:, 512:768].rearrange("p (h j) -> p h j", j=64),
                                                     axis=AX.X, op=ALU.add), ["sqq"], [skey(8)])
                DVE(lambda e: e.tensor_reduce(out=st3[:, c0 + 16:c0 + 20], in_=sqk.rearrange("p (h j) -> p h j", j=128),
                                              axis=AX.X, op=ALU.add), ["sqk"], [skey(16)])
                DVE(lambda e: e.tensor_tensor(out=st3[:, c0 + 4:c0 + 8], in0=st3[:, c0 + 4:c0 + 8], in1=st3[:, c0 + 8:c0 + 12], op=ALU.add),
                    [skey(4), skey(8)], [skey(4)])
                DVE(lambda e, sc=sc: e.tensor_tensor(out=sc(12), in0=sc(2), in1=sc(2), op=ALU.mult), [skey(2)], [skey(12)])
                DVE(lambda e, sc=sc: e.tensor_scalar(out=st3[:, c0 + 4:c0 + 8], in0=st3[:, c0 + 4:c0 + 8], scalar1=sc(12), scalar2=1.0 / 192,
                                                     op0=ALU.mult, op1=ALU.mult), [skey(4), skey(12)], [skey(4)])
                DVE(lambda e: e.tensor_scalar(out=st3[:, c0 + 4:c0 + 8], in0=st3[:, c0 + 4:c0 + 8], scalar1=EPS, scalar2=None, op0=ALU.add),
                    [skey(4)], [skey(4)])
                DVE(lambda e, sc=sc: e.tensor_tensor(out=sc(14), in0=sc(3), in1=sc(3), op=ALU.mult), [skey(3)], [skey(14)])
                DVE(lambda e, sc=sc: e.tensor_scalar(out=st3[:, c0 + 16:c0 + 20], in0=st3[:, c0 + 16:c0 + 20], scalar1=sc(14), scalar2=sc(13),
                                                     op0=ALU.mult, op1=ALU.add), [skey(16), skey(14), skey(13)], [skey(16)])
                DVE(lambda e: e.tensor_scalar(out=st3[:, c0 + 16:c0 + 20], in0=st3[:, c0 + 16:c0 + 20], scalar1=1.0 / 192, scalar2=EPS,
                                              op0=ALU.mult, op1=ALU.add), [skey(16)], [skey(16)])
                POOL(lambda e: e.tensor_tensor(out=st3[:, c0 + 8:c0 + 12], in0=st3[:, c0 + 4:c0 + 8],
                                               in1=nhalf.to_broadcast([128, 4]), op=ALU.pow), [skey(4), "nhalf", skey(8)], [skey(8)])
                POOL(lambda e: e.tensor_tensor(out=st3[:, c0 + 20:c0 + 24], in0=st3[:, c0 + 16:c0 + 20],
                                               in1=nhalf.to_broadcast([128, 4]), op=ALU.pow), [skey(16), "nhalf"], [skey(20)])
                DVE(lambda e, sc=sc: e.tensor_scalar(out=st3[:, c0 + 8:c0 + 12], in0=st3[:, c0 + 8:c0 + 12], scalar1=sc(2), scalar2=None,
                                                     op0=ALU.mult), [skey(8), skey(2)], [skey(8)])
                DVE(lambda e, sc=sc: e.tensor_scalar(out=st3[:, c0 + 24:c0 + 28], in0=st3[:, c0 + 20:c0 + 24], scalar1=sc(3), scalar2=None,
                                                     op0=ALU.mult), [skey(20), skey(3)], [skey(24)])
                fq = st3[:, c0 + 8:c0 + 12]
                rk_ = st3[:, c0 + 20:c0 + 24]
                fkn = st3[:, c0 + 24:c0 + 28]
                DVE(lambda e, fq=fq: e.tensor_tensor(out=qf[:, 0:512].rearrange("p (h j) -> p h j", j=128),
                                                     in0=bank(3).rearrange("p (h j) -> p h j", j=128),
                                                     in1=fq.unsqueeze(2).to_broadcast([128, 4, 128]), op=ALU.mult), [bk(3), skey(8), "qf"], ["qf"])
                DVE(lambda e, fq=fq: e.tensor_tensor(out=t1[:, 512:768].rearrange("p (h j) -> p h j", j=64),
                                                     in0=bank(4)[:, 0:256].rearrange("p (h j) -> p h j", j=64),
                                                     in1=fq.unsqueeze(2).to_broadcast([128, 4, 64]), op=ALU.mult), [bk(4), skey(8), "t1"], ["t1"])
                DVE(lambda e, fkn=fkn: e.tensor_tensor(out=kf[:, 0:512].rearrange("p (h j) -> p h j", j=128),
                                                       in0=bank(5).rearrange("p (h j) -> p h j", j=128),
                                                       in1=fkn.unsqueeze(2).to_broadcast([128, 4, 128]), op=ALU.mult),
                    [bk(5), skey(24), "kf"], ["kf"])
                ACT(lambda e, sc=sc, i=i: e.activation(out=Vaug[:, i, :, 0:128], in_=bank(6).rearrange("p (h j) -> p h j", j=128),
                                                       func=AF.Copy, scale=sc(3)), [bk(6), skey(3)], [("Vaug", i)])
                DVE(lambda e: e.tensor_tensor(out=t1[:, 512:768], in0=t1[:, 512:768], in1=gq_b[:, 512:768], op=ALU.mult), ["t1", "const"], ["t1"])

                def rope(E, src, nh_, dst, rk, wk, ta, tb, tak, tbk):
                    s4 = src.rearrange("p (h a r) -> p h a r", a=2, r=32)
                    a4 = ta[:, 0:nh_ * 64].rearrange("p (h a r) -> p h a r", a=2, r=32)
                    b4 = tb[:, 0:nh_ * 64].rearrange("p (h a r) -> p h a r", a=2, r=32)
                    cb = cosT[:, i, :].unsqueeze(1).unsqueeze(1).to_broadcast([128, nh_, 2, 32])
                    sb_ = sinT[:, i, :].unsqueeze(1).to_broadcast([128, nh_, 32])
                    E(lambda e: e.tensor_tensor(out=a4, in0=s4, in1=cb, op=ALU.mult), rk + ["cosT"], [tak])
                    if isinstance(dst, list):
                        E(lambda e: e.scalar_tensor_tensor(out=b4[:, :, 0, :], in0=s4[:, :, 1, :], scalar=-1.0, in1=sb_, op0=ALU.mult, op1=ALU.mult),
                          rk + ["sinT"], [tbk])
                        E(lambda e: e.tensor_tensor(out=b4[:, :, 1, :], in0=s4[:, :, 0, :], in1=sb_, op=ALU.mult), rk + ["sinT", tbk], [tbk])
                        a3 = ta[:, 0:nh_ * 64].rearrange("p (h j) -> p h j", j=64)
                        b3 = tb[:, 0:nh_ * 64].rearrange("p (h j) -> p h j", j=64)
                        for par, dv in enumerate(dst):
                            E(lambda e, par=par, dv=dv: e.tensor_tensor(out=dv, in0=a3[:, par::2, :], in1=b3[:, par::2, :], op=ALU.add),
                              [tak, tbk] + wk, wk)
                    else:
                        E(lambda e: e.tensor_tensor(out=b4[:, :, 0, :], in0=s4[:, :, 1, :], in1=sb_, op=ALU.mult), rk + ["sinT"], [tbk])
                        E(lambda e: e.tensor_tensor(out=b4[:, :, 1, :], in0=s4[:, :, 0, :], in1=sb_, op=ALU.mult), rk + ["sinT", tbk], [tbk])
                        E(lambda e: e.tensor_tensor(out=dst[:, 0:32], in0=ta[:, 0:32], in1=tb[:, 0:32], op=ALU.subtract), [tak, tbk] + wk, wk)
                        E(lambda e: e.tensor_tensor(out=dst[:, 32:64], in0=ta[:, 32:64], in1=tb[:, 32:64], op=ALU.add), [tak, tbk] + wk, wk)

                qz = qf[:, 512:1024].rearrange("p (i r) -> p i r", r=256)
                rope(DVE, t1[:, 512:768], 4, [qz[:, :, 0:64], qz[:, :, 192:256]], ["t1"], ["qf"], ta, tb, "ta", "tb")
                POOL(lambda e: e.tensor_tensor(out=krs, in0=krs, in1=gk_b[:, 512:576], op=ALU.mult), ["krs", "const"], ["krs"])
                rope(POOL, krs, 1, krr, ["krs"], ["krr"], tak_, tbk_, "tak", "tbk")
                POOL(lambda e, rk_=rk_: e.tensor_tensor(out=kf[:, 512:768].rearrange("p (h j) -> p h j", j=64),
                                                        in0=krr.unsqueeze(1).to_broadcast([128, 4, 64]),
                                                        in1=rk_.unsqueeze(2).to_broadcast([128, 4, 64]), op=ALU.mult),
                     ["krr", skey(20), "kf"], ["kf"])
                def mk_tr(i=i, its=its):
                    def pe_part():
                        for m in range(8):
                            tp(bankbf(7)[:, m * 128:(m + 1) * 128], qf[:, m * 128:(m + 1) * 128], ["qf"], [bk(7)])
                        for m in range(6):
                            tp(bankbf(0)[:, m * 128:(m + 1) * 128], kf[:, m * 128:(m + 1) * 128], ["kf"], [bk(0)])

                    def act_part():
                        ACT(lambda e: e.activation(out=hT[:, 0:4, its], in_=bankbf(7)[:, 0:512].rearrange("p (m t) -> p m t", t=128), func=AF.Copy,
                                                   scale=cols[:, C_GQN:C_GQN + 1]), [bk(7), "const"], [("hT", i)])
                        ACT(lambda e: e.activation(out=hT[:, 4:8, its], in_=bankbf(7)[:, 512:1024].rearrange("p (m t) -> p m t", t=128), func=AF.Copy),
                            [bk(7), ("hT", i)], [("hT", i)])
                        ACT(lambda e: e.activation(out=KT[:, 0:4, its], in_=bankbf(0)[:, 0:512].rearrange("p (m t) -> p m t", t=128), func=AF.Copy,
                                                   scale=cols[:, C_GKN:C_GKN + 1]), [bk(0), "const"], [("KT", i)])
                        ACT(lambda e: e.activation(out=KT[:, 4:6, its], in_=bankbf(0)[:, 512:768].rearrange("p (m t) -> p m t", t=128), func=AF.Copy),
                            [bk(0), ("KT", i)], [("KT", i)])
                    return pe_part, act_part
                pending_tr.append(mk_tr())
        for pe_part, act_part in pending_tr:
            pe_part()
            act_part()
        pending_tr = []
        if b == 0:
            tap("QT", hT[:, 0:6, :], [("hT", i) for i in range(NT)])
            tap("KT", KT, [("KT", i) for i in range(NT)])
            tap("Vaug", Vaug, [("Vaug", i) for i in range(NT)] + ["Vones"])
        S.barrier()
        R = Bump(arena, SMARK, ARENA_BYTES)
        W_o = R.take([8, D], BF16)
        for kc in range(8):
            S.dma("pool", lambda e, kc=kc: e.dma_start(out=W_o[:, kc, :], in_=w_out[kc * 128:(kc + 1) * 128, :]), "w_o", writes=["W_o"], nodeps=True)
        PT = [R.take([512], BF16) for _ in range(3)]
        obuf = R.take([4, 512], F32)
        obn = [R.take([512], BF16) for _ in range(2)]
        st4 = R.take([64], F32)
        junk = R.take([512], BF16)
        QT = hT
        it = 0
        npt = 0
        for qb in range(4):
            qs = slice(qb * 512, (qb + 1) * 512)
            qkeys = [("hT", 4 * qb + j) for j in range(4)]
            for h in range(4):
                ko = 2 + (it % 2) * 2
                it += 1
                DVE(lambda e, ko=ko: e.memset(pp[ko // 2][:, :], 0.0), [], [bk(ko), bk(ko + 1)])
                rp = slice((h % 2) * 64, (h % 2) * 64 + 64)
                rc = 4 + h // 2
                def qk(kc):
                    ksl = slice(kc * 128, (kc + 1) * 128)
                    ks_ = kc % 2
                    mm(bank(ks_), KT[:, h, ksl], QT[:, h, qs], True, False, [("KT", kc)] + qkeys, [bk(ks_)])
                    mm(bank(ks_), KT[:, rc, ksl], QT[:, 4 + h, qs], False, True, [("KT", kc)] + qkeys, [bk(ks_)])

                qk(0)
                for kc in range(NT):
                    ks_ = kc % 2
                    ps_ = npt % 3
                    npt += 1
                    ACT(lambda e, ks_=ks_, ps_=ps_: e.activation(out=PT[ps_], in_=bank(ks_), func=AF.Exp), [bk(ks_)], [("PT", ps_)])
                    if kc + 1 < NT:
                        qk(kc + 1)
                    for j in range(4):
                        ob = ko + j // 2
                        mm(bank(ob)[:, (j % 2) * 256:(j % 2) * 256 + 129], PT[ps_][:, j * 128:(j + 1) * 128], Vaug[:, kc, h, 0:129],
                           False, False, [("PT", ps_), ("Vaug", kc), "Vones"], [bk(ob)], skip=True)
                for j in range(4):
                    ob = ko + j // 2
                    o0 = (j % 2) * 256
                    c0 = ((it * 4 + j) % 16) * 2
                    DVE(lambda e, ob=ob, o0=o0, c0=c0: e.reciprocal(out=st4[:, c0:c0 + 1], in_=bank(ob)[:, o0 + 128:o0 + 129]),
                        [bk(ob)], [("st4", c0)])
                    DVE(lambda e, ob=ob, o0=o0, c0=c0, j=j, h=h: e.tensor_scalar(out=obuf[:, j, h * 128:(h + 1) * 128],
                                                                                 in0=bank(ob)[:, o0:o0 + 128], scalar1=st4[:, c0:c0 + 1],
                                                                                 scalar2=None, op0=ALU.mult),
                        [bk(ob), ("st4", c0)], [("obuf", j)])
            for j in range(4):
                i = qb * 4 + j
                c0 = 32 + (i % 8) * 4
                sj = i % 2
                ACT(lambda e, j=j, c0=c0: e.activation(out=junk[:, 0:512], in_=obuf[:, j, :], func=AF.Square, accum_out=st4[:, c0:c0 + 1]),
                    [("obuf", j)], ["junk", ("st4", c0)])
                DVE(lambda e, c0=c0: e.tensor_scalar(out=st4[:, c0 + 1:c0 + 2], in0=st4[:, c0:c0 + 1], scalar1=1.0 / 512, scalar2=EPS,
                                                     op0=ALU.mult, op1=ALU.add), [("st4", c0)], [("st4", c0 + 1)])
                POOL(lambda e, c0=c0: e.tensor_tensor(out=st4[:, c0 + 2:c0 + 3], in0=st4[:, c0 + 1:c0 + 2], in1=nhalf, op=ALU.pow),
                     [("st4", c0 + 1), "nhalf"], [("st4", c0 + 2)])
                DVE(lambda e, j=j, c0=c0, sj=sj: e.scalar_tensor_tensor(out=obn[sj], in0=obuf[:, j, :], scalar=st4[:, c0 + 2:c0 + 3],
                                                                        in1=gmo_b, op0=ALU.mult, op1=ALU.mult),
                    [("obuf", j), ("st4", c0 + 2), "const"], [("obn", sj)])
                kt = 6 + i % 2
                for m in range(4):
                    tp(bankbf(kt)[:, m * 128:(m + 1) * 128], obn[sj][:, m * 128:(m + 1) * 128], [("obn", sj)], [bk(kt)])
                ACT(lambda e, kt=kt, i=i: e.activation(out=oT[:, 4:8, i * 128:(i + 1) * 128],
                                                       in_=bankbf(kt)[:, 0:512].rearrange("p (m t) -> p m t", t=128), func=AF.Copy),
                    [bk(kt)], [("oT", 4, i)])
        tap("oT", oT, [("oT", 4, i) for i in range(NT)])
        S.barrier()
        R = Bump(arena, RMARK, SMARK)
        xin = [R.take([D], F32) for _ in range(2)]
        x1o = [R.take([D], F32) for _ in range(2)]
        nxt_seq = b + 1 < nseq
        if nxt_seq:
            xinA = [R.take([D], F32) for _ in range(2)]
            hbfA = [R.take([D], BF16) for _ in range(2)]
            junkA = R.take([D], BF16)
            stA = R.take([64], F32)
        for i in range(NT):
            sl = i % 2
            its = slice(i * 128, (i + 1) * 128)
            S.dma("sp", lambda e, sl=sl, i=i: e.dma_start(out=xin[sl], in_=x[b, i * 128:(i + 1) * 128, :]), f"xin{sl}",
                  writes=[("xin", sl)])
            kp = (i % 2) * 2
            for half in range(2):
                for m in range(8):
                    mm(bank(kp + half), oT[:, m, its], W_o[:, m, half * 512:(half + 1) * 512], m == 0, m == 7, [], [bk(kp + half)])
            DVE(lambda e, sl=sl, kp=kp: e.tensor_tensor(out=x1o[sl], in0=pp[kp // 2][:, :], in1=xin[sl], op=ALU.add),
                [bk(kp), bk(kp + 1), ("xin", sl), ("x1o", sl)], [("x1o", sl)])
            S.dma("pool", lambda e, sl=sl, i=i: e.dma_start(out=x1s[b, i * 128:(i + 1) * 128, :], in_=x1o[sl]), f"x1o{sl}",
                  reads=[("x1o", sl)])
            if nxt_seq:
                a0_tile(b + 1, i, xinA, hbfA, junkA, stA, 4)
        S.barrier()

    Bm = Bump(arena, MARK0, ARENA_BYTES)
    W_up = Bm.take([8, DFF], BF16)
    W_dn = Bm.take([32, D], BF16)
    gffn_b = Bm.take([D], F32)
    TB = 256
    NJ = TB // 128
    x1t = [[Bm.take([D], F32) for _ in range(NJ)] for _ in range(2)]
    hbf = [Bm.take([D], BF16) for _ in range(2)]
    h2Ts = [Bm.take([8, TB], BF16) for _ in range(2)]
    aT = Bm.take([32, TB], BF16)
    rl = [Bm.take([512], F32) for _ in range(2)]
    yo = [Bm.take([D], F32) for _ in range(2)]
    st5 = Bm.take([64], F32)
    S.dma("sp", lambda e: e.dma_start(out=gffn_b, in_=g_ffn.partition_broadcast(128)), "const2", writes=["gffn"])
    for kc in range(8):
        for q4 in range(4):
            S.dma("pool", lambda e, kc=kc, q4=q4: e.dma_start(out=W_up[:, kc, q4 * 1024:(q4 + 1) * 1024],
                                                             in_=w_up[kc * 128:(kc + 1) * 128, q4 * 1024:(q4 + 1) * 1024]),
                  "w_up", writes=["W_up"], nodeps=True)
    for c in range(32):
        S.dma("pool", lambda e, c=c: e.dma_start(out=W_dn[:, c, :], in_=w_down[c * 128:(c + 1) * 128, :]), "w_dn", writes=["W_dn"], nodeps=True)
    x1f = x1s.rearrange("b s d -> (b s) d")
    yf = y.rearrange("b s d -> (b s) d")
    nblk = nseq * SEQ // TB
    nyo = 0
    def norm_part(blk):
        xs = blk % 2
        for j in range(NJ):
            t = blk * NJ + j
            S.dma("sp", lambda e: e.dma_start(out=x1t[xs][j], in_=x1f[t * 128:(t + 1) * 128, :]), f"x1t{xs}{j}", writes=[("x1t", xs, j)])
            c0 = (t % 8) * 4
            sl = t % 2
            ACT(lambda e: e.activation(out=hbf[sl], in_=x1t[xs][j], func=AF.Square, accum_out=st5[:, c0:c0 + 1]),
                [("x1t", xs, j)], [("hbf", sl), ("st5", c0)])
            DVE(lambda e: e.tensor_scalar(out=st5[:, c0 + 1:c0 + 2], in0=st5[:, c0:c0 + 1], scalar1=1.0 / D, scalar2=EPS,
                                          op0=ALU.mult, op1=ALU.add), [("st5", c0)], [("st5", c0 + 1)])
            POOL(lambda e: e.tensor_tensor(out=st5[:, c0 + 2:c0 + 3], in0=st5[:, c0 + 1:c0 + 2], in1=nhalf, op=ALU.pow),
                 [("st5", c0 + 1), "nhalf"], [("st5", c0 + 2)])
            DVE(lambda e: e.scalar_tensor_tensor(out=hbf[sl], in0=x1t[xs][j], scalar=st5[:, c0 + 2:c0 + 3], in1=gffn_b,
                                                 op0=ALU.mult, op1=ALU.mult), [("x1t", xs, j), ("st5", c0 + 2), "gffn"], [("hbf", sl)])

    def tr_part(blk):
        h2T_ = h2Ts[blk % 2]
        for j in range(NJ):
            t = blk * NJ + j
            sl = t % 2
            for kc in range(8):
                tp(bankbf(0)[:, kc * 128:(kc + 1) * 128], hbf[sl][:, kc * 128:(kc + 1) * 128], [("hbf", sl)], [bk(0)])
            ACT(lambda e: e.activation(out=h2T_[:, :, j * 128:(j + 1) * 128], in_=bankbf(0).rearrange("p (k t) -> p k t", t=128),
                                       func=AF.Copy), [bk(0)], [("h2T", blk % 2, j)])

    norm_part(0)
    tr_part(0)
    for blk in range(nblk):
        xs = blk % 2
        h2T = h2Ts[blk % 2]
        hk2 = [("h2T", blk % 2, j) for j in range(NJ)] + ["W_up"]
        for cp in range(16):
            ku = 1 + cp % 2
            for c in range(2):
                cc = cp * 2 + c
                for kc in range(8):
                    mm(bank(ku)[:, c * TB:(c + 1) * TB], W_up[:, kc, cc * 128:(cc + 1) * 128], h2T[:, kc, :], kc == 0, kc == 7, hk2, [bk(ku)])
            rs = cp % 2
            ACT(lambda e, ku=ku, rs=rs: e.activation(out=rl[rs], in_=bank(ku), func=AF.Relu), [bk(ku)], [("rl", rs)])
            POOL(lambda e, rs=rs, cp=cp: e.tensor_tensor(out=aT[:, 2 * cp:2 * cp + 2, :], in0=rl[rs].rearrange("p (c t) -> p c t", t=TB),
                                                         in1=rl[rs].rearrange("p (c t) -> p c t", t=TB), op=ALU.mult),
                 [("rl", rs)], [("aT", cp)])
            if cp == 7 and blk + 1 < nblk:
                norm_part(blk + 1)
        if blk + 1 < nblk:
            tr_part(blk + 1)
        for j in range(NJ):
            t = blk * NJ + j
            kd = 4 + (t % 2) * 2
            for half in range(2):
                for c in range(32):
                    mm(bank(kd + half), aT[:, c, j * 128:(j + 1) * 128], W_dn[:, c, half * 512:(half + 1) * 512], c == 0, c == 31,
                       [("aT", c // 2), "W_dn"], [bk(kd + half)])
            ys = nyo % 2
            nyo += 1
            DVE(lambda e, ys=ys, kd=kd, xs=xs, j=j: e.tensor_tensor(out=yo[ys], in0=pp[kd // 2][:, :], in1=x1t[xs][j], op=ALU.add),
                [bk(kd), bk(kd + 1), ("x1t", xs, j), ("yo", ys)], [("yo", ys)])
            S.dma("pool", lambda e, ys=ys, t=t: e.dma_start(out=yf[t * 128:(t + 1) * 128, :], in_=yo[ys]), f"yo{ys}", reads=[("yo", ys)])
    S.barrier()

    sems = {n: es.enter_context(nc.semaphore(n)) for n in sorted(S.sem_names)}
    with nc.Block() as block:
        S.emit(block, sems)
    es.close()
    return nc, dbg_out


def _prep_inputs(inputs, nseq, ncores):
    f32 = np.float32
    g = lambda k: np.ascontiguousarray(np.asarray(inputs[k]))
    wq_ = g("w_q_up")[0].reshape(384, 4, 192)
    wq_p = np.concatenate([wq_[:, :, :128].reshape(384, 512), wq_[:, :, 128:].reshape(384, 256)], axis=1)
    wkv_ = g("w_kv_up")[0].reshape(256, 4, 256)
    wkv_p = np.concatenate([wkv_[:, :, :128].reshape(256, 512), wkv_[:, :, 128:].reshape(256, 512)], axis=1)
    invf = (10000.0 ** (-(np.arange(0, 64, 2, dtype=f32)) / f32(64))).astype(f32).reshape(1, 32)
    shared = {
        "g_mix": g("g_mix_norm").reshape(1, D), "w_in": g("w_in")[0], "lb_param": g("lb_param")[:, 0:2, :],
        "g_hg": g("g_hgrn_out")[0], "g_cq": g("g_cq").reshape(1, 384), "wq": np.ascontiguousarray(wq_p),
        "g_ckv": g("g_ckv").reshape(1, 256), "wkv": np.ascontiguousarray(wkv_p), "g_q": g("g_q_norm").reshape(1, 192),
        "g_k": g("g_k_norm").reshape(1, 192), "g_mo": g("g_mla_out").reshape(1, 512), "w_out": g("w_out")[0],
        "g_ffn": g("g_ffn_norm").reshape(1, D), "w_up": g("w_up")[0], "w_down": g("w_down")[0], "invf": invf,
    }
    x = g("x")
    pos = g("positions").astype(np.int32)
    maps = []
    for c in range(ncores):
        m = dict(shared)
        m["x"] = np.ascontiguousarray(x[c * nseq:(c + 1) * nseq])
        m["pos"] = np.ascontiguousarray(pos[c * nseq:(c + 1) * nseq])
        maps.append(m)
    return maps


def kernel(**inputs):
    nseq = 4
    nc, _ = build(nseq)
    maps = _prep_inputs(inputs, nseq, NCORES)
    res = run_bass_kernel_spmd(nc, maps, core_ids=list(range(NCORES)))
    out = np.concatenate([np.asarray(r["y"]) for r in res.results], axis=0)
    return out.astype(np.float32, copy=False)
```
